# Optimizing a Trainium2 kernel written in Bass

```python
import math
import jax, jax.numpy as jnp
from jax import lax
import numpy as np

D_MODEL = 1024
BATCH = 8
SEQ = 4096
DEPTH = 2
DEC_BATCH = 32
DEC_SEQ = 16
PAST_LEN = 4096

CHUNK = 64
H_A = 8
KV_A = 2
DH_A = 64
GROUP_A = H_A // KV_A
H_I = 8
D_IDX = 64
TOPK_MAX = 256
Q_BLOCK = 64
INDEX_SCALE = (H_I * D_IDX) ** -0.5
H_B = 4
DK_B = 128
DV_B = 128
CONV_W = 4
CONV_DIM = 2 * H_B * DK_B + H_B * DV_B
D_FF = int(math.ceil(8 * D_MODEL / 3 / 256)) * 256
DEEPNORM_ALPHA = (2 * DEPTH) ** 0.25
DEEPNORM_BETA = (8 * DEPTH) ** -0.25
LN_EPS = 1e-5
NORM_EPS = 1e-6
IN_SIZES = (H_A * DH_A, KV_A * DH_A, KV_A * DH_A, H_I * D_IDX, D_IDX, H_I, CONV_DIM, H_B, H_B, H_B * DV_B, 2 * D_MODEL)
D_IN = sum(IN_SIZES)

kernel_name = "dsa_gated_deltanet_hybrid_stream_step"


def layer_norm(x, g, b):
    xf = x.astype(jnp.float32)
    mu = jnp.mean(xf, axis=-1, keepdims=True)
    var = jnp.mean(jnp.square(xf - mu), axis=-1, keepdims=True)
    y = (xf - mu) * lax.rsqrt(var + LN_EPS) * g.astype(jnp.float32) + b.astype(jnp.float32)
    return y.astype(x.dtype)


def l2_normalize(x):
    xf = x.astype(jnp.float32)
    return xf * lax.rsqrt(jnp.sum(xf * xf, axis=-1, keepdims=True) + NORM_EPS)


def dsa_attention(q, q_idx, w_idx, k_all, v_all, kidx_all, past_len):
    B, T = q.shape[0], q.shape[1]
    L = k_all.shape[1]
    topk = min(TOPK_MAX, L // 4)
    qb = min(Q_BLOCK, T)
    nb = T // qb
    kidx_f = kidx_all.astype(jnp.float32)
    k_chunk = jnp.arange(L, dtype=jnp.int32) // CHUNK
    q_pos = past_len + jnp.arange(T, dtype=jnp.int32)

    def to_blocks(a):
        return jnp.moveaxis(a.reshape((B, nb, qb) + a.shape[2:]), 1, 0)

    def attend_block(args):
        qs, qis, wis, qpos = args
        q_chunk = qpos // CHUNK
        admissible = k_chunk[None, :] <= q_chunk[:, None]
        rel = jax.nn.relu(jnp.einsum("bthd,bsd->bths", qis.astype(jnp.float32), kidx_f))
        score = jnp.einsum("bth,bths->bts", wis.astype(jnp.float32) * INDEX_SCALE, rel)
        score = jnp.where(admissible[None], score, -jnp.inf)
        _, idx = lax.top_k(score, topk)
        valid = (idx // CHUNK) <= q_chunk[None, :, None]
        k_sel = jax.vmap(lambda kk, ii: kk[ii])(k_all, idx)
        v_sel = jax.vmap(lambda vv, ii: vv[ii])(v_all, idx)
        qg = qs.reshape(B, qb, KV_A, GROUP_A, DH_A)
        s = jnp.einsum("btgrd,btjgd->btgrj", qg, k_sel).astype(jnp.float32) * (DH_A ** -0.5)
        s = jnp.where(valid[:, :, None, None, :], s, -jnp.inf)
        p = jax.nn.softmax(s, axis=-1).astype(v_sel.dtype)
        o = jnp.einsum("btgrj,btjgd->btgrd", p, v_sel)
        return o.reshape(B, qb, H_A * DH_A)

    out = lax.map(attend_block, (to_blocks(q), to_blocks(q_idx), to_blocks(w_idx), q_pos.reshape(nb, qb)))
    return jnp.moveaxis(out, 0, 1).reshape(B, T, H_A * DH_A)


def gated_delta_rule(q, k, v, g, beta, s0):
    B, T, H = g.shape
    C = min(CHUNK, T)
    N = T // C

    def blocks(a):
        return jnp.moveaxis(a.reshape(B, N, C, H, a.shape[-1]), 3, 1).astype(jnp.float32)

    q, k, v = blocks(q), blocks(k), blocks(v)
    g = blocks(g[..., None])[..., 0]
    beta = blocks(beta[..., None])[..., 0]
    G = jnp.cumsum(g, axis=-1)
    causal = jnp.tril(jnp.ones((C, C), dtype=bool))
    strict = jnp.tril(jnp.ones((C, C), dtype=bool), -1)
    diff = G[..., :, None] - G[..., None, :]
    decay = jnp.where(causal, jnp.exp(jnp.where(causal, diff, 0.0)), 0.0)
    a_mat = jnp.where(strict, beta[..., None] * jnp.einsum("bhncd,bhnsd->bhncs", k, k) * decay, 0.0)
    rhs = jnp.concatenate([v * beta[..., None], k * (beta * jnp.exp(G))[..., None]], axis=-1)
    sol = lax.linalg.triangular_solve(a_mat, rhs, left_side=True, lower=True, unit_diagonal=True)
    u, w = sol[..., :DV_B], sol[..., DV_B:]
    qk = jnp.einsum("bhncd,bhnsd->bhncs", q, k) * decay
    q_dec = q * jnp.exp(G)[..., None]
    k_dec = k * jnp.exp(G[..., -1:] - G)[..., None]
    g_tot = jnp.exp(G[..., -1])

    def step(S, inp):
        qd, kd, uc, wc, qkc, gt = inp
        v_new = uc - jnp.einsum("bhcd,bhde->bhce", wc, S)
        o = jnp.einsum("bhcd,bhde->bhce", qd, S) + jnp.einsum("bhcs,bhse->bhce", qkc, v_new)
        S = S * gt[..., None, None] + jnp.einsum("bhcd,bhce->bhde", kd, v_new)
        return S, o

    xs = tuple(jnp.moveaxis(a, 2, 0) for a in (q_dec, k_dec, u, w, qk, g_tot))
    s_fin, o = lax.scan(step, s0.astype(jnp.float32), xs)
    o = jnp.transpose(o, (1, 0, 3, 2, 4)).reshape(B, T, H, DV_B)
    return o, s_fin.astype(s0.dtype)


def trunk_layer(x, past_k, past_v, past_kidx, conv_state, ssm_state,
                w_in, b_gate, conv_w, a_log, dt_bias, gdn_norm_w, w_proj_a, w_proj_b, w_out,
                ln1_g, ln1_b, w_up, w_down, ln2_g, ln2_b):
    B, T, _ = x.shape
    past_len = past_k.shape[1]
    f32 = jnp.float32
    h = jnp.einsum("btd,de->bte", x, w_in)
    q_a, k_a, v_a, q_i, k_i, w_i, qkv_b, a_b, b_b, gate_b, gates = jnp.split(
        h, np.cumsum(IN_SIZES)[:-1].tolist(), axis=-1)

    k_a = k_a.reshape(B, T, KV_A, DH_A)
    v_a = v_a.reshape(B, T, KV_A, DH_A)
    k_all = jnp.concatenate([past_k.astype(x.dtype), k_a], axis=1)
    v_all = jnp.concatenate([past_v.astype(x.dtype), v_a], axis=1)
    kidx_all = jnp.concatenate([past_kidx.astype(x.dtype), k_i], axis=1)
    y_a = dsa_attention(q_a.reshape(B, T, H_A, DH_A), q_i.reshape(B, T, H_I, D_IDX), w_i,
                        k_all, v_all, kidx_all, past_len)

    conv_in = jnp.concatenate([conv_state.astype(x.dtype), qkv_b], axis=1)
    conv_out = conv_in[:, 0:T] * conv_w[0]
    for j in range(1, CONV_W):
        conv_out = conv_out + conv_in[:, j:j + T] * conv_w[j]
    new_conv = conv_in[:, T:]
    qkv = jax.nn.silu(conv_out)
    q_b, k_b, v_b = jnp.split(qkv, [H_B * DK_B, 2 * H_B * DK_B], axis=-1)
    q_b = l2_normalize(q_b.reshape(B, T, H_B, DK_B)) * (DK_B ** -0.5)
    k_b = l2_normalize(k_b.reshape(B, T, H_B, DK_B))
    v_b = v_b.reshape(B, T, H_B, DV_B)
    g = -jnp.exp(a_log.astype(f32)) * jax.nn.softplus(a_b.astype(f32) + dt_bias.astype(f32))
    beta = jax.nn.sigmoid(b_b.astype(f32))
    o_b, new_ssm = gated_delta_rule(q_b, k_b, v_b, g, beta, ssm_state)
    o_b = o_b * lax.rsqrt(jnp.mean(jnp.square(o_b), axis=-1, keepdims=True) + NORM_EPS) * gdn_norm_w.astype(f32)
    o_b = o_b * jax.nn.silu(gate_b.reshape(B, T, H_B, DV_B).astype(f32))
    y_b = o_b.reshape(B, T, H_B * DV_B).astype(x.dtype)

    gate_a, gate_bb = jnp.split(jax.nn.sigmoid((gates + b_gate).astype(f32)), 2, axis=-1)
    mixed = (gate_a * jnp.einsum("bte,ed->btd", y_a, w_proj_a).astype(f32)
             + gate_bb * jnp.einsum("bte,ed->btd", y_b, w_proj_b).astype(f32))
    mix_out = jnp.einsum("btd,de->bte", mixed.astype(x.dtype), w_out)
    x = layer_norm(DEEPNORM_ALPHA * x + mix_out, ln1_g, ln1_b)

    f_gate, f_up = jnp.split(jnp.einsum("btd,df->btf", x, w_up), 2, axis=-1)
    ffn = jnp.einsum("btf,fd->btd", jax.nn.silu(f_gate) * f_up, w_down)
    x = layer_norm(DEEPNORM_ALPHA * x + ffn, ln2_g, ln2_b)
    return x, k_a, v_a, k_i, new_conv, new_ssm


def stack_layer_states(states, i):
    return jnp.stack([s[i] for s in states], axis=0)


def setup_inputs(seed: int = 0) -> dict:
    key = jax.random.key(seed)
    ks = jax.random.split(key, 22)
    f32 = jnp.float32

    def nrm(k, shape, scale):
        return jax.random.normal(k, shape, f32) * scale

    offs = np.cumsum((0,) + IN_SIZES)
    col_scale = np.ones((D_IN,), np.float32)
    col_scale[offs[2]:offs[3]] = DEEPNORM_BETA
    w_in = nrm(ks[7], (DEPTH, D_MODEL, D_IN), D_MODEL ** -0.5) * jnp.asarray(col_scale)
    a_log = jnp.log(jax.random.uniform(ks[10], (DEPTH, H_B), f32, minval=1.0, maxval=16.0))
    dt = jnp.exp(jax.random.uniform(ks[11], (DEPTH, H_B), f32, minval=math.log(1e-3), maxval=math.log(1e-1)))
    dt_bias = dt + jnp.log(-jnp.expm1(-dt))
    return {
        "x_prompt": nrm(ks[0], (BATCH, SEQ, D_MODEL), 1.0),
        "x_sample": nrm(ks[1], (DEC_BATCH, DEC_SEQ, D_MODEL), 1.0),
        "cache_k": nrm(ks[2], (DEPTH, DEC_BATCH, PAST_LEN, KV_A, DH_A), 1.0),
        "cache_v": nrm(ks[3], (DEPTH, DEC_BATCH, PAST_LEN, KV_A, DH_A), DEEPNORM_BETA),
        "cache_kidx": nrm(ks[4], (DEPTH, DEC_BATCH, PAST_LEN, D_IDX), 1.0),
        "state_conv": nrm(ks[5], (DEPTH, DEC_BATCH, CONV_W - 1, CONV_DIM), 1.0),
        "state_ssm": nrm(ks[6], (DEPTH, DEC_BATCH, H_B, DK_B, DV_B), 0.1),
        "w_in": w_in,
        "b_gate": nrm(ks[8], (DEPTH, 2 * D_MODEL), 0.02),
        "conv_w": nrm(ks[9], (DEPTH, CONV_W, CONV_DIM), CONV_W ** -0.5),
        "a_log": a_log,
        "dt_bias": dt_bias,
        "gdn_norm_w": 1.0 + nrm(ks[12], (DEPTH, DV_B), 0.02),
        "w_proj_a": nrm(ks[13], (DEPTH, H_A * DH_A, D_MODEL), (H_A * DH_A) ** -0.5),
        "w_proj_b": nrm(ks[14], (DEPTH, H_B * DV_B, D_MODEL), (H_B * DV_B) ** -0.5),
        "w_out": nrm(ks[15], (DEPTH, D_MODEL, D_MODEL), DEEPNORM_BETA * D_MODEL ** -0.5),
        "ln1_g": 1.0 + nrm(ks[16], (DEPTH, D_MODEL), 0.02),
        "ln1_b": nrm(ks[17], (DEPTH, D_MODEL), 0.02),
        "w_up": nrm(ks[18], (DEPTH, D_MODEL, 2 * D_FF), D_MODEL ** -0.5),
        "w_down": nrm(ks[19], (DEPTH, D_FF, D_MODEL), DEEPNORM_BETA * D_FF ** -0.5),
        "ln2_g": 1.0 + nrm(ks[20], (DEPTH, D_MODEL), 0.02),
        "ln2_b": nrm(ks[21], (DEPTH, D_MODEL), 0.02),
    }


def reference(x_prompt, x_sample, cache_k, cache_v, cache_kidx, state_conv, state_ssm,
              w_in, b_gate, conv_w, a_log, dt_bias, gdn_norm_w, w_proj_a, w_proj_b, w_out,
              ln1_g, ln1_b, w_up, w_down, ln2_g, ln2_b):
    yp = x_prompt
    ys = x_sample
    bp = x_prompt.shape[0]
    dt = x_prompt.dtype
    st_p = []
    st_s = []
    for l in range(DEPTH):
        prm = (w_in[l], b_gate[l], conv_w[l], a_log[l], dt_bias[l], gdn_norm_w[l], w_proj_a[l], w_proj_b[l],
               w_out[l], ln1_g[l], ln1_b[l], w_up[l], w_down[l], ln2_g[l], ln2_b[l])
        empty_kv = jnp.zeros((bp, 0, KV_A, DH_A), dt)
        yp, *sp = trunk_layer(yp, empty_kv, empty_kv, jnp.zeros((bp, 0, D_IDX), dt),
                              jnp.zeros((bp, CONV_W - 1, CONV_DIM), dt),
                              jnp.zeros((bp, H_B, DK_B, DV_B), dt), *prm)
        ys, *ss = trunk_layer(ys, cache_k[l], cache_v[l], cache_kidx[l], state_conv[l], state_ssm[l], *prm)
        st_p.append(sp)
        st_s.append(ss)
    new_k_prompt = stack_layer_states(st_p, 0)
    new_v_prompt = stack_layer_states(st_p, 1)
    new_kidx_prompt = stack_layer_states(st_p, 2)
    new_conv_prompt = stack_layer_states(st_p, 3)
    new_ssm_prompt = stack_layer_states(st_p, 4)
    new_k_sample = stack_layer_states(st_s, 0)
    new_v_sample = stack_layer_states(st_s, 1)
    new_kidx_sample = stack_layer_states(st_s, 2)
    new_conv_sample = stack_layer_states(st_s, 3)
    new_ssm_sample = stack_layer_states(st_s, 4)
    return (yp, ys, new_k_prompt, new_v_prompt, new_kidx_prompt, new_conv_prompt, new_ssm_prompt,
            new_k_sample, new_v_sample, new_kidx_sample, new_conv_sample, new_ssm_sample)
```

```python
import math
from contextlib import ExitStack
import numpy as np
import concourse.bass as bass
import concourse.mybir as mybir
from concourse.bass_utils import run_bass_kernel_spmd

F32 = mybir.dt.float32
BF16 = mybir.dt.bfloat16
AF = mybir.ActivationFunctionType
ALU = mybir.AluOpType
AX = mybir.AxisListType

D = 1024
T = 4096
TS = 16
NSQ = 4
NTOK = T + NSQ * TS
DEPTH = 2
DIN = 5456
DFF = 2816
O_QA, O_KA, O_VA, O_QI, O_KI, O_WI, O_QKV, O_AB, O_BB, O_GB, O_GATES = 0, 512, 640, 768, 1280, 1344, 1352, 2888, 2892, 2896, 3408
ALPHA = (2 * DEPTH) ** 0.25
INDEX_SCALE = (8 * 64) ** -0.5
LN_EPS = 1e-5
NORM_EPS = 1e-6
NBIS = 16
NEG = -30000.0
KCOLS = 33 * 128


class Tok:
    __slots__ = ("sem", "val", "key")

    def __init__(self, sem, val, key):
        self.sem = sem
        self.val = val
        self.key = key


class Eng:
    def __init__(self, nc, name, h):
        self.name = name
        self.key = name
        self.h = h
        self.sem = nc.alloc_semaphore("e_" + name)
        self.cnt = 0
        self.waited = {}

    def wait(self, tok):
        if tok is None:
            return
        if self.waited.get(tok.key, 0) >= tok.val:
            return
        self.h.wait_ge(tok.sem, tok.val)
        self.waited[tok.key] = tok.val


class TT:
    def __init__(self, t, name, sbuf=True):
        self.t = t
        self.name = name
        self.sbuf = sbuf
        self.w = None
        self.r = {}
        self.dsem = None

    def __getitem__(self, idx):
        return self.t[idx]


class K:
    def __init__(self, nc):
        self.nc = nc
        self.pe = Eng(nc, "pe", nc.tensor)
        self.act = Eng(nc, "act", nc.scalar)
        self.dve = Eng(nc, "dve", nc.vector)
        self.pool = Eng(nc, "pool", nc.gpsimd)
        self.sp = Eng(nc, "sp", nc.sync)
        self.engs = [self.pe, self.act, self.dve, self.pool, self.sp]
        self.dsem_pool = []
        self.ndsem = 0
        self.out_toks = {}
        self.pass_tiles = []
        self.es = None
        self.uid = 0
        self.ps = [TT(nc.alloc_psum_tensor(f"psb{i}", [128, 512], F32), f"psb{i}") for i in range(8)]

    def begin_pass(self):
        self.es = ExitStack()
        self.pass_tiles = []

    def sb(self, name, shape, dt):
        self.uid += 1
        t = self.es.enter_context(self.nc.sbuf_tensor(f"{name}_{self.uid}", list(shape), dt))
        tt = TT(t, name)
        self.pass_tiles.append(tt)
        return tt

    def get_dsem(self):
        if self.dsem_pool:
            return self.dsem_pool.pop()
        self.ndsem += 1
        return [self.nc.alloc_semaphore(f"d{self.ndsem}"), 0]

    def barrier(self, tiles):
        toks = [Tok(e.sem, e.cnt, e.key) for e in self.engs if e.cnt > 0]
        seen = set()
        for tt in tiles:
            if tt.dsem is not None and tt.dsem[1] > 0 and id(tt.dsem) not in seen:
                seen.add(id(tt.dsem))
                toks.append(Tok(tt.dsem[0], 16 * tt.dsem[1], ("d", id(tt.dsem))))
        for e in self.engs:
            for tok in toks:
                if tok.key != e.key:
                    e.wait(tok)

    def end_pass(self):
        self.barrier(self.pass_tiles)
        for tt in self.pass_tiles:
            if tt.dsem is not None:
                self.dsem_pool.append(tt.dsem)
                tt.dsem = None
        self.es.close()
        self.es = None
        self.pass_tiles = []

    def op(self, e, fn, outs=(), ins=(), inc=True):
        for t in ins:
            e.wait(t.w)
        strict = e.key != "pe"
        for t in outs:
            if t.w is not None and (strict or t.w.key != e.key):
                e.wait(t.w)
            for kk, tok in t.r.items():
                if strict or kk != e.key:
                    e.wait(tok)
        inst = fn()
        if inc:
            inst.then_inc(e.sem, 1)
            e.cnt += 1
            tok = Tok(e.sem, e.cnt, e.key)
        else:
            tok = Tok(e.sem, e.cnt + 1, e.key)
        for t in outs:
            t.w = tok
            t.r = {}
        for t in ins:
            if t not in outs:
                t.r[e.key] = tok
        return inst

    def dma(self, q, out_ap, in_ap, out_t=None, in_t=None, is_output=False):
        if in_t is not None:
            q.wait(in_t.w)
        if out_t is not None:
            q.wait(out_t.w)
            for kk, tok in out_t.r.items():
                q.wait(tok)
        own = out_t if (out_t is not None and out_t.sbuf) else in_t
        if own.dsem is None:
            own.dsem = self.get_dsem()
        ds = own.dsem
        inst = q.h.dma_start(out=out_ap, in_=in_ap)
        ds[1] += 1
        inst.then_inc(ds[0], 16)
        tok = Tok(ds[0], 16 * ds[1], ("d", id(ds)))
        if out_t is not None:
            out_t.w = tok
            out_t.r = {}
        if in_t is not None:
            in_t.r[tok.key] = tok
        if is_output:
            self.out_toks[tok.key] = tok

    def finish(self):
        for tok in self.out_toks.values():
            self.sp.wait(tok)
        self.barrier([])

    def mm(self, out_t, out_ap, a_t, a_ap, b_t, b_ap, start=True, stop=True, inc=None):
        nc = self.nc
        ins = [a_t, b_t] if b_t is not a_t else [a_t]
        return self.op(self.pe, lambda: nc.tensor.matmul(out_ap, a_ap, b_ap, start=start, stop=stop),
                       outs=[out_t], ins=ins, inc=(stop if inc is None else inc))

    def tr(self, out_t, out_ap, a_t, a_ap, id_t, id_ap, inc=True):
        nc = self.nc
        return self.op(self.pe, lambda: nc.tensor.transpose(out_ap, a_ap, id_ap), outs=[out_t], ins=[a_t, id_t], inc=inc)

    def actv(self, out_t, out_ap, in_t, in_ap, func, bias=None, scale=None, accum=None, extra_ins=(), eng=None):
        nc = self.nc
        kw = {}
        if bias is not None:
            kw["bias"] = bias
        if scale is not None:
            kw["scale"] = scale
        outs = [out_t]
        if accum is not None:
            kw["accum_out"] = accum[1]
            outs.append(accum[0])
        return self.op(self.act, lambda: nc.scalar.activation(out_ap, in_ap, func, **kw), outs=outs,
                       ins=[in_t] + list(extra_ins))

    def ts(self, e, out_t, out_ap, in_t, in_ap, s1, s2, op0, op1=None, accum=None, extra_ins=()):
        outs = [out_t]
        kw = {}
        if op1 is not None:
            kw["op1"] = op1
        if accum is not None:
            kw["accum_out"] = accum[1]
            outs.append(accum[0])
        return self.op(e, lambda: e.h.tensor_scalar(out_ap, in_ap, s1, s2, op0, **kw), outs=outs,
                       ins=[in_t] + list(extra_ins))

    def tt(self, e, out_t, out_ap, a_t, a_ap, b_t, b_ap, op):
        ins = [a_t, b_t] if b_t is not a_t else [a_t]
        return self.op(e, lambda: e.h.tensor_tensor(out_ap, a_ap, b_ap, op), outs=[out_t], ins=ins)

    def stt(self, e, out_t, out_ap, a_t, a_ap, scalar, b_t, b_ap, op0, op1, extra_ins=()):
        ins = [a_t] + ([b_t] if b_t is not a_t else []) + list(extra_ins)
        return self.op(e, lambda: e.h.scalar_tensor_tensor(out_ap, a_ap, scalar, b_ap, op0, op1), outs=[out_t], ins=ins)

    def cp(self, e, out_t, out_ap, in_t, in_ap):
        if e is self.act:
            nc = self.nc
            return self.op(e, lambda: nc.scalar.copy(out_ap, in_ap), outs=[out_t], ins=[in_t])
        return self.op(e, lambda: e.h.tensor_copy(out_ap, in_ap), outs=[out_t], ins=[in_t])

    def memset(self, e, out_t, out_ap, val):
        return self.op(e, lambda: e.h.memset(out_ap, val), outs=[out_t], ins=[])


def make_consts():
    c = {}
    c["ident"] = np.eye(128, dtype=np.float32)
    c["ones"] = np.ones((128, 128), np.float32)
    i2 = np.zeros((128, 256), np.float32)
    i2[:, 0:128] = np.eye(128)
    i2[:, 128:256] = np.eye(128)
    c["i2"] = i2
    c["i4"] = np.tile(np.eye(128, dtype=np.float32), (1, 4))
    c["i4s"] = np.tile(np.eye(16, dtype=np.float32), (1, 4))
    i2s = np.zeros((16, 32), np.float32)
    i2s[:, 0:16] = np.eye(16)
    i2s[:, 16:32] = np.eye(16)
    c["i2s"] = i2s
    tt_, cc_ = np.meshgrid(np.arange(64), np.arange(64), indexing="ij")
    c["U"] = (tt_ <= cc_).astype(np.float32)
    c["Urev"] = (tt_ > cc_).astype(np.float32)
    c["maskT"] = np.where(cc_ >= tt_, 0.0, NEG).astype(np.float32)
    c["su01"] = (cc_ > tt_).astype(np.float32)
    c["negones"] = -np.ones((128, 128), np.float32)
    c["pow2"] = np.tile((2.0 ** -(np.arange(32) + 1.0)).astype(np.float32)[None, :], (128, 1))
    return c


class Prog:
    def __init__(self, debug=False, stop_after=None):
        self.debug = debug
        self.stop_after = stop_after
        nc = bass.Bass("TRN2", target_bir_lowering=False)
        self.nc = nc
        self.k = K(nc)

        def din(name, shape):
            return nc.dram_tensor(name, list(shape), F32, kind="ExternalInput").ap()

        def dout(name, shape):
            return nc.dram_tensor(name, list(shape), F32, kind="ExternalOutput").ap()

        self.x_p = din("x_p", [T, D])
        self.x_s = din("x_s", [NSQ * TS, D])
        self.cache_k = din("cache_k", [DEPTH, NSQ, T, 128])
        self.cache_v = din("cache_v", [DEPTH, NSQ, T, 128])
        self.cache_ki = din("cache_ki", [DEPTH, NSQ, T, 64])
        self.state_conv = din("state_conv", [DEPTH, NSQ, 3, 1536])
        self.state_ssm = din("state_ssm", [DEPTH, NSQ, 4, 128, 128])
        self.w_in = din("w_in", [DEPTH, D, DIN])
        self.b_gate = din("b_gate", [DEPTH, 2048])
        self.conv_w = din("conv_w", [DEPTH, 4, 1536])
        self.a_log = din("a_log", [DEPTH, 4])
        self.dt_bias = din("dt_bias", [DEPTH, 4])
        self.gdn_norm_w = din("gdn_norm_w", [DEPTH, 128])
        self.w_proj_a = din("w_proj_a", [DEPTH, 512, D])
        self.w_proj_b = din("w_proj_b", [DEPTH, 512, D])
        self.w_out = din("w_out", [DEPTH, D, D])
        self.ln1_g = din("ln1_g", [DEPTH, D])
        self.ln1_b = din("ln1_b", [DEPTH, D])
        self.w_up = din("w_up", [DEPTH, D, 2 * DFF])
        self.w_down = din("w_down", [DEPTH, DFF, D])
        self.ln2_g = din("ln2_g", [DEPTH, D])
        self.ln2_b = din("ln2_b", [DEPTH, D])
        self.cst = {n: din("c_" + n, list(v.shape)) for n, v in make_consts().items()}

        self.y_p = dout("y_p", [T, D])
        self.y_s = dout("y_s", [NSQ * TS, D])
        self.nk_p = dout("nk_p", [DEPTH, T, 128])
        self.nv_p = dout("nv_p", [DEPTH, T, 128])
        self.nki_p = dout("nki_p", [DEPTH, T, 64])
        self.nconv_p = dout("nconv_p", [DEPTH, 3, 1536])
        self.nssm_p = dout("nssm_p", [DEPTH, 4, 128, 128])
        self.nk_s = dout("nk_s", [DEPTH, NSQ * TS, 128])
        self.nv_s = dout("nv_s", [DEPTH, NSQ * TS, 128])
        self.nki_s = dout("nki_s", [DEPTH, NSQ * TS, 64])
        self.nconv_s = dout("nconv_s", [DEPTH, NSQ, 3, 1536])
        self.nssm_s = dout("nssm_s", [DEPTH, NSQ, 4, 128, 128])

        skind = "ExternalOutput" if debug else "Internal"
        self.YT = TT(nc.dram_tensor("scr_yt", [D, NTOK], BF16, kind=skind).ap(), "YT", sbuf=False)
        self.X1 = TT(nc.dram_tensor("scr_x1", [NTOK, D], F32, kind=skind).ap(), "X1", sbuf=False)
        self.X2 = TT(nc.dram_tensor("scr_x2", [NTOK, D], F32, kind=skind).ap(), "X2", sbuf=False)
        self.build()

    def x_rows(self, l, tok0, n):
        if l == 0:
            if tok0 < T:
                return None, self.x_p[tok0:tok0 + n, :]
            return None, self.x_s[tok0 - T:tok0 - T + n, :]
        return self.X2, self.X2[tok0:tok0 + n, :]

    def load_consts(self):
        k = self.k
        nc = self.nc
        es = ExitStack()
        self.ces = es
        self.ctiles = []

        def csb(name, shape, dt):
            t = es.enter_context(nc.sbuf_tensor("k_" + name, list(shape), dt))
            tt = TT(t, name)
            self.ctiles.append(tt)
            return tt

        self.ident = csb("ident", [128, 128], F32)
        k.dma(k.sp, self.ident[:, :], self.cst["ident"][:, :], out_t=self.ident)
        self.ones = csb("ones", [128, 128], F32)
        k.dma(k.sp, self.ones[:, :], self.cst["ones"][:, :], out_t=self.ones)
        self.i2 = csb("i2", [128, 256], BF16)
        k.dma(k.pool, self.i2[:, :], self.cst["i2"][:, :], out_t=self.i2)
        self.i2s = csb("i2s", [16, 32], BF16)
        k.dma(k.pool, self.i2s[:, :], self.cst["i2s"][:, :], out_t=self.i2s)
        self.i4 = csb("i4", [128, 512], BF16)
        k.dma(k.pool, self.i4[:, :], self.cst["i4"][:, :], out_t=self.i4)
        self.i4s = csb("i4s", [16, 64], BF16)
        k.dma(k.pool, self.i4s[:, :], self.cst["i4s"][:, :], out_t=self.i4s)
        self.pow2 = csb("pow2", [128, 32], F32)
        k.dma(k.sp, self.pow2[:, :], self.cst["pow2"][:, :], out_t=self.pow2)
        for nm in ["U", "Urev", "maskT", "su01"]:
            t = csb(nm, [64, 64], F32)
            k.dma(k.sp, t[:, :], self.cst[nm][:, :], out_t=t)
            setattr(self, "c_" + nm, t)
        self.negones = csb("negones", [128, 128], F32)
        k.dma(k.sp, self.negones[:, :], self.cst["negones"][:, :], out_t=self.negones)

    def load_xT(self, l, tok0, ntok, xin, xT, col0, evac):
        k = self.k
        src_t, src = self.x_rows(l, tok0, ntok)
        k.dma(k.sp, xin[0:ntok, :], src, out_t=xin, in_t=src_t)
        for grp in range(2):
            ps = k.ps[6 + grp]
            for kk in range(4):
                kc = grp * 4 + kk
                k.tr(ps, ps[:, kk * ntok:(kk + 1) * ntok], xin, xin[0:ntok, kc * 128:(kc + 1) * 128],
                     self.ident, self.ident[0:ntok, 0:ntok], inc=(kk == 3))
            k.cp(evac, xT, xT[:, grp * 4:(grp + 1) * 4, col0:col0 + ntok],
                 ps, ps[:, 0:4 * ntok].rearrange("p (a t) -> p a t", a=4))

    def p1a(self, l):
        k = self.k
        nc = self.nc
        k.begin_pass()
        wsrc = self.w_in[l].rearrange("(kc p) e -> p kc e", p=128)
        Wfm = k.sb("Wfm", [128, 8, 1408], BF16)
        Wtm = k.sb("Wtm", [128, 8, 328], BF16)

        def wl(dst_t, d0, s0, n):
            k.dma(k.pool, dst_t[:, :, d0:d0 + n], wsrc[:, :, s0:s0 + n], out_t=dst_t)

        wl(Wfm, 0, O_QA, 512)
        wl(Wfm, 512, O_KA, 64)
        wl(Wfm, 576, O_KA, 64)
        wl(Wfm, 640, O_KA + 64, 64)
        wl(Wfm, 704, O_KA + 64, 64)
        wl(Wfm, 768, O_QI, 512)
        wl(Wfm, 1280, O_KI, 64)
        wl(Wfm, 1344, O_KI, 64)
        wl(Wtm, 0, O_KA, 256)
        wl(Wtm, 256, O_KI, 64)
        wl(Wtm, 320, O_WI, 8)

        kT2g = [k.sb(f"kT2g{g}", [128, KCOLS], BF16) for g in range(2)]
        kiT2 = k.sb("kiT2", [128, KCOLS], BF16)
        vext = k.sb("vext", [128, 33, 2, 65], BF16)
        k.memset(k.pool, vext, vext[:, :, :, 64:65], 1.0)
        xin = [k.sb(f"xin{i}", [128, D], F32) for i in range(2)]
        xT = k.sb("xT", [128, 8, 512], BF16)
        qaLo = [k.sb(f"qaLo{i}", [128, 4, 512], BF16) for i in range(2)]
        qaHi = [k.sb(f"qaHi{i}", [128, 4, 512], BF16) for i in range(2)]
        qiLo = [k.sb(f"qiLo{i}", [128, 4, 512], BF16) for i in range(2)]
        qiHi = [k.sb(f"qiHi{i}", [128, 4, 512], BF16) for i in range(2)]
        for i in range(2):
            k.memset(k.pool, qaLo[i], qaLo[i][64:128, :, :], 0.0)
            k.memset(k.pool, qiLo[i], qiLo[i][64:128, :, :], 0.0)
            k.memset(k.pool, qaHi[i], qaHi[i][0:64, :, :], 0.0)
            k.memset(k.pool, qiHi[i], qiHi[i][0:64, :, :], 0.0)
        kfm = k.sb("kfm", [128, 3, 64], BF16)
        tm1 = [k.sb(f"tm1_{i}", [128, 328], F32) for i in range(2)]
        wscs = [k.sb(f"wsc{i}", [128, 4, 8], F32) for i in range(2)]
        Ss = [k.sb(f"S{i}", [128, KCOLS], F32) for i in range(2)]
        junk = k.sb("junk", [128, KCOLS], BF16)
        MBs = [k.sb(f"MB{i}", [128, KCOLS], BF16) for i in range(2)]
        R = [k.sb(f"R{i}", [128, 512], F32) for i in range(2)]
        PT = [k.sb(f"PT{i}", [128, 512], BF16) for i in range(3)]
        rd = k.sb("rd", [128, 1024], F32)
        bcs = k.sb("bcs", [64, 1024], F32)
        yTt = k.sb("yTt", [64, 1024], BF16)
        stt_ = k.sb("stat", [128, 8], F32)
        dtab = k.sb("dtab", [128, NBIS], F32)
        trial = k.sb("trial", [128, NBIS + 1], F32)
        lo_t = k.sb("lo_t", [128, NBIS + 1], F32)
        cnt = k.sb("cnt", [128, NBIS], F32)
        dd = k.sb("dd", [128, NBIS], F32)
        ktm = k.sb("ktm", [128, 16, 128], F32)

        def stage_units(tok0, NS, Tq, nsub, key_col0, key_tile0, is_sample, par):
            wsc = wscs[par]
            units = []
            done = 0
            i = 0
            while done < NS:
                n = min(128, NS - done)
                units.append(lambda d=done, n=n, i=i: self.load_xT(l, tok0 + d, n, xin[i % 2], xT, d, k.act))
                done += n
                i += 1

            def fm(mt):
                ps = k.ps[6 + (mt % 2)]
                for kc in range(8):
                    k.mm(ps, ps[:, 0:NS], Wfm, Wfm[:, kc, mt * 128:(mt + 1) * 128], xT, xT[:, kc, 0:NS],
                         start=(kc == 0), stop=(kc == 7))
                if mt < 4:
                    k.cp(k.act, qaLo[par], qaLo[par][0:64, mt, 0:NS], ps, ps[0:64, 0:NS])
                    k.cp(k.act, qaHi[par], qaHi[par][64:128, mt, 0:NS], ps, ps[64:128, 0:NS])
                elif mt < 6:
                    g = mt - 4
                    if is_sample:
                        k.cp(k.act, kfm, kfm[:, g, 0:NS], ps, ps[:, 0:NS])
                    else:
                        k.cp(k.act, kT2g[g], kT2g[g][:, key_col0:key_col0 + NS], ps, ps[:, 0:NS])
                elif mt < 10:
                    k.cp(k.act, qiLo[par], qiLo[par][0:64, mt - 6, 0:NS], ps, ps[0:64, 0:NS])
                    k.cp(k.act, qiHi[par], qiHi[par][64:128, mt - 6, 0:NS], ps, ps[64:128, 0:NS])
                else:
                    if is_sample:
                        k.cp(k.act, kfm, kfm[:, 2, 0:NS], ps, ps[:, 0:NS])
                    else:
                        k.cp(k.act, kiT2, kiT2[:, key_col0:key_col0 + NS], ps, ps[:, 0:NS])

            for mt in (6, 7, 8, 9, 10, 4, 5):
                units.append(lambda mt=mt: fm(mt))

            def tmj(j):
                ps = k.ps[6 + (j % 2)]
                t1 = tm1[j % 2]
                for kc in range(8):
                    k.mm(ps, ps[0:Tq, 0:328], xT, xT[:, kc, j * Tq:(j + 1) * Tq], Wtm, Wtm[:, kc, 0:328],
                         start=(kc == 0), stop=(kc == 7))
                k.cp(k.act, t1, t1[0:Tq, :], ps, ps[0:Tq, 0:328])
                r0 = tok0 + j * Tq
                if is_sample:
                    r0 -= T
                    dk, dv, dki = self.nk_s, self.nv_s, self.nki_s
                else:
                    dk, dv, dki = self.nk_p, self.nv_p, self.nki_p
                k.dma(k.pool, dk[l, r0:r0 + Tq, :], t1[0:Tq, 0:128], in_t=t1, is_output=True)
                k.dma(k.pool, dv[l, r0:r0 + Tq, :], t1[0:Tq, 128:256], in_t=t1, is_output=True)
                k.dma(k.pool, dki[l, r0:r0 + Tq, :], t1[0:Tq, 256:320], in_t=t1, is_output=True)
                if not is_sample:
                    kt = key_tile0 + j
                    k.cp(k.pool, vext, vext[0:Tq, kt, :, 0:64], t1, t1[0:Tq, 128:256].rearrange("p (g d) -> p g d", g=2))
                k.ts(k.pool, wsc, wsc[0:Tq, j, :], t1, t1[0:Tq, 320:328], INDEX_SCALE, None, ALU.mult)

            for j in range(nsub):
                units.append(lambda j=j: tmj(j))
            for mt in range(4):
                units.append(lambda mt=mt: fm(mt))
            return units

        def stage_a(*args):
            for u in stage_units(*args):
                u()

        class TD:
            pass

        def idx(td, pending=None, quota=0):
            P = slice(0, td.Tq)
            qc = slice(td.j * td.Tq, (td.j + 1) * td.Tq)
            S, wsc, n = td.S, td.wsc, td.n
            nblk = (n + 511) // 512
            for kb in range(nblk):
                c0 = kb * 512
                w = min(512, n - c0)
                for h in range(8):
                    m, half = divmod(h, 2)
                    ps = k.ps[half]
                    qi = (qiLo if half == 0 else qiHi)[td.par]
                    k.mm(ps, ps[P, 0:w], qi, qi[:, m, qc], kiT2, kiT2[:, c0:c0 + w])
                    Rt = R[h % 2]
                    k.actv(Rt, Rt[P, 0:w], ps, ps[P, 0:w], AF.Relu)
                    if h == 0:
                        k.ts(k.dve, S, S[P, c0:c0 + w], Rt, Rt[P, 0:w], wsc[P, td.j, 0:1], None, ALU.mult, extra_ins=[wsc])
                    else:
                        k.stt(k.dve, S, S[P, c0:c0 + w], Rt, Rt[P, 0:w], wsc[P, td.j, h:h + 1], S, S[P, c0:c0 + w],
                              ALU.mult, ALU.add, extra_ins=[wsc])
                for _ in range(min(len(pending), (quota + nblk - 1) // nblk) if pending else 0):
                    pending.pop(0)()
                    quota -= 1
            while pending and quota > 0:
                pending.pop(0)()
                quota -= 1
            if td.corner:
                k.memset(k.dve, S, S[0:64, n - 64:n], -1.0e30)

        def bis(td):
            P = slice(0, td.Tq)
            S, MB, n = td.S, td.MB, td.n
            st = stt_
            k.op(k.dve, lambda: nc.vector.tensor_reduce(st[P, 0:1], S[P, 0:n], AX.X, ALU.max), outs=[st], ins=[S])
            if td.corner:
                k.op(k.dve, lambda: nc.vector.tensor_reduce(st[P, 1:2], S[P, 0:n - 64], AX.X, ALU.min), outs=[st], ins=[S])
                k.op(k.dve, lambda: nc.vector.tensor_reduce(st[64:128, 2:3], S[64:128, n - 64:n], AX.X, ALU.min), outs=[st], ins=[S])
                k.tt(k.dve, st, st[64:128, 1:2], st, st[64:128, 1:2], st, st[64:128, 2:3], ALU.min)
            else:
                k.op(k.dve, lambda: nc.vector.tensor_reduce(st[P, 1:2], S[P, 0:n], AX.X, ALU.min), outs=[st], ins=[S])
            k.tt(k.dve, st, st[P, 3:4], st, st[P, 0:1], st, st[P, 1:2], ALU.subtract)
            k.ts(k.dve, dtab, dtab[P, :], self.pow2, self.pow2[P, 0:NBIS], st[P, 3:4], None, ALU.mult, extra_ins=[st])
            k.cp(k.dve, lo_t, lo_t[P, 0:1], st, st[P, 1:2])
            k.tt(k.dve, trial, trial[P, 0:1], st, st[P, 1:2], dtab, dtab[P, 0:1], ALU.add)
            for it in range(NBIS):
                k.ts(k.dve, junk, junk[P, 0:n], S, S[P, 0:n], trial[P, it:it + 1], None, ALU.is_ge, ALU.add,
                     accum=(cnt, cnt[P, it:it + 1]), extra_ins=[trial])
                k.ts(k.dve, dd, dd[P, it:it + 1], cnt, cnt[P, it:it + 1], 255.5, dtab[P, it:it + 1], ALU.is_ge, ALU.mult,
                     extra_ins=[dtab])
                k.tt(k.dve, lo_t, lo_t[P, it + 1:it + 2], lo_t, lo_t[P, it:it + 1], dd, dd[P, it:it + 1], ALU.add)
                if it + 1 < NBIS:
                    k.tt(k.dve, trial, trial[P, it + 1:it + 2], lo_t, lo_t[P, it + 1:it + 2], dtab, dtab[P, it + 1:it + 2], ALU.add)
            lo = lo_t[P, NBIS:NBIS + 1]
            k.ts(k.dve, MB, MB[P, 0:n], S, S[P, 0:n], lo, NEG, ALU.is_lt, ALU.mult, extra_ins=[lo_t])

        def att_main(td):
            Tq = td.Tq
            P = slice(0, Tq)
            qc = slice(td.j * Tq, (td.j + 1) * Tq)
            MB, i4 = td.MB, td.i4
            qlo, qhi = qaLo[td.par], qaHi[td.par]
            W4 = 4 * Tq
            nkt = len(td.keytiles)
            units = [(g, ti) for g in range(2) for ti in range(nkt)]

            def s_part(u):
                g, ti = u
                c0, nk, vt = td.keytiles[ti]
                psS = k.ps[2 + (u[0] * nkt + ti) % 2]
                k.mm(psS, psS[0:nk, 0:W4], MB, MB[P, c0:c0 + nk], i4, i4[P, 0:W4], start=True, stop=False)
                k.mm(psS, psS[0:nk, 0:2 * Tq].rearrange("p (a t) -> p a t", a=2), kT2g[g], kT2g[g][:, c0:c0 + nk],
                     qlo, qlo[:, 2 * g:2 * g + 2, qc], start=False, stop=False)
                k.mm(psS, psS[0:nk, 2 * Tq:W4].rearrange("p (a t) -> p a t", a=2), kT2g[g], kT2g[g][:, c0:c0 + nk],
                     qhi, qhi[:, 2 * g:2 * g + 2, qc], start=False, stop=True)
                PTt = PT[(u[0] * nkt + ti) % 3]
                k.actv(PTt, PTt[0:nk, 0:W4], psS, psS[0:nk, 0:W4], AF.Exp, scale=0.125)

            def v_part(u):
                g, ti = u
                c0, nk, vt = td.keytiles[ti]
                Og = k.ps[4 + g]
                PTt = PT[(u[0] * nkt + ti) % 3]
                k.mm(Og, Og[0:65, 0:W4], vext, vext[0:nk, vt, g, :], PTt, PTt[0:nk, 0:W4],
                     start=(ti == 0), stop=(ti == nkt - 1))

            for ui, u in enumerate(units):
                s_part(u)
                if ui >= 1:
                    v_part(units[ui - 1])
            v_part(units[-1])

        def att_fin(td):
            Tq = td.Tq
            W4 = 4 * Tq
            for g in range(2):
                Og = k.ps[4 + g]
                k.actv(rd, rd[64:65, g * 512:g * 512 + W4], Og, Og[64:65, 0:W4], AF.Ln)
                k.actv(rd, rd[64:65, g * 512:g * 512 + W4], rd, rd[64:65, g * 512:g * 512 + W4], AF.Exp, scale=-1.0)
                psB = k.ps[6 + g]
                k.mm(psB, psB[0:64, 0:W4], self.ones, self.ones[64:65, 0:64], rd, rd[64:65, g * 512:g * 512 + W4])
                k.cp(k.act, bcs, bcs[:, g * 512:g * 512 + W4], psB, psB[0:64, 0:W4])
                k.tt(k.dve, yTt, yTt[:, g * 512:g * 512 + W4], Og, Og[0:64, 0:W4], bcs, bcs[:, g * 512:g * 512 + W4], ALU.mult)
                for b_ in range(2):
                    dst = self.YT[256 * g:256 * g + 256, td.tok_out0:td.tok_out0 + Tq].rearrange("(a b d) t -> b d a t", a=2, b=2, d=64)[b_]
                    k.dma(k.pool, dst, yTt[:, g * 512 + b_ * 2 * Tq:g * 512 + (b_ + 1) * 2 * Tq].rearrange("d (a t) -> d a t", a=2),
                          out_t=self.YT, in_t=yTt)

        tds = []
        for st in range(T // 512):
            for j in range(4):
                td = TD()
                i = st * 4 + j
                td.st, td.j, td.Tq = st, j, 128
                td.n = st * 512 + (j + 1) * 128
                td.corner = True
                td.keytiles = [(t * 128, 128, t) for t in range(td.n // 128)]
                td.tok_out0 = st * 512 + j * 128
                td.i4 = self.i4
                td.S, td.MB = Ss[i % 2], MBs[i % 2]
                td.par, td.wsc = st % 2, wscs[st % 2]
                tds.append(td)
        nt = len(tds)
        stage_a(0, 512, 128, 4, 0, 0, False, 0)
        for i in range(nt + 2):
            if 1 <= i <= nt:
                bis(tds[i - 1])
            if 2 <= i:
                att_main(tds[i - 2])
            if i < nt:
                td = tds[i]
                nst = td.st + 1
                if td.j == 0:
                    pending = stage_units(nst * 512, 512, 128, 4, nst * 512, nst * 4, False, nst % 2) if nst < T // 512 else []
                quota = (len(pending) + (3 - td.j)) // (4 - td.j)
                if td.j < 2:
                    quota = min(quota, max(0, len(pending) - 4))
                idx(td, pending, quota)
            if 2 <= i:
                att_fin(tds[i - 2])

        stage_a(T, NSQ * TS, TS, NSQ, 0, 0, True, 0)
        sds = []
        for s in range(NSQ):
            td = TD()
            td.st, td.j, td.Tq = 0, s, TS
            td.n = T + TS
            td.corner = False
            td.keytiles = [(t * 128, 128, t) for t in range(32)] + [(T, TS, 32)]
            td.tok_out0 = T + s * TS
            td.i4 = self.i4s
            td.S, td.MB = Ss[s % 2], MBs[s % 2]
            td.par, td.wsc = 0, wscs[0]
            sds.append(td)

        def prep_k(s, which):
            if which < 2:
                src = self.cache_k[l, s].rearrange("(t p) c -> p t c", p=128)[:, :, which * 64:(which + 1) * 64]
                dst = kT2g[which]
            else:
                src = self.cache_ki[l, s].rearrange("(t p) c -> p t c", p=128)
                dst = kiT2
            for hf in range(2):
                k.dma(k.sp, ktm[:, :, 0:64], src[:, hf * 16:(hf + 1) * 16, :], out_t=ktm)
                k.dma(k.sp, ktm[:, :, 64:128], src[:, hf * 16:(hf + 1) * 16, :], out_t=ktm)
                for b4 in range(4):
                    blk = hf * 4 + b4
                    ps = k.ps[6 + (blk % 2)]
                    for a in range(4):
                        k.tr(ps, ps[:, a * 128:(a + 1) * 128], ktm, ktm[:, b4 * 4 + a, :], self.ident, self.ident[:, :], inc=(a == 3))
                    k.cp(k.act, dst, dst[:, blk * 512:(blk + 1) * 512], ps, ps[:, :])
            k.cp(k.pool, dst, dst[:, T:T + TS], kfm, kfm[:, which, s * TS:(s + 1) * TS])

        def prep_kv(s):
            prep_k(s, 0)
            prep_k(s, 1)
            for g in range(2):
                k.dma(k.pool, vext[:, 0:32, g, 0:64],
                      self.cache_v[l, s].rearrange("(t p) c -> p t c", p=128)[:, :, g * 64:(g + 1) * 64], out_t=vext)
            ps = k.ps[6 + (s % 2)]
            for kc in range(8):
                k.mm(ps, ps[0:TS, 0:128], xT, xT[:, kc, s * TS:(s + 1) * TS], Wtm, Wtm[:, kc, 128:256],
                     start=(kc == 0), stop=(kc == 7))
            k.cp(k.act, vext, vext[0:TS, 32, :, 0:64], ps, ps[0:TS, 0:128].rearrange("p (g d) -> p g d", g=2))

        prep_k(0, 2)
        idx(sds[0])
        prep_kv(0)
        for s in range(NSQ):
            bis(sds[s])
            if s + 1 < NSQ:
                prep_k(s + 1, 2)
                idx(sds[s + 1])
            att_main(sds[s])
            att_fin(sds[s])
            if s + 1 < NSQ:
                prep_kv(s + 1)
        k.end_pass()


    def rows_to_fm(self, src_ap, nrows, rows_t, dst_t, dst_fn):
        k = self.k
        k.dma(k.sp, rows_t[0:nrows, :], src_ap, out_t=rows_t)
        for grp in range(3):
            ps = k.ps[6 + (grp % 2)]
            for a in range(4):
                m = grp * 4 + a
                k.tr(ps, ps[:, a * nrows:(a + 1) * nrows], rows_t, rows_t[0:nrows, m * 128:(m + 1) * 128],
                     self.ident, self.ident[0:nrows, 0:nrows], inc=(a == 3))
            for a in range(4):
                m = grp * 4 + a
                k.cp(k.act, dst_t, dst_fn(m), ps, ps[:, a * nrows:(a + 1) * nrows])

    def p1b(self, l):
        k = self.k
        nc = self.nc
        k.begin_pass()
        wsrc = self.w_in[l].rearrange("(kc p) e -> p kc e", p=128)
        Wfm = k.sb("Wfm", [128, 8, 1536], BF16)
        Wtm = k.sb("Wtm", [128, 8, 520], BF16)
        k.dma(k.pool, Wfm[:, :, :], wsrc[:, :, O_QKV:O_QKV + 1536], out_t=Wfm)
        k.dma(k.pool, Wtm[:, :, 0:8], wsrc[:, :, O_AB:O_AB + 8], out_t=Wtm)
        k.dma(k.pool, Wtm[:, :, 8:520], wsrc[:, :, O_GB:O_GB + 512], out_t=Wtm)
        rows_t = k.sb("rows", [16, 1536], F32)
        cw = k.sb("cw", [128, 12, 4], F32)
        self.rows_to_fm(self.conv_w[l], 4, rows_t, cw, lambda m: cw[:, m, :])
        nw = k.sb("nw", [128, 128], F32)
        k.dma(k.sp, nw[:, :], self.gdn_norm_w[l:l + 1, :].to_broadcast([128, 128]), out_t=nw)
        dtb = k.sb("dtb", [128, 4], F32)
        k.dma(k.sp, dtb[:, :], self.dt_bias[l:l + 1, :].to_broadcast([128, 4]), out_t=dtb)
        negA = k.sb("negA", [128, 4], F32)
        k.dma(k.sp, negA[:, :], self.a_log[l:l + 1, :].to_broadcast([128, 4]), out_t=negA)
        k.actv(negA, negA[:, :], negA, negA[:, :], AF.Exp)
        k.ts(k.dve, negA, negA[:, :], negA, negA[:, :], -1.0, None, ALU.mult)

        xin = [k.sb(f"xin{i}", [128, D], F32) for i in range(2)]
        xT = k.sb("xT", [128, 8, 512], BF16)
        cin = k.sb("cin", [128, 12, 515], F32)
        qkvs = k.sb("qkvs", [128, 12, 512], F32)
        acc = [k.sb(f"acc{i}", [128, 512], F32) for i in range(2)]
        sq = [k.sb("sq0", [128, 512], F32)] * 2
        rn = [k.sb("rn0", [128, 512], F32)] * 2
        Sst = k.sb("Sst", [128, NSQ, 4, 128], F32)
        last3 = rows_t
        NM = 16

        def smal(name, shape):
            return k.sb(name, shape, F32)

        gx = smal("gx", [64, 4, 4])
        gab = smal("gab", [64, 4, 4])
        ge1 = smal("ge1", [64, 4, 4])
        gl1 = smal("gl1", [64, 4, 4])
        gsp = smal("gsp", [64, 4, 4])
        gg = smal("gg", [64, 4, 4])
        beta = smal("beta", [64, 4, 4])
        nbeta = smal("nbeta", [64, 4, 4])
        E = smal("E", [64, 4, 12])
        negeG = smal("negeG", [64, 4, 4])
        gt128 = smal("gt128", [128, 4, 4])
        class MV:
            def __init__(self, name):
                self.tt = smal(name, [64, 512])
                self.C = 64
                self.nm = 8

            def set(self, C, nm):
                self.C = C
                self.nm = nm

            def v(self):
                return self.tt[0:self.C, 0:self.nm * self.C].rearrange("p (m c) -> p m c", m=self.nm)

        gU_, DT_, DsB_, qkT_ = MV("gU"), MV("DT"), MV("DsB"), MV("qkT")
        W_ = [MV(f"W{i}") for i in range(2)]
        X_ = [MV(f"X{i}") for i in range(2)]
        NT_ = [MV(f"NT{i}") for i in range(2)]
        allmv = [gU_, DT_, DsB_, qkT_] + W_ + X_ + NT_
        kdec = smal("kdec", [64, 4, 4, 128])
        vtm = smal("vtm", [64, 4, 4, 128])
        sg = smal("sg", [64, 4, 512])
        Z = smal("Z", [64, 4, 128])
        t1 = smal("t1", [64, 4, 128])
        vnew = smal("vnew", [64, 4, 128])
        o_t = smal("o", [64, 4, 128])
        osq = smal("osq", [64, 128])
        ss = smal("ss", [64, 4])
        rstd = smal("rstd", [64, 4])
        yb = smal("yb", [64, 4, 128])
        ybT = k.sb("ybT", [128, 4, 64], BF16)

        def bc(ap, shape, axis):
            return ap.unsqueeze(axis).to_broadcast(shape)

        def process(tok0, NS, nseq, L, C, is_sample, first, last_st):
            cinv = cin[:, :, 0:nseq * (L + 3)].rearrange("p m (s t) -> p m s t", s=nseq)
            qv = qkvs[:, :, 0:NS].rearrange("p m (s t) -> p m s t", s=nseq)
            done = 0
            i = 0
            while done < NS:
                n = min(128, NS - done)
                self.load_xT(l, tok0 + done, n, xin[i % 2], xT, done, k.act)
                done += n
                i += 1
            if is_sample:
                self.rows_to_fm(self.state_conv[l].rearrange("s j c -> (s j) c"), 12, rows_t, cin,
                                lambda m: cinv[:, m, :, 0:3])
            elif first:
                k.memset(k.pool, cin, cinv[:, :, :, 0:3], 0.0)
            for mt in range(12):
                ps = k.ps[6 + (mt % 2)]
                for kc in range(8):
                    k.mm(ps, ps[:, 0:NS], Wfm, Wfm[:, kc, mt * 128:(mt + 1) * 128], xT, xT[:, kc, 0:NS],
                         start=(kc == 0), stop=(kc == 7))
                k.cp(k.act, cin, cinv[:, mt, :, 3:3 + L], ps, ps[:, 0:NS].rearrange("p (s t) -> p s t", s=nseq))
            if is_sample or last_st:
                for sq_ in range(nseq):
                    c1 = (sq_ + 1) * L
                    for blk in range(3):
                        ps = k.ps[6 + (blk % 2)]
                        for kc in range(8):
                            k.mm(ps, ps[0:3, 0:512], xT, xT[:, kc, c1 - 3:c1], Wfm, Wfm[:, kc, blk * 512:(blk + 1) * 512],
                                 start=(kc == 0), stop=(kc == 7))
                        k.cp(k.act, last3, last3[0:3, blk * 512:(blk + 1) * 512], ps, ps[0:3, 0:512])
                    dst = self.nconv_s[l, sq_] if is_sample else self.nconv_p[l]
                    k.dma(k.sp, dst, last3[0:3, :], in_t=last3, is_output=True)
            for m in range(12):
                a_ = acc[m % 2]
                av = a_[:, 0:NS].rearrange("p (s t) -> p s t", s=nseq)
                k.ts(k.dve, a_, av, cin, cinv[:, m, :, 0:L], cw[:, m, 0:1], None, ALU.mult, extra_ins=[cw])
                for jj in range(1, 4):
                    k.stt(k.dve, a_, av, cin, cinv[:, m, :, jj:jj + L], cw[:, m, jj:jj + 1], a_, av, ALU.mult, ALU.add,
                          extra_ins=[cw])
                k.actv(qkvs, qkvs[:, m, 0:NS], a_, a_[:, 0:NS], AF.Silu)
                if m < 8:
                    s_ = sq[m % 2]
                    k.actv(s_, s_[:, 0:NS], qkvs, qkvs[:, m, 0:NS], AF.Square)
                    ps = k.ps[6 + (m % 2)]
                    k.mm(ps, ps[:, 0:NS], self.ones, self.ones[:, :], s_, s_[:, 0:NS])
                    r_ = rn[m % 2]
                    k.actv(r_, r_[:, 0:NS], ps, ps[:, 0:NS], AF.Sqrt, bias=NORM_EPS)
                    k.op(k.dve, lambda: nc.vector.reciprocal(r_[:, 0:NS], r_[:, 0:NS]), outs=[r_], ins=[r_])
                    k.stt(k.dve, qkvs, qkvs[:, m, 0:NS], qkvs, qkvs[:, m, 0:NS], (128.0 ** -0.5) if m < 4 else 1.0,
                          r_, r_[:, 0:NS], ALU.mult, ALU.mult)
            if not is_sample:
                k.cp(k.pool, cin, cinv[:, :, :, 0:3], cin, cinv[:, :, :, L:L + 3])

            nch = L // C
            if is_sample:
                batches = [[(sq_, 0) for sq_ in range(nseq)]]
            else:
                batches = [[(0, ci), (0, ci + 1)] for ci in range(0, nch, 2)]
            PC = slice(0, C)
            for batch in batches:
                nb = len(batch)
                nm = nb * 4
                for mv in allmv:
                    mv.set(C, nm)
                gU, DT, DsB, qkT = gU_.tt, DT_.tt, DsB_.tt, qkT_.tt
                gUv, DTv, DsBv, qkTv = gU_.v(), DT_.v(), DsB_.v(), qkT_.v()
                W = [w.tt for w in W_]
                Wv = [w.v() for w in W_]
                X = [x.tt for x in X_]
                Xv = [x.v() for x in X_]
                NT = [x.tt for x in NT_]
                NTv = [x.v() for x in NT_]
                for bi, (sq_, ci) in enumerate(batch):
                    col0 = sq_ * L + ci * C
                    psA = k.ps[0]
                    for kc in range(8):
                        k.mm(psA, psA[PC, 0:8], xT, xT[:, kc, col0:col0 + C], Wtm, Wtm[:, kc, 0:8], start=(kc == 0), stop=(kc == 7))
                    psB = k.ps[6 + (bi % 2)]
                    for kc in range(8):
                        k.mm(psB, psB[PC, 0:512], xT, xT[:, kc, col0:col0 + C], Wtm, Wtm[:, kc, 8:520], start=(kc == 0), stop=(kc == 7))
                    k.actv(sg, sg[PC, bi, :], psB, psB[PC, 0:512], AF.Silu)
                    k.tt(k.dve, gx, gx[PC, bi, :], psA, psA[PC, 0:4], dtb, dtb[PC, :], ALU.add)
                    k.actv(beta, beta[PC, bi, :], psA, psA[PC, 4:8], AF.Sigmoid)
                gxa = gx[PC, 0:nb, :]
                k.ts(k.dve, gab, gab[PC, 0:nb, :], gx, gxa, -1.0, None, ALU.mult)
                k.tt(k.dve, gab, gab[PC, 0:nb, :], gab, gab[PC, 0:nb, :], gx, gxa, ALU.min)
                k.actv(ge1, ge1[PC, 0:nb, :], gab, gab[PC, 0:nb, :], AF.Exp)
                k.actv(gl1, gl1[PC, 0:nb, :], ge1, ge1[PC, 0:nb, :], AF.Ln, bias=1.0)
                k.stt(k.dve, gsp, gsp[PC, 0:nb, :], gx, gxa, 0.0, gl1, gl1[PC, 0:nb, :], ALU.max, ALU.add)
                k.tt(k.dve, gg, gg[PC, 0:nb, :], gsp, gsp[PC, 0:nb, :], negA, bc(negA[PC, :], [C, nb, 4], 1), ALU.mult)
                k.ts(k.dve, nbeta, nbeta[PC, 0:nb, :], beta, beta[PC, 0:nb, :], -1.0, None, ALU.mult)
                k.tt(k.pool, sg, sg[PC, 0:nb, :].rearrange("p b (h e) -> p (b h) e", h=4), sg,
                     sg[PC, 0:nb, :].rearrange("p b (h e) -> p (b h) e", h=4), nw, bc(nw[PC, :], [C, nm, 128], 1), ALU.mult)
                psG = k.ps[0]
                for bi in range(nb):
                    gcol = gg[PC, bi, :]
                    k.mm(psG, psG[PC, bi * 16:bi * 16 + 4], self.c_U, self.c_U[PC, PC], gg, gcol)
                    k.mm(psG, psG[PC, bi * 16 + 4:bi * 16 + 8], self.c_Urev, self.c_Urev[PC, PC], gg, gcol)
                    k.mm(psG, psG[PC, bi * 16 + 8:bi * 16 + 12], self.ones, self.ones[PC, PC], gg, gcol)
                    k.mm(psG, psG[:, 64 + bi * 4:64 + bi * 4 + 4], self.ones, self.ones[PC, :], gg, gcol)
                k.actv(E, E[PC, 0:nb, :], psG, psG[PC, 0:nb * 16].rearrange("p (b e) -> p b e", b=nb)[:, :, 0:12], AF.Exp)
                k.actv(gt128, gt128[:, 0:nb, :], psG, psG[:, 64:64 + nb * 4].rearrange("p (b e) -> p b e", b=nb), AF.Exp)
                k.ts(k.dve, negeG, negeG[PC, 0:nb, :], E, E[PC, 0:nb, 0:4], -1.0, None, ALU.mult)
                k.tt(k.dve, gU, gUv, self.c_U, bc(self.c_U[PC, PC], [C, nm, C], 1),
                     gg, bc(gg[PC, 0:nb, :].rearrange("p b h -> p (b h)"), [C, nm, C], 2), ALU.mult)
                psD = k.ps[1]
                for mi in range(nm):
                    o_ = psD[PC, mi * C:(mi + 1) * C]
                    k.mm(psD, o_, self.ones, self.ones[PC, PC], gU, gUv[:, mi, :], start=True, stop=False)
                    k.mm(psD, o_, gU, gUv[:, mi, :], self.negones, self.negones[PC, PC], start=False, stop=False)
                    k.mm(psD, o_, self.ident, self.ident[PC, PC], self.c_maskT, self.c_maskT[PC, PC], start=False, stop=True)
                k.actv(DT, DTv, psD, psD[PC, 0:nm * C].rearrange("p (m c) -> p m c", m=nm), AF.Exp)
                k.tt(k.pool, DsB, DsBv, DT, DTv, self.c_su01, bc(self.c_su01[PC, PC], [C, nm, C], 1), ALU.mult)
                k.tt(k.pool, DsB, DsBv, DsB, DsBv, nbeta,
                     bc(nbeta[PC, 0:nb, :].rearrange("p b h -> p (b h)"), [C, nm, C], 2), ALU.mult)
                psK = k.ps[2]
                psQ = k.ps[3]
                for bi, (sq_, ci) in enumerate(batch):
                    cs = slice(ci * C, (ci + 1) * C)
                    for h in range(4):
                        mi = bi * 4 + h
                        k.mm(psK, psK[PC, mi * C:(mi + 1) * C], qkvs, qv[:, 4 + h, sq_, cs], qkvs, qv[:, 4 + h, sq_, cs])
                        k.mm(psQ, psQ[PC, mi * C:(mi + 1) * C], qkvs, qv[:, 4 + h, sq_, cs], qkvs, qv[:, h, sq_, cs])
                W0 = W[0]
                W0v = Wv[0]
                k.tt(k.dve, W0, W0v, psK, psK[PC, 0:nm * C].rearrange("p (m c) -> p m c", m=nm), DsB, DsBv, ALU.mult)
                k.tt(k.dve, qkT, qkTv, psQ, psQ[PC, 0:nm * C].rearrange("p (m c) -> p m c", m=nm), DT, DTv, ALU.mult)
                psX = k.ps[2]
                for mi in range(nm):
                    k.tr(psX, psX[PC, mi * C:(mi + 1) * C], W0, W0v[:, mi, :], self.ident, self.ident[PC, PC], inc=(mi == nm - 1))
                k.cp(k.act, X[0], Xv[0], psX, psX[PC, 0:nm * C].rearrange("p (m c) -> p m c", m=nm))
                k.tt(k.dve, NT[0], NTv[0], W0, W0v, self.ident, bc(self.ident[PC, PC], [C, nm, C], 1), ALU.add)
                nlev = int(round(math.log2(C)))
                cur = 0
                for lev in range(1, nlev):
                    nxt = 1 - cur
                    lastlev = (lev == nlev - 1)
                    psW, psX2, psP = k.ps[1], k.ps[2], k.ps[3]
                    for mi in range(nm):
                        k.mm(psX2, psX2[PC, mi * C:(mi + 1) * C], W[cur], Wv[cur][:, mi, :], X[cur], Xv[cur][:, mi, :])
                    k.cp(k.act, X[nxt], Xv[nxt], psX2, psX2[PC, 0:nm * C].rearrange("p (m c) -> p m c", m=nm))
                    if not lastlev:
                        for mi in range(nm):
                            k.mm(psW, psW[PC, mi * C:(mi + 1) * C], X[cur], Xv[cur][:, mi, :], W[cur], Wv[cur][:, mi, :])
                        k.cp(k.act, W[nxt], Wv[nxt], psW, psW[PC, 0:nm * C].rearrange("p (m c) -> p m c", m=nm))
                    for mi in range(nm):
                        k.mm(psP, psP[PC, mi * C:(mi + 1) * C], X[nxt], Xv[nxt][:, mi, :], NT[cur], NTv[cur][:, mi, :])
                    k.tt(k.dve, NT[nxt], NTv[nxt], psP, psP[PC, 0:nm * C].rearrange("p (m c) -> p m c", m=nm),
                         NT[cur], NTv[cur], ALU.add)
                    cur = nxt
                NTf = NT[cur]
                NTfv = NTv[cur]
                for bi, (sq_, ci) in enumerate(batch):
                    cs = slice(ci * C, (ci + 1) * C)
                    psk = k.ps[4]
                    psv = k.ps[5]
                    for h in range(4):
                        k.tr(psk, psk[PC, h * 128:(h + 1) * 128], qkvs, qv[:, 4 + h, sq_, cs], self.ident, self.ident[:, :], inc=(h == 3))
                    for h in range(4):
                        k.tr(psv, psv[PC, h * 128:(h + 1) * 128], qkvs, qv[:, 8 + h, sq_, cs], self.ident, self.ident[:, :], inc=(h == 3))
                    k.tt(k.dve, kdec, kdec[PC, bi, :, :], psk, psk[PC, :].rearrange("p (h e) -> p h e", h=4),
                         E, bc(E[PC, bi, 4:8], [C, 4, 128], 2), ALU.mult)
                    k.cp(k.act, vtm, vtm[PC, bi, :, :], psv, psv[PC, :].rearrange("p (h e) -> p h e", h=4))
                for bi, (sq_, ci) in enumerate(batch):
                    cs = slice(ci * C, (ci + 1) * C)
                    Sv = Sst[:, sq_, :, :]
                    pskS, psqS, psNZ, psqkv, psdS = k.ps[4], k.ps[5], k.ps[0], k.ps[6], k.ps[7]
                    for h in range(4):
                        k.mm(pskS, pskS[PC, h * 128:(h + 1) * 128], qkvs, qv[:, 4 + h, sq_, cs], Sst, Sv[:, h, :], inc=(h == 3))
                    for h in range(4):
                        k.mm(psqS, psqS[PC, h * 128:(h + 1) * 128], qkvs, qv[:, h, sq_, cs], Sst, Sv[:, h, :], inc=(h == 3))
                    k.tt(k.dve, Z, Z[PC, :, :], pskS, pskS[PC, :].rearrange("p (h e) -> p h e", h=4),
                         negeG, bc(negeG[PC, bi, :], [C, 4, 128], 2), ALU.mult)
                    k.tt(k.dve, Z, Z[PC, :, :], Z, Z[PC, :, :], vtm, vtm[PC, bi, :, :], ALU.add)
                    k.tt(k.dve, t1, t1[PC, :, :], psqS, psqS[PC, :].rearrange("p (h e) -> p h e", h=4),
                         E, bc(E[PC, bi, 0:4], [C, 4, 128], 2), ALU.mult)
                    for h in range(4):
                        k.mm(psNZ, psNZ[PC, h * 128:(h + 1) * 128], NTf, NTfv[:, bi * 4 + h, :], Z, Z[PC, h, :], inc=(h == 3))
                    k.tt(k.dve, vnew, vnew[PC, :, :], psNZ, psNZ[PC, :].rearrange("p (h e) -> p h e", h=4),
                         beta, bc(beta[PC, bi, :], [C, 4, 128], 2), ALU.mult)
                    for h in range(4):
                        k.mm(psqkv, psqkv[PC, h * 128:(h + 1) * 128], qkT, qkTv[:, bi * 4 + h, :], vnew, vnew[PC, h, :], inc=(h == 3))
                    for h in range(4):
                        k.mm(psdS, psdS[:, h * 128:(h + 1) * 128], kdec, kdec[PC, bi, h, :], vnew, vnew[PC, h, :], inc=(h == 3))
                    k.tt(k.dve, o_t, o_t[PC, :, :], psqkv, psqkv[PC, :].rearrange("p (h e) -> p h e", h=4), t1, t1[PC, :, :], ALU.add)
                    k.tt(k.dve, Sst, Sv, Sst, Sv, gt128, bc(gt128[:, bi, :], [128, 4, 128], 2), ALU.mult)
                    k.tt(k.dve, Sst, Sv, Sst, Sv, psdS, psdS[:, :].rearrange("p (h e) -> p h e", h=4), ALU.add)
                    for h in range(4):
                        k.actv(osq, osq[PC, :], o_t, o_t[PC, h, :], AF.Square, accum=(ss, ss[PC, h:h + 1]))
                    k.actv(rstd, rstd[PC, :], ss, ss[PC, :], AF.Sqrt, bias=NORM_EPS, scale=1.0 / 128.0)
                    k.op(k.dve, lambda: nc.vector.reciprocal(rstd[PC, :], rstd[PC, :]), outs=[rstd], ins=[rstd])
                    k.tt(k.pool, yb, yb[PC, :, :], o_t, o_t[PC, :, :], rstd, bc(rstd[PC, :], [C, 4, 128], 2), ALU.mult)
                    k.tt(k.pool, yb, yb[PC, :, :], yb, yb[PC, :, :], sg, sg[PC, bi, :].rearrange("p (h e) -> p h e", h=4), ALU.mult)
                    psT = k.ps[6]
                    for h in range(4):
                        k.tr(psT, psT[:, h * C:(h + 1) * C], yb, yb[PC, h, :], self.ident, self.ident[PC, PC], inc=(h == 3))
                    k.cp(k.act, ybT, ybT[:, :, 0:C], psT, psT[:, 0:4 * C].rearrange("p (h c) -> p h c", h=4))
                    tk = tok0 + sq_ * L + ci * C
                    k.dma(k.sp, self.YT[512:1024, tk:tk + C].rearrange("(m p) t -> p m t", p=128), ybT[:, :, 0:C],
                          out_t=self.YT, in_t=ybT)

        k.memset(k.pool, Sst, Sst[:, 0, :, :], 0.0)
        nst = T // 512
        for st in range(nst):
            process(st * 512, 512, 1, 512, 64, False, st == 0, st == nst - 1)
        k.dma(k.sp, self.nssm_p[l].rearrange("h a b -> a h b"), Sst[:, 0, :, :], in_t=Sst, is_output=True)
        for s in range(NSQ):
            k.dma(k.sp, Sst[:, s, :, :], self.state_ssm[l, s].rearrange("h a b -> a h b"), out_t=Sst)
        process(T, NSQ * TS, NSQ, TS, TS, True, True, True)
        for s in range(NSQ):
            k.dma(k.sp, self.nssm_s[l, s].rearrange("h a b -> a h b"), Sst[:, s, :, :], in_t=Sst, is_output=True)
        k.end_pass()


    def layer_norm(self, r, n, gbc, bbc, junk, stat, out_t):
        k = self.k
        nc = self.nc
        P = slice(0, n)
        k.actv(junk, junk[P, :], r, r[P, :], AF.Identity, accum=(stat, stat[P, 0:1]))
        k.actv(junk, junk[P, :], r, r[P, :], AF.Square, accum=(stat, stat[P, 1:2]))
        k.ts(k.dve, stat, stat[P, 2:3], stat, stat[P, 0:1], -1.0 / D, None, ALU.mult)
        k.tt(k.dve, stat, stat[P, 3:4], stat, stat[P, 2:3], stat, stat[P, 2:3], ALU.mult)
        k.stt(k.dve, stat, stat[P, 4:5], stat, stat[P, 1:2], 1.0 / D, stat, stat[P, 3:4], ALU.mult, ALU.subtract)
        k.actv(stat, stat[P, 5:6], stat, stat[P, 4:5], AF.Sqrt, bias=LN_EPS)
        k.op(k.dve, lambda: nc.vector.reciprocal(stat[P, 6:7], stat[P, 5:6]), outs=[stat], ins=[stat])
        k.ts(k.dve, r, r[P, :], r, r[P, :], stat[P, 2:3], stat[P, 6:7], ALU.add, ALU.mult, extra_ins=[stat])
        k.tt(k.pool, r, r[P, :], r, r[P, :], gbc, gbc[P, :], ALU.mult)
        k.tt(k.pool, out_t, out_t[P, :], r, r[P, :], bbc, bbc[P, :], ALU.add)

    def p2(self, l):
        k = self.k
        nc = self.nc
        k.begin_pass()
        wsrc = self.w_in[l].rearrange("(kc p) e -> p kc e", p=128)
        Wg = k.sb("Wg", [128, 8, 2048], BF16)
        k.dma(k.pool, Wg[:, :, :], wsrc[:, :, O_GATES:O_GATES + 2048], out_t=Wg)
        Wpa = k.sb("Wpa", [128, 4, D], BF16)
        k.dma(k.pool, Wpa[:, :, :], self.w_proj_a[l].rearrange("(kc p) e -> p kc e", p=128), out_t=Wpa)
        Wpb = k.sb("Wpb", [128, 4, D], BF16)
        k.dma(k.pool, Wpb[:, :, :], self.w_proj_b[l].rearrange("(kc p) e -> p kc e", p=128), out_t=Wpb)
        Wo = k.sb("Wo", [128, 8, D], BF16)
        k.dma(k.pool, Wo[:, :, :], self.w_out[l].rearrange("(kc p) e -> p kc e", p=128), out_t=Wo)
        bg = k.sb("bg", [128, 2048], F32)
        k.dma(k.sp, bg[:, :], self.b_gate[l:l + 1, :].to_broadcast([128, 2048]), out_t=bg)
        gbc = k.sb("gbc", [128, D], F32)
        k.dma(k.sp, gbc[:, :], self.ln1_g[l:l + 1, :].to_broadcast([128, D]), out_t=gbc)
        bbc = k.sb("bbc", [128, D], F32)
        k.dma(k.sp, bbc[:, :], self.ln1_b[l:l + 1, :].to_broadcast([128, D]), out_t=bbc)
        xin = [k.sb(f"xin{i}", [128, D], F32) for i in range(2)]
        xT = [k.sb(f"xT{i}", [128, 8, 128], BF16) for i in range(2)]
        yT = [k.sb(f"yT{i}", [128, 8, 128], BF16) for i in range(2)]
        sgate = k.sb("sgate", [128, 2048], F32)
        mixed = k.sb("mixed", [128, D], F32)
        tmp = k.sb("tmp", [128, 512], F32)
        mixT = k.sb("mixT", [128, 8, 128], BF16)
        r = [k.sb(f"r{i}", [128, D], F32) for i in range(2)]
        junk = k.sb("junk", [128, D], BF16)
        stat = k.sb("stat", [128, 8], F32)
        tiles = [(t * 128, 128) for t in range(T // 128)] + [(T, NSQ * TS)]
        for ti, (tok0, n) in enumerate(tiles):
            P = slice(0, n)
            xi = xin[ti % 2]
            xt = xT[ti % 2]
            yt = yT[ti % 2]
            rr = r[ti % 2]
            self.load_xT(l, tok0, n, xi, xt, 0, k.act)
            k.dma(k.sp, yt[:, :, 0:n], self.YT[:, tok0:tok0 + n].rearrange("(kc p) t -> p kc t", p=128), out_t=yt, in_t=self.YT)
            for blk in range(4):
                ps = k.ps[blk % 4]
                for kc in range(8):
                    k.mm(ps, ps[P, :], xt, xt[:, kc, 0:n], Wg, Wg[:, kc, blk * 512:(blk + 1) * 512], start=(kc == 0), stop=(kc == 7))
                k.tt(k.dve, sgate, sgate[P, blk * 512:(blk + 1) * 512], ps, ps[P, :], bg, bg[P, blk * 512:(blk + 1) * 512], ALU.add)
                k.actv(sgate, sgate[P, blk * 512:(blk + 1) * 512], sgate, sgate[P, blk * 512:(blk + 1) * 512], AF.Sigmoid)
            for blk in range(2):
                psa = k.ps[4 + blk]
                psb = k.ps[6 + blk]
                for kc in range(4):
                    k.mm(psa, psa[P, :], yt, yt[:, kc, 0:n], Wpa, Wpa[:, kc, blk * 512:(blk + 1) * 512], start=(kc == 0), stop=(kc == 3))
                for kc in range(4):
                    k.mm(psb, psb[P, :], yt, yt[:, 4 + kc, 0:n], Wpb, Wpb[:, kc, blk * 512:(blk + 1) * 512], start=(kc == 0), stop=(kc == 3))
                cs = slice(blk * 512, (blk + 1) * 512)
                k.tt(k.dve, mixed, mixed[P, cs], psa, psa[P, :], sgate, sgate[P, cs], ALU.mult)
                k.tt(k.dve, tmp, tmp[P, :], psb, psb[P, :], sgate, sgate[P, 1024 + blk * 512:1024 + (blk + 1) * 512], ALU.mult)
                k.tt(k.pool, mixed, mixed[P, cs], mixed, mixed[P, cs], tmp, tmp[P, :], ALU.add)
            for grp in range(2):
                ps = k.ps[grp]
                for kk in range(4):
                    kc = grp * 4 + kk
                    k.tr(ps, ps[:, kk * n:(kk + 1) * n], mixed, mixed[P, kc * 128:(kc + 1) * 128], self.ident, self.ident[P, P], inc=(kk == 3))
                k.cp(k.act, mixT, mixT[:, grp * 4:(grp + 1) * 4, 0:n], ps, ps[:, 0:4 * n].rearrange("p (a t) -> p a t", a=4))
            for blk in range(2):
                ps = k.ps[2 + blk]
                for kc in range(8):
                    k.mm(ps, ps[P, :], mixT, mixT[:, kc, 0:n], Wo, Wo[:, kc, blk * 512:(blk + 1) * 512], start=(kc == 0), stop=(kc == 7))
                cs = slice(blk * 512, (blk + 1) * 512)
                k.stt(k.dve, rr, rr[P, cs], xi, xi[P, cs], ALPHA, ps, ps[P, :], ALU.mult, ALU.add)
            self.layer_norm(rr, n, gbc, bbc, junk, stat, rr)
            k.dma(k.sp, self.X1[tok0:tok0 + n, :], rr[P, :], out_t=self.X1, in_t=rr)
        k.end_pass()

    def p3(self, l):
        k = self.k
        nc = self.nc
        k.begin_pass()
        Wup = k.sb("Wup", [128, 8, 2 * DFF], BF16)
        usrc = self.w_up[l].rearrange("(kc p) e -> p kc e", p=128)
        for q4 in range(4):
            k.dma(k.pool, Wup[:, :, q4 * 1408:(q4 + 1) * 1408], usrc[:, :, q4 * 1408:(q4 + 1) * 1408], out_t=Wup)
        Wdn = k.sb("Wdn", [128, 22, D], BF16)
        k.dma(k.pool, Wdn[:, :, :], self.w_down[l].rearrange("(kc p) e -> p kc e", p=128), out_t=Wdn)
        gbc = k.sb("gbc", [128, D], F32)
        k.dma(k.sp, gbc[:, :], self.ln2_g[l:l + 1, :].to_broadcast([128, D]), out_t=gbc)
        bbc = k.sb("bbc", [128, D], F32)
        k.dma(k.sp, bbc[:, :], self.ln2_b[l:l + 1, :].to_broadcast([128, D]), out_t=bbc)
        xin = [k.sb(f"xin{i}", [128, D], F32) for i in range(2)]
        xT = k.sb("xT", [128, 8, 512], BF16)
        fT = k.sb("fT", [128, 22, 512], BF16)
        tmp = [k.sb(f"tmp{i}", [128, 512], F32) for i in range(2)]
        rr = k.sb("r", [128, D], F32)
        junk = k.sb("junk", [128, D], BF16)
        stat = k.sb("stat", [128, 8], F32)
        last = (l == DEPTH - 1)
        sts = [(st * 512, 512) for st in range(T // 512)] + [(T, NSQ * TS)]
        xcnt = 0
        for (tok0, NS) in sts:
            subs = []
            done = 0
            while done < NS:
                n = min(128, NS - done)
                subs.append((done, n))
                done += n
            for (c0, n) in subs:
                xi = xin[xcnt % 2]
                xcnt += 1
                k.dma(k.sp, xi[0:n, :], self.X1[tok0 + c0:tok0 + c0 + n, :], out_t=xi, in_t=self.X1)
                for grp in range(2):
                    ps = k.ps[6 + grp]
                    for kk in range(4):
                        kc = grp * 4 + kk
                        k.tr(ps, ps[:, kk * n:(kk + 1) * n], xi, xi[0:n, kc * 128:(kc + 1) * 128], self.ident, self.ident[0:n, 0:n], inc=(kk == 3))
                    k.cp(k.act, xT, xT[:, grp * 4:(grp + 1) * 4, c0:c0 + n], ps, ps[:, 0:4 * n].rearrange("p (a t) -> p a t", a=4))
            for fc in range(22):
                psA = k.ps[(2 * fc) % 4]
                psB = k.ps[(2 * fc + 1) % 4]
                for kc in range(8):
                    k.mm(psA, psA[:, 0:NS], Wup, Wup[:, kc, fc * 128:(fc + 1) * 128], xT, xT[:, kc, 0:NS], start=(kc == 0), stop=(kc == 7))
                for kc in range(8):
                    k.mm(psB, psB[:, 0:NS], Wup, Wup[:, kc, DFF + fc * 128:DFF + (fc + 1) * 128], xT, xT[:, kc, 0:NS], start=(kc == 0), stop=(kc == 7))
                tm = tmp[fc % 2]
                k.actv(tm, tm[:, 0:NS], psA, psA[:, 0:NS], AF.Silu)
                k.tt(k.dve, fT, fT[:, fc, 0:NS], tm, tm[:, 0:NS], psB, psB[:, 0:NS], ALU.mult)
            for (c0, n) in subs:
                P = slice(0, n)
                xi = xin[xcnt % 2]
                xcnt += 1
                k.dma(k.sp, xi[0:n, :], self.X1[tok0 + c0:tok0 + c0 + n, :], out_t=xi, in_t=self.X1)
                for blk in range(2):
                    ps = k.ps[4 + blk]
                    for fc in range(22):
                        k.mm(ps, ps[P, :], fT, fT[:, fc, c0:c0 + n], Wdn, Wdn[:, fc, blk * 512:(blk + 1) * 512], start=(fc == 0), stop=(fc == 21))
                    cs = slice(blk * 512, (blk + 1) * 512)
                    k.stt(k.dve, rr, rr[P, cs], xi, xi[P, cs], ALPHA, ps, ps[P, :], ALU.mult, ALU.add)
                self.layer_norm(rr, n, gbc, bbc, junk, stat, rr)
                t0 = tok0 + c0
                if not last:
                    k.dma(k.sp, self.X2[t0:t0 + n, :], rr[P, :], out_t=self.X2, in_t=rr)
                elif t0 < T:
                    k.dma(k.sp, self.y_p[t0:t0 + n, :], rr[P, :], in_t=rr, is_output=True)
                else:
                    k.dma(k.sp, self.y_s[t0 - T:t0 - T + n, :], rr[P, :], in_t=rr, is_output=True)
        k.end_pass()

    def build(self):
        k = self.k
        self.load_consts()
        for l in range(DEPTH):
            if not (self.stop_after is not None and self.stop_after[0] == "p1b_only"):
                self.p1a(l)
            if self.stop_after == ("p1a", l):
                break
            self.p1b(l)
            if self.stop_after is not None and self.stop_after[0] in ("p1b", "p1b_only") and self.stop_after[1] == l:
                break
            self.p2(l)
            if self.stop_after == ("p2", l):
                break
            self.p3(l)
            if self.stop_after == ("p3", l):
                break
        k.finish()


def shard_inputs(inputs, c):
    f = lambda a: np.ascontiguousarray(a, dtype=np.float32)
    sl = slice(NSQ * c, NSQ * (c + 1))
    m = {
        "x_p": f(inputs["x_prompt"][c]),
        "x_s": f(inputs["x_sample"][sl].reshape(NSQ * TS, D)),
        "cache_k": f(inputs["cache_k"][:, sl].reshape(DEPTH, NSQ, T, 128)),
        "cache_v": f(inputs["cache_v"][:, sl].reshape(DEPTH, NSQ, T, 128)),
        "cache_ki": f(inputs["cache_kidx"][:, sl]),
        "state_conv": f(inputs["state_conv"][:, sl]),
        "state_ssm": f(inputs["state_ssm"][:, sl]),
    }
    for n in ["w_in", "b_gate", "conv_w", "a_log", "dt_bias", "gdn_norm_w", "w_proj_a", "w_proj_b", "w_out",
              "ln1_g", "ln1_b", "w_up", "w_down", "ln2_g", "ln2_b"]:
        m[n] = f(inputs[n])
    for n, v in make_consts().items():
        m["c_" + n] = v
    return m


def run(inputs, debug=False, stop_after=None, trace=False):
    prog = Prog(debug=debug, stop_after=stop_after)
    in_maps = [shard_inputs(inputs, c) for c in range(8)]
    res = run_bass_kernel_spmd(prog.nc, in_maps, core_ids=list(range(8)), trace=trace)
    return res


def kernel(**inputs):
    res = run(inputs)
    r = res.results
    cat = lambda n: np.stack([r[c][n] for c in range(8)], axis=0)
    y_p = cat("y_p")
    y_s = cat("y_s").reshape(32, TS, D)
    nk_p = np.transpose(cat("nk_p"), (1, 0, 2, 3)).reshape(DEPTH, 8, T, 2, 64)
    nv_p = np.transpose(cat("nv_p"), (1, 0, 2, 3)).reshape(DEPTH, 8, T, 2, 64)
    nki_p = np.transpose(cat("nki_p"), (1, 0, 2, 3))
    nconv_p = np.transpose(cat("nconv_p"), (1, 0, 2, 3))
    nssm_p = np.transpose(cat("nssm_p"), (1, 0, 2, 3, 4))
    nk_s = np.transpose(cat("nk_s"), (1, 0, 2, 3)).reshape(DEPTH, 32, TS, 2, 64)
    nv_s = np.transpose(cat("nv_s"), (1, 0, 2, 3)).reshape(DEPTH, 32, TS, 2, 64)
    nki_s = np.transpose(cat("nki_s"), (1, 0, 2, 3)).reshape(DEPTH, 32, TS, 64)
    nconv_s = np.transpose(cat("nconv_s"), (1, 0, 2, 3, 4)).reshape(DEPTH, 32, 3, 1536)
    nssm_s = np.transpose(cat("nssm_s"), (1, 0, 2, 3, 4, 5)).reshape(DEPTH, 32, 4, 128, 128)
    return tuple(np.ascontiguousarray(a, dtype=np.float32) for a in
                 (y_p, y_s, nk_p, nv_p, nki_p, nconv_p, nssm_p, nk_s, nv_s, nki_s, nconv_s, nssm_s))
```

```python
import math
from contextlib import ExitStack
import numpy as np
import concourse.bass as bass
import concourse.mybir as mybir
from concourse.bass_utils import run_bass_kernel_spmd

F32 = mybir.dt.float32
BF16 = mybir.dt.bfloat16
AF = mybir.ActivationFunctionType
ALU = mybir.AluOpType
AX = mybir.AxisListType

D = 1024
T = 4096
TS = 16
NSQ = 4
NTOK = T + NSQ * TS
DEPTH = 2
DIN = 5456
DFF = 2816
O_QA, O_KA, O_VA, O_QI, O_KI, O_WI, O_QKV, O_AB, O_BB, O_GB, O_GATES = 0, 512, 640, 768, 1280, 1344, 1352, 2888, 2892, 2896, 3408
ALPHA = (2 * DEPTH) ** 0.25
INDEX_SCALE = (8 * 64) ** -0.5
LN_EPS = 1e-5
NORM_EPS = 1e-6
NBIS = 16
NEG = -30000.0
KCOLS = 33 * 128


class Tok:
    __slots__ = ("sem", "val", "key")

    def __init__(self, sem, val, key):
        self.sem = sem
        self.val = val
        self.key = key


class Eng:
    def __init__(self, nc, name, h):
        self.name = name
        self.key = name
        self.h = h
        self.sem = nc.alloc_semaphore("e_" + name)
        self.cnt = 0
        self.waited = {}

    def wait(self, tok):
        if tok is None:
            return
        if self.waited.get(tok.key, 0) >= tok.val:
            return
        self.h.wait_ge(tok.sem, tok.val)
        self.waited[tok.key] = tok.val


class TT:
    def __init__(self, t, name, sbuf=True):
        self.t = t
        self.name = name
        self.sbuf = sbuf
        self.w = None
        self.r = {}
        self.dsem = None

    def __getitem__(self, idx):
        return self.t[idx]


class K:
    def __init__(self, nc):
        self.nc = nc
        self.pe = Eng(nc, "pe", nc.tensor)
        self.act = Eng(nc, "act", nc.scalar)
        self.dve = Eng(nc, "dve", nc.vector)
        self.pool = Eng(nc, "pool", nc.gpsimd)
        self.sp = Eng(nc, "sp", nc.sync)
        self.engs = [self.pe, self.act, self.dve, self.pool, self.sp]
        self.dsem_pool = []
        self.ndsem = 0
        self.out_toks = {}
        self.pass_tiles = []
        self.es = None
        self.uid = 0
        self.ps = [TT(nc.alloc_psum_tensor(f"psb{i}", [128, 512], F32), f"psb{i}") for i in range(8)]

    def begin_pass(self):
        self.es = ExitStack()
        self.pass_tiles = []

    def sb(self, name, shape, dt):
        self.uid += 1
        t = self.es.enter_context(self.nc.sbuf_tensor(f"{name}_{self.uid}", list(shape), dt))
        tt = TT(t, name)
        self.pass_tiles.append(tt)
        return tt

    def get_dsem(self):
        if self.dsem_pool:
            return self.dsem_pool.pop()
        self.ndsem += 1
        return [self.nc.alloc_semaphore(f"d{self.ndsem}"), 0]

    def barrier(self, tiles):
        toks = [Tok(e.sem, e.cnt, e.key) for e in self.engs if e.cnt > 0]
        seen = set()
        for tt in tiles:
            if tt.dsem is not None and tt.dsem[1] > 0 and id(tt.dsem) not in seen:
                seen.add(id(tt.dsem))
                toks.append(Tok(tt.dsem[0], 16 * tt.dsem[1], ("d", id(tt.dsem))))
        for e in self.engs:
            for tok in toks:
                if tok.key != e.key:
                    e.wait(tok)

    def end_pass(self):
        self.barrier(self.pass_tiles)
        for tt in self.pass_tiles:
            if tt.dsem is not None:
                self.dsem_pool.append(tt.dsem)
                tt.dsem = None
        self.es.close()
        self.es = None
        self.pass_tiles = []

    def op(self, e, fn, outs=(), ins=(), inc=True):
        for t in ins:
            e.wait(t.w)
        strict = e.key != "pe"
        for t in outs:
            if t.w is not None and (strict or t.w.key != e.key):
                e.wait(t.w)
            for kk, tok in t.r.items():
                if strict or kk != e.key:
                    e.wait(tok)
        inst = fn()
        if inc:
            inst.then_inc(e.sem, 1)
            e.cnt += 1
            tok = Tok(e.sem, e.cnt, e.key)
        else:
            tok = Tok(e.sem, e.cnt + 1, e.key)
        for t in outs:
            t.w = tok
            t.r = {}
        for t in ins:
            if t not in outs:
                t.r[e.key] = tok
        return inst

    def dma(self, q, out_ap, in_ap, out_t=None, in_t=None, is_output=False):
        if in_t is not None:
            q.wait(in_t.w)
        if out_t is not None:
            q.wait(out_t.w)
            for kk, tok in out_t.r.items():
                q.wait(tok)
        own = out_t if (out_t is not None and out_t.sbuf) else in_t
        if own.dsem is None:
            own.dsem = self.get_dsem()
        ds = own.dsem
        inst = q.h.dma_start(out=out_ap, in_=in_ap)
        ds[1] += 1
        inst.then_inc(ds[0], 16)
        tok = Tok(ds[0], 16 * ds[1], ("d", id(ds)))
        if out_t is not None:
            out_t.w = tok
            out_t.r = {}
        if in_t is not None:
            in_t.r[tok.key] = tok
        if is_output:
            self.out_toks[tok.key] = tok

    def finish(self):
        for tok in self.out_toks.values():
            self.sp.wait(tok)
        self.barrier([])

    def mm(self, out_t, out_ap, a_t, a_ap, b_t, b_ap, start=True, stop=True, inc=None):
        nc = self.nc
        ins = [a_t, b_t] if b_t is not a_t else [a_t]
        return self.op(self.pe, lambda: nc.tensor.matmul(out_ap, a_ap, b_ap, start=start, stop=stop),
                       outs=[out_t], ins=ins, inc=(stop if inc is None else inc))

    def tr(self, out_t, out_ap, a_t, a_ap, id_t, id_ap, inc=True):
        nc = self.nc
        return self.op(self.pe, lambda: nc.tensor.transpose(out_ap, a_ap, id_ap), outs=[out_t], ins=[a_t, id_t], inc=inc)

    def actv(self, out_t, out_ap, in_t, in_ap, func, bias=None, scale=None, accum=None, extra_ins=(), eng=None):
        nc = self.nc
        kw = {}
        if bias is not None:
            kw["bias"] = bias
        if scale is not None:
            kw["scale"] = scale
        outs = [out_t]
        if accum is not None:
            kw["accum_out"] = accum[1]
            outs.append(accum[0])
        return self.op(self.act, lambda: nc.scalar.activation(out_ap, in_ap, func, **kw), outs=outs,
                       ins=[in_t] + list(extra_ins))

    def ts(self, e, out_t, out_ap, in_t, in_ap, s1, s2, op0, op1=None, accum=None, extra_ins=()):
        outs = [out_t]
        kw = {}
        if op1 is not None:
            kw["op1"] = op1
        if accum is not None:
            kw["accum_out"] = accum[1]
            outs.append(accum[0])
        return self.op(e, lambda: e.h.tensor_scalar(out_ap, in_ap, s1, s2, op0, **kw), outs=outs,
                       ins=[in_t] + list(extra_ins))

    def tt(self, e, out_t, out_ap, a_t, a_ap, b_t, b_ap, op):
        ins = [a_t, b_t] if b_t is not a_t else [a_t]
        return self.op(e, lambda: e.h.tensor_tensor(out_ap, a_ap, b_ap, op), outs=[out_t], ins=ins)

    def stt(self, e, out_t, out_ap, a_t, a_ap, scalar, b_t, b_ap, op0, op1, extra_ins=()):
        ins = [a_t] + ([b_t] if b_t is not a_t else []) + list(extra_ins)
        return self.op(e, lambda: e.h.scalar_tensor_tensor(out_ap, a_ap, scalar, b_ap, op0, op1), outs=[out_t], ins=ins)

    def cp(self, e, out_t, out_ap, in_t, in_ap):
        if e is self.act:
            nc = self.nc
            return self.op(e, lambda: nc.scalar.copy(out_ap, in_ap), outs=[out_t], ins=[in_t])
        return self.op(e, lambda: e.h.tensor_copy(out_ap, in_ap), outs=[out_t], ins=[in_t])

    def memset(self, e, out_t, out_ap, val):
        return self.op(e, lambda: e.h.memset(out_ap, val), outs=[out_t], ins=[])


def make_consts():
    c = {}
    c["ident"] = np.eye(128, dtype=np.float32)
    c["ones"] = np.ones((128, 128), np.float32)
    i2 = np.zeros((128, 256), np.float32)
    i2[:, 0:128] = np.eye(128)
    i2[:, 128:256] = np.eye(128)
    c["i2"] = i2
    c["i4"] = np.tile(np.eye(128, dtype=np.float32), (1, 4))
    c["i4s"] = np.tile(np.eye(16, dtype=np.float32), (1, 4))
    i2s = np.zeros((16, 32), np.float32)
    i2s[:, 0:16] = np.eye(16)
    i2s[:, 16:32] = np.eye(16)
    c["i2s"] = i2s
    tt_, cc_ = np.meshgrid(np.arange(64), np.arange(64), indexing="ij")
    c["U"] = (tt_ <= cc_).astype(np.float32)
    c["Urev"] = (tt_ > cc_).astype(np.float32)
    c["maskT"] = np.where(cc_ >= tt_, 0.0, NEG).astype(np.float32)
    c["su01"] = (cc_ > tt_).astype(np.float32)
    c["caus01"] = (cc_ >= tt_).astype(np.float32)
    c["negones"] = -np.ones((128, 128), np.float32)
    c["pow2"] = np.tile((2.0 ** -(np.arange(32) + 1.0)).astype(np.float32)[None, :], (128, 1))
    return c


class Prog:
    def __init__(self, debug=False, stop_after=None):
        self.debug = debug
        self.stop_after = stop_after
        nc = bass.Bass("TRN2", target_bir_lowering=False)
        self.nc = nc
        self.k = K(nc)

        def din(name, shape):
            return nc.dram_tensor(name, list(shape), F32, kind="ExternalInput").ap()

        def dout(name, shape):
            return nc.dram_tensor(name, list(shape), F32, kind="ExternalOutput").ap()

        self.x_p = din("x_p", [T, D])
        self.x_s = din("x_s", [NSQ * TS, D])
        self.cache_k = din("cache_k", [DEPTH, NSQ, T, 128])
        self.cache_v = din("cache_v", [DEPTH, NSQ, T, 128])
        self.cache_ki = din("cache_ki", [DEPTH, NSQ, T, 64])
        self.state_conv = din("state_conv", [DEPTH, NSQ, 3, 1536])
        self.state_ssm = din("state_ssm", [DEPTH, NSQ, 4, 128, 128])
        self.w_in = din("w_in", [DEPTH, D, DIN])
        self.b_gate = din("b_gate", [DEPTH, 2048])
        self.conv_w = din("conv_w", [DEPTH, 4, 1536])
        self.a_log = din("a_log", [DEPTH, 4])
        self.dt_bias = din("dt_bias", [DEPTH, 4])
        self.gdn_norm_w = din("gdn_norm_w", [DEPTH, 128])
        self.w_proj_a = din("w_proj_a", [DEPTH, 512, D])
        self.w_proj_b = din("w_proj_b", [DEPTH, 512, D])
        self.w_out = din("w_out", [DEPTH, D, D])
        self.ln1_g = din("ln1_g", [DEPTH, D])
        self.ln1_b = din("ln1_b", [DEPTH, D])
        self.w_up = din("w_up", [DEPTH, D, 2 * DFF])
        self.w_down = din("w_down", [DEPTH, DFF, D])
        self.ln2_g = din("ln2_g", [DEPTH, D])
        self.ln2_b = din("ln2_b", [DEPTH, D])
        self.cst = {n: din("c_" + n, list(v.shape)) for n, v in make_consts().items()}

        self.y_p = dout("y_p", [T, D])
        self.y_s = dout("y_s", [NSQ * TS, D])
        self.nk_p = dout("nk_p", [DEPTH, T, 128])
        self.nv_p = dout("nv_p", [DEPTH, T, 128])
        self.nki_p = dout("nki_p", [DEPTH, T, 64])
        self.nconv_p = dout("nconv_p", [DEPTH, 3, 1536])
        self.nssm_p = dout("nssm_p", [DEPTH, 4, 128, 128])
        self.nk_s = dout("nk_s", [DEPTH, NSQ * TS, 128])
        self.nv_s = dout("nv_s", [DEPTH, NSQ * TS, 128])
        self.nki_s = dout("nki_s", [DEPTH, NSQ * TS, 64])
        self.nconv_s = dout("nconv_s", [DEPTH, NSQ, 3, 1536])
        self.nssm_s = dout("nssm_s", [DEPTH, NSQ, 4, 128, 128])

        skind = "ExternalOutput" if debug else "Internal"
        self.YT = TT(nc.dram_tensor("scr_yt", [D, NTOK], BF16, kind=skind).ap(), "YT", sbuf=False)
        self.X1 = TT(nc.dram_tensor("scr_x1", [NTOK, D], F32, kind=skind).ap(), "X1", sbuf=False)
        self.X2 = TT(nc.dram_tensor("scr_x2", [NTOK, D], F32, kind=skind).ap(), "X2", sbuf=False)
        self.build()

    def x_rows(self, l, tok0, n):
        if l == 0:
            if tok0 < T:
                return None, self.x_p[tok0:tok0 + n, :]
            return None, self.x_s[tok0 - T:tok0 - T + n, :]
        return self.X2, self.X2[tok0:tok0 + n, :]

    def load_consts(self):
        k = self.k
        nc = self.nc
        es = ExitStack()
        self.ces = es
        self.ctiles = []

        def csb(name, shape, dt):
            t = es.enter_context(nc.sbuf_tensor("k_" + name, list(shape), dt))
            tt = TT(t, name)
            self.ctiles.append(tt)
            return tt

        self.ident = csb("ident", [128, 128], F32)
        k.dma(k.sp, self.ident[:, :], self.cst["ident"][:, :], out_t=self.ident)
        self.ones = csb("ones", [128, 128], F32)
        k.dma(k.sp, self.ones[:, :], self.cst["ones"][:, :], out_t=self.ones)
        self.i2 = csb("i2", [128, 256], BF16)
        k.dma(k.pool, self.i2[:, :], self.cst["i2"][:, :], out_t=self.i2)
        self.i2s = csb("i2s", [16, 32], BF16)
        k.dma(k.pool, self.i2s[:, :], self.cst["i2s"][:, :], out_t=self.i2s)
        self.i4 = csb("i4", [128, 512], BF16)
        k.dma(k.pool, self.i4[:, :], self.cst["i4"][:, :], out_t=self.i4)
        self.i4s = csb("i4s", [16, 64], BF16)
        k.dma(k.pool, self.i4s[:, :], self.cst["i4s"][:, :], out_t=self.i4s)
        self.pow2 = csb("pow2", [128, 32], F32)
        k.dma(k.sp, self.pow2[:, :], self.cst["pow2"][:, :], out_t=self.pow2)
        for nm in ["U", "Urev", "maskT", "su01", "caus01"]:
            t = csb(nm, [64, 64], F32)
            k.dma(k.sp, t[:, :], self.cst[nm][:, :], out_t=t)
            setattr(self, "c_" + nm, t)
        self.negones = csb("negones", [128, 128], F32)
        k.dma(k.sp, self.negones[:, :], self.cst["negones"][:, :], out_t=self.negones)

    def load_xT(self, l, tok0, ntok, xin, xT, col0, evac):
        k = self.k
        src_t, src = self.x_rows(l, tok0, ntok)
        k.dma(k.sp, xin[0:ntok, :], src, out_t=xin, in_t=src_t)
        for grp in range(2):
            ps = k.ps[6 + grp]
            for kk in range(4):
                kc = grp * 4 + kk
                k.tr(ps, ps[:, kk * ntok:(kk + 1) * ntok], xin, xin[0:ntok, kc * 128:(kc + 1) * 128],
                     self.ident, self.ident[0:ntok, 0:ntok], inc=(kk == 3))
            k.cp(evac, xT, xT[:, grp * 4:(grp + 1) * 4, col0:col0 + ntok],
                 ps, ps[:, 0:4 * ntok].rearrange("p (a t) -> p a t", a=4))

    def p1a(self, l):
        k = self.k
        nc = self.nc
        k.begin_pass()
        wsrc = self.w_in[l].rearrange("(kc p) e -> p kc e", p=128)
        Wfm = k.sb("Wfm", [128, 8, 1408], BF16)
        Wtm = k.sb("Wtm", [128, 8, 328], BF16)

        def wl(dst_t, d0, s0, n):
            k.dma(k.pool, dst_t[:, :, d0:d0 + n], wsrc[:, :, s0:s0 + n], out_t=dst_t)

        wl(Wfm, 0, O_QA, 512)
        wl(Wfm, 512, O_KA, 64)
        wl(Wfm, 576, O_KA, 64)
        wl(Wfm, 640, O_KA + 64, 64)
        wl(Wfm, 704, O_KA + 64, 64)
        wl(Wfm, 768, O_QI, 512)
        wl(Wfm, 1280, O_KI, 64)
        wl(Wfm, 1344, O_KI, 64)
        wl(Wtm, 0, O_KA, 256)
        wl(Wtm, 256, O_KI, 64)
        wl(Wtm, 320, O_WI, 8)

        kT2g = [k.sb(f"kT2g{g}", [128, KCOLS], BF16) for g in range(2)]
        kiT2 = k.sb("kiT2", [128, KCOLS], BF16)
        vext = k.sb("vext", [128, 33, 2, 65], BF16)
        k.memset(k.pool, vext, vext[:, :, :, 64:65], 1.0)
        xin = [k.sb(f"xin{i}", [128, D], F32) for i in range(2)]
        xT = k.sb("xT", [128, 8, 512], BF16)
        qaLo = [k.sb(f"qaLo{i}", [128, 4, 512], BF16) for i in range(2)]
        qaHi = [k.sb(f"qaHi{i}", [128, 4, 512], BF16) for i in range(2)]
        qiLo = [k.sb(f"qiLo{i}", [128, 4, 512], BF16) for i in range(2)]
        qiHi = [k.sb(f"qiHi{i}", [128, 4, 512], BF16) for i in range(2)]
        for i in range(2):
            k.memset(k.pool, qaLo[i], qaLo[i][64:128, :, :], 0.0)
            k.memset(k.pool, qiLo[i], qiLo[i][64:128, :, :], 0.0)
            k.memset(k.pool, qaHi[i], qaHi[i][0:64, :, :], 0.0)
            k.memset(k.pool, qiHi[i], qiHi[i][0:64, :, :], 0.0)
        kfm = k.sb("kfm", [128, 3, 64], BF16)
        tm1 = [k.sb(f"tm1_{i}", [128, 328], F32) for i in range(2)]
        wscs = [k.sb(f"wsc{i}", [128, 4, 8], F32) for i in range(2)]
        Ss = [k.sb(f"S{i}", [128, KCOLS], F32) for i in range(2)]
        junk = k.sb("junk", [128, KCOLS], BF16)
        MBs = [k.sb(f"MB{i}", [128, KCOLS], BF16) for i in range(2)]
        R = [k.sb(f"R{i}", [128, 512], F32) for i in range(2)]
        PT = [k.sb(f"PT{i}", [128, 512], BF16) for i in range(3)]
        rd = k.sb("rd", [128, 1024], F32)
        bcs = k.sb("bcs", [64, 1024], F32)
        yTt = k.sb("yTt", [64, 1024], BF16)
        stt_ = k.sb("stat", [128, 8], F32)
        dtab = k.sb("dtab", [128, NBIS], F32)
        trial = k.sb("trial", [128, NBIS + 1], F32)
        lo_t = k.sb("lo_t", [128, NBIS + 1], F32)
        cnt = k.sb("cnt", [128, NBIS], F32)
        dd = k.sb("dd", [128, NBIS], F32)
        ktm = k.sb("ktm", [128, 16, 128], F32)

        def stage_units(tok0, NS, Tq, nsub, key_col0, key_tile0, is_sample, par):
            wsc = wscs[par]
            units = []
            done = 0
            i = 0
            while done < NS:
                n = min(128, NS - done)
                units.append(lambda d=done, n=n, i=i: self.load_xT(l, tok0 + d, n, xin[i % 2], xT, d, k.act))
                done += n
                i += 1

            def fm(mt):
                ps = k.ps[6 + (mt % 2)]
                for kc in range(8):
                    k.mm(ps, ps[:, 0:NS], Wfm, Wfm[:, kc, mt * 128:(mt + 1) * 128], xT, xT[:, kc, 0:NS],
                         start=(kc == 0), stop=(kc == 7))
                if mt < 4:
                    k.cp(k.act, qaLo[par], qaLo[par][0:64, mt, 0:NS], ps, ps[0:64, 0:NS])
                    k.cp(k.act, qaHi[par], qaHi[par][64:128, mt, 0:NS], ps, ps[64:128, 0:NS])
                elif mt < 6:
                    g = mt - 4
                    if is_sample:
                        k.cp(k.act, kfm, kfm[:, g, 0:NS], ps, ps[:, 0:NS])
                    else:
                        k.cp(k.act, kT2g[g], kT2g[g][:, key_col0:key_col0 + NS], ps, ps[:, 0:NS])
                elif mt < 10:
                    k.cp(k.act, qiLo[par], qiLo[par][0:64, mt - 6, 0:NS], ps, ps[0:64, 0:NS])
                    k.cp(k.act, qiHi[par], qiHi[par][64:128, mt - 6, 0:NS], ps, ps[64:128, 0:NS])
                else:
                    if is_sample:
                        k.cp(k.act, kfm, kfm[:, 2, 0:NS], ps, ps[:, 0:NS])
                    else:
                        k.cp(k.act, kiT2, kiT2[:, key_col0:key_col0 + NS], ps, ps[:, 0:NS])

            for mt in (6, 7, 8, 9, 10, 4, 5):
                units.append(lambda mt=mt: fm(mt))

            def tmj(j):
                ps = k.ps[6 + (j % 2)]
                t1 = tm1[j % 2]
                for kc in range(8):
                    k.mm(ps, ps[0:Tq, 0:328], xT, xT[:, kc, j * Tq:(j + 1) * Tq], Wtm, Wtm[:, kc, 0:328],
                         start=(kc == 0), stop=(kc == 7))
                k.cp(k.act, t1, t1[0:Tq, :], ps, ps[0:Tq, 0:328])
                r0 = tok0 + j * Tq
                if is_sample:
                    r0 -= T
                    dk, dv, dki = self.nk_s, self.nv_s, self.nki_s
                else:
                    dk, dv, dki = self.nk_p, self.nv_p, self.nki_p
                k.dma(k.pool, dk[l, r0:r0 + Tq, :], t1[0:Tq, 0:128], in_t=t1, is_output=True)
                k.dma(k.pool, dv[l, r0:r0 + Tq, :], t1[0:Tq, 128:256], in_t=t1, is_output=True)
                k.dma(k.pool, dki[l, r0:r0 + Tq, :], t1[0:Tq, 256:320], in_t=t1, is_output=True)
                if not is_sample:
                    kt = key_tile0 + j
                    k.cp(k.pool, vext, vext[0:Tq, kt, :, 0:64], t1, t1[0:Tq, 128:256].rearrange("p (g d) -> p g d", g=2))
                k.ts(k.pool, wsc, wsc[0:Tq, j, :], t1, t1[0:Tq, 320:328], INDEX_SCALE, None, ALU.mult)

            for j in range(nsub):
                units.append(lambda j=j: tmj(j))
            for mt in range(4):
                units.append(lambda mt=mt: fm(mt))
            return units

        def stage_a(*args):
            for u in stage_units(*args):
                u()

        class TD:
            pass

        def idx(td, pending=None, quota=0):
            P = slice(0, td.Tq)
            qc = slice(td.j * td.Tq, (td.j + 1) * td.Tq)
            S, wsc, n = td.S, td.wsc, td.n
            nblk = (n + 511) // 512
            for kb in range(nblk):
                c0 = kb * 512
                w = min(512, n - c0)
                for h in range(8):
                    m, half = divmod(h, 2)
                    ps = k.ps[half]
                    qi = (qiLo if half == 0 else qiHi)[td.par]
                    k.mm(ps, ps[P, 0:w], qi, qi[:, m, qc], kiT2, kiT2[:, c0:c0 + w])
                    Rt = R[h % 2]
                    k.actv(Rt, Rt[P, 0:w], ps, ps[P, 0:w], AF.Relu)
                    if h == 0:
                        k.ts(k.dve, S, S[P, c0:c0 + w], Rt, Rt[P, 0:w], wsc[P, td.j, 0:1], None, ALU.mult, extra_ins=[wsc])
                    else:
                        k.stt(k.dve, S, S[P, c0:c0 + w], Rt, Rt[P, 0:w], wsc[P, td.j, h:h + 1], S, S[P, c0:c0 + w],
                              ALU.mult, ALU.add, extra_ins=[wsc])
                for _ in range(min(len(pending), (quota + nblk - 1) // nblk) if pending else 0):
                    pending.pop(0)()
                    quota -= 1
            while pending and quota > 0:
                pending.pop(0)()
                quota -= 1
            if td.corner:
                k.memset(k.dve, S, S[0:64, n - 64:n], -1.0e30)

        def bis(td):
            P = slice(0, td.Tq)
            S, MB, n = td.S, td.MB, td.n
            st = stt_
            k.op(k.dve, lambda: nc.vector.tensor_reduce(st[P, 0:1], S[P, 0:n], AX.X, ALU.max), outs=[st], ins=[S])
            if td.corner:
                k.op(k.dve, lambda: nc.vector.tensor_reduce(st[P, 1:2], S[P, 0:n - 64], AX.X, ALU.min), outs=[st], ins=[S])
                k.op(k.dve, lambda: nc.vector.tensor_reduce(st[64:128, 2:3], S[64:128, n - 64:n], AX.X, ALU.min), outs=[st], ins=[S])
                k.tt(k.dve, st, st[64:128, 1:2], st, st[64:128, 1:2], st, st[64:128, 2:3], ALU.min)
            else:
                k.op(k.dve, lambda: nc.vector.tensor_reduce(st[P, 1:2], S[P, 0:n], AX.X, ALU.min), outs=[st], ins=[S])
            k.tt(k.dve, st, st[P, 3:4], st, st[P, 0:1], st, st[P, 1:2], ALU.subtract)
            k.ts(k.dve, dtab, dtab[P, :], self.pow2, self.pow2[P, 0:NBIS], st[P, 3:4], None, ALU.mult, extra_ins=[st])
            k.cp(k.dve, lo_t, lo_t[P, 0:1], st, st[P, 1:2])
            k.tt(k.dve, trial, trial[P, 0:1], st, st[P, 1:2], dtab, dtab[P, 0:1], ALU.add)
            for it in range(NBIS):
                k.ts(k.dve, junk, junk[P, 0:n], S, S[P, 0:n], trial[P, it:it + 1], None, ALU.is_ge, ALU.add,
                     accum=(cnt, cnt[P, it:it + 1]), extra_ins=[trial])
                k.ts(k.dve, dd, dd[P, it:it + 1], cnt, cnt[P, it:it + 1], 255.5, dtab[P, it:it + 1], ALU.is_ge, ALU.mult,
                     extra_ins=[dtab])
                k.tt(k.dve, lo_t, lo_t[P, it + 1:it + 2], lo_t, lo_t[P, it:it + 1], dd, dd[P, it:it + 1], ALU.add)
                if it + 1 < NBIS:
                    k.tt(k.dve, trial, trial[P, it + 1:it + 2], lo_t, lo_t[P, it + 1:it + 2], dtab, dtab[P, it + 1:it + 2], ALU.add)
            lo = lo_t[P, NBIS:NBIS + 1]
            k.ts(k.dve, MB, MB[P, 0:n], S, S[P, 0:n], lo, NEG, ALU.is_lt, ALU.mult, extra_ins=[lo_t])

        def att_main(td):
            Tq = td.Tq
            P = slice(0, Tq)
            qc = slice(td.j * Tq, (td.j + 1) * Tq)
            MB, i4 = td.MB, td.i4
            qlo, qhi = qaLo[td.par], qaHi[td.par]
            W4 = 4 * Tq
            nkt = len(td.keytiles)
            units = [(g, ti) for g in range(2) for ti in range(nkt)]

            def s_part(u):
                g, ti = u
                c0, nk, vt = td.keytiles[ti]
                psS = k.ps[2 + (u[0] * nkt + ti) % 2]
                k.mm(psS, psS[0:nk, 0:W4], MB, MB[P, c0:c0 + nk], i4, i4[P, 0:W4], start=True, stop=False)
                k.mm(psS, psS[0:nk, 0:2 * Tq].rearrange("p (a t) -> p a t", a=2), kT2g[g], kT2g[g][:, c0:c0 + nk],
                     qlo, qlo[:, 2 * g:2 * g + 2, qc], start=False, stop=False)
                k.mm(psS, psS[0:nk, 2 * Tq:W4].rearrange("p (a t) -> p a t", a=2), kT2g[g], kT2g[g][:, c0:c0 + nk],
                     qhi, qhi[:, 2 * g:2 * g + 2, qc], start=False, stop=True)
                PTt = PT[(u[0] * nkt + ti) % 3]
                k.actv(PTt, PTt[0:nk, 0:W4], psS, psS[0:nk, 0:W4], AF.Exp, scale=0.125)

            def v_part(u):
                g, ti = u
                c0, nk, vt = td.keytiles[ti]
                Og = k.ps[4 + g]
                PTt = PT[(u[0] * nkt + ti) % 3]
                k.mm(Og, Og[0:65, 0:W4], vext, vext[0:nk, vt, g, :], PTt, PTt[0:nk, 0:W4],
                     start=(ti == 0), stop=(ti == nkt - 1))

            for ui, u in enumerate(units):
                s_part(u)
                if ui >= 1:
                    v_part(units[ui - 1])
            v_part(units[-1])

        def att_fin(td):
            Tq = td.Tq
            W4 = 4 * Tq
            for g in range(2):
                Og = k.ps[4 + g]
                k.actv(rd, rd[64:65, g * 512:g * 512 + W4], Og, Og[64:65, 0:W4], AF.Ln)
                k.actv(rd, rd[64:65, g * 512:g * 512 + W4], rd, rd[64:65, g * 512:g * 512 + W4], AF.Exp, scale=-1.0)
                psB = k.ps[6 + g]
                k.mm(psB, psB[0:64, 0:W4], self.ones, self.ones[64:65, 0:64], rd, rd[64:65, g * 512:g * 512 + W4])
                k.cp(k.act, bcs, bcs[:, g * 512:g * 512 + W4], psB, psB[0:64, 0:W4])
                k.tt(k.dve, yTt, yTt[:, g * 512:g * 512 + W4], Og, Og[0:64, 0:W4], bcs, bcs[:, g * 512:g * 512 + W4], ALU.mult)
                for b_ in range(2):
                    dst = self.YT[256 * g:256 * g + 256, td.tok_out0:td.tok_out0 + Tq].rearrange("(a b d) t -> b d a t", a=2, b=2, d=64)[b_]
                    k.dma(k.pool, dst, yTt[:, g * 512 + b_ * 2 * Tq:g * 512 + (b_ + 1) * 2 * Tq].rearrange("d (a t) -> d a t", a=2),
                          out_t=self.YT, in_t=yTt)

        tds = []
        for st in range(T // 512):
            for j in range(4):
                td = TD()
                i = st * 4 + j
                td.st, td.j, td.Tq = st, j, 128
                td.n = st * 512 + (j + 1) * 128
                td.corner = True
                td.keytiles = [(t * 128, 128, t) for t in range(td.n // 128)]
                td.tok_out0 = st * 512 + j * 128
                td.i4 = self.i4
                td.S, td.MB = Ss[i % 2], MBs[i % 2]
                td.par, td.wsc = st % 2, wscs[st % 2]
                tds.append(td)
        nt = len(tds)
        stage_a(0, 512, 128, 4, 0, 0, False, 0)
        for i in range(nt + 2):
            if 1 <= i <= nt:
                bis(tds[i - 1])
            if 2 <= i:
                att_main(tds[i - 2])
            if i < nt:
                td = tds[i]
                nst = td.st + 1
                if td.j == 0:
                    pending = stage_units(nst * 512, 512, 128, 4, nst * 512, nst * 4, False, nst % 2) if nst < T // 512 else []
                quota = (len(pending) + (3 - td.j)) // (4 - td.j)
                if td.j < 2:
                    quota = min(quota, max(0, len(pending) - 4))
                idx(td, pending, quota)
            if 2 <= i:
                att_fin(tds[i - 2])

        stage_a(T, NSQ * TS, TS, NSQ, 0, 0, True, 0)
        sds = []
        for s in range(NSQ):
            td = TD()
            td.st, td.j, td.Tq = 0, s, TS
            td.n = T + TS
            td.corner = False
            td.keytiles = [(t * 128, 128, t) for t in range(32)] + [(T, TS, 32)]
            td.tok_out0 = T + s * TS
            td.i4 = self.i4s
            td.S, td.MB = Ss[s % 2], MBs[s % 2]
            td.par, td.wsc = 0, wscs[0]
            sds.append(td)

        def prep_k(s, which):
            if which < 2:
                src = self.cache_k[l, s].rearrange("(t p) c -> p t c", p=128)[:, :, which * 64:(which + 1) * 64]
                dst = kT2g[which]
            else:
                src = self.cache_ki[l, s].rearrange("(t p) c -> p t c", p=128)
                dst = kiT2
            for hf in range(2):
                k.dma(k.sp, ktm[:, :, 0:64], src[:, hf * 16:(hf + 1) * 16, :], out_t=ktm)
                k.dma(k.sp, ktm[:, :, 64:128], src[:, hf * 16:(hf + 1) * 16, :], out_t=ktm)
                for b4 in range(4):
                    blk = hf * 4 + b4
                    ps = k.ps[6 + (blk % 2)]
                    for a in range(4):
                        k.tr(ps, ps[:, a * 128:(a + 1) * 128], ktm, ktm[:, b4 * 4 + a, :], self.ident, self.ident[:, :], inc=(a == 3))
                    k.cp(k.act, dst, dst[:, blk * 512:(blk + 1) * 512], ps, ps[:, :])
            k.cp(k.pool, dst, dst[:, T:T + TS], kfm, kfm[:, which, s * TS:(s + 1) * TS])

        def prep_kv(s):
            prep_k(s, 0)
            prep_k(s, 1)
            for g in range(2):
                k.dma(k.pool, vext[:, 0:32, g, 0:64],
                      self.cache_v[l, s].rearrange("(t p) c -> p t c", p=128)[:, :, g * 64:(g + 1) * 64], out_t=vext)
            ps = k.ps[6 + (s % 2)]
            for kc in range(8):
                k.mm(ps, ps[0:TS, 0:128], xT, xT[:, kc, s * TS:(s + 1) * TS], Wtm, Wtm[:, kc, 128:256],
                     start=(kc == 0), stop=(kc == 7))
            k.cp(k.act, vext, vext[0:TS, 32, :, 0:64], ps, ps[0:TS, 0:128].rearrange("p (g d) -> p g d", g=2))

        prep_k(0, 2)
        idx(sds[0])
        prep_kv(0)
        for s in range(NSQ):
            bis(sds[s])
            if s + 1 < NSQ:
                prep_k(s + 1, 2)
                idx(sds[s + 1])
            att_main(sds[s])
            att_fin(sds[s])
            if s + 1 < NSQ:
                prep_kv(s + 1)
        k.end_pass()


    def rows_to_fm(self, src_ap, nrows, rows_t, dst_t, dst_fn):
        k = self.k
        k.dma(k.sp, rows_t[0:nrows, :], src_ap, out_t=rows_t)
        for grp in range(3):
            ps = k.ps[6 + (grp % 2)]
            for a in range(4):
                m = grp * 4 + a
                k.tr(ps, ps[:, a * nrows:(a + 1) * nrows], rows_t, rows_t[0:nrows, m * 128:(m + 1) * 128],
                     self.ident, self.ident[0:nrows, 0:nrows], inc=(a == 3))
            for a in range(4):
                m = grp * 4 + a
                k.cp(k.act, dst_t, dst_fn(m), ps, ps[:, a * nrows:(a + 1) * nrows])

    def p1b(self, l):
        k = self.k
        nc = self.nc
        k.begin_pass()
        wsrc = self.w_in[l].rearrange("(kc p) e -> p kc e", p=128)
        Wfm = k.sb("Wfm", [128, 8, 1536], BF16)
        Wtm = k.sb("Wtm", [128, 8, 520], BF16)
        k.dma(k.pool, Wfm[:, :, :], wsrc[:, :, O_QKV:O_QKV + 1536], out_t=Wfm)
        k.dma(k.pool, Wtm[:, :, 0:8], wsrc[:, :, O_AB:O_AB + 8], out_t=Wtm)
        k.dma(k.pool, Wtm[:, :, 8:520], wsrc[:, :, O_GB:O_GB + 512], out_t=Wtm)
        rows_t = k.sb("rows", [16, 1536], F32)
        cw = k.sb("cw", [128, 12, 4], F32)
        self.rows_to_fm(self.conv_w[l], 4, rows_t, cw, lambda m: cw[:, m, :])
        nw = k.sb("nw", [128, 128], F32)
        k.dma(k.sp, nw[:, :], self.gdn_norm_w[l:l + 1, :].to_broadcast([128, 128]), out_t=nw)
        dtb = k.sb("dtb", [128, 4], F32)
        k.dma(k.sp, dtb[:, :], self.dt_bias[l:l + 1, :].to_broadcast([128, 4]), out_t=dtb)
        negA = k.sb("negA", [128, 4], F32)
        k.dma(k.sp, negA[:, :], self.a_log[l:l + 1, :].to_broadcast([128, 4]), out_t=negA)
        k.actv(negA, negA[:, :], negA, negA[:, :], AF.Exp)
        k.ts(k.dve, negA, negA[:, :], negA, negA[:, :], -1.0, None, ALU.mult)

        xin = [k.sb(f"xin{i}", [128, D], F32) for i in range(2)]
        xT = k.sb("xT", [128, 8, 512], BF16)
        cin = k.sb("cin", [128, 12, 515], F32)
        qkvs = k.sb("qkvs", [128, 12, 512], F32)
        qkb = k.sb("qkb", [128, 8, 512], BF16)
        Sbf = k.sb("Sbf", [128, NSQ, 4, 128], BF16)
        acc = [k.sb(f"acc{i}", [128, 512], F32) for i in range(2)]
        sq = [k.sb(f"sq{i}", [128, 512], F32) for i in range(2)]
        rn = [k.sb(f"rn{i}", [128, 512], F32) for i in range(2)]
        s2s = [k.sb(f"s2_{i}", [128, 512], F32) for i in range(2)]
        Sst = k.sb("Sst", [128, NSQ, 4, 128], F32)
        last3 = rows_t
        NM = 16

        def smal(name, shape):
            return k.sb(name, shape, F32)

        gx = smal("gx", [64, 2, 4])
        gab = smal("gab", [64, 2, 4])
        ge1 = smal("ge1", [64, 2, 4])
        gl1 = smal("gl1", [64, 2, 4])
        gsp = smal("gsp", [64, 2, 4])
        gg = smal("gg", [64, 2, 4])
        nbeta = smal("nbeta", [64, 2, 4])
        beta2 = [smal(f"beta{i}", [64, 2, 4]) for i in range(2)]
        E2 = [smal(f"E{i}", [64, 2, 12]) for i in range(2)]
        negeG2 = [smal(f"negeG{i}", [64, 2, 4]) for i in range(2)]
        gt1282 = [smal(f"gt128{i}", [128, 2, 4]) for i in range(2)]
        sgtmp = smal("sgtmp", [64, 512])

        class MV:
            def __init__(self, name, dt=F32):
                self.tt = k.sb(name, [64, 512], dt)
                self.C = 64
                self.nm = 8

            def set(self, C, nm):
                self.C = C
                self.nm = nm

            def v(self):
                return self.tt[0:self.C, 0:self.nm * self.C].rearrange("p (m c) -> p m c", m=self.nm)

        gU_, DT_, DsB_, W0f_, DTc_, dtmp_ = MV("gU"), MV("DT"), MV("DsB"), MV("W0f"), MV("DTc"), MV("dtmp")
        W_ = [MV(f"W{i}", BF16) for i in range(2)]
        X_ = [MV(f"X{i}", BF16) for i in range(2)]
        NT_ = [MV(f"NT{i}", BF16) for i in range(2)]
        NTf_ = [MV(f"NTf{i}", BF16) for i in range(2)]
        qkT2_ = [MV(f"qkT{i}", BF16) for i in range(2)]
        allmv = [gU_, DT_, DsB_, W0f_, DTc_, dtmp_] + W_ + X_ + NT_
        negG = smal("negG", [64, 2, 4])
        kdec2 = [k.sb(f"kdec{i}", [64, 2, 4, 128], BF16) for i in range(2)]
        vtm2 = [smal(f"vtm{i}", [64, 2, 4, 128]) for i in range(2)]
        sg2 = [smal(f"sg{i}", [64, 2, 512]) for i in range(2)]
        Z = k.sb("Z", [64, 4, 128], BF16)
        Zf = smal("Zf", [64, 4, 128])
        t1 = smal("t1", [64, 4, 128])
        vnew = k.sb("vnew", [64, 4, 128], BF16)
        o_t = smal("o", [64, 4, 128])
        osq = smal("osq", [64, 128])
        ss = smal("ss", [64, 4])
        rstd = smal("rstd", [64, 4])
        yb = smal("yb", [64, 4, 128])
        ybT = k.sb("ybT", [128, 4, 64], BF16)

        def bc(ap, shape, axis):
            return ap.unsqueeze(axis).to_broadcast(shape)

        def process(tok0, NS, nseq, L, C, is_sample, first, last_st):
            cinv = cin[:, :, 0:nseq * (L + 3)].rearrange("p m (s t) -> p m s t", s=nseq)
            qv = qkvs[:, :, 0:NS].rearrange("p m (s t) -> p m s t", s=nseq)
            qb = qkb[:, :, 0:NS].rearrange("p m (s t) -> p m s t", s=nseq)
            done = 0
            i = 0
            while done < NS:
                n = min(128, NS - done)
                self.load_xT(l, tok0 + done, n, xin[i % 2], xT, done, k.act)
                done += n
                i += 1
            if is_sample:
                self.rows_to_fm(self.state_conv[l].rearrange("s j c -> (s j) c"), 12, rows_t, cin,
                                lambda m: cinv[:, m, :, 0:3])
            elif first:
                k.memset(k.pool, cin, cinv[:, :, :, 0:3], 0.0)
            for mt in range(12):
                ps = k.ps[6 + (mt % 2)]
                for kc in range(8):
                    k.mm(ps, ps[:, 0:NS], Wfm, Wfm[:, kc, mt * 128:(mt + 1) * 128], xT, xT[:, kc, 0:NS],
                         start=(kc == 0), stop=(kc == 7))
                k.cp(k.act, cin, cinv[:, mt, :, 3:3 + L], ps, ps[:, 0:NS].rearrange("p (s t) -> p s t", s=nseq))
            if is_sample or last_st:
                for sq_ in range(nseq):
                    c1 = (sq_ + 1) * L
                    for blk in range(3):
                        ps = k.ps[6 + (blk % 2)]
                        for kc in range(8):
                            k.mm(ps, ps[0:3, 0:512], xT, xT[:, kc, c1 - 3:c1], Wfm, Wfm[:, kc, blk * 512:(blk + 1) * 512],
                                 start=(kc == 0), stop=(kc == 7))
                        k.cp(k.act, last3, last3[0:3, blk * 512:(blk + 1) * 512], ps, ps[0:3, 0:512])
                    dst = self.nconv_s[l, sq_] if is_sample else self.nconv_p[l]
                    k.dma(k.sp, dst, last3[0:3, :], in_t=last3, is_output=True)
            for m in range(12):
                a_ = acc[m % 2]
                av = a_[:, 0:NS].rearrange("p (s t) -> p s t", s=nseq)
                k.ts(k.dve, a_, av, cin, cinv[:, m, :, 0:L], cw[:, m, 0:1], None, ALU.mult, extra_ins=[cw])
                for jj in range(1, 4):
                    k.stt(k.dve, a_, av, cin, cinv[:, m, :, jj:jj + L], cw[:, m, jj:jj + 1], a_, av, ALU.mult, ALU.add,
                          extra_ins=[cw])
                s2 = s2s[m % 2]
                k.actv(s2, s2[:, 0:NS], a_, a_[:, 0:NS], AF.Exp, scale=-1.0)
                k.actv(s2, s2[:, 0:NS], s2, s2[:, 0:NS], AF.Ln, bias=1.0)
                k.actv(s2, s2[:, 0:NS], s2, s2[:, 0:NS], AF.Exp, scale=-1.0)
                k.tt(k.pool, qkvs, qkvs[:, m, 0:NS], a_, a_[:, 0:NS], s2, s2[:, 0:NS], ALU.mult)

            def n_sq(m):
                s_ = sq[m % 2]
                k.actv(s_, s_[:, 0:NS], qkvs, qkvs[:, m, 0:NS], AF.Square)
                ps = k.ps[6 + (m % 2)]
                k.mm(ps, ps[:, 0:NS], self.ones, self.ones[:, :], s_, s_[:, 0:NS])

            def n_fin(m):
                ps = k.ps[6 + (m % 2)]
                r_ = rn[m % 2]
                k.actv(r_, r_[:, 0:NS], ps, ps[:, 0:NS], AF.Ln, bias=NORM_EPS)
                k.actv(r_, r_[:, 0:NS], r_, r_[:, 0:NS], AF.Exp, scale=-0.5)
                k.stt(k.dve, qkvs, qkvs[:, m, 0:NS], qkvs, qkvs[:, m, 0:NS], (128.0 ** -0.5) if m < 4 else 1.0,
                      r_, r_[:, 0:NS], ALU.mult, ALU.mult)
                k.cp(k.act if m % 2 else k.pool, qkb, qkb[:, m, 0:NS], qkvs, qkvs[:, m, 0:NS])

            n_sq(0)
            for m in range(8):
                if m + 1 < 8:
                    n_sq(m + 1)
                n_fin(m)
            if not is_sample:
                k.cp(k.pool, cin, cinv[:, :, :, 0:3], cin, cinv[:, :, :, L:L + 3])

            nch = L // C
            if is_sample:
                batches = [[(0, 0), (1, 0)], [(2, 0), (3, 0)]]
            else:
                batches = [[(0, ci), (0, ci + 1)] for ci in range(0, nch, 2)]
            PC = slice(0, C)
            nb = 2
            nm = 8
            for mv in allmv:
                mv.set(C, nm)
            gU, DT, DsB, W0f, DTc, dtmp = gU_.tt, DT_.tt, DsB_.tt, W0f_.tt, DTc_.tt, dtmp_.tt
            gUv, DTv, DsBv, W0fv, DTcv, dtmpv = gU_.v(), DT_.v(), DsB_.v(), W0f_.v(), DTc_.v(), dtmp_.v()
            W = [w.tt for w in W_]
            Wv = [w.v() for w in W_]
            X = [x.tt for x in X_]
            Xv = [x.v() for x in X_]
            NT = [x.tt for x in NT_]
            NTv = [x.v() for x in NT_]
            mvv = lambda ps: ps[PC, 0:nm * C].rearrange("p (m c) -> p m c", m=nm)

            def prep_units(batch, par):
                Ep, negeGp, gt128p, betap = E2[par], negeG2[par], gt1282[par], beta2[par]
                kdecp, vtmp, sgp = kdec2[par], vtm2[par], sg2[par]
                NTfp, qkTp = NTf_[par], qkT2_[par]
                NTfp.set(C, nm)
                qkTp.set(C, nm)
                U = []

                def u_tm(bi):
                    sq_, ci = batch[bi]
                    col0 = sq_ * L + ci * C
                    psA = k.ps[0]
                    for kc in range(8):
                        k.mm(psA, psA[PC, 0:8], xT, xT[:, kc, col0:col0 + C], Wtm, Wtm[:, kc, 0:8], start=(kc == 0), stop=(kc == 7))
                    k.tt(k.dve, gx, gx[PC, bi, :], psA, psA[PC, 0:4], dtb, dtb[PC, :], ALU.add)
                    k.actv(ge1, ge1[PC, bi, :], psA, psA[PC, 4:8], AF.Exp, scale=-1.0)
                    psB = k.ps[0]
                    for kc in range(8):
                        k.mm(psB, psB[PC, 0:512], xT, xT[:, kc, col0:col0 + C], Wtm, Wtm[:, kc, 8:520], start=(kc == 0), stop=(kc == 7))
                    k.actv(sgtmp, sgtmp[PC, :], psB, psB[PC, 0:512], AF.Exp, scale=-1.0)
                    k.actv(sgtmp, sgtmp[PC, :], sgtmp, sgtmp[PC, :], AF.Ln, bias=1.0)
                    k.actv(sgtmp, sgtmp[PC, :], sgtmp, sgtmp[PC, :], AF.Exp, scale=-1.0)
                    k.tt(k.dve, sgp, sgp[PC, bi, :], psB, psB[PC, 0:512], sgtmp, sgtmp[PC, :], ALU.mult)
                for bi in range(nb):
                    U.append(lambda bi=bi: u_tm(bi))

                def u_gates():
                    k.actv(ge1, ge1[PC, 0:nb, :], ge1, ge1[PC, 0:nb, :], AF.Ln, bias=1.0)
                    k.actv(betap, betap[PC, 0:nb, :], ge1, ge1[PC, 0:nb, :], AF.Exp, scale=-1.0)
                    gxa = gx[PC, 0:nb, :]
                    k.ts(k.dve, gab, gab[PC, 0:nb, :], gx, gxa, -1.0, None, ALU.mult)
                    k.tt(k.dve, gab, gab[PC, 0:nb, :], gab, gab[PC, 0:nb, :], gx, gxa, ALU.min)
                    k.actv(gl1, gl1[PC, 0:nb, :], gab, gab[PC, 0:nb, :], AF.Exp)
                    k.actv(gl1, gl1[PC, 0:nb, :], gl1, gl1[PC, 0:nb, :], AF.Ln, bias=1.0)
                    k.stt(k.dve, gsp, gsp[PC, 0:nb, :], gx, gxa, 0.0, gl1, gl1[PC, 0:nb, :], ALU.max, ALU.add)
                    k.tt(k.dve, gg, gg[PC, 0:nb, :], gsp, gsp[PC, 0:nb, :], negA, bc(negA[PC, :], [C, nb, 4], 1), ALU.mult)
                    k.ts(k.dve, nbeta, nbeta[PC, 0:nb, :], betap, betap[PC, 0:nb, :], -1.0, None, ALU.mult)
                    k.tt(k.pool, sgp, sgp[PC, 0:nb, :].rearrange("p b (h e) -> p (b h) e", h=4), sgp,
                         sgp[PC, 0:nb, :].rearrange("p b (h e) -> p (b h) e", h=4), nw, bc(nw[PC, :], [C, nm, 128], 1), ALU.mult)
                U.append(u_gates)

                def u_decay():
                    psG = k.ps[0]
                    for bi in range(nb):
                        gcol = gg[PC, bi, :]
                        k.mm(psG, psG[PC, bi * 16:bi * 16 + 4], self.c_U, self.c_U[PC, PC], gg, gcol)
                        k.mm(psG, psG[PC, bi * 16 + 4:bi * 16 + 8], self.c_Urev, self.c_Urev[PC, PC], gg, gcol)
                        k.mm(psG, psG[PC, bi * 16 + 8:bi * 16 + 12], self.ones, self.ones[PC, PC], gg, gcol)
                        k.mm(psG, psG[:, 64 + bi * 4:64 + bi * 4 + 4], self.ones, self.ones[PC, :], gg, gcol)
                    k.actv(Ep, Ep[PC, 0:nb, :], psG, psG[PC, 0:nb * 16].rearrange("p (b e) -> p b e", b=nb)[:, :, 0:12], AF.Exp)
                    k.actv(gt128p, gt128p[:, 0:nb, :], psG, psG[:, 64:64 + nb * 4].rearrange("p (b e) -> p b e", b=nb), AF.Exp)
                    k.ts(k.dve, negeGp, negeGp[PC, 0:nb, :], Ep, Ep[PC, 0:nb, 0:4], -1.0, None, ALU.mult)
                    k.ts(k.dve, negG, negG[PC, 0:nb, :], psG, psG[PC, 0:nb * 16].rearrange("p (b e) -> p b e", b=nb)[:, :, 0:4],
                         -1.0, None, ALU.mult)
                    k.tt(k.dve, gU, gUv, self.c_U, bc(self.c_U[PC, PC], [C, nm, C], 1),
                         gg, bc(gg[PC, 0:nb, :].rearrange("p b h -> p (b h)"), [C, nm, C], 2), ALU.mult)
                U.append(u_decay)

                def u_diff():
                    psD = k.ps[1]
                    for mi in range(nm):
                        k.mm(psD, psD[PC, mi * C:(mi + 1) * C], self.ones, self.ones[PC, PC], gU, gUv[:, mi, :])
                    k.tt(k.dve, dtmp, dtmpv, psD, mvv(psD), negG,
                         bc(negG[PC, 0:nb, :].rearrange("p b h -> p (b h)"), [C, nm, C], 2), ALU.add)
                    k.ts(k.dve, dtmp, dtmpv, dtmp, dtmpv, 0.0, None, ALU.min)
                    k.actv(DT, DTv, dtmp, dtmpv, AF.Exp)
                    k.tt(k.pool, DTc, DTcv, DT, DTv, self.c_caus01, bc(self.c_caus01[PC, PC], [C, nm, C], 1), ALU.mult)
                    k.tt(k.pool, DsB, DsBv, DT, DTv, self.c_su01, bc(self.c_su01[PC, PC], [C, nm, C], 1), ALU.mult)
                    k.tt(k.pool, DsB, DsBv, DsB, DsBv, nbeta,
                         bc(nbeta[PC, 0:nb, :].rearrange("p b h -> p (b h)"), [C, nm, C], 2), ALU.mult)
                U.append(u_diff)

                def u_kk():
                    psK = k.ps[2]
                    psQ = k.ps[3]
                    for bi, (sq_, ci) in enumerate(batch):
                        cs = slice(ci * C, (ci + 1) * C)
                        for h in range(4):
                            mi = bi * 4 + h
                            k.mm(psK, psK[PC, mi * C:(mi + 1) * C], qkb, qb[:, 4 + h, sq_, cs], qkb, qb[:, 4 + h, sq_, cs])
                            k.mm(psQ, psQ[PC, mi * C:(mi + 1) * C], qkb, qb[:, 4 + h, sq_, cs], qkb, qb[:, h, sq_, cs])
                    k.tt(k.dve, W0f, W0fv, psK, mvv(psK), DsB, DsBv, ALU.mult)
                    k.cp(k.pool, W[0], Wv[0], W0f, W0fv)
                    k.tt(k.dve, qkTp.tt, qkTp.v(), psQ, mvv(psQ), DTc, DTcv, ALU.mult)
                U.append(u_kk)

                def u_x0():
                    psX = k.ps[2]
                    for mi in range(nm):
                        k.tr(psX, psX[PC, mi * C:(mi + 1) * C], W0f, W0fv[:, mi, :], self.ident, self.ident[PC, PC], inc=(mi == nm - 1))
                    k.cp(k.act, X[0], Xv[0], psX, mvv(psX))
                    k.tt(k.dve, NT[0], NTv[0], W0f, W0fv, self.ident, bc(self.ident[PC, PC], [C, nm, C], 1), ALU.add)
                U.append(u_x0)

                nlev = int(round(math.log2(C)))
                for lev in range(1, nlev):
                    cur = (lev - 1) % 2
                    nxt = 1 - cur
                    lastlev = (lev == nlev - 1)

                    def u_x(cur=cur, nxt=nxt):
                        psX2 = k.ps[2]
                        for mi in range(nm):
                            k.mm(psX2, psX2[PC, mi * C:(mi + 1) * C], W[cur], Wv[cur][:, mi, :], X[cur], Xv[cur][:, mi, :])
                        k.cp(k.act, X[nxt], Xv[nxt], psX2, mvv(psX2))
                    U.append(u_x)
                    if not lastlev:
                        def u_w(cur=cur, nxt=nxt):
                            psW = k.ps[1]
                            for mi in range(nm):
                                k.mm(psW, psW[PC, mi * C:(mi + 1) * C], X[cur], Xv[cur][:, mi, :], W[cur], Wv[cur][:, mi, :])
                            k.cp(k.act, W[nxt], Wv[nxt], psW, mvv(psW))
                        U.append(u_w)

                    def u_p(cur=cur, nxt=nxt, lastlev=lastlev):
                        psP = k.ps[3]
                        for mi in range(nm):
                            k.mm(psP, psP[PC, mi * C:(mi + 1) * C], X[nxt], Xv[nxt][:, mi, :], NT[cur], NTv[cur][:, mi, :])
                        if lastlev:
                            k.tt(k.dve, NTfp.tt, NTfp.v(), psP, mvv(psP), NT[cur], NTv[cur], ALU.add)
                        else:
                            k.tt(k.dve, NT[nxt], NTv[nxt], psP, mvv(psP), NT[cur], NTv[cur], ALU.add)
                    U.append(u_p)

                def u_kv(bi):
                    sq_, ci = batch[bi]
                    cs = slice(ci * C, (ci + 1) * C)
                    psk = k.ps[0]
                    for h in range(4):
                        k.tr(psk, psk[PC, h * 128:(h + 1) * 128], qkvs, qv[:, 4 + h, sq_, cs], self.ident, self.ident[:, :], inc=(h == 3))
                    k.tt(k.dve, kdecp, kdecp[PC, bi, :, :], psk, psk[PC, :].rearrange("p (h e) -> p h e", h=4),
                         Ep, bc(Ep[PC, bi, 4:8], [C, 4, 128], 2), ALU.mult)
                    psv = k.ps[0]
                    for h in range(4):
                        k.tr(psv, psv[PC, h * 128:(h + 1) * 128], qkvs, qv[:, 8 + h, sq_, cs], self.ident, self.ident[:, :], inc=(h == 3))
                    k.cp(k.act, vtmp, vtmp[PC, bi, :, :], psv, psv[PC, :].rearrange("p (h e) -> p h e", h=4))
                for bi in range(nb):
                    U.append(lambda bi=bi: u_kv(bi))
                return U

            def rec_units(batch, par):
                Ep, negeGp, gt128p, betap = E2[par], negeG2[par], gt1282[par], beta2[par]
                kdecp, vtmp, sgp = kdec2[par], vtm2[par], sg2[par]
                NTfv, qkTv = NTf_[par].v(), qkT2_[par].v()
                NTf, qkT = NTf_[par].tt, qkT2_[par].tt
                U = []
                for bi, (sq_, ci) in enumerate(batch):
                    cs = slice(ci * C, (ci + 1) * C)
                    Sv = Sst[:, sq_, :, :]
                    pskS, psqS, psNZ, psqkv, psdS = k.ps[4], k.ps[5], k.ps[6], k.ps[7], k.ps[6]

                    def r1(bi=bi, sq_=sq_, cs=cs, Sv=Sv):
                        for h in range(4):
                            k.mm(pskS, pskS[PC, h * 128:(h + 1) * 128], qkb, qb[:, 4 + h, sq_, cs], Sbf, Sbf[:, sq_, h, :], inc=(h == 3))
                        for h in range(4):
                            k.mm(psqS, psqS[PC, h * 128:(h + 1) * 128], qkb, qb[:, h, sq_, cs], Sbf, Sbf[:, sq_, h, :], inc=(h == 3))
                        k.tt(k.dve, Zf, Zf[PC, :, :], pskS, pskS[PC, :].rearrange("p (h e) -> p h e", h=4),
                             negeGp, bc(negeGp[PC, bi, :], [C, 4, 128], 2), ALU.mult)
                        k.tt(k.dve, Z, Z[PC, :, :], Zf, Zf[PC, :, :], vtmp, vtmp[PC, bi, :, :], ALU.add)
                        k.tt(k.dve, t1, t1[PC, :, :], psqS, psqS[PC, :].rearrange("p (h e) -> p h e", h=4),
                             Ep, bc(Ep[PC, bi, 0:4], [C, 4, 128], 2), ALU.mult)
                    U.append(r1)

                    def r2(bi=bi):
                        for h in range(4):
                            k.mm(psNZ, psNZ[PC, h * 128:(h + 1) * 128], NTf, NTfv[:, bi * 4 + h, :], Z, Z[PC, h, :], inc=(h == 3))
                        k.tt(k.dve, vnew, vnew[PC, :, :], psNZ, psNZ[PC, :].rearrange("p (h e) -> p h e", h=4),
                             betap, bc(betap[PC, bi, :], [C, 4, 128], 2), ALU.mult)
                    U.append(r2)

                    def r3(bi=bi, Sv=Sv, sq_=sq_):
                        for h in range(4):
                            k.mm(psdS, psdS[:, h * 128:(h + 1) * 128], kdecp, kdecp[PC, bi, h, :], vnew, vnew[PC, h, :], inc=(h == 3))
                        for h in range(4):
                            k.mm(psqkv, psqkv[PC, h * 128:(h + 1) * 128], qkT, qkTv[:, bi * 4 + h, :], vnew, vnew[PC, h, :], inc=(h == 3))
                        k.tt(k.dve, Sst, Sv, Sst, Sv, gt128p, bc(gt128p[:, bi, :], [128, 4, 128], 2), ALU.mult)
                        k.tt(k.dve, Sst, Sv, Sst, Sv, psdS, psdS[:, :].rearrange("p (h e) -> p h e", h=4), ALU.add)
                        k.cp(k.pool, Sbf, Sbf[:, sq_, :, :], Sst, Sv)
                        k.tt(k.dve, o_t, o_t[PC, :, :], psqkv, psqkv[PC, :].rearrange("p (h e) -> p h e", h=4), t1, t1[PC, :, :], ALU.add)
                    U.append(r3)

                    def r4(bi=bi, sq_=sq_, ci=ci):
                        for h in range(4):
                            k.actv(osq, osq[PC, :], o_t, o_t[PC, h, :], AF.Square, accum=(ss, ss[PC, h:h + 1]))
                        k.actv(rstd, rstd[PC, :], ss, ss[PC, :], AF.Ln, bias=NORM_EPS, scale=1.0 / 128.0)
                        k.actv(rstd, rstd[PC, :], rstd, rstd[PC, :], AF.Exp, scale=-0.5)
                        k.tt(k.pool, yb, yb[PC, :, :], o_t, o_t[PC, :, :], rstd, bc(rstd[PC, :], [C, 4, 128], 2), ALU.mult)
                        k.tt(k.pool, yb, yb[PC, :, :], yb, yb[PC, :, :], sgp, sgp[PC, bi, :].rearrange("p (h e) -> p h e", h=4), ALU.mult)
                        psT = k.ps[4]
                        for h in range(4):
                            k.tr(psT, psT[:, h * C:(h + 1) * C], yb, yb[PC, h, :], self.ident, self.ident[PC, PC], inc=(h == 3))
                        k.cp(k.act, ybT, ybT[:, :, 0:C], psT, psT[:, 0:4 * C].rearrange("p (h c) -> p h c", h=4))
                        tk = tok0 + sq_ * L + ci * C
                        k.dma(k.pool, self.YT[512:1024, tk:tk + C].rearrange("(m p) t -> p m t", p=128), ybT[:, :, 0:C],
                              out_t=self.YT, in_t=ybT)
                    U.append(r4)
                return U

            for u in prep_units(batches[0], 0):
                u()
            for b_ in range(len(batches)):
                R = rec_units(batches[b_], b_ % 2)
                N = prep_units(batches[b_ + 1], (b_ + 1) % 2) if b_ + 1 < len(batches) else []
                per = (len(N) + len(R) - 1) // len(R) if N else 0
                for r in R:
                    r()
                    for _ in range(per):
                        if N:
                            N.pop(0)()
                while N:
                    N.pop(0)()

        k.memset(k.pool, Sst, Sst[:, 0, :, :], 0.0)
        k.memset(k.pool, Sbf, Sbf[:, 0, :, :], 0.0)
        nst = T // 512
        for st in range(nst):
            process(st * 512, 512, 1, 512, 64, False, st == 0, st == nst - 1)
        k.dma(k.sp, self.nssm_p[l].rearrange("h a b -> a h b"), Sst[:, 0, :, :], in_t=Sst, is_output=True)
        for s in range(NSQ):
            k.dma(k.sp, Sst[:, s, :, :], self.state_ssm[l, s].rearrange("h a b -> a h b"), out_t=Sst)
            k.cp(k.pool, Sbf, Sbf[:, s, :, :], Sst, Sst[:, s, :, :])
        process(T, NSQ * TS, NSQ, TS, TS, True, True, True)
        for s in range(NSQ):
            k.dma(k.sp, self.nssm_s[l, s].rearrange("h a b -> a h b"), Sst[:, s, :, :], in_t=Sst, is_output=True)
        k.end_pass()


    def layer_norm(self, r, n, gbc, bbc, junk, stat, out_t):
        k = self.k
        nc = self.nc
        P = slice(0, n)
        k.actv(junk, junk[P, :], r, r[P, :], AF.Identity, accum=(stat, stat[P, 0:1]))
        k.actv(junk, junk[P, :], r, r[P, :], AF.Square, accum=(stat, stat[P, 1:2]))
        k.ts(k.dve, stat, stat[P, 2:3], stat, stat[P, 0:1], -1.0 / D, None, ALU.mult)
        k.tt(k.dve, stat, stat[P, 3:4], stat, stat[P, 2:3], stat, stat[P, 2:3], ALU.mult)
        k.stt(k.dve, stat, stat[P, 4:5], stat, stat[P, 1:2], 1.0 / D, stat, stat[P, 3:4], ALU.mult, ALU.subtract)
        k.actv(stat, stat[P, 5:6], stat, stat[P, 4:5], AF.Sqrt, bias=LN_EPS)
        k.op(k.dve, lambda: nc.vector.reciprocal(stat[P, 6:7], stat[P, 5:6]), outs=[stat], ins=[stat])
        k.ts(k.dve, r, r[P, :], r, r[P, :], stat[P, 2:3], stat[P, 6:7], ALU.add, ALU.mult, extra_ins=[stat])
        k.tt(k.pool, r, r[P, :], r, r[P, :], gbc, gbc[P, :], ALU.mult)
        k.tt(k.pool, out_t, out_t[P, :], r, r[P, :], bbc, bbc[P, :], ALU.add)

    def p2(self, l):
        k = self.k
        nc = self.nc
        k.begin_pass()
        wsrc = self.w_in[l].rearrange("(kc p) e -> p kc e", p=128)
        Wg = k.sb("Wg", [128, 8, 2048], BF16)
        k.dma(k.pool, Wg[:, :, :], wsrc[:, :, O_GATES:O_GATES + 2048], out_t=Wg)
        Wpa = k.sb("Wpa", [128, 4, D], BF16)
        k.dma(k.pool, Wpa[:, :, :], self.w_proj_a[l].rearrange("(kc p) e -> p kc e", p=128), out_t=Wpa)
        Wpb = k.sb("Wpb", [128, 4, D], BF16)
        k.dma(k.pool, Wpb[:, :, :], self.w_proj_b[l].rearrange("(kc p) e -> p kc e", p=128), out_t=Wpb)
        Wo = k.sb("Wo", [128, 8, D], BF16)
        k.dma(k.pool, Wo[:, :, :], self.w_out[l].rearrange("(kc p) e -> p kc e", p=128), out_t=Wo)
        bg = k.sb("bg", [128, 2048], F32)
        k.dma(k.sp, bg[:, :], self.b_gate[l:l + 1, :].to_broadcast([128, 2048]), out_t=bg)
        gbc = k.sb("gbc", [128, D], F32)
        k.dma(k.sp, gbc[:, :], self.ln1_g[l:l + 1, :].to_broadcast([128, D]), out_t=gbc)
        bbc = k.sb("bbc", [128, D], F32)
        k.dma(k.sp, bbc[:, :], self.ln1_b[l:l + 1, :].to_broadcast([128, D]), out_t=bbc)
        xin = [k.sb(f"xin{i}", [128, D], F32) for i in range(2)]
        xT = [k.sb(f"xT{i}", [128, 8, 128], BF16) for i in range(2)]
        yT = [k.sb(f"yT{i}", [128, 8, 128], BF16) for i in range(2)]
        sgate = k.sb("sgate", [128, 2048], F32)
        mixed = k.sb("mixed", [128, D], F32)
        tmp = k.sb("tmp", [128, 512], F32)
        mixT = k.sb("mixT", [128, 8, 128], BF16)
        r = [k.sb(f"r{i}", [128, D], F32) for i in range(2)]
        junk = k.sb("junk", [128, D], BF16)
        stat = k.sb("stat", [128, 8], F32)
        tiles = [(t * 128, 128) for t in range(T // 128)] + [(T, NSQ * TS)]
        for ti, (tok0, n) in enumerate(tiles):
            P = slice(0, n)
            xi = xin[ti % 2]
            xt = xT[ti % 2]
            yt = yT[ti % 2]
            rr = r[ti % 2]
            self.load_xT(l, tok0, n, xi, xt, 0, k.act)
            k.dma(k.sp, yt[:, :, 0:n], self.YT[:, tok0:tok0 + n].rearrange("(kc p) t -> p kc t", p=128), out_t=yt, in_t=self.YT)
            for blk in range(4):
                ps = k.ps[blk % 4]
                for kc in range(8):
                    k.mm(ps, ps[P, :], xt, xt[:, kc, 0:n], Wg, Wg[:, kc, blk * 512:(blk + 1) * 512], start=(kc == 0), stop=(kc == 7))
                k.tt(k.dve, sgate, sgate[P, blk * 512:(blk + 1) * 512], ps, ps[P, :], bg, bg[P, blk * 512:(blk + 1) * 512], ALU.add)
                k.actv(sgate, sgate[P, blk * 512:(blk + 1) * 512], sgate, sgate[P, blk * 512:(blk + 1) * 512], AF.Sigmoid)
            for blk in range(2):
                psa = k.ps[4 + blk]
                psb = k.ps[6 + blk]
                for kc in range(4):
                    k.mm(psa, psa[P, :], yt, yt[:, kc, 0:n], Wpa, Wpa[:, kc, blk * 512:(blk + 1) * 512], start=(kc == 0), stop=(kc == 3))
                for kc in range(4):
                    k.mm(psb, psb[P, :], yt, yt[:, 4 + kc, 0:n], Wpb, Wpb[:, kc, blk * 512:(blk + 1) * 512], start=(kc == 0), stop=(kc == 3))
                cs = slice(blk * 512, (blk + 1) * 512)
                k.tt(k.dve, mixed, mixed[P, cs], psa, psa[P, :], sgate, sgate[P, cs], ALU.mult)
                k.tt(k.dve, tmp, tmp[P, :], psb, psb[P, :], sgate, sgate[P, 1024 + blk * 512:1024 + (blk + 1) * 512], ALU.mult)
                k.tt(k.pool, mixed, mixed[P, cs], mixed, mixed[P, cs], tmp, tmp[P, :], ALU.add)
            for grp in range(2):
                ps = k.ps[grp]
                for kk in range(4):
                    kc = grp * 4 + kk
                    k.tr(ps, ps[:, kk * n:(kk + 1) * n], mixed, mixed[P, kc * 128:(kc + 1) * 128], self.ident, self.ident[P, P], inc=(kk == 3))
                k.cp(k.act, mixT, mixT[:, grp * 4:(grp + 1) * 4, 0:n], ps, ps[:, 0:4 * n].rearrange("p (a t) -> p a t", a=4))
            for blk in range(2):
                ps = k.ps[2 + blk]
                for kc in range(8):
                    k.mm(ps, ps[P, :], mixT, mixT[:, kc, 0:n], Wo, Wo[:, kc, blk * 512:(blk + 1) * 512], start=(kc == 0), stop=(kc == 7))
                cs = slice(blk * 512, (blk + 1) * 512)
                k.stt(k.dve, rr, rr[P, cs], xi, xi[P, cs], ALPHA, ps, ps[P, :], ALU.mult, ALU.add)
            self.layer_norm(rr, n, gbc, bbc, junk, stat, rr)
            k.dma(k.sp, self.X1[tok0:tok0 + n, :], rr[P, :], out_t=self.X1, in_t=rr)
        k.end_pass()

    def p3(self, l):
        k = self.k
        nc = self.nc
        k.begin_pass()
        Wup = k.sb("Wup", [128, 8, 2 * DFF], BF16)
        usrc = self.w_up[l].rearrange("(kc p) e -> p kc e", p=128)
        for q4 in range(4):
            k.dma(k.pool, Wup[:, :, q4 * 1408:(q4 + 1) * 1408], usrc[:, :, q4 * 1408:(q4 + 1) * 1408], out_t=Wup)
        Wdn = k.sb("Wdn", [128, 22, D], BF16)
        k.dma(k.pool, Wdn[:, :, :], self.w_down[l].rearrange("(kc p) e -> p kc e", p=128), out_t=Wdn)
        gbc = k.sb("gbc", [128, D], F32)
        k.dma(k.sp, gbc[:, :], self.ln2_g[l:l + 1, :].to_broadcast([128, D]), out_t=gbc)
        bbc = k.sb("bbc", [128, D], F32)
        k.dma(k.sp, bbc[:, :], self.ln2_b[l:l + 1, :].to_broadcast([128, D]), out_t=bbc)
        xin = [k.sb(f"xin{i}", [128, D], F32) for i in range(2)]
        xT = k.sb("xT", [128, 8, 512], BF16)
        fT = k.sb("fT", [128, 22, 512], BF16)
        tmp = [k.sb(f"tmp{i}", [128, 512], F32) for i in range(2)]
        rr = k.sb("r", [128, D], F32)
        junk = k.sb("junk", [128, D], BF16)
        stat = k.sb("stat", [128, 8], F32)
        last = (l == DEPTH - 1)
        sts = [(st * 512, 512) for st in range(T // 512)] + [(T, NSQ * TS)]
        xcnt = 0
        for (tok0, NS) in sts:
            subs = []
            done = 0
            while done < NS:
                n = min(128, NS - done)
                subs.append((done, n))
                done += n
            for (c0, n) in subs:
                xi = xin[xcnt % 2]
                xcnt += 1
                k.dma(k.sp, xi[0:n, :], self.X1[tok0 + c0:tok0 + c0 + n, :], out_t=xi, in_t=self.X1)
                for grp in range(2):
                    ps = k.ps[6 + grp]
                    for kk in range(4):
                        kc = grp * 4 + kk
                        k.tr(ps, ps[:, kk * n:(kk + 1) * n], xi, xi[0:n, kc * 128:(kc + 1) * 128], self.ident, self.ident[0:n, 0:n], inc=(kk == 3))
                    k.cp(k.act, xT, xT[:, grp * 4:(grp + 1) * 4, c0:c0 + n], ps, ps[:, 0:4 * n].rearrange("p (a t) -> p a t", a=4))
            for fc in range(22):
                psA = k.ps[(2 * fc) % 4]
                psB = k.ps[(2 * fc + 1) % 4]
                for kc in range(8):
                    k.mm(psA, psA[:, 0:NS], Wup, Wup[:, kc, fc * 128:(fc + 1) * 128], xT, xT[:, kc, 0:NS], start=(kc == 0), stop=(kc == 7))
                for kc in range(8):
                    k.mm(psB, psB[:, 0:NS], Wup, Wup[:, kc, DFF + fc * 128:DFF + (fc + 1) * 128], xT, xT[:, kc, 0:NS], start=(kc == 0), stop=(kc == 7))
                tm = tmp[fc % 2]
                k.actv(tm, tm[:, 0:NS], psA, psA[:, 0:NS], AF.Silu)
                k.tt(k.dve, fT, fT[:, fc, 0:NS], tm, tm[:, 0:NS], psB, psB[:, 0:NS], ALU.mult)
            for (c0, n) in subs:
                P = slice(0, n)
                xi = xin[xcnt % 2]
                xcnt += 1
                k.dma(k.sp, xi[0:n, :], self.X1[tok0 + c0:tok0 + c0 + n, :], out_t=xi, in_t=self.X1)
                for blk in range(2):
                    ps = k.ps[4 + blk]
                    for fc in range(22):
                        k.mm(ps, ps[P, :], fT, fT[:, fc, c0:c0 + n], Wdn, Wdn[:, fc, blk * 512:(blk + 1) * 512], start=(fc == 0), stop=(fc == 21))
                    cs = slice(blk * 512, (blk + 1) * 512)
                    k.stt(k.dve, rr, rr[P, cs], xi, xi[P, cs], ALPHA, ps, ps[P, :], ALU.mult, ALU.add)
                self.layer_norm(rr, n, gbc, bbc, junk, stat, rr)
                t0 = tok0 + c0
                if not last:
                    k.dma(k.sp, self.X2[t0:t0 + n, :], rr[P, :], out_t=self.X2, in_t=rr)
                elif t0 < T:
                    k.dma(k.sp, self.y_p[t0:t0 + n, :], rr[P, :], in_t=rr, is_output=True)
                else:
                    k.dma(k.sp, self.y_s[t0 - T:t0 - T + n, :], rr[P, :], in_t=rr, is_output=True)
        k.end_pass()

    def build(self):
        k = self.k
        self.load_consts()
        sa = self.stop_after
        only = sa[0][:-5] if (sa is not None and sa[0].endswith("_only")) else None
        for l in range(DEPTH):
            for name, fn in (("p1a", self.p1a), ("p1b", self.p1b), ("p2", self.p2), ("p3", self.p3)):
                if only is not None and name != only:
                    continue
                fn(l)
                if sa is not None and sa[1] == l and (sa[0] == name or only == name):
                    break
            else:
                continue
            break
        k.finish()


def shard_inputs(inputs, c):
    f = lambda a: np.ascontiguousarray(a, dtype=np.float32)
    sl = slice(NSQ * c, NSQ * (c + 1))
    m = {
        "x_p": f(inputs["x_prompt"][c]),
        "x_s": f(inputs["x_sample"][sl].reshape(NSQ * TS, D)),
        "cache_k": f(inputs["cache_k"][:, sl].reshape(DEPTH, NSQ, T, 128)),
        "cache_v": f(inputs["cache_v"][:, sl].reshape(DEPTH, NSQ, T, 128)),
        "cache_ki": f(inputs["cache_kidx"][:, sl]),
        "state_conv": f(inputs["state_conv"][:, sl]),
        "state_ssm": f(inputs["state_ssm"][:, sl]),
    }
    for n in ["w_in", "b_gate", "conv_w", "a_log", "dt_bias", "gdn_norm_w", "w_proj_a", "w_proj_b", "w_out",
              "ln1_g", "ln1_b", "w_up", "w_down", "ln2_g", "ln2_b"]:
        m[n] = f(inputs[n])
    for n, v in make_consts().items():
        m["c_" + n] = v
    return m


def run(inputs, debug=False, stop_after=None, trace=False):
    prog = Prog(debug=debug, stop_after=stop_after)
    in_maps = [shard_inputs(inputs, c) for c in range(8)]
    res = run_bass_kernel_spmd(prog.nc, in_maps, core_ids=list(range(8)), trace=trace)
    return res


def kernel(**inputs):
    res = run(inputs)
    r = res.results
    cat = lambda n: np.stack([r[c][n] for c in range(8)], axis=0)
    y_p = cat("y_p")
    y_s = cat("y_s").reshape(32, TS, D)
    nk_p = np.transpose(cat("nk_p"), (1, 0, 2, 3)).reshape(DEPTH, 8, T, 2, 64)
    nv_p = np.transpose(cat("nv_p"), (1, 0, 2, 3)).reshape(DEPTH, 8, T, 2, 64)
    nki_p = np.transpose(cat("nki_p"), (1, 0, 2, 3))
    nconv_p = np.transpose(cat("nconv_p"), (1, 0, 2, 3))
    nssm_p = np.transpose(cat("nssm_p"), (1, 0, 2, 3, 4))
    nk_s = np.transpose(cat("nk_s"), (1, 0, 2, 3)).reshape(DEPTH, 32, TS, 2, 64)
    nv_s = np.transpose(cat("nv_s"), (1, 0, 2, 3)).reshape(DEPTH, 32, TS, 2, 64)
    nki_s = np.transpose(cat("nki_s"), (1, 0, 2, 3)).reshape(DEPTH, 32, TS, 64)
    nconv_s = np.transpose(cat("nconv_s"), (1, 0, 2, 3, 4)).reshape(DEPTH, 32, 3, 1536)
    nssm_s = np.transpose(cat("nssm_s"), (1, 0, 2, 3, 4, 5)).reshape(DEPTH, 32, 4, 128, 128)
    return tuple(np.ascontiguousarray(a, dtype=np.float32) for a in
                 (y_p, y_s, nk_p, nv_p, nki_p, nconv_p, nssm_p, nk_s, nv_s, nki_s, nconv_s, nssm_s))
```

```python
import math
from contextlib import ExitStack
import numpy as np
import concourse.bass as bass
import concourse.mybir as mybir
from concourse.bass_utils import run_bass_kernel_spmd

F32 = mybir.dt.float32
BF16 = mybir.dt.bfloat16
AF = mybir.ActivationFunctionType
ALU = mybir.AluOpType
AX = mybir.AxisListType

D = 1024
T = 4096
TS = 16
NSQ = 4
NTOK = T + NSQ * TS
DEPTH = 2
DIN = 5456
DFF = 2816
O_QA, O_KA, O_VA, O_QI, O_KI, O_WI, O_QKV, O_AB, O_BB, O_GB, O_GATES = 0, 512, 640, 768, 1280, 1344, 1352, 2888, 2892, 2896, 3408
ALPHA = (2 * DEPTH) ** 0.25
INDEX_SCALE = (8 * 64) ** -0.5
LN_EPS = 1e-5
NORM_EPS = 1e-6
NBIS = 16
NEG = -30000.0
KCOLS = 33 * 128


class Tok:
    __slots__ = ("sem", "val", "key")

    def __init__(self, sem, val, key):
        self.sem = sem
        self.val = val
        self.key = key


class Eng:
    def __init__(self, nc, name, h):
        self.name = name
        self.key = name
        self.h = h
        self.sem = nc.alloc_semaphore("e_" + name)
        self.cnt = 0
        self.waited = {}

    def wait(self, tok):
        if tok is None:
            return
        if self.waited.get(tok.key, 0) >= tok.val:
            return
        self.h.wait_ge(tok.sem, tok.val)
        self.waited[tok.key] = tok.val


class TT:
    def __init__(self, t, name, sbuf=True):
        self.t = t
        self.name = name
        self.sbuf = sbuf
        self.w = None
        self.r = {}
        self.dsem = None

    def __getitem__(self, idx):
        return self.t[idx]


class K:
    def __init__(self, nc):
        self.nc = nc
        self.pe = Eng(nc, "pe", nc.tensor)
        self.act = Eng(nc, "act", nc.scalar)
        self.dve = Eng(nc, "dve", nc.vector)
        self.pool = Eng(nc, "pool", nc.gpsimd)
        self.sp = Eng(nc, "sp", nc.sync)
        self.engs = [self.pe, self.act, self.dve, self.pool, self.sp]
        self.dsem_pool = []
        self.ndsem = 0
        self.out_toks = {}
        self.pass_tiles = []
        self.es = None
        self.uid = 0
        self.ps = [TT(nc.alloc_psum_tensor(f"psb{i}", [128, 512], F32), f"psb{i}") for i in range(8)]

    def begin_pass(self):
        self.es = ExitStack()
        self.pass_tiles = []

    def sb(self, name, shape, dt):
        self.uid += 1
        t = self.es.enter_context(self.nc.sbuf_tensor(f"{name}_{self.uid}", list(shape), dt))
        tt = TT(t, name)
        self.pass_tiles.append(tt)
        return tt

    def get_dsem(self):
        if self.dsem_pool:
            return self.dsem_pool.pop()
        self.ndsem += 1
        return [self.nc.alloc_semaphore(f"d{self.ndsem}"), 0]

    def barrier(self, tiles):
        toks = [Tok(e.sem, e.cnt, e.key) for e in self.engs if e.cnt > 0]
        seen = set()
        for tt in tiles:
            for ds in (tt.dsem or {}).values():
                if ds[1] > 0 and id(ds) not in seen:
                    seen.add(id(ds))
                    toks.append(Tok(ds[0], 16 * ds[1], ("d", id(ds))))
        for e in self.engs:
            for tok in toks:
                if tok.key != e.key:
                    e.wait(tok)

    def end_pass(self):
        self.barrier(self.pass_tiles)
        for tt in self.pass_tiles:
            if tt.dsem is not None:
                self.dsem_pool.extend(tt.dsem.values())
                tt.dsem = None
        self.es.close()
        self.es = None
        self.pass_tiles = []

    def op(self, e, fn, outs=(), ins=(), inc=True):
        for t in ins:
            e.wait(t.w)
        strict = e.key != "pe"
        for t in outs:
            if t.w is not None and (strict or t.w.key != e.key):
                e.wait(t.w)
            for kk, tok in t.r.items():
                if strict or kk != e.key:
                    e.wait(tok)
        inst = fn()
        if inc:
            inst.then_inc(e.sem, 1)
            e.cnt += 1
            tok = Tok(e.sem, e.cnt, e.key)
        else:
            tok = Tok(e.sem, e.cnt + 1, e.key)
        for t in outs:
            t.w = tok
            t.r = {}
        for t in ins:
            if t not in outs:
                t.r[e.key] = tok
        return inst

    def dma(self, q, out_ap, in_ap, out_t=None, in_t=None, is_output=False):
        if in_t is not None:
            q.wait(in_t.w)
        if out_t is not None:
            q.wait(out_t.w)
            for kk, tok in out_t.r.items():
                q.wait(tok)
        own = out_t if (out_t is not None and out_t.sbuf) else in_t
        if own.dsem is None:
            own.dsem = {}
        if q.key not in own.dsem:
            own.dsem[q.key] = self.get_dsem()
        ds = own.dsem[q.key]
        inst = q.h.dma_start(out=out_ap, in_=in_ap)
        ds[1] += 1
        inst.then_inc(ds[0], 16)
        tok = Tok(ds[0], 16 * ds[1], ("d", id(ds)))
        if out_t is not None:
            out_t.w = tok
            out_t.r = {}
        if in_t is not None:
            in_t.r[tok.key] = tok
        if is_output:
            self.out_toks[tok.key] = tok

    def finish(self):
        for tok in self.out_toks.values():
            self.sp.wait(tok)
        self.barrier([])

    def mm(self, out_t, out_ap, a_t, a_ap, b_t, b_ap, start=True, stop=True, inc=None):
        nc = self.nc
        ins = [a_t, b_t] if b_t is not a_t else [a_t]
        return self.op(self.pe, lambda: nc.tensor.matmul(out_ap, a_ap, b_ap, start=start, stop=stop),
                       outs=[out_t], ins=ins, inc=(stop if inc is None else inc))

    def tr(self, out_t, out_ap, a_t, a_ap, id_t, id_ap, inc=True):
        nc = self.nc
        return self.op(self.pe, lambda: nc.tensor.transpose(out_ap, a_ap, id_ap), outs=[out_t], ins=[a_t, id_t], inc=inc)

    def actv(self, out_t, out_ap, in_t, in_ap, func, bias=None, scale=None, accum=None, extra_ins=(), eng=None):
        nc = self.nc
        kw = {}
        if bias is not None:
            kw["bias"] = bias
        if scale is not None:
            kw["scale"] = scale
        outs = [out_t]
        if accum is not None:
            kw["accum_out"] = accum[1]
            outs.append(accum[0])
        return self.op(self.act, lambda: nc.scalar.activation(out_ap, in_ap, func, **kw), outs=outs,
                       ins=[in_t] + list(extra_ins))

    def ts(self, e, out_t, out_ap, in_t, in_ap, s1, s2, op0, op1=None, accum=None, extra_ins=()):
        outs = [out_t]
        kw = {}
        if op1 is not None:
            kw["op1"] = op1
        if accum is not None:
            kw["accum_out"] = accum[1]
            outs.append(accum[0])
        return self.op(e, lambda: e.h.tensor_scalar(out_ap, in_ap, s1, s2, op0, **kw), outs=outs,
                       ins=[in_t] + list(extra_ins))

    def tt(self, e, out_t, out_ap, a_t, a_ap, b_t, b_ap, op):
        ins = [a_t, b_t] if b_t is not a_t else [a_t]
        return self.op(e, lambda: e.h.tensor_tensor(out_ap, a_ap, b_ap, op), outs=[out_t], ins=ins)

    def stt(self, e, out_t, out_ap, a_t, a_ap, scalar, b_t, b_ap, op0, op1, extra_ins=()):
        ins = [a_t] + ([b_t] if b_t is not a_t else []) + list(extra_ins)
        return self.op(e, lambda: e.h.scalar_tensor_tensor(out_ap, a_ap, scalar, b_ap, op0, op1), outs=[out_t], ins=ins)

    def cp(self, e, out_t, out_ap, in_t, in_ap):
        if e is self.act:
            nc = self.nc
            return self.op(e, lambda: nc.scalar.copy(out_ap, in_ap), outs=[out_t], ins=[in_t])
        return self.op(e, lambda: e.h.tensor_copy(out_ap, in_ap), outs=[out_t], ins=[in_t])

    def memset(self, e, out_t, out_ap, val):
        return self.op(e, lambda: e.h.memset(out_ap, val), outs=[out_t], ins=[])


def make_consts():
    c = {}
    c["ident"] = np.eye(128, dtype=np.float32)
    c["ones"] = np.ones((128, 128), np.float32)
    i2 = np.zeros((128, 256), np.float32)
    i2[:, 0:128] = np.eye(128)
    i2[:, 128:256] = np.eye(128)
    c["i2"] = i2
    c["i4"] = np.tile(np.eye(128, dtype=np.float32), (1, 4))
    c["i4s"] = np.tile(np.eye(16, dtype=np.float32), (1, 4))
    i2s = np.zeros((16, 32), np.float32)
    i2s[:, 0:16] = np.eye(16)
    i2s[:, 16:32] = np.eye(16)
    c["i2s"] = i2s
    tt_, cc_ = np.meshgrid(np.arange(64), np.arange(64), indexing="ij")
    c["U"] = (tt_ <= cc_).astype(np.float32)
    c["Urev"] = (tt_ > cc_).astype(np.float32)
    c["maskT"] = np.where(cc_ >= tt_, 0.0, NEG).astype(np.float32)
    c["su01"] = (cc_ > tt_).astype(np.float32)
    c["caus01"] = (cc_ >= tt_).astype(np.float32)
    c["negones"] = -np.ones((128, 128), np.float32)
    c["pow2"] = np.tile((2.0 ** -(np.arange(32) + 1.0)).astype(np.float32)[None, :], (128, 1))
    return c


class Prog:
    def __init__(self, debug=False, stop_after=None):
        self.debug = debug
        self.stop_after = stop_after
        nc = bass.Bass("TRN2", target_bir_lowering=False)
        self.nc = nc
        self.k = K(nc)

        def din(name, shape):
            return nc.dram_tensor(name, list(shape), F32, kind="ExternalInput").ap()

        def dout(name, shape):
            return nc.dram_tensor(name, list(shape), F32, kind="ExternalOutput").ap()

        self.x_p = din("x_p", [T, D])
        self.x_s = din("x_s", [NSQ * TS, D])
        self.cache_k = din("cache_k", [DEPTH, NSQ, T, 128])
        self.cache_v = din("cache_v", [DEPTH, NSQ, T, 128])
        self.cache_ki = din("cache_ki", [DEPTH, NSQ, T, 64])
        self.state_conv = din("state_conv", [DEPTH, NSQ, 3, 1536])
        self.state_ssm = din("state_ssm", [DEPTH, NSQ, 4, 128, 128])
        self.w_in = din("w_in", [DEPTH, D, DIN])
        self.b_gate = din("b_gate", [DEPTH, 2048])
        self.conv_w = din("conv_w", [DEPTH, 4, 1536])
        self.a_log = din("a_log", [DEPTH, 4])
        self.dt_bias = din("dt_bias", [DEPTH, 4])
        self.gdn_norm_w = din("gdn_norm_w", [DEPTH, 128])
        self.w_proj_a = din("w_proj_a", [DEPTH, 512, D])
        self.w_proj_b = din("w_proj_b", [DEPTH, 512, D])
        self.w_out = din("w_out", [DEPTH, D, D])
        self.ln1_g = din("ln1_g", [DEPTH, D])
        self.ln1_b = din("ln1_b", [DEPTH, D])
        self.w_up = din("w_up", [DEPTH, D, 2 * DFF])
        self.w_down = din("w_down", [DEPTH, DFF, D])
        self.ln2_g = din("ln2_g", [DEPTH, D])
        self.ln2_b = din("ln2_b", [DEPTH, D])
        self.cst = {n: din("c_" + n, list(v.shape)) for n, v in make_consts().items()}

        self.y_p = dout("y_p", [T, D])
        self.y_s = dout("y_s", [NSQ * TS, D])
        self.nk_p = dout("nk_p", [DEPTH, T, 128])
        self.nv_p = dout("nv_p", [DEPTH, T, 128])
        self.nki_p = dout("nki_p", [DEPTH, T, 64])
        self.nconv_p = dout("nconv_p", [DEPTH, 3, 1536])
        self.nssm_p = dout("nssm_p", [DEPTH, 4, 128, 128])
        self.nk_s = dout("nk_s", [DEPTH, NSQ * TS, 128])
        self.nv_s = dout("nv_s", [DEPTH, NSQ * TS, 128])
        self.nki_s = dout("nki_s", [DEPTH, NSQ * TS, 64])
        self.nconv_s = dout("nconv_s", [DEPTH, NSQ, 3, 1536])
        self.nssm_s = dout("nssm_s", [DEPTH, NSQ, 4, 128, 128])

        skind = "ExternalOutput" if debug else "Internal"
        self.YT = TT(nc.dram_tensor("scr_yt", [D, NTOK], BF16, kind=skind).ap(), "YT", sbuf=False)
        self.X1 = TT(nc.dram_tensor("scr_x1", [NTOK, D], F32, kind=skind).ap(), "X1", sbuf=False)
        self.X2 = TT(nc.dram_tensor("scr_x2", [NTOK, D], F32, kind=skind).ap(), "X2", sbuf=False)
        self.build()

    def x_rows(self, l, tok0, n):
        if l == 0:
            if tok0 < T:
                return None, self.x_p[tok0:tok0 + n, :]
            return None, self.x_s[tok0 - T:tok0 - T + n, :]
        return self.X2, self.X2[tok0:tok0 + n, :]

    def load_consts(self):
        k = self.k
        nc = self.nc
        es = ExitStack()
        self.ces = es
        self.ctiles = []

        def csb(name, shape, dt):
            t = es.enter_context(nc.sbuf_tensor("k_" + name, list(shape), dt))
            tt = TT(t, name)
            self.ctiles.append(tt)
            return tt

        self.ident = csb("ident", [128, 128], F32)
        k.dma(k.sp, self.ident[:, :], self.cst["ident"][:, :], out_t=self.ident)
        self.ones = csb("ones", [128, 128], F32)
        k.dma(k.sp, self.ones[:, :], self.cst["ones"][:, :], out_t=self.ones)
        self.i2 = csb("i2", [128, 256], BF16)
        k.dma(k.pool, self.i2[:, :], self.cst["i2"][:, :], out_t=self.i2)
        self.i2s = csb("i2s", [16, 32], BF16)
        k.dma(k.pool, self.i2s[:, :], self.cst["i2s"][:, :], out_t=self.i2s)
        self.i4 = csb("i4", [128, 512], BF16)
        k.dma(k.pool, self.i4[:, :], self.cst["i4"][:, :], out_t=self.i4)
        self.i4s = csb("i4s", [16, 64], BF16)
        k.dma(k.pool, self.i4s[:, :], self.cst["i4s"][:, :], out_t=self.i4s)
        self.pow2 = csb("pow2", [128, 32], F32)
        k.dma(k.sp, self.pow2[:, :], self.cst["pow2"][:, :], out_t=self.pow2)
        for nm in ["U", "Urev", "maskT", "su01", "caus01"]:
            t = csb(nm, [64, 64], F32)
            k.dma(k.sp, t[:, :], self.cst[nm][:, :], out_t=t)
            setattr(self, "c_" + nm, t)
        self.negones = csb("negones", [128, 128], F32)
        k.dma(k.sp, self.negones[:, :], self.cst["negones"][:, :], out_t=self.negones)

    def load_xT(self, l, tok0, ntok, xin, xT, col0, evac):
        k = self.k
        src_t, src = self.x_rows(l, tok0, ntok)
        k.dma(k.sp, xin[0:ntok, :], src, out_t=xin, in_t=src_t)
        for grp in range(2):
            ps = k.ps[6 + grp]
            for kk in range(4):
                kc = grp * 4 + kk
                k.tr(ps, ps[:, kk * ntok:(kk + 1) * ntok], xin, xin[0:ntok, kc * 128:(kc + 1) * 128],
                     self.ident, self.ident[0:ntok, 0:ntok], inc=(kk == 3))
            k.cp(evac, xT, xT[:, grp * 4:(grp + 1) * 4, col0:col0 + ntok],
                 ps, ps[:, 0:4 * ntok].rearrange("p (a t) -> p a t", a=4))

    def p1a(self, l):
        k = self.k
        nc = self.nc
        k.begin_pass()
        wsrc = self.w_in[l].rearrange("(kc p) e -> p kc e", p=128)
        Wfm = k.sb("Wfm", [128, 8, 1408], BF16)
        Wtm = k.sb("Wtm", [128, 8, 328], BF16)

        def wl(dst_t, d0, s0, n):
            k.dma(k.pool, dst_t[:, :, d0:d0 + n], wsrc[:, :, s0:s0 + n], out_t=dst_t)

        wl(Wfm, 0, O_QA, 512)
        wl(Wfm, 512, O_KA, 64)
        wl(Wfm, 576, O_KA, 64)
        wl(Wfm, 640, O_KA + 64, 64)
        wl(Wfm, 704, O_KA + 64, 64)
        wl(Wfm, 768, O_QI, 512)
        wl(Wfm, 1280, O_KI, 64)
        wl(Wfm, 1344, O_KI, 64)
        wl(Wtm, 0, O_KA, 256)
        wl(Wtm, 256, O_KI, 64)
        wl(Wtm, 320, O_WI, 8)

        kT2g = [k.sb(f"kT2g{g}", [128, KCOLS], BF16) for g in range(2)]
        kiT2 = k.sb("kiT2", [128, KCOLS], BF16)
        vext = k.sb("vext", [128, 33, 2, 65], BF16)
        k.memset(k.pool, vext, vext[:, :, :, 64:65], 1.0)
        xin = [k.sb(f"xin{i}", [128, D], F32) for i in range(2)]
        xT = k.sb("xT", [128, 8, 512], BF16)
        qaLo = [k.sb(f"qaLo{i}", [128, 4, 512], BF16) for i in range(2)]
        qaHi = [k.sb(f"qaHi{i}", [128, 4, 512], BF16) for i in range(2)]
        qiLo = [k.sb(f"qiLo{i}", [128, 4, 512], BF16) for i in range(2)]
        qiHi = [k.sb(f"qiHi{i}", [128, 4, 512], BF16) for i in range(2)]
        for i in range(2):
            k.memset(k.pool, qaLo[i], qaLo[i][64:128, :, :], 0.0)
            k.memset(k.pool, qiLo[i], qiLo[i][64:128, :, :], 0.0)
            k.memset(k.pool, qaHi[i], qaHi[i][0:64, :, :], 0.0)
            k.memset(k.pool, qiHi[i], qiHi[i][0:64, :, :], 0.0)
        kfm = k.sb("kfm", [128, 3, 64], BF16)
        tm1 = [k.sb(f"tm1_{i}", [128, 328], F32) for i in range(2)]
        wscs = [k.sb(f"wsc{i}", [128, 4, 8], F32) for i in range(2)]
        Ss = [k.sb(f"S{i}", [128, KCOLS], F32) for i in range(2)]
        junk = k.sb("junk", [128, KCOLS], BF16)
        MBs = [k.sb(f"MB{i}", [128, KCOLS], BF16) for i in range(2)]
        R = [k.sb(f"R{i}", [128, 512], F32) for i in range(2)]
        PT = [k.sb(f"PT{i}", [128, 512], BF16) for i in range(3)]
        rd = k.sb("rd", [128, 1024], F32)
        bcs = k.sb("bcs", [64, 1024], F32)
        yTt = k.sb("yTt", [64, 1024], BF16)
        stt_ = k.sb("stat", [128, 8], F32)
        dtab = k.sb("dtab", [128, NBIS], F32)
        trial = k.sb("trial", [128, NBIS + 1], F32)
        lo_t = k.sb("lo_t", [128, NBIS + 1], F32)
        cnt = k.sb("cnt", [128, NBIS], F32)
        dd = k.sb("dd", [128, NBIS], F32)
        ktm = k.sb("ktm", [128, 16, 128], F32)

        def stage_units(tok0, NS, Tq, nsub, key_col0, key_tile0, is_sample, par):
            wsc = wscs[par]
            units = []
            done = 0
            i = 0
            while done < NS:
                n = min(128, NS - done)
                units.append(lambda d=done, n=n, i=i: self.load_xT(l, tok0 + d, n, xin[i % 2], xT, d, k.act))
                done += n
                i += 1

            def fm(mt):
                ps = k.ps[6 + (mt % 2)]
                for kc in range(8):
                    k.mm(ps, ps[:, 0:NS], Wfm, Wfm[:, kc, mt * 128:(mt + 1) * 128], xT, xT[:, kc, 0:NS],
                         start=(kc == 0), stop=(kc == 7))
                if mt < 4:
                    k.cp(k.act, qaLo[par], qaLo[par][0:64, mt, 0:NS], ps, ps[0:64, 0:NS])
                    k.cp(k.act, qaHi[par], qaHi[par][64:128, mt, 0:NS], ps, ps[64:128, 0:NS])
                elif mt < 6:
                    g = mt - 4
                    if is_sample:
                        k.cp(k.act, kfm, kfm[:, g, 0:NS], ps, ps[:, 0:NS])
                    else:
                        k.cp(k.act, kT2g[g], kT2g[g][:, key_col0:key_col0 + NS], ps, ps[:, 0:NS])
                elif mt < 10:
                    k.cp(k.act, qiLo[par], qiLo[par][0:64, mt - 6, 0:NS], ps, ps[0:64, 0:NS])
                    k.cp(k.act, qiHi[par], qiHi[par][64:128, mt - 6, 0:NS], ps, ps[64:128, 0:NS])
                else:
                    if is_sample:
                        k.cp(k.act, kfm, kfm[:, 2, 0:NS], ps, ps[:, 0:NS])
                    else:
                        k.cp(k.act, kiT2, kiT2[:, key_col0:key_col0 + NS], ps, ps[:, 0:NS])

            for mt in (6, 7, 8, 9, 10, 4, 5):
                units.append(lambda mt=mt: fm(mt))

            def tmj(j):
                ps = k.ps[6 + (j % 2)]
                t1 = tm1[j % 2]
                for kc in range(8):
                    k.mm(ps, ps[0:Tq, 0:328], xT, xT[:, kc, j * Tq:(j + 1) * Tq], Wtm, Wtm[:, kc, 0:328],
                         start=(kc == 0), stop=(kc == 7))
                k.cp(k.act, t1, t1[0:Tq, :], ps, ps[0:Tq, 0:328])
                r0 = tok0 + j * Tq
                if is_sample:
                    r0 -= T
                    dk, dv, dki = self.nk_s, self.nv_s, self.nki_s
                else:
                    dk, dv, dki = self.nk_p, self.nv_p, self.nki_p
                k.dma(k.pool, dk[l, r0:r0 + Tq, :], t1[0:Tq, 0:128], in_t=t1, is_output=True)
                k.dma(k.pool, dv[l, r0:r0 + Tq, :], t1[0:Tq, 128:256], in_t=t1, is_output=True)
                k.dma(k.pool, dki[l, r0:r0 + Tq, :], t1[0:Tq, 256:320], in_t=t1, is_output=True)
                if not is_sample:
                    kt = key_tile0 + j
                    k.cp(k.pool, vext, vext[0:Tq, kt, :, 0:64], t1, t1[0:Tq, 128:256].rearrange("p (g d) -> p g d", g=2))
                k.ts(k.pool, wsc, wsc[0:Tq, j, :], t1, t1[0:Tq, 320:328], INDEX_SCALE, None, ALU.mult)

            for j in range(nsub):
                units.append(lambda j=j: tmj(j))
            for mt in range(4):
                units.append(lambda mt=mt: fm(mt))
            return units

        def stage_a(*args):
            for u in stage_units(*args):
                u()

        class TD:
            pass

        def idx(td, pending=None, quota=0):
            P = slice(0, td.Tq)
            qc = slice(td.j * td.Tq, (td.j + 1) * td.Tq)
            S, wsc, n = td.S, td.wsc, td.n
            nblk = (n + 511) // 512
            for kb in range(nblk):
                c0 = kb * 512
                w = min(512, n - c0)
                for h in range(8):
                    m, half = divmod(h, 2)
                    ps = k.ps[half]
                    qi = (qiLo if half == 0 else qiHi)[td.par]
                    k.mm(ps, ps[P, 0:w], qi, qi[:, m, qc], kiT2, kiT2[:, c0:c0 + w])
                    Rt = R[h % 2]
                    k.actv(Rt, Rt[P, 0:w], ps, ps[P, 0:w], AF.Relu)
                    if h == 0:
                        k.ts(k.dve, S, S[P, c0:c0 + w], Rt, Rt[P, 0:w], wsc[P, td.j, 0:1], None, ALU.mult, extra_ins=[wsc])
                    else:
                        k.stt(k.dve, S, S[P, c0:c0 + w], Rt, Rt[P, 0:w], wsc[P, td.j, h:h + 1], S, S[P, c0:c0 + w],
                              ALU.mult, ALU.add, extra_ins=[wsc])
                for _ in range(min(len(pending), (quota + nblk - 1) // nblk) if pending else 0):
                    pending.pop(0)()
                    quota -= 1
            while pending and quota > 0:
                pending.pop(0)()
                quota -= 1
            if td.corner:
                k.memset(k.dve, S, S[0:64, n - 64:n], -1.0e30)

        def bis(td):
            P = slice(0, td.Tq)
            S, MB, n = td.S, td.MB, td.n
            st = stt_
            k.op(k.dve, lambda: nc.vector.tensor_reduce(st[P, 0:1], S[P, 0:n], AX.X, ALU.max), outs=[st], ins=[S])
            if td.corner:
                k.op(k.dve, lambda: nc.vector.tensor_reduce(st[P, 1:2], S[P, 0:n - 64], AX.X, ALU.min), outs=[st], ins=[S])
                k.op(k.dve, lambda: nc.vector.tensor_reduce(st[64:128, 2:3], S[64:128, n - 64:n], AX.X, ALU.min), outs=[st], ins=[S])
                k.tt(k.dve, st, st[64:128, 1:2], st, st[64:128, 1:2], st, st[64:128, 2:3], ALU.min)
            else:
                k.op(k.dve, lambda: nc.vector.tensor_reduce(st[P, 1:2], S[P, 0:n], AX.X, ALU.min), outs=[st], ins=[S])
            k.tt(k.dve, st, st[P, 3:4], st, st[P, 0:1], st, st[P, 1:2], ALU.subtract)
            k.ts(k.dve, dtab, dtab[P, :], self.pow2, self.pow2[P, 0:NBIS], st[P, 3:4], None, ALU.mult, extra_ins=[st])
            k.cp(k.dve, lo_t, lo_t[P, 0:1], st, st[P, 1:2])
            k.tt(k.dve, trial, trial[P, 0:1], st, st[P, 1:2], dtab, dtab[P, 0:1], ALU.add)
            for it in range(NBIS):
                k.ts(k.dve, junk, junk[P, 0:n], S, S[P, 0:n], trial[P, it:it + 1], None, ALU.is_ge, ALU.add,
                     accum=(cnt, cnt[P, it:it + 1]), extra_ins=[trial])
                k.ts(k.dve, dd, dd[P, it:it + 1], cnt, cnt[P, it:it + 1], 255.5, dtab[P, it:it + 1], ALU.is_ge, ALU.mult,
                     extra_ins=[dtab])
                k.tt(k.dve, lo_t, lo_t[P, it + 1:it + 2], lo_t, lo_t[P, it:it + 1], dd, dd[P, it:it + 1], ALU.add)
                if it + 1 < NBIS:
                    k.tt(k.dve, trial, trial[P, it + 1:it + 2], lo_t, lo_t[P, it + 1:it + 2], dtab, dtab[P, it + 1:it + 2], ALU.add)
            lo = lo_t[P, NBIS:NBIS + 1]
            k.ts(k.dve, MB, MB[P, 0:n], S, S[P, 0:n], lo, NEG, ALU.is_lt, ALU.mult, extra_ins=[lo_t])

        def att_main(td):
            Tq = td.Tq
            P = slice(0, Tq)
            qc = slice(td.j * Tq, (td.j + 1) * Tq)
            MB, i4 = td.MB, td.i4
            qlo, qhi = qaLo[td.par], qaHi[td.par]
            W4 = 4 * Tq
            nkt = len(td.keytiles)
            units = [(g, ti) for g in range(2) for ti in range(nkt)]

            def s_part(u):
                g, ti = u
                c0, nk, vt = td.keytiles[ti]
                psS = k.ps[2 + (u[0] * nkt + ti) % 2]
                k.mm(psS, psS[0:nk, 0:W4], MB, MB[P, c0:c0 + nk], i4, i4[P, 0:W4], start=True, stop=False)
                k.mm(psS, psS[0:nk, 0:2 * Tq].rearrange("p (a t) -> p a t", a=2), kT2g[g], kT2g[g][:, c0:c0 + nk],
                     qlo, qlo[:, 2 * g:2 * g + 2, qc], start=False, stop=False)
                k.mm(psS, psS[0:nk, 2 * Tq:W4].rearrange("p (a t) -> p a t", a=2), kT2g[g], kT2g[g][:, c0:c0 + nk],
                     qhi, qhi[:, 2 * g:2 * g + 2, qc], start=False, stop=True)
                PTt = PT[(u[0] * nkt + ti) % 3]
                k.actv(PTt, PTt[0:nk, 0:W4], psS, psS[0:nk, 0:W4], AF.Exp, scale=0.125)

            def v_part(u):
                g, ti = u
                c0, nk, vt = td.keytiles[ti]
                Og = k.ps[4 + g]
                PTt = PT[(u[0] * nkt + ti) % 3]
                k.mm(Og, Og[0:65, 0:W4], vext, vext[0:nk, vt, g, :], PTt, PTt[0:nk, 0:W4],
                     start=(ti == 0), stop=(ti == nkt - 1))

            for ui, u in enumerate(units):
                s_part(u)
                if ui >= 1:
                    v_part(units[ui - 1])
            v_part(units[-1])

        def att_fin(td):
            Tq = td.Tq
            W4 = 4 * Tq
            for g in range(2):
                Og = k.ps[4 + g]
                k.actv(rd, rd[64:65, g * 512:g * 512 + W4], Og, Og[64:65, 0:W4], AF.Ln)
                k.actv(rd, rd[64:65, g * 512:g * 512 + W4], rd, rd[64:65, g * 512:g * 512 + W4], AF.Exp, scale=-1.0)
                psB = k.ps[6 + g]
                k.mm(psB, psB[0:64, 0:W4], self.ones, self.ones[64:65, 0:64], rd, rd[64:65, g * 512:g * 512 + W4])
                k.cp(k.act, bcs, bcs[:, g * 512:g * 512 + W4], psB, psB[0:64, 0:W4])
                k.tt(k.dve, yTt, yTt[:, g * 512:g * 512 + W4], Og, Og[0:64, 0:W4], bcs, bcs[:, g * 512:g * 512 + W4], ALU.mult)
                for b_ in range(2):
                    dst = self.YT[256 * g:256 * g + 256, td.tok_out0:td.tok_out0 + Tq].rearrange("(a b d) t -> b d a t", a=2, b=2, d=64)[b_]
                    k.dma(k.pool, dst, yTt[:, g * 512 + b_ * 2 * Tq:g * 512 + (b_ + 1) * 2 * Tq].rearrange("d (a t) -> d a t", a=2),
                          out_t=self.YT, in_t=yTt)

        tds = []
        for st in range(T // 512):
            for j in range(4):
                td = TD()
                i = st * 4 + j
                td.st, td.j, td.Tq = st, j, 128
                td.n = st * 512 + (j + 1) * 128
                td.corner = True
                td.keytiles = [(t * 128, 128, t) for t in range(td.n // 128)]
                td.tok_out0 = st * 512 + j * 128
                td.i4 = self.i4
                td.S, td.MB = Ss[i % 2], MBs[i % 2]
                td.par, td.wsc = st % 2, wscs[st % 2]
                tds.append(td)
        nt = len(tds)
        stage_a(0, 512, 128, 4, 0, 0, False, 0)
        for i in range(nt + 2):
            if 1 <= i <= nt:
                bis(tds[i - 1])
            if 2 <= i:
                att_main(tds[i - 2])
            if i < nt:
                td = tds[i]
                nst = td.st + 1
                if td.j == 0:
                    pending = stage_units(nst * 512, 512, 128, 4, nst * 512, nst * 4, False, nst % 2) if nst < T // 512 else []
                quota = (len(pending) + (3 - td.j)) // (4 - td.j)
                if td.j < 2:
                    quota = min(quota, max(0, len(pending) - 4))
                idx(td, pending, quota)
            if 2 <= i:
                att_fin(tds[i - 2])

        stage_a(T, NSQ * TS, TS, NSQ, 0, 0, True, 0)
        sds = []
        for s in range(NSQ):
            td = TD()
            td.st, td.j, td.Tq = 0, s, TS
            td.n = T + TS
            td.corner = False
            td.keytiles = [(t * 128, 128, t) for t in range(32)] + [(T, TS, 32)]
            td.tok_out0 = T + s * TS
            td.i4 = self.i4s
            td.S, td.MB = Ss[s % 2], MBs[s % 2]
            td.par, td.wsc = 0, wscs[0]
            sds.append(td)

        def prep_k(s, which):
            if which < 2:
                src = self.cache_k[l, s].rearrange("(t p) c -> p t c", p=128)[:, :, which * 64:(which + 1) * 64]
                dst = kT2g[which]
            else:
                src = self.cache_ki[l, s].rearrange("(t p) c -> p t c", p=128)
                dst = kiT2
            for hf in range(2):
                k.dma(k.sp, ktm[:, :, 0:64], src[:, hf * 16:(hf + 1) * 16, :], out_t=ktm)
                k.dma(k.sp, ktm[:, :, 64:128], src[:, hf * 16:(hf + 1) * 16, :], out_t=ktm)
                for b4 in range(4):
                    blk = hf * 4 + b4
                    ps = k.ps[6 + (blk % 2)]
                    for a in range(4):
                        k.tr(ps, ps[:, a * 128:(a + 1) * 128], ktm, ktm[:, b4 * 4 + a, :], self.ident, self.ident[:, :], inc=(a == 3))
                    k.cp(k.act, dst, dst[:, blk * 512:(blk + 1) * 512], ps, ps[:, :])
            k.cp(k.pool, dst, dst[:, T:T + TS], kfm, kfm[:, which, s * TS:(s + 1) * TS])

        def prep_kv(s):
            prep_k(s, 0)
            prep_k(s, 1)
            for g in range(2):
                k.dma(k.pool, vext[:, 0:32, g, 0:64],
                      self.cache_v[l, s].rearrange("(t p) c -> p t c", p=128)[:, :, g * 64:(g + 1) * 64], out_t=vext)
            ps = k.ps[6 + (s % 2)]
            for kc in range(8):
                k.mm(ps, ps[0:TS, 0:128], xT, xT[:, kc, s * TS:(s + 1) * TS], Wtm, Wtm[:, kc, 128:256],
                     start=(kc == 0), stop=(kc == 7))
            k.cp(k.act, vext, vext[0:TS, 32, :, 0:64], ps, ps[0:TS, 0:128].rearrange("p (g d) -> p g d", g=2))

        prep_k(0, 2)
        idx(sds[0])
        prep_kv(0)
        for s in range(NSQ):
            bis(sds[s])
            if s + 1 < NSQ:
                prep_k(s + 1, 2)
                idx(sds[s + 1])
            att_main(sds[s])
            att_fin(sds[s])
            if s + 1 < NSQ:
                prep_kv(s + 1)
        k.end_pass()


    def rows_to_fm(self, src_ap, nrows, rows_t, dst_t, dst_fn):
        k = self.k
        k.dma(k.sp, rows_t[0:nrows, :], src_ap, out_t=rows_t)
        for grp in range(3):
            ps = k.ps[6 + (grp % 2)]
            for a in range(4):
                m = grp * 4 + a
                k.tr(ps, ps[:, a * nrows:(a + 1) * nrows], rows_t, rows_t[0:nrows, m * 128:(m + 1) * 128],
                     self.ident, self.ident[0:nrows, 0:nrows], inc=(a == 3))
            for a in range(4):
                m = grp * 4 + a
                k.cp(k.act, dst_t, dst_fn(m), ps, ps[:, a * nrows:(a + 1) * nrows])

    def p1b(self, l):
        k = self.k
        nc = self.nc
        k.begin_pass()
        wsrc = self.w_in[l].rearrange("(kc p) e -> p kc e", p=128)
        Wfm = k.sb("Wfm", [128, 8, 1536], BF16)
        Wtm = k.sb("Wtm", [128, 8, 520], BF16)
        k.dma(k.pool, Wfm[:, :, :], wsrc[:, :, O_QKV:O_QKV + 1536], out_t=Wfm)
        k.dma(k.pool, Wtm[:, :, 0:8], wsrc[:, :, O_AB:O_AB + 8], out_t=Wtm)
        k.dma(k.pool, Wtm[:, :, 8:520], wsrc[:, :, O_GB:O_GB + 512], out_t=Wtm)
        rows_t = k.sb("rows", [16, 1536], F32)
        cw = k.sb("cw", [128, 12, 4], F32)
        self.rows_to_fm(self.conv_w[l], 4, rows_t, cw, lambda m: cw[:, m, :])
        nw = k.sb("nw", [128, 128], F32)
        k.dma(k.sp, nw[:, :], self.gdn_norm_w[l:l + 1, :].to_broadcast([128, 128]), out_t=nw)
        dtb = k.sb("dtb", [128, 4], F32)
        k.dma(k.sp, dtb[:, :], self.dt_bias[l:l + 1, :].to_broadcast([128, 4]), out_t=dtb)
        negA = k.sb("negA", [128, 4], F32)
        k.dma(k.sp, negA[:, :], self.a_log[l:l + 1, :].to_broadcast([128, 4]), out_t=negA)
        k.actv(negA, negA[:, :], negA, negA[:, :], AF.Exp)
        k.ts(k.dve, negA, negA[:, :], negA, negA[:, :], -1.0, None, ALU.mult)

        xin = [k.sb(f"xin{i}", [128, D], F32) for i in range(2)]
        xT = k.sb("xT", [128, 8, 512], BF16)
        cin = k.sb("cin", [128, 12, 515], F32)
        qkvs = k.sb("qkvs", [128, 12, 512], F32)
        qkb = k.sb("qkb", [128, 8, 512], BF16)
        Sbf = k.sb("Sbf", [128, NSQ, 4, 128], BF16)
        acc = [k.sb(f"acc{i}", [128, 512], F32) for i in range(2)]
        sq = [k.sb(f"sq{i}", [128, 512], F32) for i in range(2)]
        rn = [k.sb(f"rn{i}", [128, 512], F32) for i in range(2)]
        s2s = [k.sb(f"s2_{i}", [128, 512], F32) for i in range(2)]
        Sst = k.sb("Sst", [128, NSQ, 4, 128], F32)
        last3 = rows_t
        NM = 16

        def smal(name, shape):
            return k.sb(name, shape, F32)

        gx = smal("gx", [64, 2, 4])
        gab = smal("gab", [64, 2, 4])
        ge1 = smal("ge1", [64, 2, 4])
        gl1 = smal("gl1", [64, 2, 4])
        gsp = smal("gsp", [64, 2, 4])
        gg = smal("gg", [64, 2, 4])
        nbeta = smal("nbeta", [64, 2, 4])
        beta2 = [smal(f"beta{i}", [64, 2, 4]) for i in range(2)]
        E2 = [smal(f"E{i}", [64, 2, 12]) for i in range(2)]
        negeG2 = [smal(f"negeG{i}", [64, 2, 4]) for i in range(2)]
        gt1282 = [smal(f"gt128{i}", [128, 2, 4]) for i in range(2)]
        sgtmp = smal("sgtmp", [64, 512])

        class MV:
            def __init__(self, name, dt=F32):
                self.tt = k.sb(name, [64, 512], dt)
                self.C = 64
                self.nm = 8

            def set(self, C, nm):
                self.C = C
                self.nm = nm

            def v(self):
                return self.tt[0:self.C, 0:self.nm * self.C].rearrange("p (m c) -> p m c", m=self.nm)

        gU_, DT_, DsB_, W0f_, DTc_, dtmp_ = MV("gU"), MV("DT"), MV("DsB"), MV("W0f"), MV("DTc"), MV("dtmp")
        W_ = [MV(f"W{i}", BF16) for i in range(2)]
        X_ = [MV(f"X{i}", BF16) for i in range(2)]
        NT_ = [MV(f"NT{i}", BF16) for i in range(2)]
        NTf_ = [MV(f"NTf{i}", BF16) for i in range(2)]
        qkT2_ = [MV(f"qkT{i}", BF16) for i in range(2)]
        allmv = [gU_, DT_, DsB_, W0f_, DTc_, dtmp_] + W_ + X_ + NT_
        negG = smal("negG", [64, 2, 4])
        kdec2 = [k.sb(f"kdec{i}", [64, 2, 4, 128], BF16) for i in range(2)]
        vtm2 = [smal(f"vtm{i}", [64, 2, 4, 128]) for i in range(2)]
        sg2 = [smal(f"sg{i}", [64, 2, 512]) for i in range(2)]
        Z = k.sb("Z", [64, 4, 128], BF16)
        Zf = smal("Zf", [64, 4, 128])
        t1 = smal("t1", [64, 4, 128])
        vnew = k.sb("vnew", [64, 4, 128], BF16)
        o_t = smal("o", [64, 4, 128])
        osq = smal("osq", [64, 128])
        ss = smal("ss", [64, 4])
        rstd = smal("rstd", [64, 4])
        yb = smal("yb", [64, 4, 128])
        ybT = k.sb("ybT", [128, 4, 64], BF16)

        def bc(ap, shape, axis):
            return ap.unsqueeze(axis).to_broadcast(shape)

        def process(tok0, NS, nseq, L, C, is_sample, first, last_st):
            cinv = cin[:, :, 0:nseq * (L + 3)].rearrange("p m (s t) -> p m s t", s=nseq)
            qv = qkvs[:, :, 0:NS].rearrange("p m (s t) -> p m s t", s=nseq)
            qb = qkb[:, :, 0:NS].rearrange("p m (s t) -> p m s t", s=nseq)
            done = 0
            i = 0
            while done < NS:
                n = min(128, NS - done)
                self.load_xT(l, tok0 + done, n, xin[i % 2], xT, done, k.act)
                done += n
                i += 1
            if is_sample:
                self.rows_to_fm(self.state_conv[l].rearrange("s j c -> (s j) c"), 12, rows_t, cin,
                                lambda m: cinv[:, m, :, 0:3])
            elif first:
                k.memset(k.pool, cin, cinv[:, :, :, 0:3], 0.0)
            for mt in range(12):
                ps = k.ps[6 + (mt % 2)]
                for kc in range(8):
                    k.mm(ps, ps[:, 0:NS], Wfm, Wfm[:, kc, mt * 128:(mt + 1) * 128], xT, xT[:, kc, 0:NS],
                         start=(kc == 0), stop=(kc == 7))
                k.cp(k.act, cin, cinv[:, mt, :, 3:3 + L], ps, ps[:, 0:NS].rearrange("p (s t) -> p s t", s=nseq))
            if is_sample or last_st:
                for sq_ in range(nseq):
                    c1 = (sq_ + 1) * L
                    for blk in range(3):
                        ps = k.ps[6 + (blk % 2)]
                        for kc in range(8):
                            k.mm(ps, ps[0:3, 0:512], xT, xT[:, kc, c1 - 3:c1], Wfm, Wfm[:, kc, blk * 512:(blk + 1) * 512],
                                 start=(kc == 0), stop=(kc == 7))
                        k.cp(k.act, last3, last3[0:3, blk * 512:(blk + 1) * 512], ps, ps[0:3, 0:512])
                    dst = self.nconv_s[l, sq_] if is_sample else self.nconv_p[l]
                    k.dma(k.sp, dst, last3[0:3, :], in_t=last3, is_output=True)
            for m in range(12):
                a_ = acc[m % 2]
                av = a_[:, 0:NS].rearrange("p (s t) -> p s t", s=nseq)
                k.ts(k.dve, a_, av, cin, cinv[:, m, :, 0:L], cw[:, m, 0:1], None, ALU.mult, extra_ins=[cw])
                for jj in range(1, 4):
                    k.stt(k.dve, a_, av, cin, cinv[:, m, :, jj:jj + L], cw[:, m, jj:jj + 1], a_, av, ALU.mult, ALU.add,
                          extra_ins=[cw])
                s2 = s2s[m % 2]
                k.actv(s2, s2[:, 0:NS], a_, a_[:, 0:NS], AF.Exp, scale=-1.0)
                k.actv(s2, s2[:, 0:NS], s2, s2[:, 0:NS], AF.Ln, bias=1.0)
                k.actv(s2, s2[:, 0:NS], s2, s2[:, 0:NS], AF.Exp, scale=-1.0)
                k.tt(k.pool, qkvs, qkvs[:, m, 0:NS], a_, a_[:, 0:NS], s2, s2[:, 0:NS], ALU.mult)

            def n_sq(m):
                s_ = sq[m % 2]
                k.actv(s_, s_[:, 0:NS], qkvs, qkvs[:, m, 0:NS], AF.Square)
                ps = k.ps[6 + (m % 2)]
                k.mm(ps, ps[:, 0:NS], self.ones, self.ones[:, :], s_, s_[:, 0:NS])

            def n_fin(m):
                ps = k.ps[6 + (m % 2)]
                r_ = rn[m % 2]
                k.actv(r_, r_[:, 0:NS], ps, ps[:, 0:NS], AF.Ln, bias=NORM_EPS)
                k.actv(r_, r_[:, 0:NS], r_, r_[:, 0:NS], AF.Exp, scale=-0.5)
                k.stt(k.dve, qkvs, qkvs[:, m, 0:NS], qkvs, qkvs[:, m, 0:NS], (128.0 ** -0.5) if m < 4 else 1.0,
                      r_, r_[:, 0:NS], ALU.mult, ALU.mult)
                k.cp(k.act if m % 2 else k.pool, qkb, qkb[:, m, 0:NS], qkvs, qkvs[:, m, 0:NS])

            n_sq(0)
            for m in range(8):
                if m + 1 < 8:
                    n_sq(m + 1)
                n_fin(m)
            if not is_sample:
                k.cp(k.pool, cin, cinv[:, :, :, 0:3], cin, cinv[:, :, :, L:L + 3])

            nch = L // C
            if is_sample:
                batches = [[(0, 0), (1, 0)], [(2, 0), (3, 0)]]
            else:
                batches = [[(0, ci), (0, ci + 1)] for ci in range(0, nch, 2)]
            PC = slice(0, C)
            nb = 2
            nm = 8
            for mv in allmv:
                mv.set(C, nm)
            gU, DT, DsB, W0f, DTc, dtmp = gU_.tt, DT_.tt, DsB_.tt, W0f_.tt, DTc_.tt, dtmp_.tt
            gUv, DTv, DsBv, W0fv, DTcv, dtmpv = gU_.v(), DT_.v(), DsB_.v(), W0f_.v(), DTc_.v(), dtmp_.v()
            W = [w.tt for w in W_]
            Wv = [w.v() for w in W_]
            X = [x.tt for x in X_]
            Xv = [x.v() for x in X_]
            NT = [x.tt for x in NT_]
            NTv = [x.v() for x in NT_]
            mvv = lambda ps: ps[PC, 0:nm * C].rearrange("p (m c) -> p m c", m=nm)

            def prep_units(batch, par):
                Ep, negeGp, gt128p, betap = E2[par], negeG2[par], gt1282[par], beta2[par]
                kdecp, vtmp, sgp = kdec2[par], vtm2[par], sg2[par]
                NTfp, qkTp = NTf_[par], qkT2_[par]
                NTfp.set(C, nm)
                qkTp.set(C, nm)
                U = []

                def u_tm(bi):
                    sq_, ci = batch[bi]
                    col0 = sq_ * L + ci * C
                    psA = k.ps[0]
                    for kc in range(8):
                        k.mm(psA, psA[PC, 0:8], xT, xT[:, kc, col0:col0 + C], Wtm, Wtm[:, kc, 0:8], start=(kc == 0), stop=(kc == 7))
                    k.tt(k.dve, gx, gx[PC, bi, :], psA, psA[PC, 0:4], dtb, dtb[PC, :], ALU.add)
                    k.actv(ge1, ge1[PC, bi, :], psA, psA[PC, 4:8], AF.Exp, scale=-1.0)
                    psB = k.ps[0]
                    for kc in range(8):
                        k.mm(psB, psB[PC, 0:512], xT, xT[:, kc, col0:col0 + C], Wtm, Wtm[:, kc, 8:520], start=(kc == 0), stop=(kc == 7))
                    k.actv(sgtmp, sgtmp[PC, :], psB, psB[PC, 0:512], AF.Exp, scale=-1.0)
                    k.actv(sgtmp, sgtmp[PC, :], sgtmp, sgtmp[PC, :], AF.Ln, bias=1.0)
                    k.actv(sgtmp, sgtmp[PC, :], sgtmp, sgtmp[PC, :], AF.Exp, scale=-1.0)
                    k.tt(k.dve, sgp, sgp[PC, bi, :], psB, psB[PC, 0:512], sgtmp, sgtmp[PC, :], ALU.mult)
                for bi in range(nb):
                    U.append(lambda bi=bi: u_tm(bi))

                def u_gates():
                    k.actv(ge1, ge1[PC, 0:nb, :], ge1, ge1[PC, 0:nb, :], AF.Ln, bias=1.0)
                    k.actv(betap, betap[PC, 0:nb, :], ge1, ge1[PC, 0:nb, :], AF.Exp, scale=-1.0)
                    gxa = gx[PC, 0:nb, :]
                    k.ts(k.dve, gab, gab[PC, 0:nb, :], gx, gxa, -1.0, None, ALU.mult)
                    k.tt(k.dve, gab, gab[PC, 0:nb, :], gab, gab[PC, 0:nb, :], gx, gxa, ALU.min)
                    k.actv(gl1, gl1[PC, 0:nb, :], gab, gab[PC, 0:nb, :], AF.Exp)
                    k.actv(gl1, gl1[PC, 0:nb, :], gl1, gl1[PC, 0:nb, :], AF.Ln, bias=1.0)
                    k.stt(k.dve, gsp, gsp[PC, 0:nb, :], gx, gxa, 0.0, gl1, gl1[PC, 0:nb, :], ALU.max, ALU.add)
                    k.tt(k.dve, gg, gg[PC, 0:nb, :], gsp, gsp[PC, 0:nb, :], negA, bc(negA[PC, :], [C, nb, 4], 1), ALU.mult)
                    k.ts(k.dve, nbeta, nbeta[PC, 0:nb, :], betap, betap[PC, 0:nb, :], -1.0, None, ALU.mult)
                    k.tt(k.pool, sgp, sgp[PC, 0:nb, :].rearrange("p b (h e) -> p (b h) e", h=4), sgp,
                         sgp[PC, 0:nb, :].rearrange("p b (h e) -> p (b h) e", h=4), nw, bc(nw[PC, :], [C, nm, 128], 1), ALU.mult)
                U.append(u_gates)

                def u_decay():
                    psG = k.ps[0]
                    for bi in range(nb):
                        gcol = gg[PC, bi, :]
                        k.mm(psG, psG[PC, bi * 16:bi * 16 + 4], self.c_U, self.c_U[PC, PC], gg, gcol)
                        k.mm(psG, psG[PC, bi * 16 + 4:bi * 16 + 8], self.c_Urev, self.c_Urev[PC, PC], gg, gcol)
                        k.mm(psG, psG[PC, bi * 16 + 8:bi * 16 + 12], self.ones, self.ones[PC, PC], gg, gcol)
                        k.mm(psG, psG[:, 64 + bi * 4:64 + bi * 4 + 4], self.ones, self.ones[PC, :], gg, gcol)
                    k.actv(Ep, Ep[PC, 0:nb, :], psG, psG[PC, 0:nb * 16].rearrange("p (b e) -> p b e", b=nb)[:, :, 0:12], AF.Exp)
                    k.actv(gt128p, gt128p[:, 0:nb, :], psG, psG[:, 64:64 + nb * 4].rearrange("p (b e) -> p b e", b=nb), AF.Exp)
                    k.ts(k.dve, negeGp, negeGp[PC, 0:nb, :], Ep, Ep[PC, 0:nb, 0:4], -1.0, None, ALU.mult)
                    k.ts(k.dve, negG, negG[PC, 0:nb, :], psG, psG[PC, 0:nb * 16].rearrange("p (b e) -> p b e", b=nb)[:, :, 0:4],
                         -1.0, None, ALU.mult)
                    k.tt(k.dve, gU, gUv, self.c_U, bc(self.c_U[PC, PC], [C, nm, C], 1),
                         gg, bc(gg[PC, 0:nb, :].rearrange("p b h -> p (b h)"), [C, nm, C], 2), ALU.mult)
                U.append(u_decay)

                def u_diff():
                    psD = k.ps[1]
                    for mi in range(nm):
                        k.mm(psD, psD[PC, mi * C:(mi + 1) * C], self.ones, self.ones[PC, PC], gU, gUv[:, mi, :])
                    k.tt(k.dve, dtmp, dtmpv, psD, mvv(psD), negG,
                         bc(negG[PC, 0:nb, :].rearrange("p b h -> p (b h)"), [C, nm, C], 2), ALU.add)
                    k.ts(k.dve, dtmp, dtmpv, dtmp, dtmpv, 0.0, None, ALU.min)
                    k.actv(DT, DTv, dtmp, dtmpv, AF.Exp)
                    k.tt(k.pool, DTc, DTcv, DT, DTv, self.c_caus01, bc(self.c_caus01[PC, PC], [C, nm, C], 1), ALU.mult)
                    k.tt(k.pool, DsB, DsBv, DT, DTv, self.c_su01, bc(self.c_su01[PC, PC], [C, nm, C], 1), ALU.mult)
                    k.tt(k.pool, DsB, DsBv, DsB, DsBv, nbeta,
                         bc(nbeta[PC, 0:nb, :].rearrange("p b h -> p (b h)"), [C, nm, C], 2), ALU.mult)
                U.append(u_diff)

                def u_kk():
                    psK = k.ps[2]
                    psQ = k.ps[3]
                    for bi, (sq_, ci) in enumerate(batch):
                        cs = slice(ci * C, (ci + 1) * C)
                        for h in range(4):
                            mi = bi * 4 + h
                            k.mm(psK, psK[PC, mi * C:(mi + 1) * C], qkb, qb[:, 4 + h, sq_, cs], qkb, qb[:, 4 + h, sq_, cs])
                            k.mm(psQ, psQ[PC, mi * C:(mi + 1) * C], qkb, qb[:, 4 + h, sq_, cs], qkb, qb[:, h, sq_, cs])
                    k.tt(k.dve, W0f, W0fv, psK, mvv(psK), DsB, DsBv, ALU.mult)
                    k.cp(k.pool, W[0], Wv[0], W0f, W0fv)
                    k.tt(k.dve, qkTp.tt, qkTp.v(), psQ, mvv(psQ), DTc, DTcv, ALU.mult)
                U.append(u_kk)

                def u_x0():
                    psX = k.ps[2]
                    for mi in range(nm):
                        k.tr(psX, psX[PC, mi * C:(mi + 1) * C], W0f, W0fv[:, mi, :], self.ident, self.ident[PC, PC], inc=(mi == nm - 1))
                    k.cp(k.act, X[0], Xv[0], psX, mvv(psX))
                    k.tt(k.dve, NT[0], NTv[0], W0f, W0fv, self.ident, bc(self.ident[PC, PC], [C, nm, C], 1), ALU.add)
                U.append(u_x0)

                nlev = int(round(math.log2(C)))
                for lev in range(1, nlev):
                    cur = (lev - 1) % 2
                    nxt = 1 - cur
                    lastlev = (lev == nlev - 1)

                    def u_x(cur=cur, nxt=nxt):
                        psX2 = k.ps[2]
                        for mi in range(nm):
                            k.mm(psX2, psX2[PC, mi * C:(mi + 1) * C], W[cur], Wv[cur][:, mi, :], X[cur], Xv[cur][:, mi, :])
                        k.cp(k.act, X[nxt], Xv[nxt], psX2, mvv(psX2))
                    U.append(u_x)
                    if not lastlev:
                        def u_w(cur=cur, nxt=nxt):
                            psW = k.ps[1]
                            for mi in range(nm):
                                k.mm(psW, psW[PC, mi * C:(mi + 1) * C], X[cur], Xv[cur][:, mi, :], W[cur], Wv[cur][:, mi, :])
                            k.cp(k.act, W[nxt], Wv[nxt], psW, mvv(psW))
                        U.append(u_w)

                    def u_p(cur=cur, nxt=nxt, lastlev=lastlev):
                        psP = k.ps[3]
                        for mi in range(nm):
                            k.mm(psP, psP[PC, mi * C:(mi + 1) * C], X[nxt], Xv[nxt][:, mi, :], NT[cur], NTv[cur][:, mi, :])
                        if lastlev:
                            k.tt(k.dve, NTfp.tt, NTfp.v(), psP, mvv(psP), NT[cur], NTv[cur], ALU.add)
                        else:
                            k.tt(k.dve, NT[nxt], NTv[nxt], psP, mvv(psP), NT[cur], NTv[cur], ALU.add)
                    U.append(u_p)

                def u_kv(bi):
                    sq_, ci = batch[bi]
                    cs = slice(ci * C, (ci + 1) * C)
                    psk = k.ps[0]
                    for h in range(4):
                        k.tr(psk, psk[PC, h * 128:(h + 1) * 128], qkvs, qv[:, 4 + h, sq_, cs], self.ident, self.ident[:, :], inc=(h == 3))
                    k.tt(k.dve, kdecp, kdecp[PC, bi, :, :], psk, psk[PC, :].rearrange("p (h e) -> p h e", h=4),
                         Ep, bc(Ep[PC, bi, 4:8], [C, 4, 128], 2), ALU.mult)
                    psv = k.ps[0]
                    for h in range(4):
                        k.tr(psv, psv[PC, h * 128:(h + 1) * 128], qkvs, qv[:, 8 + h, sq_, cs], self.ident, self.ident[:, :], inc=(h == 3))
                    k.cp(k.act, vtmp, vtmp[PC, bi, :, :], psv, psv[PC, :].rearrange("p (h e) -> p h e", h=4))
                for bi in range(nb):
                    U.append(lambda bi=bi: u_kv(bi))
                return U

            def rec_units(batch, par):
                Ep, negeGp, gt128p, betap = E2[par], negeG2[par], gt1282[par], beta2[par]
                kdecp, vtmp, sgp = kdec2[par], vtm2[par], sg2[par]
                NTfv, qkTv = NTf_[par].v(), qkT2_[par].v()
                NTf, qkT = NTf_[par].tt, qkT2_[par].tt
                U = []
                for bi, (sq_, ci) in enumerate(batch):
                    cs = slice(ci * C, (ci + 1) * C)
                    Sv = Sst[:, sq_, :, :]
                    pskS, psqS, psNZ, psqkv, psdS = k.ps[4], k.ps[5], k.ps[6], k.ps[7], k.ps[6]

                    def r1(bi=bi, sq_=sq_, cs=cs, Sv=Sv):
                        for h in range(4):
                            k.mm(pskS, pskS[PC, h * 128:(h + 1) * 128], qkb, qb[:, 4 + h, sq_, cs], Sbf, Sbf[:, sq_, h, :], inc=(h == 3))
                        for h in range(4):
                            k.mm(psqS, psqS[PC, h * 128:(h + 1) * 128], qkb, qb[:, h, sq_, cs], Sbf, Sbf[:, sq_, h, :], inc=(h == 3))
                        k.tt(k.dve, Zf, Zf[PC, :, :], pskS, pskS[PC, :].rearrange("p (h e) -> p h e", h=4),
                             negeGp, bc(negeGp[PC, bi, :], [C, 4, 128], 2), ALU.mult)
                        k.tt(k.dve, Z, Z[PC, :, :], Zf, Zf[PC, :, :], vtmp, vtmp[PC, bi, :, :], ALU.add)
                        k.tt(k.dve, t1, t1[PC, :, :], psqS, psqS[PC, :].rearrange("p (h e) -> p h e", h=4),
                             Ep, bc(Ep[PC, bi, 0:4], [C, 4, 128], 2), ALU.mult)
                    U.append(r1)

                    def r2(bi=bi):
                        for h in range(4):
                            k.mm(psNZ, psNZ[PC, h * 128:(h + 1) * 128], NTf, NTfv[:, bi * 4 + h, :], Z, Z[PC, h, :], inc=(h == 3))
                        k.tt(k.dve, vnew, vnew[PC, :, :], psNZ, psNZ[PC, :].rearrange("p (h e) -> p h e", h=4),
                             betap, bc(betap[PC, bi, :], [C, 4, 128], 2), ALU.mult)
                    U.append(r2)

                    def r3(bi=bi, Sv=Sv, sq_=sq_):
                        for h in range(4):
                            k.mm(psdS, psdS[:, h * 128:(h + 1) * 128], kdecp, kdecp[PC, bi, h, :], vnew, vnew[PC, h, :], inc=(h == 3))
                        for h in range(4):
                            k.mm(psqkv, psqkv[PC, h * 128:(h + 1) * 128], qkT, qkTv[:, bi * 4 + h, :], vnew, vnew[PC, h, :], inc=(h == 3))
                        k.tt(k.dve, Sst, Sv, Sst, Sv, gt128p, bc(gt128p[:, bi, :], [128, 4, 128], 2), ALU.mult)
                        k.tt(k.dve, Sst, Sv, Sst, Sv, psdS, psdS[:, :].rearrange("p (h e) -> p h e", h=4), ALU.add)
                        k.cp(k.pool, Sbf, Sbf[:, sq_, :, :], Sst, Sv)
                        k.tt(k.dve, o_t, o_t[PC, :, :], psqkv, psqkv[PC, :].rearrange("p (h e) -> p h e", h=4), t1, t1[PC, :, :], ALU.add)
                    U.append(r3)

                    def r4(bi=bi, sq_=sq_, ci=ci):
                        for h in range(4):
                            k.actv(osq, osq[PC, :], o_t, o_t[PC, h, :], AF.Square, accum=(ss, ss[PC, h:h + 1]))
                        k.actv(rstd, rstd[PC, :], ss, ss[PC, :], AF.Ln, bias=NORM_EPS, scale=1.0 / 128.0)
                        k.actv(rstd, rstd[PC, :], rstd, rstd[PC, :], AF.Exp, scale=-0.5)
                        k.tt(k.pool, yb, yb[PC, :, :], o_t, o_t[PC, :, :], rstd, bc(rstd[PC, :], [C, 4, 128], 2), ALU.mult)
                        k.tt(k.pool, yb, yb[PC, :, :], yb, yb[PC, :, :], sgp, sgp[PC, bi, :].rearrange("p (h e) -> p h e", h=4), ALU.mult)
                        psT = k.ps[4]
                        for h in range(4):
                            k.tr(psT, psT[:, h * C:(h + 1) * C], yb, yb[PC, h, :], self.ident, self.ident[PC, PC], inc=(h == 3))
                        k.cp(k.act, ybT, ybT[:, :, 0:C], psT, psT[:, 0:4 * C].rearrange("p (h c) -> p h c", h=4))
                        tk = tok0 + sq_ * L + ci * C
                        k.dma(k.pool, self.YT[512:1024, tk:tk + C].rearrange("(m p) t -> p m t", p=128), ybT[:, :, 0:C],
                              out_t=self.YT, in_t=ybT)
                    U.append(r4)
                return U

            for u in prep_units(batches[0], 0):
                u()
            for b_ in range(len(batches)):
                R = rec_units(batches[b_], b_ % 2)
                N = prep_units(batches[b_ + 1], (b_ + 1) % 2) if b_ + 1 < len(batches) else []
                per = (len(N) + len(R) - 1) // len(R) if N else 0
                for r in R:
                    r()
                    for _ in range(per):
                        if N:
                            N.pop(0)()
                while N:
                    N.pop(0)()

        k.memset(k.pool, Sst, Sst[:, 0, :, :], 0.0)
        k.memset(k.pool, Sbf, Sbf[:, 0, :, :], 0.0)
        nst = T // 512
        for st in range(nst):
            process(st * 512, 512, 1, 512, 64, False, st == 0, st == nst - 1)
        k.dma(k.sp, self.nssm_p[l].rearrange("h a b -> a h b"), Sst[:, 0, :, :], in_t=Sst, is_output=True)
        for s in range(NSQ):
            k.dma(k.sp, Sst[:, s, :, :], self.state_ssm[l, s].rearrange("h a b -> a h b"), out_t=Sst)
            k.cp(k.pool, Sbf, Sbf[:, s, :, :], Sst, Sst[:, s, :, :])
        process(T, NSQ * TS, NSQ, TS, TS, True, True, True)
        for s in range(NSQ):
            k.dma(k.sp, self.nssm_s[l, s].rearrange("h a b -> a h b"), Sst[:, s, :, :], in_t=Sst, is_output=True)
        k.end_pass()


    def layer_norm(self, r, n, gbc, bbc, junk, stat, out_t):
        k = self.k
        nc = self.nc
        P = slice(0, n)
        k.actv(junk, junk[P, :], r, r[P, :], AF.Identity, accum=(stat, stat[P, 0:1]))
        k.actv(junk, junk[P, :], r, r[P, :], AF.Square, accum=(stat, stat[P, 1:2]))
        k.ts(k.dve, stat, stat[P, 2:3], stat, stat[P, 0:1], -1.0 / D, None, ALU.mult)
        k.tt(k.dve, stat, stat[P, 3:4], stat, stat[P, 2:3], stat, stat[P, 2:3], ALU.mult)
        k.stt(k.dve, stat, stat[P, 4:5], stat, stat[P, 1:2], 1.0 / D, stat, stat[P, 3:4], ALU.mult, ALU.subtract)
        k.actv(stat, stat[P, 5:6], stat, stat[P, 4:5], AF.Sqrt, bias=LN_EPS)
        k.op(k.dve, lambda: nc.vector.reciprocal(stat[P, 6:7], stat[P, 5:6]), outs=[stat], ins=[stat])
        k.ts(k.dve, r, r[P, :], r, r[P, :], stat[P, 2:3], stat[P, 6:7], ALU.add, ALU.mult, extra_ins=[stat])
        k.tt(k.pool, r, r[P, :], r, r[P, :], gbc, gbc[P, :], ALU.mult)
        k.tt(k.pool, out_t, out_t[P, :], r, r[P, :], bbc, bbc[P, :], ALU.add)

    def p2(self, l):
        k = self.k
        nc = self.nc
        k.begin_pass()
        wsrc = self.w_in[l].rearrange("(kc p) e -> p kc e", p=128)
        Wg = k.sb("Wg", [128, 8, 2048], BF16)
        k.dma(k.pool, Wg[:, :, :], wsrc[:, :, O_GATES:O_GATES + 2048], out_t=Wg)
        Wpa = k.sb("Wpa", [128, 4, D], BF16)
        k.dma(k.pool, Wpa[:, :, :], self.w_proj_a[l].rearrange("(kc p) e -> p kc e", p=128), out_t=Wpa)
        Wpb = k.sb("Wpb", [128, 4, D], BF16)
        k.dma(k.pool, Wpb[:, :, :], self.w_proj_b[l].rearrange("(kc p) e -> p kc e", p=128), out_t=Wpb)
        Wo = k.sb("Wo", [128, 8, D], BF16)
        k.dma(k.pool, Wo[:, :, :], self.w_out[l].rearrange("(kc p) e -> p kc e", p=128), out_t=Wo)
        bg = k.sb("bg", [128, 2048], F32)
        k.dma(k.sp, bg[:, :], self.b_gate[l:l + 1, :].to_broadcast([128, 2048]), out_t=bg)
        gbc = k.sb("gbc", [128, D], F32)
        k.dma(k.sp, gbc[:, :], self.ln1_g[l:l + 1, :].to_broadcast([128, D]), out_t=gbc)
        bbc = k.sb("bbc", [128, D], F32)
        k.dma(k.sp, bbc[:, :], self.ln1_b[l:l + 1, :].to_broadcast([128, D]), out_t=bbc)
        xin = [k.sb(f"xin{i}", [128, D], F32) for i in range(2)]
        xT = [k.sb(f"xT{i}", [128, 8, 128], BF16) for i in range(2)]
        yT = [k.sb(f"yT{i}", [128, 8, 128], BF16) for i in range(2)]
        sgates = [k.sb(f"sgate{i}", [128, 2048], F32) for i in range(2)]
        mixeds = [k.sb(f"mixed{i}", [128, D], F32) for i in range(2)]
        tmps = [k.sb(f"tmp{i}", [128, 512], F32) for i in range(2)]
        mixT = k.sb("mixT", [128, 8, 128], BF16)
        r = [k.sb(f"r{i}", [128, D], F32) for i in range(2)]
        junk = k.sb("junk", [128, D], BF16)
        stat = k.sb("stat", [128, 8], F32)
        tiles = [(t * 128, 128) for t in range(T // 128)] + [(T, NSQ * TS)]

        def stage_x(ti):
            tok0, n = tiles[ti]
            P = slice(0, n)
            xi, xt, yt = xin[ti % 2], xT[ti % 2], yT[ti % 2]
            sgate, mixed = sgates[ti % 2], mixeds[ti % 2]
            self.load_xT(l, tok0, n, xi, xt, 0, k.act)
            k.dma(k.sp, yt[:, :, 0:n], self.YT[:, tok0:tok0 + n].rearrange("(kc p) t -> p kc t", p=128), out_t=yt, in_t=self.YT)
            for blk in range(4):
                ps = k.ps[blk % 2]
                for kc in range(8):
                    k.mm(ps, ps[P, :], xt, xt[:, kc, 0:n], Wg, Wg[:, kc, blk * 512:(blk + 1) * 512], start=(kc == 0), stop=(kc == 7))
                k.tt(k.dve, sgate, sgate[P, blk * 512:(blk + 1) * 512], ps, ps[P, :], bg, bg[P, blk * 512:(blk + 1) * 512], ALU.add)
                k.actv(sgate, sgate[P, blk * 512:(blk + 1) * 512], sgate, sgate[P, blk * 512:(blk + 1) * 512], AF.Sigmoid)
            for blk in range(2):
                psa = k.ps[4 + blk]
                psb = k.ps[6 + blk]
                tmp = tmps[blk]
                for kc in range(4):
                    k.mm(psa, psa[P, :], yt, yt[:, kc, 0:n], Wpa, Wpa[:, kc, blk * 512:(blk + 1) * 512], start=(kc == 0), stop=(kc == 3))
                for kc in range(4):
                    k.mm(psb, psb[P, :], yt, yt[:, 4 + kc, 0:n], Wpb, Wpb[:, kc, blk * 512:(blk + 1) * 512], start=(kc == 0), stop=(kc == 3))
                cs = slice(blk * 512, (blk + 1) * 512)
                k.tt(k.dve, mixed, mixed[P, cs], psa, psa[P, :], sgate, sgate[P, cs], ALU.mult)
                k.tt(k.dve, tmp, tmp[P, :], psb, psb[P, :], sgate, sgate[P, 1024 + blk * 512:1024 + (blk + 1) * 512], ALU.mult)
                k.tt(k.pool, mixed, mixed[P, cs], mixed, mixed[P, cs], tmp, tmp[P, :], ALU.add)

        def stage_y(ti):
            tok0, n = tiles[ti]
            P = slice(0, n)
            xi, mixed, rr = xin[ti % 2], mixeds[ti % 2], r[ti % 2]
            for grp in range(2):
                ps = k.ps[2 + grp]
                for kk in range(4):
                    kc = grp * 4 + kk
                    k.tr(ps, ps[:, kk * n:(kk + 1) * n], mixed, mixed[P, kc * 128:(kc + 1) * 128], self.ident, self.ident[P, P], inc=(kk == 3))
                k.cp(k.act, mixT, mixT[:, grp * 4:(grp + 1) * 4, 0:n], ps, ps[:, 0:4 * n].rearrange("p (a t) -> p a t", a=4))
            for blk in range(2):
                ps = k.ps[2 + blk]
                for kc in range(8):
                    k.mm(ps, ps[P, :], mixT, mixT[:, kc, 0:n], Wo, Wo[:, kc, blk * 512:(blk + 1) * 512], start=(kc == 0), stop=(kc == 7))
                cs = slice(blk * 512, (blk + 1) * 512)
                k.stt(k.dve, rr, rr[P, cs], xi, xi[P, cs], ALPHA, ps, ps[P, :], ALU.mult, ALU.add)
            self.layer_norm(rr, n, gbc, bbc, junk, stat, rr)
            k.dma(k.pool, self.X1[tok0:tok0 + n, :], rr[P, :], out_t=self.X1, in_t=rr)

        stage_x(0)
        for ti in range(len(tiles)):
            if ti + 1 < len(tiles):
                stage_x(ti + 1)
            stage_y(ti)
        k.end_pass()

    def p3(self, l):
        k = self.k
        nc = self.nc
        k.begin_pass()
        Wup = k.sb("Wup", [128, 8, 2 * DFF], BF16)
        usrc = self.w_up[l].rearrange("(kc p) e -> p kc e", p=128)
        for q4 in range(4):
            k.dma(k.pool, Wup[:, :, q4 * 1408:(q4 + 1) * 1408], usrc[:, :, q4 * 1408:(q4 + 1) * 1408], out_t=Wup)
        Wdn = k.sb("Wdn", [128, 22, D], BF16)
        k.dma(k.pool, Wdn[:, :, :], self.w_down[l].rearrange("(kc p) e -> p kc e", p=128), out_t=Wdn)
        gbc = k.sb("gbc", [128, D], F32)
        k.dma(k.sp, gbc[:, :], self.ln2_g[l:l + 1, :].to_broadcast([128, D]), out_t=gbc)
        bbc = k.sb("bbc", [128, D], F32)
        k.dma(k.sp, bbc[:, :], self.ln2_b[l:l + 1, :].to_broadcast([128, D]), out_t=bbc)
        xin = [k.sb(f"xin{i}", [128, D], F32) for i in range(2)]
        xT = k.sb("xT", [128, 8, 512], BF16)
        fT = k.sb("fT", [128, 22, 512], BF16)
        tmp = [k.sb(f"tmp{i}", [128, 512], F32) for i in range(2)]
        rrs = [k.sb(f"r{i}", [128, D], F32) for i in range(2)]
        junk = k.sb("junk", [128, D], BF16)
        stat = k.sb("stat", [128, 8], F32)
        last = (l == DEPTH - 1)
        sts = [(st * 512, 512) for st in range(T // 512)] + [(T, NSQ * TS)]

        def subs_of(NS):
            subs = []
            done = 0
            while done < NS:
                n = min(128, NS - done)
                subs.append((done, n))
                done += n
            return subs

        xcnt = [0]

        def transposes(si):
            tok0, NS = sts[si]
            for (c0, n) in subs_of(NS):
                xi = xin[xcnt[0] % 2]
                xcnt[0] += 1
                k.dma(k.sp, xi[0:n, :], self.X1[tok0 + c0:tok0 + c0 + n, :], out_t=xi, in_t=self.X1)
                for grp in range(2):
                    ps = k.ps[6 + grp]
                    for kk in range(4):
                        kc = grp * 4 + kk
                        k.tr(ps, ps[:, kk * n:(kk + 1) * n], xi, xi[0:n, kc * 128:(kc + 1) * 128], self.ident, self.ident[0:n, 0:n], inc=(kk == 3))
                    k.cp(k.act, xT, xT[:, grp * 4:(grp + 1) * 4, c0:c0 + n], ps, ps[:, 0:4 * n].rearrange("p (a t) -> p a t", a=4))

        def up(si):
            tok0, NS = sts[si]
            for fc in range(22):
                psA = k.ps[(2 * fc) % 4]
                psB = k.ps[(2 * fc + 1) % 4]
                for kc in range(8):
                    k.mm(psA, psA[:, 0:NS], Wup, Wup[:, kc, fc * 128:(fc + 1) * 128], xT, xT[:, kc, 0:NS], start=(kc == 0), stop=(kc == 7))
                for kc in range(8):
                    k.mm(psB, psB[:, 0:NS], Wup, Wup[:, kc, DFF + fc * 128:DFF + (fc + 1) * 128], xT, xT[:, kc, 0:NS], start=(kc == 0), stop=(kc == 7))
                tm = tmp[fc % 2]
                k.actv(tm, tm[:, 0:NS], psA, psA[:, 0:NS], AF.Silu)
                k.tt(k.dve, fT, fT[:, fc, 0:NS], tm, tm[:, 0:NS], psB, psB[:, 0:NS], ALU.mult)

        rcnt = [0]

        def down(si):
            tok0, NS = sts[si]
            for (c0, n) in subs_of(NS):
                P = slice(0, n)
                rr = rrs[rcnt[0] % 2]
                rcnt[0] += 1
                k.dma(k.sp, rr[0:n, :], self.X1[tok0 + c0:tok0 + c0 + n, :], out_t=rr, in_t=self.X1)
                for blk in range(2):
                    ps = k.ps[4 + blk]
                    for fc in range(22):
                        k.mm(ps, ps[P, :], fT, fT[:, fc, c0:c0 + n], Wdn, Wdn[:, fc, blk * 512:(blk + 1) * 512], start=(fc == 0), stop=(fc == 21))
                    cs = slice(blk * 512, (blk + 1) * 512)
                    k.stt(k.dve, rr, rr[P, cs], rr, rr[P, cs], ALPHA, ps, ps[P, :], ALU.mult, ALU.add)
                self.layer_norm(rr, n, gbc, bbc, junk, stat, rr)
                t0 = tok0 + c0
                if not last:
                    k.dma(k.pool, self.X2[t0:t0 + n, :], rr[P, :], out_t=self.X2, in_t=rr)
                elif t0 < T:
                    k.dma(k.pool, self.y_p[t0:t0 + n, :], rr[P, :], in_t=rr, is_output=True)
                else:
                    k.dma(k.pool, self.y_s[t0 - T:t0 - T + n, :], rr[P, :], in_t=rr, is_output=True)

        transposes(0)
        for si in range(len(sts)):
            up(si)
            if si + 1 < len(sts):
                transposes(si + 1)
            down(si)
        k.end_pass()

    def build(self):
        k = self.k
        self.load_consts()
        sa = self.stop_after
        only = sa[0][:-5] if (sa is not None and sa[0].endswith("_only")) else None
        for l in range(DEPTH):
            for name, fn in (("p1a", self.p1a), ("p1b", self.p1b), ("p2", self.p2), ("p3", self.p3)):
                if only is not None and name != only:
                    continue
                fn(l)
                if sa is not None and sa[1] == l and (sa[0] == name or only == name):
                    break
            else:
                continue
            break
        k.finish()


def shard_inputs(inputs, c):
    f = lambda a: np.ascontiguousarray(a, dtype=np.float32)
    sl = slice(NSQ * c, NSQ * (c + 1))
    m = {
        "x_p": f(inputs["x_prompt"][c]),
        "x_s": f(inputs["x_sample"][sl].reshape(NSQ * TS, D)),
        "cache_k": f(inputs["cache_k"][:, sl].reshape(DEPTH, NSQ, T, 128)),
        "cache_v": f(inputs["cache_v"][:, sl].reshape(DEPTH, NSQ, T, 128)),
        "cache_ki": f(inputs["cache_kidx"][:, sl]),
        "state_conv": f(inputs["state_conv"][:, sl]),
        "state_ssm": f(inputs["state_ssm"][:, sl]),
    }
    for n in ["w_in", "b_gate", "conv_w", "a_log", "dt_bias", "gdn_norm_w", "w_proj_a", "w_proj_b", "w_out",
              "ln1_g", "ln1_b", "w_up", "w_down", "ln2_g", "ln2_b"]:
        m[n] = f(inputs[n])
    for n, v in make_consts().items():
        m["c_" + n] = v
    return m


def run(inputs, debug=False, stop_after=None, trace=False):
    prog = Prog(debug=debug, stop_after=stop_after)
    in_maps = [shard_inputs(inputs, c) for c in range(8)]
    res = run_bass_kernel_spmd(prog.nc, in_maps, core_ids=list(range(8)), trace=trace)
    return res


def kernel(**inputs):
    res = run(inputs)
    r = res.results
    cat = lambda n: np.stack([r[c][n] for c in range(8)], axis=0)
    y_p = cat("y_p")
    y_s = cat("y_s").reshape(32, TS, D)
    nk_p = np.transpose(cat("nk_p"), (1, 0, 2, 3)).reshape(DEPTH, 8, T, 2, 64)
    nv_p = np.transpose(cat("nv_p"), (1, 0, 2, 3)).reshape(DEPTH, 8, T, 2, 64)
    nki_p = np.transpose(cat("nki_p"), (1, 0, 2, 3))
    nconv_p = np.transpose(cat("nconv_p"), (1, 0, 2, 3))
    nssm_p = np.transpose(cat("nssm_p"), (1, 0, 2, 3, 4))
    nk_s = np.transpose(cat("nk_s"), (1, 0, 2, 3)).reshape(DEPTH, 32, TS, 2, 64)
    nv_s = np.transpose(cat("nv_s"), (1, 0, 2, 3)).reshape(DEPTH, 32, TS, 2, 64)
    nki_s = np.transpose(cat("nki_s"), (1, 0, 2, 3)).reshape(DEPTH, 32, TS, 64)
    nconv_s = np.transpose(cat("nconv_s"), (1, 0, 2, 3, 4)).reshape(DEPTH, 32, 3, 1536)
    nssm_s = np.transpose(cat("nssm_s"), (1, 0, 2, 3, 4, 5)).reshape(DEPTH, 32, 4, 128, 128)
    return tuple(np.ascontiguousarray(a, dtype=np.float32) for a in
                 (y_p, y_s, nk_p, nv_p, nki_p, nconv_p, nssm_p, nk_s, nv_s, nki_s, nconv_s, nssm_s))
```

```python
import math
from contextlib import ExitStack
import numpy as np
import concourse.bass as bass
import concourse.mybir as mybir
from concourse.bass_utils import run_bass_kernel_spmd

F32 = mybir.dt.float32
BF16 = mybir.dt.bfloat16
AF = mybir.ActivationFunctionType
ALU = mybir.AluOpType
AX = mybir.AxisListType

D = 1024
T = 4096
TS = 16
NSQ = 4
NTOK = T + NSQ * TS
DEPTH = 2
DIN = 5456
DFF = 2816
O_QA, O_KA, O_VA, O_QI, O_KI, O_WI, O_QKV, O_AB, O_BB, O_GB, O_GATES = 0, 512, 640, 768, 1280, 1344, 1352, 2888, 2892, 2896, 3408
ALPHA = (2 * DEPTH) ** 0.25
INDEX_SCALE = (8 * 64) ** -0.5
LN_EPS = 1e-5
NORM_EPS = 1e-6
NBIS = 16
NEG = -30000.0
KCOLS = 33 * 128


class Tok:
    __slots__ = ("sem", "val", "key")

    def __init__(self, sem, val, key):
        self.sem = sem
        self.val = val
        self.key = key


class Eng:
    def __init__(self, nc, name, h):
        self.name = name
        self.key = name
        self.h = h
        self.sem = nc.alloc_semaphore("e_" + name)
        self.cnt = 0
        self.waited = {}

    def wait(self, tok):
        if tok is None:
            return
        if self.waited.get(tok.key, 0) >= tok.val:
            return
        self.h.wait_ge(tok.sem, tok.val)
        self.waited[tok.key] = tok.val


class TT:
    def __init__(self, t, name, sbuf=True):
        self.t = t
        self.name = name
        self.sbuf = sbuf
        self.w = None
        self.r = {}
        self.dsem = None

    def __getitem__(self, idx):
        return self.t[idx]


class K:
    def __init__(self, nc):
        self.nc = nc
        self.pe = Eng(nc, "pe", nc.tensor)
        self.act = Eng(nc, "act", nc.scalar)
        self.dve = Eng(nc, "dve", nc.vector)
        self.pool = Eng(nc, "pool", nc.gpsimd)
        self.sp = Eng(nc, "sp", nc.sync)
        self.engs = [self.pe, self.act, self.dve, self.pool, self.sp]
        self.dsem_pool = []
        self.ndsem = 0
        self.out_toks = {}
        self.pass_tiles = []
        self.es = None
        self.uid = 0
        self.ps = [TT(nc.alloc_psum_tensor(f"psb{i}", [128, 512], F32), f"psb{i}") for i in range(8)]

    def begin_pass(self):
        self.es = ExitStack()
        self.pass_tiles = []

    def sb(self, name, shape, dt):
        self.uid += 1
        t = self.es.enter_context(self.nc.sbuf_tensor(f"{name}_{self.uid}", list(shape), dt))
        tt = TT(t, name)
        self.pass_tiles.append(tt)
        return tt

    def get_dsem(self):
        if self.dsem_pool:
            return self.dsem_pool.pop()
        self.ndsem += 1
        return [self.nc.alloc_semaphore(f"d{self.ndsem}"), 0]

    def barrier(self, tiles):
        toks = [Tok(e.sem, e.cnt, e.key) for e in self.engs if e.cnt > 0]
        seen = set()
        for tt in tiles:
            for ds in (tt.dsem or {}).values():
                if ds[1] > 0 and id(ds) not in seen:
                    seen.add(id(ds))
                    toks.append(Tok(ds[0], 16 * ds[1], ("d", id(ds))))
        for e in self.engs:
            for tok in toks:
                if tok.key != e.key:
                    e.wait(tok)

    def end_pass(self):
        self.barrier(self.pass_tiles)
        for tt in self.pass_tiles:
            if tt.dsem is not None:
                self.dsem_pool.extend(tt.dsem.values())
                tt.dsem = None
        self.es.close()
        self.es = None
        self.pass_tiles = []

    def op(self, e, fn, outs=(), ins=(), inc=True):
        for t in ins:
            e.wait(t.w)
        strict = e.key != "pe"
        for t in outs:
            if t.w is not None and (strict or t.w.key != e.key):
                e.wait(t.w)
            for kk, tok in t.r.items():
                if strict or kk != e.key:
                    e.wait(tok)
        inst = fn()
        if inc:
            inst.then_inc(e.sem, 1)
            e.cnt += 1
            tok = Tok(e.sem, e.cnt, e.key)
        else:
            tok = Tok(e.sem, e.cnt + 1, e.key)
        for t in outs:
            t.w = tok
            t.r = {}
        for t in ins:
            if t not in outs:
                t.r[e.key] = tok
        return inst

    def dma(self, q, out_ap, in_ap, out_t=None, in_t=None, is_output=False):
        if in_t is not None:
            q.wait(in_t.w)
        if out_t is not None:
            q.wait(out_t.w)
            for kk, tok in out_t.r.items():
                q.wait(tok)
        own = out_t if (out_t is not None and out_t.sbuf) else in_t
        if own.dsem is None:
            own.dsem = {}
        if q.key not in own.dsem:
            own.dsem[q.key] = self.get_dsem()
        ds = own.dsem[q.key]
        inst = q.h.dma_start(out=out_ap, in_=in_ap)
        ds[1] += 1
        inst.then_inc(ds[0], 16)
        tok = Tok(ds[0], 16 * ds[1], ("d", id(ds)))
        if out_t is not None:
            out_t.w = tok
            out_t.r = {}
        if in_t is not None:
            in_t.r[tok.key] = tok
        if is_output:
            self.out_toks[tok.key] = tok

    def finish(self):
        for tok in self.out_toks.values():
            self.sp.wait(tok)
        self.barrier([])

    def mm(self, out_t, out_ap, a_t, a_ap, b_t, b_ap, start=True, stop=True, inc=None):
        nc = self.nc
        ins = [a_t, b_t] if b_t is not a_t else [a_t]
        return self.op(self.pe, lambda: nc.tensor.matmul(out_ap, a_ap, b_ap, start=start, stop=stop),
                       outs=[out_t], ins=ins, inc=(stop if inc is None else inc))

    def tr(self, out_t, out_ap, a_t, a_ap, id_t, id_ap, inc=True):
        nc = self.nc
        return self.op(self.pe, lambda: nc.tensor.transpose(out_ap, a_ap, id_ap), outs=[out_t], ins=[a_t, id_t], inc=inc)

    def actv(self, out_t, out_ap, in_t, in_ap, func, bias=None, scale=None, accum=None, extra_ins=(), eng=None):
        nc = self.nc
        kw = {}
        if bias is not None:
            kw["bias"] = bias
        if scale is not None:
            kw["scale"] = scale
        outs = [out_t]
        if accum is not None:
            kw["accum_out"] = accum[1]
            outs.append(accum[0])
        return self.op(self.act, lambda: nc.scalar.activation(out_ap, in_ap, func, **kw), outs=outs,
                       ins=[in_t] + list(extra_ins))

    def ts(self, e, out_t, out_ap, in_t, in_ap, s1, s2, op0, op1=None, accum=None, extra_ins=()):
        outs = [out_t]
        kw = {}
        if op1 is not None:
            kw["op1"] = op1
        if accum is not None:
            kw["accum_out"] = accum[1]
            outs.append(accum[0])
        return self.op(e, lambda: e.h.tensor_scalar(out_ap, in_ap, s1, s2, op0, **kw), outs=outs,
                       ins=[in_t] + list(extra_ins))

    def tt(self, e, out_t, out_ap, a_t, a_ap, b_t, b_ap, op):
        ins = [a_t, b_t] if b_t is not a_t else [a_t]
        return self.op(e, lambda: e.h.tensor_tensor(out_ap, a_ap, b_ap, op), outs=[out_t], ins=ins)

    def stt(self, e, out_t, out_ap, a_t, a_ap, scalar, b_t, b_ap, op0, op1, extra_ins=()):
        ins = [a_t] + ([b_t] if b_t is not a_t else []) + list(extra_ins)
        return self.op(e, lambda: e.h.scalar_tensor_tensor(out_ap, a_ap, scalar, b_ap, op0, op1), outs=[out_t], ins=ins)

    def cp(self, e, out_t, out_ap, in_t, in_ap):
        if e is self.act:
            nc = self.nc
            return self.op(e, lambda: nc.scalar.copy(out_ap, in_ap), outs=[out_t], ins=[in_t])
        return self.op(e, lambda: e.h.tensor_copy(out_ap, in_ap), outs=[out_t], ins=[in_t])

    def memset(self, e, out_t, out_ap, val):
        return self.op(e, lambda: e.h.memset(out_ap, val), outs=[out_t], ins=[])


def make_consts():
    c = {}
    c["ident"] = np.eye(128, dtype=np.float32)
    c["ones"] = np.ones((128, 128), np.float32)
    i2 = np.zeros((128, 256), np.float32)
    i2[:, 0:128] = np.eye(128)
    i2[:, 128:256] = np.eye(128)
    c["i2"] = i2
    c["i4"] = np.tile(np.eye(128, dtype=np.float32), (1, 4))
    c["i4s"] = np.tile(np.eye(16, dtype=np.float32), (1, 4))
    i2s = np.zeros((16, 32), np.float32)
    i2s[:, 0:16] = np.eye(16)
    i2s[:, 16:32] = np.eye(16)
    c["i2s"] = i2s
    tt_, cc_ = np.meshgrid(np.arange(64), np.arange(64), indexing="ij")
    c["U"] = (tt_ <= cc_).astype(np.float32)
    c["Urev"] = (tt_ > cc_).astype(np.float32)
    c["maskT"] = np.where(cc_ >= tt_, 0.0, NEG).astype(np.float32)
    c["su01"] = (cc_ > tt_).astype(np.float32)
    c["caus01"] = (cc_ >= tt_).astype(np.float32)
    c["negones"] = -np.ones((128, 128), np.float32)
    c["pow2"] = np.tile((2.0 ** -(np.arange(32) + 1.0)).astype(np.float32)[None, :], (128, 1))
    return c


class Prog:
    def __init__(self, debug=False, stop_after=None):
        self.debug = debug
        self.stop_after = stop_after
        nc = bass.Bass("TRN2", target_bir_lowering=False)
        self.nc = nc
        self.k = K(nc)

        def din(name, shape):
            return nc.dram_tensor(name, list(shape), F32, kind="ExternalInput").ap()

        def dout(name, shape):
            return nc.dram_tensor(name, list(shape), F32, kind="ExternalOutput").ap()

        self.x_p = din("x_p", [T, D])
        self.x_s = din("x_s", [NSQ * TS, D])
        self.cache_k = din("cache_k", [DEPTH, NSQ, T, 128])
        self.cache_v = din("cache_v", [DEPTH, NSQ, T, 128])
        self.cache_ki = din("cache_ki", [DEPTH, NSQ, T, 64])
        self.state_conv = din("state_conv", [DEPTH, NSQ, 3, 1536])
        self.state_ssm = din("state_ssm", [DEPTH, NSQ, 4, 128, 128])
        self.w_in = din("w_in", [DEPTH, D, DIN])
        self.b_gate = din("b_gate", [DEPTH, 2048])
        self.conv_w = din("conv_w", [DEPTH, 4, 1536])
        self.a_log = din("a_log", [DEPTH, 4])
        self.dt_bias = din("dt_bias", [DEPTH, 4])
        self.gdn_norm_w = din("gdn_norm_w", [DEPTH, 128])
        self.w_proj_a = din("w_proj_a", [DEPTH, 512, D])
        self.w_proj_b = din("w_proj_b", [DEPTH, 512, D])
        self.w_out = din("w_out", [DEPTH, D, D])
        self.ln1_g = din("ln1_g", [DEPTH, D])
        self.ln1_b = din("ln1_b", [DEPTH, D])
        self.w_up = din("w_up", [DEPTH, D, 2 * DFF])
        self.w_down = din("w_down", [DEPTH, DFF, D])
        self.ln2_g = din("ln2_g", [DEPTH, D])
        self.ln2_b = din("ln2_b", [DEPTH, D])
        self.cst = {n: din("c_" + n, list(v.shape)) for n, v in make_consts().items()}

        self.y_p = dout("y_p", [T, D])
        self.y_s = dout("y_s", [NSQ * TS, D])
        self.nk_p = dout("nk_p", [DEPTH, T, 128])
        self.nv_p = dout("nv_p", [DEPTH, T, 128])
        self.nki_p = dout("nki_p", [DEPTH, T, 64])
        self.nconv_p = dout("nconv_p", [DEPTH, 3, 1536])
        self.nssm_p = dout("nssm_p", [DEPTH, 4, 128, 128])
        self.nk_s = dout("nk_s", [DEPTH, NSQ * TS, 128])
        self.nv_s = dout("nv_s", [DEPTH, NSQ * TS, 128])
        self.nki_s = dout("nki_s", [DEPTH, NSQ * TS, 64])
        self.nconv_s = dout("nconv_s", [DEPTH, NSQ, 3, 1536])
        self.nssm_s = dout("nssm_s", [DEPTH, NSQ, 4, 128, 128])

        skind = "ExternalOutput" if debug else "Internal"
        self.YT = TT(nc.dram_tensor("scr_yt", [D, NTOK], BF16, kind=skind).ap(), "YT", sbuf=False)
        self.X1 = TT(nc.dram_tensor("scr_x1", [NTOK, D], F32, kind=skind).ap(), "X1", sbuf=False)
        self.X2 = TT(nc.dram_tensor("scr_x2", [NTOK, D], F32, kind=skind).ap(), "X2", sbuf=False)
        self.build()

    def x_rows(self, l, tok0, n):
        if l == 0:
            if tok0 < T:
                return None, self.x_p[tok0:tok0 + n, :]
            return None, self.x_s[tok0 - T:tok0 - T + n, :]
        return self.X2, self.X2[tok0:tok0 + n, :]

    def load_consts(self):
        k = self.k
        nc = self.nc
        es = ExitStack()
        self.ces = es
        self.ctiles = []

        def csb(name, shape, dt):
            t = es.enter_context(nc.sbuf_tensor("k_" + name, list(shape), dt))
            tt = TT(t, name)
            self.ctiles.append(tt)
            return tt

        self.ident = csb("ident", [128, 128], F32)
        k.dma(k.sp, self.ident[:, :], self.cst["ident"][:, :], out_t=self.ident)
        self.ones = csb("ones", [128, 128], F32)
        k.dma(k.sp, self.ones[:, :], self.cst["ones"][:, :], out_t=self.ones)
        self.i2 = csb("i2", [128, 256], BF16)
        k.dma(k.pool, self.i2[:, :], self.cst["i2"][:, :], out_t=self.i2)
        self.i2s = csb("i2s", [16, 32], BF16)
        k.dma(k.pool, self.i2s[:, :], self.cst["i2s"][:, :], out_t=self.i2s)
        self.i4 = csb("i4", [128, 512], BF16)
        k.dma(k.pool, self.i4[:, :], self.cst["i4"][:, :], out_t=self.i4)
        self.i4s = csb("i4s", [16, 64], BF16)
        k.dma(k.pool, self.i4s[:, :], self.cst["i4s"][:, :], out_t=self.i4s)
        self.pow2 = csb("pow2", [128, 32], F32)
        k.dma(k.sp, self.pow2[:, :], self.cst["pow2"][:, :], out_t=self.pow2)
        for nm in ["U", "Urev", "maskT", "su01", "caus01"]:
            t = csb(nm, [64, 64], F32)
            k.dma(k.sp, t[:, :], self.cst[nm][:, :], out_t=t)
            setattr(self, "c_" + nm, t)
        self.negones = csb("negones", [128, 128], F32)
        k.dma(k.sp, self.negones[:, :], self.cst["negones"][:, :], out_t=self.negones)

    def load_xT(self, l, tok0, ntok, xin, xT, col0, evac):
        k = self.k
        src_t, src = self.x_rows(l, tok0, ntok)
        k.dma(k.sp, xin[0:ntok, :], src, out_t=xin, in_t=src_t)
        for grp in range(2):
            ps = k.ps[6 + grp]
            for kk in range(4):
                kc = grp * 4 + kk
                k.tr(ps, ps[:, kk * ntok:(kk + 1) * ntok], xin, xin[0:ntok, kc * 128:(kc + 1) * 128],
                     self.ident, self.ident[0:ntok, 0:ntok], inc=(kk == 3))
            k.cp(evac, xT, xT[:, grp * 4:(grp + 1) * 4, col0:col0 + ntok],
                 ps, ps[:, 0:4 * ntok].rearrange("p (a t) -> p a t", a=4))

    def p1a(self, l):
        k = self.k
        nc = self.nc
        k.begin_pass()
        wsrc = self.w_in[l].rearrange("(kc p) e -> p kc e", p=128)
        Wfm = k.sb("Wfm", [128, 8, 1408], BF16)
        Wtm = k.sb("Wtm", [128, 8, 328], BF16)

        def wl(dst_t, d0, s0, n):
            k.dma(k.pool, dst_t[:, :, d0:d0 + n], wsrc[:, :, s0:s0 + n], out_t=dst_t)

        wl(Wfm, 0, O_QA, 512)
        wl(Wfm, 512, O_KA, 64)
        wl(Wfm, 576, O_KA, 64)
        wl(Wfm, 640, O_KA + 64, 64)
        wl(Wfm, 704, O_KA + 64, 64)
        wl(Wfm, 768, O_QI, 512)
        wl(Wfm, 1280, O_KI, 64)
        wl(Wfm, 1344, O_KI, 64)
        wl(Wtm, 0, O_KA, 256)
        wl(Wtm, 256, O_KI, 64)
        wl(Wtm, 320, O_WI, 8)

        kT2g = [k.sb(f"kT2g{g}", [128, KCOLS], BF16) for g in range(2)]
        kiT2 = k.sb("kiT2", [128, KCOLS], BF16)
        vext = k.sb("vext", [128, 33, 2, 65], BF16)
        k.memset(k.pool, vext, vext[:, :, :, 64:65], 1.0)
        xin = [k.sb(f"xin{i}", [128, D], F32) for i in range(2)]
        xT = k.sb("xT", [128, 8, 512], BF16)
        qaLo = [k.sb(f"qaLo{i}", [128, 4, 512], BF16) for i in range(2)]
        qaHi = [k.sb(f"qaHi{i}", [128, 4, 512], BF16) for i in range(2)]
        qiLo = [k.sb(f"qiLo{i}", [128, 4, 512], BF16) for i in range(2)]
        qiHi = [k.sb(f"qiHi{i}", [128, 4, 512], BF16) for i in range(2)]
        for i in range(2):
            k.memset(k.pool, qaLo[i], qaLo[i][64:128, :, :], 0.0)
            k.memset(k.pool, qiLo[i], qiLo[i][64:128, :, :], 0.0)
            k.memset(k.pool, qaHi[i], qaHi[i][0:64, :, :], 0.0)
            k.memset(k.pool, qiHi[i], qiHi[i][0:64, :, :], 0.0)
        kfm = k.sb("kfm", [128, 3, 64], BF16)
        tm1 = [k.sb(f"tm1_{i}", [128, 328], F32) for i in range(2)]
        wscs = [k.sb(f"wsc{i}", [128, 4, 8], F32) for i in range(2)]
        Ss = [k.sb(f"S{i}", [128, KCOLS], F32) for i in range(2)]
        junk = k.sb("junk", [128, KCOLS], BF16)
        MBs = [k.sb(f"MB{i}", [128, KCOLS], BF16) for i in range(2)]
        R = [k.sb(f"R{i}", [128, 512], F32) for i in range(2)]
        PT = [k.sb(f"PT{i}", [128, 512], BF16) for i in range(3)]
        rd = k.sb("rd", [128, 1024], F32)
        bcs = k.sb("bcs", [64, 1024], F32)
        yTt = k.sb("yTt", [64, 1024], BF16)
        stt_ = k.sb("stat", [128, 8], F32)
        dtab = k.sb("dtab", [128, NBIS], F32)
        ndtab = k.sb("ndtab", [128, NBIS], F32)
        trial = k.sb("trial", [128, NBIS + 1], F32)
        lo_t = k.sb("lo_t", [128, NBIS + 1], F32)
        cnt = k.sb("cnt", [128, NBIS], F32)
        dd = k.sb("dd", [128, NBIS], F32)
        ktm = k.sb("ktm", [128, 16, 128], F32)

        def stage_units(tok0, NS, Tq, nsub, key_col0, key_tile0, is_sample, par):
            wsc = wscs[par]
            units = []
            done = 0
            i = 0
            while done < NS:
                n = min(128, NS - done)
                units.append(lambda d=done, n=n, i=i: self.load_xT(l, tok0 + d, n, xin[i % 2], xT, d, k.act))
                done += n
                i += 1

            def fm(mt):
                ps = k.ps[6 + (mt % 2)]
                for kc in range(8):
                    k.mm(ps, ps[:, 0:NS], Wfm, Wfm[:, kc, mt * 128:(mt + 1) * 128], xT, xT[:, kc, 0:NS],
                         start=(kc == 0), stop=(kc == 7))
                if mt < 4:
                    k.cp(k.act, qaLo[par], qaLo[par][0:64, mt, 0:NS], ps, ps[0:64, 0:NS])
                    k.cp(k.act, qaHi[par], qaHi[par][64:128, mt, 0:NS], ps, ps[64:128, 0:NS])
                elif mt < 6:
                    g = mt - 4
                    if is_sample:
                        k.cp(k.act, kfm, kfm[:, g, 0:NS], ps, ps[:, 0:NS])
                    else:
                        k.cp(k.act, kT2g[g], kT2g[g][:, key_col0:key_col0 + NS], ps, ps[:, 0:NS])
                elif mt < 10:
                    k.cp(k.act, qiLo[par], qiLo[par][0:64, mt - 6, 0:NS], ps, ps[0:64, 0:NS])
                    k.cp(k.act, qiHi[par], qiHi[par][64:128, mt - 6, 0:NS], ps, ps[64:128, 0:NS])
                else:
                    if is_sample:
                        k.cp(k.act, kfm, kfm[:, 2, 0:NS], ps, ps[:, 0:NS])
                    else:
                        k.cp(k.act, kiT2, kiT2[:, key_col0:key_col0 + NS], ps, ps[:, 0:NS])

            for mt in (6, 7, 8, 9, 10, 4, 5):
                units.append(lambda mt=mt: fm(mt))

            def tmj(j):
                ps = k.ps[6 + (j % 2)]
                t1 = tm1[j % 2]
                for kc in range(8):
                    k.mm(ps, ps[0:Tq, 0:328], xT, xT[:, kc, j * Tq:(j + 1) * Tq], Wtm, Wtm[:, kc, 0:328],
                         start=(kc == 0), stop=(kc == 7))
                k.cp(k.act, t1, t1[0:Tq, :], ps, ps[0:Tq, 0:328])
                r0 = tok0 + j * Tq
                if is_sample:
                    r0 -= T
                    dk, dv, dki = self.nk_s, self.nv_s, self.nki_s
                else:
                    dk, dv, dki = self.nk_p, self.nv_p, self.nki_p
                k.dma(k.pool, dk[l, r0:r0 + Tq, :], t1[0:Tq, 0:128], in_t=t1, is_output=True)
                k.dma(k.pool, dv[l, r0:r0 + Tq, :], t1[0:Tq, 128:256], in_t=t1, is_output=True)
                k.dma(k.pool, dki[l, r0:r0 + Tq, :], t1[0:Tq, 256:320], in_t=t1, is_output=True)
                if not is_sample:
                    kt = key_tile0 + j
                    k.cp(k.pool, vext, vext[0:Tq, kt, :, 0:64], t1, t1[0:Tq, 128:256].rearrange("p (g d) -> p g d", g=2))
                k.ts(k.pool, wsc, wsc[0:Tq, j, :], t1, t1[0:Tq, 320:328], INDEX_SCALE, None, ALU.mult)

            for j in range(nsub):
                units.append(lambda j=j: tmj(j))
            for mt in range(4):
                units.append(lambda mt=mt: fm(mt))
            return units

        def stage_a(*args):
            for u in stage_units(*args):
                u()

        class TD:
            pass

        def idx_units(td, pending=None, quota=0):
            P = slice(0, td.Tq)
            qc = slice(td.j * td.Tq, (td.j + 1) * td.Tq)
            S, wsc, n = td.S, td.wsc, td.n
            nblk = (n + 511) // 512
            U = []

            def blk(kb):
                c0 = kb * 512
                w = min(512, n - c0)
                for h in range(8):
                    m, half = divmod(h, 2)
                    ps = k.ps[half]
                    qi = (qiLo if half == 0 else qiHi)[td.par]
                    k.mm(ps, ps[P, 0:w], qi, qi[:, m, qc], kiT2, kiT2[:, c0:c0 + w])
                    Rt = R[h % 2]
                    k.actv(Rt, Rt[P, 0:w], ps, ps[P, 0:w], AF.Relu)
                    if h == 0:
                        k.ts(k.dve, S, S[P, c0:c0 + w], Rt, Rt[P, 0:w], wsc[P, td.j, 0:1], None, ALU.mult, extra_ins=[wsc])
                    else:
                        k.stt(k.dve, S, S[P, c0:c0 + w], Rt, Rt[P, 0:w], wsc[P, td.j, h:h + 1], S, S[P, c0:c0 + w],
                              ALU.mult, ALU.add, extra_ins=[wsc])
                if kb == nblk - 1 and td.corner:
                    k.memset(k.dve, S, S[0:64, n - 64:n], -1.0e30)

            take = []
            if pending:
                for _ in range(min(quota, len(pending))):
                    take.append(pending.pop(0))
            per = (len(take) + nblk - 1) // nblk if take else 0
            for kb in range(nblk):
                U.append(lambda kb=kb: blk(kb))
                for _ in range(per):
                    if take:
                        U.append(take.pop(0))
            U.extend(take)
            return U

        def bis_units(td):
            P = slice(0, td.Tq)
            S, MB, n = td.S, td.MB, td.n
            st = stt_
            U = []

            def pro():
                k.op(k.dve, lambda: nc.vector.tensor_reduce(st[P, 0:1], S[P, 0:n], AX.X, ALU.max), outs=[st], ins=[S])
                if td.corner:
                    k.op(k.dve, lambda: nc.vector.tensor_reduce(st[P, 1:2], S[P, 0:n - 64], AX.X, ALU.min), outs=[st], ins=[S])
                    k.op(k.dve, lambda: nc.vector.tensor_reduce(st[64:128, 2:3], S[64:128, n - 64:n], AX.X, ALU.min), outs=[st], ins=[S])
                    k.tt(k.dve, st, st[64:128, 1:2], st, st[64:128, 1:2], st, st[64:128, 2:3], ALU.min)
                else:
                    k.op(k.dve, lambda: nc.vector.tensor_reduce(st[P, 1:2], S[P, 0:n], AX.X, ALU.min), outs=[st], ins=[S])
                k.tt(k.dve, st, st[P, 3:4], st, st[P, 0:1], st, st[P, 1:2], ALU.subtract)
                k.ts(k.dve, dtab, dtab[P, :], self.pow2, self.pow2[P, 0:NBIS], st[P, 3:4], None, ALU.mult, extra_ins=[st])
                k.ts(k.dve, ndtab, ndtab[P, :], dtab, dtab[P, :], -1.0, None, ALU.mult)
                k.cp(k.dve, lo_t, lo_t[P, 0:1], st, st[P, 1:2])
                if td.cnt_eng == "act":
                    k.stt(k.dve, trial, trial[P, 0:1], st, st[P, 1:2], -1.0, ndtab, ndtab[P, 0:1], ALU.mult, ALU.add)
                else:
                    k.tt(k.dve, trial, trial[P, 0:1], st, st[P, 1:2], dtab, dtab[P, 0:1], ALU.add)
            U.append(pro)

            def it_(it):
                if td.cnt_eng == "act":
                    k.actv(junk, junk[P, 0:n], S, S[P, 0:n], AF.Sign, bias=trial[P, it:it + 1],
                           accum=(cnt, cnt[P, it:it + 1]), extra_ins=[trial])
                    thr = float(511 - n)
                else:
                    k.ts(k.dve, junk, junk[P, 0:n], S, S[P, 0:n], trial[P, it:it + 1], None, ALU.is_ge, ALU.add,
                         accum=(cnt, cnt[P, it:it + 1]), extra_ins=[trial])
                    thr = 255.5
                k.ts(k.dve, dd, dd[P, it:it + 1], cnt, cnt[P, it:it + 1], thr, dtab[P, it:it + 1], ALU.is_ge, ALU.mult,
                     extra_ins=[dtab])
                k.tt(k.dve, lo_t, lo_t[P, it + 1:it + 2], lo_t, lo_t[P, it:it + 1], dd, dd[P, it:it + 1], ALU.add)
                if it + 1 < NBIS:
                    if td.cnt_eng == "act":
                        k.stt(k.dve, trial, trial[P, it + 1:it + 2], lo_t, lo_t[P, it + 1:it + 2], -1.0, ndtab, ndtab[P, it + 1:it + 2],
                              ALU.mult, ALU.add)
                    else:
                        k.tt(k.dve, trial, trial[P, it + 1:it + 2], lo_t, lo_t[P, it + 1:it + 2], dtab, dtab[P, it + 1:it + 2], ALU.add)
            for it in range(NBIS):
                U.append(lambda it=it: it_(it))

            def epi():
                lo = lo_t[P, NBIS:NBIS + 1]
                k.ts(k.dve, MB, MB[P, 0:n], S, S[P, 0:n], lo, NEG, ALU.is_lt, ALU.mult, extra_ins=[lo_t])
            U.append(epi)
            return U

        def att_units(td):
            Tq = td.Tq
            P = slice(0, Tq)
            qc = slice(td.j * Tq, (td.j + 1) * Tq)
            MB, i4 = td.MB, td.i4
            qlo, qhi = qaLo[td.par], qaHi[td.par]
            W4 = 4 * Tq
            nkt = len(td.keytiles)
            units = [(g, ti) for g in range(2) for ti in range(nkt)]

            def s_part(u):
                g, ti = u
                c0, nk, vt = td.keytiles[ti]
                psS = k.ps[2 + (u[0] * nkt + ti) % 2]
                k.mm(psS, psS[0:nk, 0:W4], MB, MB[P, c0:c0 + nk], i4, i4[P, 0:W4], start=True, stop=False)
                k.mm(psS, psS[0:nk, 0:2 * Tq].rearrange("p (a t) -> p a t", a=2), kT2g[g], kT2g[g][:, c0:c0 + nk],
                     qlo, qlo[:, 2 * g:2 * g + 2, qc], start=False, stop=False)
                k.mm(psS, psS[0:nk, 2 * Tq:W4].rearrange("p (a t) -> p a t", a=2), kT2g[g], kT2g[g][:, c0:c0 + nk],
                     qhi, qhi[:, 2 * g:2 * g + 2, qc], start=False, stop=True)
                PTt = PT[(u[0] * nkt + ti) % 3]
                k.actv(PTt, PTt[0:nk, 0:W4], psS, psS[0:nk, 0:W4], AF.Exp, scale=0.125)

            def v_part(u):
                g, ti = u
                c0, nk, vt = td.keytiles[ti]
                Og = k.ps[4 + g]
                PTt = PT[(u[0] * nkt + ti) % 3]
                k.mm(Og, Og[0:65, 0:W4], vext, vext[0:nk, vt, g, :], PTt, PTt[0:nk, 0:W4],
                     start=(ti == 0), stop=(ti == nkt - 1))

            U = []
            for ui, u in enumerate(units):
                def both(ui=ui, u=u):
                    s_part(u)
                    if ui >= 1:
                        v_part(units[ui - 1])
                U.append(both)
            U.append(lambda: v_part(units[-1]))
            return U

        def run_all(units):
            for u in units:
                u()

        def idx(td):
            run_all(idx_units(td))

        def bis(td):
            run_all(bis_units(td))

        def att_main(td):
            run_all(att_units(td))

        def merge(lists):
            items = []
            for li, L_ in enumerate(lists):
                for j, u in enumerate(L_):
                    items.append(((j + 0.5) / len(L_), li, j, u))
            items.sort(key=lambda x: (x[0], x[1], x[2]))
            return [x[3] for x in items]

        def att_fin(td):
            Tq = td.Tq
            W4 = 4 * Tq
            for g in range(2):
                Og = k.ps[4 + g]
                k.actv(rd, rd[64:65, g * 512:g * 512 + W4], Og, Og[64:65, 0:W4], AF.Ln)
                k.actv(rd, rd[64:65, g * 512:g * 512 + W4], rd, rd[64:65, g * 512:g * 512 + W4], AF.Exp, scale=-1.0)
                psB = k.ps[6 + g]
                k.mm(psB, psB[0:64, 0:W4], self.ones, self.ones[64:65, 0:64], rd, rd[64:65, g * 512:g * 512 + W4])
                k.cp(k.act, bcs, bcs[:, g * 512:g * 512 + W4], psB, psB[0:64, 0:W4])
                k.tt(k.dve, yTt, yTt[:, g * 512:g * 512 + W4], Og, Og[0:64, 0:W4], bcs, bcs[:, g * 512:g * 512 + W4], ALU.mult)
                for b_ in range(2):
                    dst = self.YT[256 * g:256 * g + 256, td.tok_out0:td.tok_out0 + Tq].rearrange("(a b d) t -> b d a t", a=2, b=2, d=64)[b_]
                    k.dma(k.pool, dst, yTt[:, g * 512 + b_ * 2 * Tq:g * 512 + (b_ + 1) * 2 * Tq].rearrange("d (a t) -> d a t", a=2),
                          out_t=self.YT, in_t=yTt)

        tds = []
        for st in range(T // 512):
            for j in range(4):
                td = TD()
                i = st * 4 + j
                td.st, td.j, td.Tq = st, j, 128
                td.n = st * 512 + (j + 1) * 128
                td.corner = True
                td.keytiles = [(t * 128, 128, t) for t in range(td.n // 128)]
                td.tok_out0 = st * 512 + j * 128
                td.i4 = self.i4
                td.S, td.MB = Ss[i % 2], MBs[i % 2]
                td.par, td.wsc = st % 2, wscs[st % 2]
                td.cnt_eng = "act" if (i % 2 == 1) else "dve"
                tds.append(td)
        nt = len(tds)
        stage_a(0, 512, 128, 4, 0, 0, False, 0)
        pending = []
        for i in range(nt + 2):
            lists = []
            if 1 <= i <= nt:
                lists.append(bis_units(tds[i - 1]))
            if 2 <= i:
                lists.append(att_units(tds[i - 2]))
            if i < nt:
                td = tds[i]
                nst = td.st + 1
                if td.j == 0:
                    pending = stage_units(nst * 512, 512, 128, 4, nst * 512, nst * 4, False, nst % 2) if nst < T // 512 else []
                quota = (len(pending) + (3 - td.j)) // (4 - td.j)
                if td.j < 2:
                    quota = min(quota, max(0, len(pending) - 4))
                lists.append(idx_units(td, pending, quota))
            run_all(merge(lists))
            if 2 <= i:
                att_fin(tds[i - 2])

        stage_a(T, NSQ * TS, TS, NSQ, 0, 0, True, 0)
        sds = []
        for s in range(NSQ):
            td = TD()
            td.st, td.j, td.Tq = 0, s, TS
            td.n = T + TS
            td.corner = False
            td.keytiles = [(t * 128, 128, t) for t in range(32)] + [(T, TS, 32)]
            td.tok_out0 = T + s * TS
            td.i4 = self.i4s
            td.S, td.MB = Ss[s % 2], MBs[s % 2]
            td.par, td.wsc = 0, wscs[0]
            td.cnt_eng = "act"
            sds.append(td)

        def prep_k(s, which):
            if which < 2:
                src = self.cache_k[l, s].rearrange("(t p) c -> p t c", p=128)[:, :, which * 64:(which + 1) * 64]
                dst = kT2g[which]
            else:
                src = self.cache_ki[l, s].rearrange("(t p) c -> p t c", p=128)
                dst = kiT2
            for hf in range(2):
                k.dma(k.sp, ktm[:, :, 0:64], src[:, hf * 16:(hf + 1) * 16, :], out_t=ktm)
                k.dma(k.sp, ktm[:, :, 64:128], src[:, hf * 16:(hf + 1) * 16, :], out_t=ktm)
                for b4 in range(4):
                    blk = hf * 4 + b4
                    ps = k.ps[6 + (blk % 2)]
                    for a in range(4):
                        k.tr(ps, ps[:, a * 128:(a + 1) * 128], ktm, ktm[:, b4 * 4 + a, :], self.ident, self.ident[:, :], inc=(a == 3))
                    k.cp(k.act, dst, dst[:, blk * 512:(blk + 1) * 512], ps, ps[:, :])
            k.cp(k.pool, dst, dst[:, T:T + TS], kfm, kfm[:, which, s * TS:(s + 1) * TS])

        def prep_kv(s):
            prep_k(s, 0)
            prep_k(s, 1)
            for g in range(2):
                k.dma(k.pool, vext[:, 0:32, g, 0:64],
                      self.cache_v[l, s].rearrange("(t p) c -> p t c", p=128)[:, :, g * 64:(g + 1) * 64], out_t=vext)
            ps = k.ps[6 + (s % 2)]
            for kc in range(8):
                k.mm(ps, ps[0:TS, 0:128], xT, xT[:, kc, s * TS:(s + 1) * TS], Wtm, Wtm[:, kc, 128:256],
                     start=(kc == 0), stop=(kc == 7))
            k.cp(k.act, vext, vext[0:TS, 32, :, 0:64], ps, ps[0:TS, 0:128].rearrange("p (g d) -> p g d", g=2))

        prep_k(0, 2)
        idx(sds[0])
        prep_kv(0)
        for s in range(NSQ):
            bis(sds[s])
            if s + 1 < NSQ:
                prep_k(s + 1, 2)
                idx(sds[s + 1])
            att_main(sds[s])
            att_fin(sds[s])
            if s + 1 < NSQ:
                prep_kv(s + 1)
        k.end_pass()


    def rows_to_fm(self, src_ap, nrows, rows_t, dst_t, dst_fn):
        k = self.k
        k.dma(k.sp, rows_t[0:nrows, :], src_ap, out_t=rows_t)
        for grp in range(3):
            ps = k.ps[6 + (grp % 2)]
            for a in range(4):
                m = grp * 4 + a
                k.tr(ps, ps[:, a * nrows:(a + 1) * nrows], rows_t, rows_t[0:nrows, m * 128:(m + 1) * 128],
                     self.ident, self.ident[0:nrows, 0:nrows], inc=(a == 3))
            for a in range(4):
                m = grp * 4 + a
                k.cp(k.act, dst_t, dst_fn(m), ps, ps[:, a * nrows:(a + 1) * nrows])

    def p1b(self, l):
        k = self.k
        nc = self.nc
        k.begin_pass()
        wsrc = self.w_in[l].rearrange("(kc p) e -> p kc e", p=128)
        Wfm = k.sb("Wfm", [128, 8, 1536], BF16)
        Wtm = k.sb("Wtm", [128, 8, 520], BF16)
        k.dma(k.pool, Wfm[:, :, :], wsrc[:, :, O_QKV:O_QKV + 1536], out_t=Wfm)
        k.dma(k.pool, Wtm[:, :, 0:8], wsrc[:, :, O_AB:O_AB + 8], out_t=Wtm)
        k.dma(k.pool, Wtm[:, :, 8:520], wsrc[:, :, O_GB:O_GB + 512], out_t=Wtm)
        rows_t = k.sb("rows", [16, 1536], F32)
        cw = k.sb("cw", [128, 12, 4], F32)
        self.rows_to_fm(self.conv_w[l], 4, rows_t, cw, lambda m: cw[:, m, :])
        nw = k.sb("nw", [128, 128], F32)
        k.dma(k.sp, nw[:, :], self.gdn_norm_w[l:l + 1, :].to_broadcast([128, 128]), out_t=nw)
        dtb = k.sb("dtb", [128, 4], F32)
        k.dma(k.sp, dtb[:, :], self.dt_bias[l:l + 1, :].to_broadcast([128, 4]), out_t=dtb)
        negA = k.sb("negA", [128, 4], F32)
        k.dma(k.sp, negA[:, :], self.a_log[l:l + 1, :].to_broadcast([128, 4]), out_t=negA)
        k.actv(negA, negA[:, :], negA, negA[:, :], AF.Exp)
        k.ts(k.dve, negA, negA[:, :], negA, negA[:, :], -1.0, None, ALU.mult)

        xin = [k.sb("xin0", [128, D], F32)] * 2
        xT = k.sb("xT", [128, 8, 512], BF16)
        cin = k.sb("cin", [128, 12, 515], F32)
        qkvs = k.sb("qkvs", [128, 12, 512], F32)
        qkb = k.sb("qkb", [128, 8, 512], BF16)
        Sbf = k.sb("Sbf", [128, NSQ, 4, 128], BF16)
        acc = [k.sb(f"acc{i}", [128, 512], F32) for i in range(2)]
        sq = [k.sb(f"sq{i}", [128, 512], F32) for i in range(2)]
        rn = [k.sb(f"rn{i}", [128, 512], F32) for i in range(2)]
        s2s = [k.sb(f"s2_{i}", [128, 512], F32) for i in range(2)]
        Sst = k.sb("Sst", [128, NSQ, 4, 128], F32)
        last3 = rows_t
        NM = 16

        def smal(name, shape):
            return k.sb(name, shape, F32)

        NCK = 8
        gx = smal("gx", [64, NCK, 4])
        gab = smal("gab", [64, NCK, 4])
        ge1 = smal("ge1", [64, NCK, 4])
        gl1 = smal("gl1", [64, NCK, 4])
        gsp = smal("gsp", [64, NCK, 4])
        gg = smal("gg", [64, NCK, 4])
        nbeta = smal("nbeta", [64, NCK, 4])
        beta = smal("beta", [64, NCK, 4])
        E = smal("E", [64, NCK, 12])
        negeG = smal("negeG", [64, NCK, 4])
        gt128 = smal("gt128", [128, NCK, 4])
        sgA = smal("sgA", [64, NCK, 512])
        sgtmp = smal("sgtmp", [64, 512])

        class MV:
            def __init__(self, name, dt=F32):
                self.tt = k.sb(name, [64, 512], dt)
                self.C = 64
                self.nm = 8

            def set(self, C, nm):
                self.C = C
                self.nm = nm

            def v(self):
                return self.tt[0:self.C, 0:self.nm * self.C].rearrange("p (m c) -> p m c", m=self.nm)

        gU_, DT_, DsB_, W0f_, DTc_, dtmp_ = MV("gU"), MV("DT"), MV("DsB"), MV("W0f"), MV("DTc"), MV("dtmp")
        W_ = [MV(f"W{i}", BF16) for i in range(2)]
        X_ = [MV(f"X{i}", BF16) for i in range(2)]
        NT_ = [MV(f"NT{i}", BF16) for i in range(2)]
        NTf_ = [MV(f"NTf{i}", BF16) for i in range(2)]
        qkT2_ = [MV(f"qkT{i}", BF16) for i in range(2)]
        allmv = [gU_, DT_, DsB_, W0f_, DTc_, dtmp_] + W_ + X_ + NT_
        negG = smal("negG", [64, NCK, 4])
        kdec2 = [k.sb(f"kdec{i}", [64, 2, 4, 128], BF16) for i in range(2)]
        vtm2 = [smal(f"vtm{i}", [64, 2, 4, 128]) for i in range(2)]
        Z = k.sb("Z", [64, 4, 128], BF16)
        Zf = smal("Zf", [64, 4, 128])
        t1 = smal("t1", [64, 4, 128])
        vnew = k.sb("vnew", [64, 4, 128], BF16)
        o_t = smal("o", [64, 4, 128])
        osq = smal("osq", [64, 128])
        ss = smal("ss", [64, 4])
        rstd = smal("rstd", [64, 4])
        yb = smal("yb", [64, 4, 128])
        ybT = k.sb("ybT", [128, 4, 64], BF16)

        def bc(ap, shape, axis):
            return ap.unsqueeze(axis).to_broadcast(shape)

        def process(tok0, NS, nseq, L, C, is_sample, first, last_st):
            cinv = cin[:, :, 0:nseq * (L + 3)].rearrange("p m (s t) -> p m s t", s=nseq)
            qv = qkvs[:, :, 0:NS].rearrange("p m (s t) -> p m s t", s=nseq)
            qb = qkb[:, :, 0:NS].rearrange("p m (s t) -> p m s t", s=nseq)
            done = 0
            i = 0
            while done < NS:
                n = min(128, NS - done)
                self.load_xT(l, tok0 + done, n, xin[i % 2], xT, done, k.act)
                done += n
                i += 1
            if is_sample:
                self.rows_to_fm(self.state_conv[l].rearrange("s j c -> (s j) c"), 12, rows_t, cin,
                                lambda m: cinv[:, m, :, 0:3])
            elif first:
                k.memset(k.pool, cin, cinv[:, :, :, 0:3], 0.0)
            for mt in range(12):
                ps = k.ps[6 + (mt % 2)]
                for kc in range(8):
                    k.mm(ps, ps[:, 0:NS], Wfm, Wfm[:, kc, mt * 128:(mt + 1) * 128], xT, xT[:, kc, 0:NS],
                         start=(kc == 0), stop=(kc == 7))
                k.cp(k.act, cin, cinv[:, mt, :, 3:3 + L], ps, ps[:, 0:NS].rearrange("p (s t) -> p s t", s=nseq))
            if is_sample or last_st:
                for sq_ in range(nseq):
                    c1 = (sq_ + 1) * L
                    for blk in range(3):
                        ps = k.ps[6 + (blk % 2)]
                        for kc in range(8):
                            k.mm(ps, ps[0:3, 0:512], xT, xT[:, kc, c1 - 3:c1], Wfm, Wfm[:, kc, blk * 512:(blk + 1) * 512],
                                 start=(kc == 0), stop=(kc == 7))
                        k.cp(k.act, last3, last3[0:3, blk * 512:(blk + 1) * 512], ps, ps[0:3, 0:512])
                    dst = self.nconv_s[l, sq_] if is_sample else self.nconv_p[l]
                    k.dma(k.sp, dst, last3[0:3, :], in_t=last3, is_output=True)
            for m in range(12):
                a_ = acc[m % 2]
                av = a_[:, 0:NS].rearrange("p (s t) -> p s t", s=nseq)
                k.ts(k.dve, a_, av, cin, cinv[:, m, :, 0:L], cw[:, m, 0:1], None, ALU.mult, extra_ins=[cw])
                for jj in range(1, 4):
                    k.stt(k.dve, a_, av, cin, cinv[:, m, :, jj:jj + L], cw[:, m, jj:jj + 1], a_, av, ALU.mult, ALU.add,
                          extra_ins=[cw])
                s2 = s2s[m % 2]
                k.actv(s2, s2[:, 0:NS], a_, a_[:, 0:NS], AF.Exp, scale=-1.0)
                k.actv(s2, s2[:, 0:NS], s2, s2[:, 0:NS], AF.Ln, bias=1.0)
                k.actv(s2, s2[:, 0:NS], s2, s2[:, 0:NS], AF.Exp, scale=-1.0)
                k.tt(k.pool, qkvs, qkvs[:, m, 0:NS], a_, a_[:, 0:NS], s2, s2[:, 0:NS], ALU.mult)

            def n_sq(m):
                s_ = sq[m % 2]
                k.actv(s_, s_[:, 0:NS], qkvs, qkvs[:, m, 0:NS], AF.Square)
                ps = k.ps[6 + (m % 2)]
                k.mm(ps, ps[:, 0:NS], self.ones, self.ones[:, :], s_, s_[:, 0:NS])

            def n_fin(m):
                ps = k.ps[6 + (m % 2)]
                r_ = rn[m % 2]
                k.actv(r_, r_[:, 0:NS], ps, ps[:, 0:NS], AF.Ln, bias=NORM_EPS)
                k.actv(r_, r_[:, 0:NS], r_, r_[:, 0:NS], AF.Exp, scale=-0.5)
                k.stt(k.dve, qkvs, qkvs[:, m, 0:NS], qkvs, qkvs[:, m, 0:NS], (128.0 ** -0.5) if m < 4 else 1.0,
                      r_, r_[:, 0:NS], ALU.mult, ALU.mult)
                k.cp(k.act if m % 2 else k.pool, qkb, qkb[:, m, 0:NS], qkvs, qkvs[:, m, 0:NS])

            n_sq(0)
            for m in range(8):
                if m + 1 < 8:
                    n_sq(m + 1)
                n_fin(m)
            if not is_sample:
                k.cp(k.pool, cin, cinv[:, :, :, 0:3], cin, cinv[:, :, :, L:L + 3])

            nch = L // C
            if is_sample:
                batches = [[(0, 0), (1, 0)], [(2, 0), (3, 0)]]
            else:
                batches = [[(0, ci), (0, ci + 1)] for ci in range(0, nch, 2)]
            PC = slice(0, C)
            nb = 2
            nm = 8
            for mv in allmv:
                mv.set(C, nm)
            gU, DT, DsB, W0f, DTc, dtmp = gU_.tt, DT_.tt, DsB_.tt, W0f_.tt, DTc_.tt, dtmp_.tt
            gUv, DTv, DsBv, W0fv, DTcv, dtmpv = gU_.v(), DT_.v(), DsB_.v(), W0f_.v(), DTc_.v(), dtmp_.v()
            W = [w.tt for w in W_]
            Wv = [w.v() for w in W_]
            X = [x.tt for x in X_]
            Xv = [x.v() for x in X_]
            NT = [x.tt for x in NT_]
            NTv = [x.v() for x in NT_]
            mvv = lambda ps: ps[PC, 0:nm * C].rearrange("p (m c) -> p m c", m=nm)

            chunks = [c for bt in batches for c in bt]
            nck = len(chunks)

            def prep_super():
                for ci_, (sq_, ci) in enumerate(chunks):
                    col0 = sq_ * L + ci * C
                    psA = k.ps[0]
                    for kc in range(8):
                        k.mm(psA, psA[PC, 0:8], xT, xT[:, kc, col0:col0 + C], Wtm, Wtm[:, kc, 0:8], start=(kc == 0), stop=(kc == 7))
                    k.tt(k.dve, gx, gx[PC, ci_, :], psA, psA[PC, 0:4], dtb, dtb[PC, :], ALU.add)
                    k.actv(ge1, ge1[PC, ci_, :], psA, psA[PC, 4:8], AF.Exp, scale=-1.0)
                    psB = k.ps[1 + (ci_ % 2)]
                    for kc in range(8):
                        k.mm(psB, psB[PC, 0:512], xT, xT[:, kc, col0:col0 + C], Wtm, Wtm[:, kc, 8:520], start=(kc == 0), stop=(kc == 7))
                    k.actv(sgtmp, sgtmp[PC, :], psB, psB[PC, 0:512], AF.Exp, scale=-1.0)
                    k.actv(sgtmp, sgtmp[PC, :], sgtmp, sgtmp[PC, :], AF.Ln, bias=1.0)
                    k.actv(sgtmp, sgtmp[PC, :], sgtmp, sgtmp[PC, :], AF.Exp, scale=-1.0)
                    k.tt(k.dve, sgA, sgA[PC, ci_, :], psB, psB[PC, 0:512], sgtmp, sgtmp[PC, :], ALU.mult)
                    k.tt(k.pool, sgA, sgA[PC, ci_, :].rearrange("p (h e) -> p h e", h=4), sgA,
                         sgA[PC, ci_, :].rearrange("p (h e) -> p h e", h=4), nw, bc(nw[PC, :], [C, 4, 128], 1), ALU.mult)
                A_ = slice(0, nck)
                k.actv(ge1, ge1[PC, A_, :], ge1, ge1[PC, A_, :], AF.Ln, bias=1.0)
                k.actv(beta, beta[PC, A_, :], ge1, ge1[PC, A_, :], AF.Exp, scale=-1.0)
                gxa = gx[PC, A_, :]
                k.ts(k.dve, gab, gab[PC, A_, :], gx, gxa, -1.0, None, ALU.mult)
                k.tt(k.dve, gab, gab[PC, A_, :], gab, gab[PC, A_, :], gx, gxa, ALU.min)
                k.actv(gl1, gl1[PC, A_, :], gab, gab[PC, A_, :], AF.Exp)
                k.actv(gl1, gl1[PC, A_, :], gl1, gl1[PC, A_, :], AF.Ln, bias=1.0)
                k.stt(k.dve, gsp, gsp[PC, A_, :], gx, gxa, 0.0, gl1, gl1[PC, A_, :], ALU.max, ALU.add)
                k.tt(k.dve, gg, gg[PC, A_, :], gsp, gsp[PC, A_, :], negA, bc(negA[PC, :], [C, nck, 4], 1), ALU.mult)
                k.ts(k.dve, nbeta, nbeta[PC, A_, :], beta, beta[PC, A_, :], -1.0, None, ALU.mult)
                psG = k.ps[0]
                for ci_ in range(nck):
                    gcol = gg[PC, ci_, :]
                    k.mm(psG, psG[PC, ci_ * 16:ci_ * 16 + 4], self.c_U, self.c_U[PC, PC], gg, gcol)
                    k.mm(psG, psG[PC, ci_ * 16 + 4:ci_ * 16 + 8], self.c_Urev, self.c_Urev[PC, PC], gg, gcol)
                    k.mm(psG, psG[PC, ci_ * 16 + 8:ci_ * 16 + 12], self.ones, self.ones[PC, PC], gg, gcol)
                    k.mm(psG, psG[:, 256 + ci_ * 4:256 + ci_ * 4 + 4], self.ones, self.ones[PC, :], gg, gcol)
                pG = psG[PC, 0:nck * 16].rearrange("p (b e) -> p b e", b=nck)
                k.actv(E, E[PC, A_, :], psG, pG[:, :, 0:12], AF.Exp)
                k.actv(gt128, gt128[:, A_, :], psG, psG[:, 256:256 + nck * 4].rearrange("p (b e) -> p b e", b=nck), AF.Exp)
                k.ts(k.dve, negeG, negeG[PC, A_, :], E, E[PC, A_, 0:4], -1.0, None, ALU.mult)
                k.ts(k.dve, negG, negG[PC, A_, :], psG, pG[:, :, 0:4], -1.0, None, ALU.mult)

            prep_super()

            def prep_units(batch, par, b0):
                B_ = slice(b0, b0 + nb)
                kdecp, vtmp = kdec2[par], vtm2[par]
                NTfp, qkTp = NTf_[par], qkT2_[par]
                NTfp.set(C, nm)
                qkTp.set(C, nm)
                U = []

                def u_gu():
                    k.tt(k.dve, gU, gUv, self.c_U, bc(self.c_U[PC, PC], [C, nm, C], 1),
                         gg, bc(gg[PC, B_, :].rearrange("p b h -> p (b h)"), [C, nm, C], 2), ALU.mult)
                U.append(u_gu)

                def u_diff():
                    psD = k.ps[1]
                    for mi in range(nm):
                        k.mm(psD, psD[PC, mi * C:(mi + 1) * C], self.ones, self.ones[PC, PC], gU, gUv[:, mi, :])
                    k.tt(k.dve, dtmp, dtmpv, psD, mvv(psD), negG,
                         bc(negG[PC, B_, :].rearrange("p b h -> p (b h)"), [C, nm, C], 2), ALU.add)
                    k.ts(k.dve, dtmp, dtmpv, dtmp, dtmpv, 0.0, None, ALU.min)
                    k.actv(DT, DTv, dtmp, dtmpv, AF.Exp)
                    k.tt(k.pool, DTc, DTcv, DT, DTv, self.c_caus01, bc(self.c_caus01[PC, PC], [C, nm, C], 1), ALU.mult)
                    k.tt(k.pool, DsB, DsBv, DT, DTv, self.c_su01, bc(self.c_su01[PC, PC], [C, nm, C], 1), ALU.mult)
                    k.tt(k.pool, DsB, DsBv, DsB, DsBv, nbeta,
                         bc(nbeta[PC, B_, :].rearrange("p b h -> p (b h)"), [C, nm, C], 2), ALU.mult)
                U.append(u_diff)

                def u_kk():
                    psK = k.ps[2]
                    psQ = k.ps[3]
                    for bi, (sq_, ci) in enumerate(batch):
                        cs = slice(ci * C, (ci + 1) * C)
                        for h in range(4):
                            mi = bi * 4 + h
                            k.mm(psK, psK[PC, mi * C:(mi + 1) * C], qkb, qb[:, 4 + h, sq_, cs], qkb, qb[:, 4 + h, sq_, cs])
                            k.mm(psQ, psQ[PC, mi * C:(mi + 1) * C], qkb, qb[:, 4 + h, sq_, cs], qkb, qb[:, h, sq_, cs])
                    k.tt(k.dve, W0f, W0fv, psK, mvv(psK), DsB, DsBv, ALU.mult)
                    k.cp(k.pool, W[0], Wv[0], W0f, W0fv)
                    k.tt(k.dve, qkTp.tt, qkTp.v(), psQ, mvv(psQ), DTc, DTcv, ALU.mult)
                U.append(u_kk)

                def u_x0():
                    psX = k.ps[2]
                    for mi in range(nm):
                        k.tr(psX, psX[PC, mi * C:(mi + 1) * C], W0f, W0fv[:, mi, :], self.ident, self.ident[PC, PC], inc=(mi == nm - 1))
                    k.cp(k.act, X[0], Xv[0], psX, mvv(psX))
                    k.tt(k.dve, NT[0], NTv[0], W0f, W0fv, self.ident, bc(self.ident[PC, PC], [C, nm, C], 1), ALU.add)
                U.append(u_x0)

                nlev = int(round(math.log2(C)))
                for lev in range(1, nlev):
                    cur = (lev - 1) % 2
                    nxt = 1 - cur
                    lastlev = (lev == nlev - 1)

                    def u_x(cur=cur, nxt=nxt):
                        psX2 = k.ps[2]
                        for mi in range(nm):
                            k.mm(psX2, psX2[PC, mi * C:(mi + 1) * C], W[cur], Wv[cur][:, mi, :], X[cur], Xv[cur][:, mi, :])
                        k.cp(k.act, X[nxt], Xv[nxt], psX2, mvv(psX2))
                    U.append(u_x)
                    if not lastlev:
                        def u_w(cur=cur, nxt=nxt):
                            psW = k.ps[1]
                            for mi in range(nm):
                                k.mm(psW, psW[PC, mi * C:(mi + 1) * C], X[cur], Xv[cur][:, mi, :], W[cur], Wv[cur][:, mi, :])
                            k.cp(k.act, W[nxt], Wv[nxt], psW, mvv(psW))
                        U.append(u_w)

                    def u_p(cur=cur, nxt=nxt, lastlev=lastlev):
                        psP = k.ps[3]
                        for mi in range(nm):
                            k.mm(psP, psP[PC, mi * C:(mi + 1) * C], X[nxt], Xv[nxt][:, mi, :], NT[cur], NTv[cur][:, mi, :])
                        if lastlev:
                            k.tt(k.dve, NTfp.tt, NTfp.v(), psP, mvv(psP), NT[cur], NTv[cur], ALU.add)
                        else:
                            k.tt(k.dve, NT[nxt], NTv[nxt], psP, mvv(psP), NT[cur], NTv[cur], ALU.add)
                    U.append(u_p)

                def u_kv(bi):
                    sq_, ci = batch[bi]
                    cs = slice(ci * C, (ci + 1) * C)
                    psk = k.ps[0]
                    for h in range(4):
                        k.tr(psk, psk[PC, h * 128:(h + 1) * 128], qkvs, qv[:, 4 + h, sq_, cs], self.ident, self.ident[:, :], inc=(h == 3))
                    k.tt(k.dve, kdecp, kdecp[PC, bi, :, :], psk, psk[PC, :].rearrange("p (h e) -> p h e", h=4),
                         E, bc(E[PC, b0 + bi, 4:8], [C, 4, 128], 2), ALU.mult)
                    psv = k.ps[0]
                    for h in range(4):
                        k.tr(psv, psv[PC, h * 128:(h + 1) * 128], qkvs, qv[:, 8 + h, sq_, cs], self.ident, self.ident[:, :], inc=(h == 3))
                    k.cp(k.act, vtmp, vtmp[PC, bi, :, :], psv, psv[PC, :].rearrange("p (h e) -> p h e", h=4))
                for bi in range(nb):
                    U.append(lambda bi=bi: u_kv(bi))
                return U

            def rec_units(batch, par, b0):
                kdecp, vtmp = kdec2[par], vtm2[par]
                NTfv, qkTv = NTf_[par].v(), qkT2_[par].v()
                NTf, qkT = NTf_[par].tt, qkT2_[par].tt
                U = []
                for bi, (sq_, ci) in enumerate(batch):
                    cs = slice(ci * C, (ci + 1) * C)
                    Sv = Sst[:, sq_, :, :]
                    pskS, psqS, psNZ, psqkv, psdS = k.ps[4], k.ps[5], k.ps[6], k.ps[7], k.ps[6]

                    def r1(bi=bi, sq_=sq_, cs=cs, Sv=Sv):
                        for h in range(4):
                            k.mm(pskS, pskS[PC, h * 128:(h + 1) * 128], qkb, qb[:, 4 + h, sq_, cs], Sbf, Sbf[:, sq_, h, :], inc=(h == 3))
                        for h in range(4):
                            k.mm(psqS, psqS[PC, h * 128:(h + 1) * 128], qkb, qb[:, h, sq_, cs], Sbf, Sbf[:, sq_, h, :], inc=(h == 3))
                        k.tt(k.dve, Zf, Zf[PC, :, :], pskS, pskS[PC, :].rearrange("p (h e) -> p h e", h=4),
                             negeG, bc(negeG[PC, b0 + bi, :], [C, 4, 128], 2), ALU.mult)
                        k.tt(k.dve, Z, Z[PC, :, :], Zf, Zf[PC, :, :], vtmp, vtmp[PC, bi, :, :], ALU.add)
                        k.tt(k.dve, t1, t1[PC, :, :], psqS, psqS[PC, :].rearrange("p (h e) -> p h e", h=4),
                             E, bc(E[PC, b0 + bi, 0:4], [C, 4, 128], 2), ALU.mult)
                    U.append(r1)

                    def r2(bi=bi):
                        for h in range(4):
                            k.mm(psNZ, psNZ[PC, h * 128:(h + 1) * 128], NTf, NTfv[:, bi * 4 + h, :], Z, Z[PC, h, :], inc=(h == 3))
                        k.tt(k.dve, vnew, vnew[PC, :, :], psNZ, psNZ[PC, :].rearrange("p (h e) -> p h e", h=4),
                             beta, bc(beta[PC, b0 + bi, :], [C, 4, 128], 2), ALU.mult)
                    U.append(r2)

                    def r3(bi=bi, Sv=Sv, sq_=sq_):
                        for h in range(4):
                            k.mm(psdS, psdS[:, h * 128:(h + 1) * 128], kdecp, kdecp[PC, bi, h, :], vnew, vnew[PC, h, :], inc=(h == 3))
                        for h in range(4):
                            k.mm(psqkv, psqkv[PC, h * 128:(h + 1) * 128], qkT, qkTv[:, bi * 4 + h, :], vnew, vnew[PC, h, :], inc=(h == 3))
                        k.tt(k.dve, Sst, Sv, Sst, Sv, gt128, bc(gt128[:, b0 + bi, :], [128, 4, 128], 2), ALU.mult)
                        k.tt(k.dve, Sst, Sv, Sst, Sv, psdS, psdS[:, :].rearrange("p (h e) -> p h e", h=4), ALU.add)
                        k.cp(k.pool, Sbf, Sbf[:, sq_, :, :], Sst, Sv)
                        k.tt(k.dve, o_t, o_t[PC, :, :], psqkv, psqkv[PC, :].rearrange("p (h e) -> p h e", h=4), t1, t1[PC, :, :], ALU.add)
                    U.append(r3)

                    def r4(bi=bi, sq_=sq_, ci=ci):
                        for h in range(4):
                            k.actv(osq, osq[PC, :], o_t, o_t[PC, h, :], AF.Square, accum=(ss, ss[PC, h:h + 1]))
                        k.actv(rstd, rstd[PC, :], ss, ss[PC, :], AF.Ln, bias=NORM_EPS, scale=1.0 / 128.0)
                        k.actv(rstd, rstd[PC, :], rstd, rstd[PC, :], AF.Exp, scale=-0.5)
                        k.tt(k.pool, yb, yb[PC, :, :], o_t, o_t[PC, :, :], rstd, bc(rstd[PC, :], [C, 4, 128], 2), ALU.mult)
                        k.tt(k.pool, yb, yb[PC, :, :], yb, yb[PC, :, :], sgA, sgA[PC, b0 + bi, :].rearrange("p (h e) -> p h e", h=4), ALU.mult)
                        psT = k.ps[4]
                        for h in range(4):
                            k.tr(psT, psT[:, h * C:(h + 1) * C], yb, yb[PC, h, :], self.ident, self.ident[PC, PC], inc=(h == 3))
                        k.cp(k.act, ybT, ybT[:, :, 0:C], psT, psT[:, 0:4 * C].rearrange("p (h c) -> p h c", h=4))
                        tk = tok0 + sq_ * L + ci * C
                        k.dma(k.pool, self.YT[512:1024, tk:tk + C].rearrange("(m p) t -> p m t", p=128), ybT[:, :, 0:C],
                              out_t=self.YT, in_t=ybT)
                    U.append(r4)
                return U

            for u in prep_units(batches[0], 0, 0):
                u()
            for b_ in range(len(batches)):
                R = rec_units(batches[b_], b_ % 2, b_ * nb)
                N = prep_units(batches[b_ + 1], (b_ + 1) % 2, (b_ + 1) * nb) if b_ + 1 < len(batches) else []
                per = (len(N) + len(R) - 1) // len(R) if N else 0
                for r in R:
                    r()
                    for _ in range(per):
                        if N:
                            N.pop(0)()
                while N:
                    N.pop(0)()

        k.memset(k.pool, Sst, Sst[:, 0, :, :], 0.0)
        k.memset(k.pool, Sbf, Sbf[:, 0, :, :], 0.0)
        nst = T // 512
        for st in range(nst):
            process(st * 512, 512, 1, 512, 64, False, st == 0, st == nst - 1)
        k.dma(k.sp, self.nssm_p[l].rearrange("h a b -> a h b"), Sst[:, 0, :, :], in_t=Sst, is_output=True)
        for s in range(NSQ):
            k.dma(k.sp, Sst[:, s, :, :], self.state_ssm[l, s].rearrange("h a b -> a h b"), out_t=Sst)
            k.cp(k.pool, Sbf, Sbf[:, s, :, :], Sst, Sst[:, s, :, :])
        process(T, NSQ * TS, NSQ, TS, TS, True, True, True)
        for s in range(NSQ):
            k.dma(k.sp, self.nssm_s[l, s].rearrange("h a b -> a h b"), Sst[:, s, :, :], in_t=Sst, is_output=True)
        k.end_pass()


    def layer_norm(self, r, n, gbc, bbc, junk, stat, out_t):
        k = self.k
        nc = self.nc
        P = slice(0, n)
        k.actv(junk, junk[P, :], r, r[P, :], AF.Identity, accum=(stat, stat[P, 0:1]))
        k.actv(junk, junk[P, :], r, r[P, :], AF.Square, accum=(stat, stat[P, 1:2]))
        k.ts(k.dve, stat, stat[P, 2:3], stat, stat[P, 0:1], -1.0 / D, None, ALU.mult)
        k.tt(k.dve, stat, stat[P, 3:4], stat, stat[P, 2:3], stat, stat[P, 2:3], ALU.mult)
        k.stt(k.dve, stat, stat[P, 4:5], stat, stat[P, 1:2], 1.0 / D, stat, stat[P, 3:4], ALU.mult, ALU.subtract)
        k.actv(stat, stat[P, 5:6], stat, stat[P, 4:5], AF.Sqrt, bias=LN_EPS)
        k.op(k.dve, lambda: nc.vector.reciprocal(stat[P, 6:7], stat[P, 5:6]), outs=[stat], ins=[stat])
        k.ts(k.dve, r, r[P, :], r, r[P, :], stat[P, 2:3], stat[P, 6:7], ALU.add, ALU.mult, extra_ins=[stat])
        k.tt(k.pool, r, r[P, :], r, r[P, :], gbc, gbc[P, :], ALU.mult)
        k.tt(k.pool, out_t, out_t[P, :], r, r[P, :], bbc, bbc[P, :], ALU.add)

    def p2(self, l):
        k = self.k
        nc = self.nc
        k.begin_pass()
        wsrc = self.w_in[l].rearrange("(kc p) e -> p kc e", p=128)
        Wg = k.sb("Wg", [128, 8, 2048], BF16)
        k.dma(k.pool, Wg[:, :, :], wsrc[:, :, O_GATES:O_GATES + 2048], out_t=Wg)
        Wpa = k.sb("Wpa", [128, 4, D], BF16)
        k.dma(k.pool, Wpa[:, :, :], self.w_proj_a[l].rearrange("(kc p) e -> p kc e", p=128), out_t=Wpa)
        Wpb = k.sb("Wpb", [128, 4, D], BF16)
        k.dma(k.pool, Wpb[:, :, :], self.w_proj_b[l].rearrange("(kc p) e -> p kc e", p=128), out_t=Wpb)
        Wo = k.sb("Wo", [128, 8, D], BF16)
        k.dma(k.pool, Wo[:, :, :], self.w_out[l].rearrange("(kc p) e -> p kc e", p=128), out_t=Wo)
        bg = k.sb("bg", [128, 2048], F32)
        k.dma(k.sp, bg[:, :], self.b_gate[l:l + 1, :].to_broadcast([128, 2048]), out_t=bg)
        gbc = k.sb("gbc", [128, D], F32)
        k.dma(k.sp, gbc[:, :], self.ln1_g[l:l + 1, :].to_broadcast([128, D]), out_t=gbc)
        bbc = k.sb("bbc", [128, D], F32)
        k.dma(k.sp, bbc[:, :], self.ln1_b[l:l + 1, :].to_broadcast([128, D]), out_t=bbc)
        xin = [k.sb(f"xin{i}", [128, D], F32) for i in range(2)]
        xT = [k.sb(f"xT{i}", [128, 8, 128], BF16) for i in range(2)]
        yT = [k.sb(f"yT{i}", [128, 8, 128], BF16) for i in range(2)]
        sgates = [k.sb(f"sgate{i}", [128, 2048], F32) for i in range(2)]
        mixeds = [k.sb(f"mixed{i}", [128, D], F32) for i in range(2)]
        tmps = [k.sb(f"tmp{i}", [128, 512], F32) for i in range(2)]
        mixT = k.sb("mixT", [128, 8, 128], BF16)
        r = [k.sb(f"r{i}", [128, D], F32) for i in range(2)]
        junk = k.sb("junk", [128, D], BF16)
        stat = k.sb("stat", [128, 8], F32)
        tiles = [(t * 128, 128) for t in range(T // 128)] + [(T, NSQ * TS)]

        def stage_x(ti):
            tok0, n = tiles[ti]
            P = slice(0, n)
            xi, xt, yt = xin[ti % 2], xT[ti % 2], yT[ti % 2]
            sgate, mixed = sgates[ti % 2], mixeds[ti % 2]
            self.load_xT(l, tok0, n, xi, xt, 0, k.act)
            k.dma(k.sp, yt[:, :, 0:n], self.YT[:, tok0:tok0 + n].rearrange("(kc p) t -> p kc t", p=128), out_t=yt, in_t=self.YT)
            for blk in range(4):
                ps = k.ps[blk % 2]
                for kc in range(8):
                    k.mm(ps, ps[P, :], xt, xt[:, kc, 0:n], Wg, Wg[:, kc, blk * 512:(blk + 1) * 512], start=(kc == 0), stop=(kc == 7))
                k.tt(k.dve, sgate, sgate[P, blk * 512:(blk + 1) * 512], ps, ps[P, :], bg, bg[P, blk * 512:(blk + 1) * 512], ALU.add)
                k.actv(sgate, sgate[P, blk * 512:(blk + 1) * 512], sgate, sgate[P, blk * 512:(blk + 1) * 512], AF.Sigmoid)
            for blk in range(2):
                psa = k.ps[4 + blk]
                psb = k.ps[6 + blk]
                tmp = tmps[blk]
                for kc in range(4):
                    k.mm(psa, psa[P, :], yt, yt[:, kc, 0:n], Wpa, Wpa[:, kc, blk * 512:(blk + 1) * 512], start=(kc == 0), stop=(kc == 3))
                for kc in range(4):
                    k.mm(psb, psb[P, :], yt, yt[:, 4 + kc, 0:n], Wpb, Wpb[:, kc, blk * 512:(blk + 1) * 512], start=(kc == 0), stop=(kc == 3))
                cs = slice(blk * 512, (blk + 1) * 512)
                k.tt(k.dve, mixed, mixed[P, cs], psa, psa[P, :], sgate, sgate[P, cs], ALU.mult)
                k.tt(k.dve, tmp, tmp[P, :], psb, psb[P, :], sgate, sgate[P, 1024 + blk * 512:1024 + (blk + 1) * 512], ALU.mult)
                k.tt(k.pool, mixed, mixed[P, cs], mixed, mixed[P, cs], tmp, tmp[P, :], ALU.add)

        def stage_y(ti):
            tok0, n = tiles[ti]
            P = slice(0, n)
            xi, mixed, rr = xin[ti % 2], mixeds[ti % 2], r[ti % 2]
            for grp in range(2):
                ps = k.ps[2 + grp]
                for kk in range(4):
                    kc = grp * 4 + kk
                    k.tr(ps, ps[:, kk * n:(kk + 1) * n], mixed, mixed[P, kc * 128:(kc + 1) * 128], self.ident, self.ident[P, P], inc=(kk == 3))
                k.cp(k.act, mixT, mixT[:, grp * 4:(grp + 1) * 4, 0:n], ps, ps[:, 0:4 * n].rearrange("p (a t) -> p a t", a=4))
            for blk in range(2):
                ps = k.ps[2 + blk]
                for kc in range(8):
                    k.mm(ps, ps[P, :], mixT, mixT[:, kc, 0:n], Wo, Wo[:, kc, blk * 512:(blk + 1) * 512], start=(kc == 0), stop=(kc == 7))
                cs = slice(blk * 512, (blk + 1) * 512)
                k.stt(k.dve, rr, rr[P, cs], xi, xi[P, cs], ALPHA, ps, ps[P, :], ALU.mult, ALU.add)
            self.layer_norm(rr, n, gbc, bbc, junk, stat, rr)
            k.dma(k.pool, self.X1[tok0:tok0 + n, :], rr[P, :], out_t=self.X1, in_t=rr)

        stage_x(0)
        for ti in range(len(tiles)):
            if ti + 1 < len(tiles):
                stage_x(ti + 1)
            stage_y(ti)
        k.end_pass()

    def p3(self, l):
        k = self.k
        nc = self.nc
        k.begin_pass()
        Wup = k.sb("Wup", [128, 8, 2 * DFF], BF16)
        usrc = self.w_up[l].rearrange("(kc p) e -> p kc e", p=128)
        for q4 in range(4):
            k.dma(k.pool, Wup[:, :, q4 * 1408:(q4 + 1) * 1408], usrc[:, :, q4 * 1408:(q4 + 1) * 1408], out_t=Wup)
        Wdn = k.sb("Wdn", [128, 22, D], BF16)
        k.dma(k.pool, Wdn[:, :, :], self.w_down[l].rearrange("(kc p) e -> p kc e", p=128), out_t=Wdn)
        gbc = k.sb("gbc", [128, D], F32)
        k.dma(k.sp, gbc[:, :], self.ln2_g[l:l + 1, :].to_broadcast([128, D]), out_t=gbc)
        bbc = k.sb("bbc", [128, D], F32)
        k.dma(k.sp, bbc[:, :], self.ln2_b[l:l + 1, :].to_broadcast([128, D]), out_t=bbc)
        xin = [k.sb(f"xin{i}", [128, D], F32) for i in range(2)]
        xT = k.sb("xT", [128, 8, 512], BF16)
        fT = k.sb("fT", [128, 22, 512], BF16)
        tmp = [k.sb(f"tmp{i}", [128, 512], F32) for i in range(2)]
        rrs = [k.sb(f"r{i}", [128, D], F32) for i in range(2)]
        junk = k.sb("junk", [128, D], BF16)
        stat = k.sb("stat", [128, 8], F32)
        last = (l == DEPTH - 1)
        sts = [(st * 512, 512) for st in range(T // 512)] + [(T, NSQ * TS)]

        def subs_of(NS):
            subs = []
            done = 0
            while done < NS:
                n = min(128, NS - done)
                subs.append((done, n))
                done += n
            return subs

        xcnt = [0]

        def transposes(si):
            tok0, NS = sts[si]
            for (c0, n) in subs_of(NS):
                xi = xin[xcnt[0] % 2]
                xcnt[0] += 1
                k.dma(k.sp, xi[0:n, :], self.X1[tok0 + c0:tok0 + c0 + n, :], out_t=xi, in_t=self.X1)
                for grp in range(2):
                    ps = k.ps[6 + grp]
                    for kk in range(4):
                        kc = grp * 4 + kk
                        k.tr(ps, ps[:, kk * n:(kk + 1) * n], xi, xi[0:n, kc * 128:(kc + 1) * 128], self.ident, self.ident[0:n, 0:n], inc=(kk == 3))
                    k.cp(k.act, xT, xT[:, grp * 4:(grp + 1) * 4, c0:c0 + n], ps, ps[:, 0:4 * n].rearrange("p (a t) -> p a t", a=4))

        def up(si):
            tok0, NS = sts[si]
            for fc in range(22):
                psA = k.ps[(2 * fc) % 4]
                psB = k.ps[(2 * fc + 1) % 4]
                for kc in range(8):
                    k.mm(psA, psA[:, 0:NS], Wup, Wup[:, kc, fc * 128:(fc + 1) * 128], xT, xT[:, kc, 0:NS], start=(kc == 0), stop=(kc == 7))
                for kc in range(8):
                    k.mm(psB, psB[:, 0:NS], Wup, Wup[:, kc, DFF + fc * 128:DFF + (fc + 1) * 128], xT, xT[:, kc, 0:NS], start=(kc == 0), stop=(kc == 7))
                tm = tmp[fc % 2]
                k.actv(tm, tm[:, 0:NS], psA, psA[:, 0:NS], AF.Silu)
                k.tt(k.dve, fT, fT[:, fc, 0:NS], tm, tm[:, 0:NS], psB, psB[:, 0:NS], ALU.mult)

        rcnt = [0]

        def down(si):
            tok0, NS = sts[si]
            for (c0, n) in subs_of(NS):
                P = slice(0, n)
                rr = rrs[rcnt[0] % 2]
                rcnt[0] += 1
                k.dma(k.sp, rr[0:n, :], self.X1[tok0 + c0:tok0 + c0 + n, :], out_t=rr, in_t=self.X1)
                for blk in range(2):
                    ps = k.ps[4 + blk]
                    for fc in range(22):
                        k.mm(ps, ps[P, :], fT, fT[:, fc, c0:c0 + n], Wdn, Wdn[:, fc, blk * 512:(blk + 1) * 512], start=(fc == 0), stop=(fc == 21))
                    cs = slice(blk * 512, (blk + 1) * 512)
                    k.stt(k.dve, rr, rr[P, cs], rr, rr[P, cs], ALPHA, ps, ps[P, :], ALU.mult, ALU.add)
                self.layer_norm(rr, n, gbc, bbc, junk, stat, rr)
                t0 = tok0 + c0
                if not last:
                    k.dma(k.pool, self.X2[t0:t0 + n, :], rr[P, :], out_t=self.X2, in_t=rr)
                elif t0 < T:
                    k.dma(k.pool, self.y_p[t0:t0 + n, :], rr[P, :], in_t=rr, is_output=True)
                else:
                    k.dma(k.pool, self.y_s[t0 - T:t0 - T + n, :], rr[P, :], in_t=rr, is_output=True)

        transposes(0)
        for si in range(len(sts)):
            up(si)
            if si + 1 < len(sts):
                transposes(si + 1)
            down(si)
        k.end_pass()

    def build(self):
        k = self.k
        self.load_consts()
        sa = self.stop_after
        only = sa[0][:-5] if (sa is not None and sa[0].endswith("_only")) else None
        for l in range(DEPTH):
            for name, fn in (("p1a", self.p1a), ("p1b", self.p1b), ("p2", self.p2), ("p3", self.p3)):
                if only is not None and name != only:
                    continue
                fn(l)
                if sa is not None and sa[1] == l and (sa[0] == name or only == name):
                    break
            else:
                continue
            break
        k.finish()


def shard_inputs(inputs, c):
    f = lambda a: np.ascontiguousarray(a, dtype=np.float32)
    sl = slice(NSQ * c, NSQ * (c + 1))
    m = {
        "x_p": f(inputs["x_prompt"][c]),
        "x_s": f(inputs["x_sample"][sl].reshape(NSQ * TS, D)),
        "cache_k": f(inputs["cache_k"][:, sl].reshape(DEPTH, NSQ, T, 128)),
        "cache_v": f(inputs["cache_v"][:, sl].reshape(DEPTH, NSQ, T, 128)),
        "cache_ki": f(inputs["cache_kidx"][:, sl]),
        "state_conv": f(inputs["state_conv"][:, sl]),
        "state_ssm": f(inputs["state_ssm"][:, sl]),
    }
    for n in ["w_in", "b_gate", "conv_w", "a_log", "dt_bias", "gdn_norm_w", "w_proj_a", "w_proj_b", "w_out",
              "ln1_g", "ln1_b", "w_up", "w_down", "ln2_g", "ln2_b"]:
        m[n] = f(inputs[n])
    for n, v in make_consts().items():
        m["c_" + n] = v
    return m


def run(inputs, debug=False, stop_after=None, trace=False):
    prog = Prog(debug=debug, stop_after=stop_after)
    in_maps = [shard_inputs(inputs, c) for c in range(8)]
    res = run_bass_kernel_spmd(prog.nc, in_maps, core_ids=list(range(8)), trace=trace)
    return res


def kernel(**inputs):
    res = run(inputs)
    r = res.results
    cat = lambda n: np.stack([r[c][n] for c in range(8)], axis=0)
    y_p = cat("y_p")
    y_s = cat("y_s").reshape(32, TS, D)
    nk_p = np.transpose(cat("nk_p"), (1, 0, 2, 3)).reshape(DEPTH, 8, T, 2, 64)
    nv_p = np.transpose(cat("nv_p"), (1, 0, 2, 3)).reshape(DEPTH, 8, T, 2, 64)
    nki_p = np.transpose(cat("nki_p"), (1, 0, 2, 3))
    nconv_p = np.transpose(cat("nconv_p"), (1, 0, 2, 3))
    nssm_p = np.transpose(cat("nssm_p"), (1, 0, 2, 3, 4))
    nk_s = np.transpose(cat("nk_s"), (1, 0, 2, 3)).reshape(DEPTH, 32, TS, 2, 64)
    nv_s = np.transpose(cat("nv_s"), (1, 0, 2, 3)).reshape(DEPTH, 32, TS, 2, 64)
    nki_s = np.transpose(cat("nki_s"), (1, 0, 2, 3)).reshape(DEPTH, 32, TS, 64)
    nconv_s = np.transpose(cat("nconv_s"), (1, 0, 2, 3, 4)).reshape(DEPTH, 32, 3, 1536)
    nssm_s = np.transpose(cat("nssm_s"), (1, 0, 2, 3, 4, 5)).reshape(DEPTH, 32, 4, 128, 128)
    return tuple(np.ascontiguousarray(a, dtype=np.float32) for a in
                 (y_p, y_s, nk_p, nv_p, nki_p, nconv_p, nssm_p, nk_s, nv_s, nki_s, nconv_s, nssm_s))
```

```python
import math
from contextlib import ExitStack
import numpy as np
import concourse.bass as bass
import concourse.mybir as mybir
from concourse.bass_utils import run_bass_kernel_spmd

F32 = mybir.dt.float32
BF16 = mybir.dt.bfloat16
AF = mybir.ActivationFunctionType
ALU = mybir.AluOpType
AX = mybir.AxisListType

D = 1024
T = 4096
TS = 16
NSQ = 4
NTOK = T + NSQ * TS
DEPTH = 2
DIN = 5456
DFF = 2816
O_QA, O_KA, O_VA, O_QI, O_KI, O_WI, O_QKV, O_AB, O_BB, O_GB, O_GATES = 0, 512, 640, 768, 1280, 1344, 1352, 2888, 2892, 2896, 3408
ALPHA = (2 * DEPTH) ** 0.25
INDEX_SCALE = (8 * 64) ** -0.5
LN_EPS = 1e-5
NORM_EPS = 1e-6
NBIS = 16
NEG = -30000.0
KCOLS = 33 * 128


class Tok:
    __slots__ = ("sem", "val", "key")

    def __init__(self, sem, val, key):
        self.sem = sem
        self.val = val
        self.key = key


class Eng:
    def __init__(self, nc, name, h):
        self.name = name
        self.key = name
        self.h = h
        self.sem = nc.alloc_semaphore("e_" + name)
        self.cnt = 0
        self.waited = {}

    def wait(self, tok):
        if tok is None:
            return
        if self.waited.get(tok.key, 0) >= tok.val:
            return
        self.h.wait_ge(tok.sem, tok.val)
        self.waited[tok.key] = tok.val


class TT:
    def __init__(self, t, name, sbuf=True):
        self.t = t
        self.name = name
        self.sbuf = sbuf
        self.w = None
        self.r = {}
        self.dsem = None

    def __getitem__(self, idx):
        return self.t[idx]


class K:
    def __init__(self, nc):
        self.nc = nc
        self.pe = Eng(nc, "pe", nc.tensor)
        self.act = Eng(nc, "act", nc.scalar)
        self.dve = Eng(nc, "dve", nc.vector)
        self.pool = Eng(nc, "pool", nc.gpsimd)
        self.sp = Eng(nc, "sp", nc.sync)
        self.engs = [self.pe, self.act, self.dve, self.pool, self.sp]
        self.dsem_pool = []
        self.ndsem = 0
        self.out_toks = {}
        self.pass_tiles = []
        self.es = None
        self.uid = 0
        self.ps = [TT(nc.alloc_psum_tensor(f"psb{i}", [128, 512], F32), f"psb{i}") for i in range(8)]

    def begin_pass(self):
        self.es = ExitStack()
        self.pass_tiles = []

    def sb(self, name, shape, dt):
        self.uid += 1
        t = self.es.enter_context(self.nc.sbuf_tensor(f"{name}_{self.uid}", list(shape), dt))
        tt = TT(t, name)
        self.pass_tiles.append(tt)
        return tt

    def get_dsem(self):
        if self.dsem_pool:
            return self.dsem_pool.pop()
        self.ndsem += 1
        return [self.nc.alloc_semaphore(f"d{self.ndsem}"), 0]

    def barrier(self, tiles):
        toks = [Tok(e.sem, e.cnt, e.key) for e in self.engs if e.cnt > 0]
        seen = set()
        for tt in tiles:
            for ds in (tt.dsem or {}).values():
                if ds[1] > 0 and id(ds) not in seen:
                    seen.add(id(ds))
                    toks.append(Tok(ds[0], 16 * ds[1], ("d", id(ds))))
        for e in self.engs:
            for tok in toks:
                if tok.key != e.key:
                    e.wait(tok)

    def end_pass(self):
        self.barrier(self.pass_tiles)
        for tt in self.pass_tiles:
            if tt.dsem is not None:
                self.dsem_pool.extend(tt.dsem.values())
                tt.dsem = None
        self.es.close()
        self.es = None
        self.pass_tiles = []

    def op(self, e, fn, outs=(), ins=(), inc=True):
        for t in ins:
            e.wait(t.w)
        strict = e.key != "pe"
        for t in outs:
            if t.w is not None and (strict or t.w.key != e.key):
                e.wait(t.w)
            for kk, tok in t.r.items():
                if strict or kk != e.key:
                    e.wait(tok)
        inst = fn()
        if inc:
            inst.then_inc(e.sem, 1)
            e.cnt += 1
            tok = Tok(e.sem, e.cnt, e.key)
        else:
            tok = Tok(e.sem, e.cnt + 1, e.key)
        for t in outs:
            t.w = tok
            t.r = {}
        for t in ins:
            if t not in outs:
                t.r[e.key] = tok
        return inst

    def dma(self, q, out_ap, in_ap, out_t=None, in_t=None, is_output=False):
        if in_t is not None:
            q.wait(in_t.w)
        if out_t is not None:
            q.wait(out_t.w)
            for kk, tok in out_t.r.items():
                q.wait(tok)
        own = out_t if (out_t is not None and out_t.sbuf) else in_t
        if own.dsem is None:
            own.dsem = {}
        if q.key not in own.dsem:
            own.dsem[q.key] = self.get_dsem()
        ds = own.dsem[q.key]
        inst = q.h.dma_start(out=out_ap, in_=in_ap)
        ds[1] += 1
        inst.then_inc(ds[0], 16)
        tok = Tok(ds[0], 16 * ds[1], ("d", id(ds)))
        if out_t is not None:
            out_t.w = tok
            out_t.r = {}
        if in_t is not None:
            in_t.r[tok.key] = tok
        if is_output:
            self.out_toks[tok.key] = tok

    def finish(self):
        for tok in self.out_toks.values():
            self.sp.wait(tok)
        self.barrier([])

    def mm(self, out_t, out_ap, a_t, a_ap, b_t, b_ap, start=True, stop=True, inc=None):
        nc = self.nc
        ins = [a_t, b_t] if b_t is not a_t else [a_t]
        return self.op(self.pe, lambda: nc.tensor.matmul(out_ap, a_ap, b_ap, start=start, stop=stop),
                       outs=[out_t], ins=ins, inc=(stop if inc is None else inc))

    def tr(self, out_t, out_ap, a_t, a_ap, id_t, id_ap, inc=True):
        nc = self.nc
        return self.op(self.pe, lambda: nc.tensor.transpose(out_ap, a_ap, id_ap), outs=[out_t], ins=[a_t, id_t], inc=inc)

    def actv(self, out_t, out_ap, in_t, in_ap, func, bias=None, scale=None, accum=None, extra_ins=(), eng=None):
        nc = self.nc
        kw = {}
        if bias is not None:
            kw["bias"] = bias
        if scale is not None:
            kw["scale"] = scale
        outs = [out_t]
        if accum is not None:
            kw["accum_out"] = accum[1]
            outs.append(accum[0])
        return self.op(self.act, lambda: nc.scalar.activation(out_ap, in_ap, func, **kw), outs=outs,
                       ins=[in_t] + list(extra_ins))

    def ts(self, e, out_t, out_ap, in_t, in_ap, s1, s2, op0, op1=None, accum=None, extra_ins=()):
        outs = [out_t]
        kw = {}
        if op1 is not None:
            kw["op1"] = op1
        if accum is not None:
            kw["accum_out"] = accum[1]
            outs.append(accum[0])
        return self.op(e, lambda: e.h.tensor_scalar(out_ap, in_ap, s1, s2, op0, **kw), outs=outs,
                       ins=[in_t] + list(extra_ins))

    def tt(self, e, out_t, out_ap, a_t, a_ap, b_t, b_ap, op):
        ins = [a_t, b_t] if b_t is not a_t else [a_t]
        return self.op(e, lambda: e.h.tensor_tensor(out_ap, a_ap, b_ap, op), outs=[out_t], ins=ins)

    def stt(self, e, out_t, out_ap, a_t, a_ap, scalar, b_t, b_ap, op0, op1, extra_ins=()):
        ins = [a_t] + ([b_t] if b_t is not a_t else []) + list(extra_ins)
        return self.op(e, lambda: e.h.scalar_tensor_tensor(out_ap, a_ap, scalar, b_ap, op0, op1), outs=[out_t], ins=ins)

    def cp(self, e, out_t, out_ap, in_t, in_ap):
        if e is self.act:
            nc = self.nc
            return self.op(e, lambda: nc.scalar.copy(out_ap, in_ap), outs=[out_t], ins=[in_t])
        return self.op(e, lambda: e.h.tensor_copy(out_ap, in_ap), outs=[out_t], ins=[in_t])

    def memset(self, e, out_t, out_ap, val):
        return self.op(e, lambda: e.h.memset(out_ap, val), outs=[out_t], ins=[])


def make_consts():
    c = {}
    c["ident"] = np.eye(128, dtype=np.float32)
    c["ones"] = np.ones((128, 128), np.float32)
    i2 = np.zeros((128, 256), np.float32)
    i2[:, 0:128] = np.eye(128)
    i2[:, 128:256] = np.eye(128)
    c["i2"] = i2
    c["i4"] = np.tile(np.eye(128, dtype=np.float32), (1, 4))
    c["i4s"] = np.tile(np.eye(16, dtype=np.float32), (1, 4))
    i2s = np.zeros((16, 32), np.float32)
    i2s[:, 0:16] = np.eye(16)
    i2s[:, 16:32] = np.eye(16)
    c["i2s"] = i2s
    tt_, cc_ = np.meshgrid(np.arange(64), np.arange(64), indexing="ij")
    c["U"] = (tt_ <= cc_).astype(np.float32)
    c["Urev"] = (tt_ > cc_).astype(np.float32)
    c["maskT"] = np.where(cc_ >= tt_, 0.0, NEG).astype(np.float32)
    c["su01"] = (cc_ > tt_).astype(np.float32)
    c["caus01"] = (cc_ >= tt_).astype(np.float32)
    c["negones"] = -np.ones((128, 128), np.float32)
    c["pow2"] = np.tile((2.0 ** -(np.arange(32) + 1.0)).astype(np.float32)[None, :], (128, 1))
    return c


class Prog:
    def __init__(self, debug=False, stop_after=None):
        self.debug = debug
        self.stop_after = stop_after
        nc = bass.Bass("TRN2", target_bir_lowering=False)
        self.nc = nc
        self.k = K(nc)

        def din(name, shape):
            return nc.dram_tensor(name, list(shape), F32, kind="ExternalInput").ap()

        def dout(name, shape):
            return nc.dram_tensor(name, list(shape), F32, kind="ExternalOutput").ap()

        self.x_p = din("x_p", [T, D])
        self.x_s = din("x_s", [NSQ * TS, D])
        self.cache_k = din("cache_k", [DEPTH, NSQ, T, 128])
        self.cache_v = din("cache_v", [DEPTH, NSQ, T, 128])
        self.cache_ki = din("cache_ki", [DEPTH, NSQ, T, 64])
        self.state_conv = din("state_conv", [DEPTH, NSQ, 3, 1536])
        self.state_ssm = din("state_ssm", [DEPTH, NSQ, 4, 128, 128])
        self.w_in = din("w_in", [DEPTH, D, DIN])
        self.b_gate = din("b_gate", [DEPTH, 2048])
        self.conv_w = din("conv_w", [DEPTH, 4, 1536])
        self.a_log = din("a_log", [DEPTH, 4])
        self.dt_bias = din("dt_bias", [DEPTH, 4])
        self.gdn_norm_w = din("gdn_norm_w", [DEPTH, 128])
        self.w_proj_a = din("w_proj_a", [DEPTH, 512, D])
        self.w_proj_b = din("w_proj_b", [DEPTH, 512, D])
        self.w_out = din("w_out", [DEPTH, D, D])
        self.ln1_g = din("ln1_g", [DEPTH, D])
        self.ln1_b = din("ln1_b", [DEPTH, D])
        self.w_up = din("w_up", [DEPTH, D, 2 * DFF])
        self.w_down = din("w_down", [DEPTH, DFF, D])
        self.ln2_g = din("ln2_g", [DEPTH, D])
        self.ln2_b = din("ln2_b", [DEPTH, D])
        self.cst = {n: din("c_" + n, list(v.shape)) for n, v in make_consts().items()}

        self.y_p = dout("y_p", [T, D])
        self.y_s = dout("y_s", [NSQ * TS, D])
        self.nk_p = dout("nk_p", [DEPTH, T, 128])
        self.nv_p = dout("nv_p", [DEPTH, T, 128])
        self.nki_p = dout("nki_p", [DEPTH, T, 64])
        self.nconv_p = dout("nconv_p", [DEPTH, 3, 1536])
        self.nssm_p = dout("nssm_p", [DEPTH, 4, 128, 128])
        self.nk_s = dout("nk_s", [DEPTH, NSQ * TS, 128])
        self.nv_s = dout("nv_s", [DEPTH, NSQ * TS, 128])
        self.nki_s = dout("nki_s", [DEPTH, NSQ * TS, 64])
        self.nconv_s = dout("nconv_s", [DEPTH, NSQ, 3, 1536])
        self.nssm_s = dout("nssm_s", [DEPTH, NSQ, 4, 128, 128])

        skind = "ExternalOutput" if debug else "Internal"
        self.YT = TT(nc.dram_tensor("scr_yt", [D, NTOK], BF16, kind=skind).ap(), "YT", sbuf=False)
        self.X1 = TT(nc.dram_tensor("scr_x1", [NTOK, D], F32, kind=skind).ap(), "X1", sbuf=False)
        self.X2 = TT(nc.dram_tensor("scr_x2", [NTOK, D], F32, kind=skind).ap(), "X2", sbuf=False)
        self.build()

    def x_rows(self, l, tok0, n):
        if l == 0:
            if tok0 < T:
                return None, self.x_p[tok0:tok0 + n, :]
            return None, self.x_s[tok0 - T:tok0 - T + n, :]
        return self.X2, self.X2[tok0:tok0 + n, :]

    def load_consts(self):
        k = self.k
        nc = self.nc
        es = ExitStack()
        self.ces = es
        self.ctiles = []

        def csb(name, shape, dt):
            t = es.enter_context(nc.sbuf_tensor("k_" + name, list(shape), dt))
            tt = TT(t, name)
            self.ctiles.append(tt)
            return tt

        self.ident = csb("ident", [128, 128], F32)
        k.dma(k.sp, self.ident[:, :], self.cst["ident"][:, :], out_t=self.ident)
        self.ones = csb("ones", [128, 128], F32)
        k.dma(k.sp, self.ones[:, :], self.cst["ones"][:, :], out_t=self.ones)
        self.i2 = csb("i2", [128, 256], BF16)
        k.dma(k.pool, self.i2[:, :], self.cst["i2"][:, :], out_t=self.i2)
        self.i2s = csb("i2s", [16, 32], BF16)
        k.dma(k.pool, self.i2s[:, :], self.cst["i2s"][:, :], out_t=self.i2s)
        self.i4 = csb("i4", [128, 512], BF16)
        k.dma(k.pool, self.i4[:, :], self.cst["i4"][:, :], out_t=self.i4)
        self.i4s = csb("i4s", [16, 64], BF16)
        k.dma(k.pool, self.i4s[:, :], self.cst["i4s"][:, :], out_t=self.i4s)
        self.pow2 = csb("pow2", [128, 32], F32)
        k.dma(k.sp, self.pow2[:, :], self.cst["pow2"][:, :], out_t=self.pow2)
        for nm in ["U", "Urev", "maskT", "su01", "caus01"]:
            t = csb(nm, [64, 64], F32)
            k.dma(k.sp, t[:, :], self.cst[nm][:, :], out_t=t)
            setattr(self, "c_" + nm, t)
        self.negones = csb("negones", [128, 128], F32)
        k.dma(k.sp, self.negones[:, :], self.cst["negones"][:, :], out_t=self.negones)

    def load_xT(self, l, tok0, ntok, xin, xT, col0, evac):
        k = self.k
        src_t, src = self.x_rows(l, tok0, ntok)
        k.dma(k.sp, xin[0:ntok, :], src, out_t=xin, in_t=src_t)
        for grp in range(2):
            ps = k.ps[6 + grp]
            for kk in range(4):
                kc = grp * 4 + kk
                k.tr(ps, ps[:, kk * ntok:(kk + 1) * ntok], xin, xin[0:ntok, kc * 128:(kc + 1) * 128],
                     self.ident, self.ident[0:ntok, 0:ntok], inc=(kk == 3))
            k.cp(evac, xT, xT[:, grp * 4:(grp + 1) * 4, col0:col0 + ntok],
                 ps, ps[:, 0:4 * ntok].rearrange("p (a t) -> p a t", a=4))

    def p1a(self, l):
        k = self.k
        nc = self.nc
        k.begin_pass()
        wsrc = self.w_in[l].rearrange("(kc p) e -> p kc e", p=128)
        Wfm = k.sb("Wfm", [128, 8, 1280], BF16)
        Wtm = k.sb("Wtm", [128, 8, 328], BF16)

        def wl(dst_t, d0, s0, n):
            k.dma(k.pool, dst_t[:, :, d0:d0 + n], wsrc[:, :, s0:s0 + n], out_t=dst_t)

        for m in range(4):
            wl(Wfm, m * 128, O_QA + m * 64, 64)
            wl(Wfm, m * 128 + 64, O_QA + (4 + m) * 64, 64)
        wl(Wfm, 512, O_KA, 128)
        wl(Wfm, 640, O_QI, 512)
        wl(Wfm, 1152, O_KI, 64)
        wl(Wfm, 1216, O_KI, 64)
        wl(Wtm, 0, O_KA, 256)
        wl(Wtm, 256, O_KI, 64)
        wl(Wtm, 320, O_WI, 8)

        kTb = k.sb("kTb", [128, KCOLS], BF16)
        kiT2 = k.sb("kiT2", [128, KCOLS], BF16)
        vext = k.sb("vext", [128, 33, 2, 65], BF16)
        k.memset(k.pool, vext, vext[:, :, :, 64:65], 1.0)
        xin = [k.sb(f"xin{i}", [128, D], F32) for i in range(2)]
        xT = k.sb("xT", [128, 8, 512], BF16)
        qaLo = [k.sb(f"qaLo{i}", [128, 4, 512], BF16) for i in range(2)]
        qaHi = [k.sb(f"qaHi{i}", [128, 4, 512], BF16) for i in range(2)]
        qiLo = [k.sb(f"qiLo{i}", [128, 4, 512], BF16) for i in range(2)]
        qiHi = [k.sb(f"qiHi{i}", [128, 4, 512], BF16) for i in range(2)]
        for i in range(2):
            k.memset(k.pool, qaLo[i], qaLo[i][64:128, :, :], 0.0)
            k.memset(k.pool, qiLo[i], qiLo[i][64:128, :, :], 0.0)
            k.memset(k.pool, qaHi[i], qaHi[i][0:64, :, :], 0.0)
            k.memset(k.pool, qiHi[i], qiHi[i][0:64, :, :], 0.0)
        kfm = k.sb("kfm", [128, 3, 64], BF16)
        tm1 = [k.sb(f"tm1_{i}", [128, 328], F32) for i in range(2)]
        wscs = [k.sb(f"wsc{i}", [128, 4, 8], F32) for i in range(2)]
        Ss = [k.sb(f"S{i}", [128, KCOLS], F32) for i in range(2)]
        junk = k.sb("junk", [128, KCOLS], BF16)
        MBs = [k.sb(f"MB{i}", [128, KCOLS], BF16) for i in range(2)]
        R = [k.sb(f"R{i}", [128, 512], F32) for i in range(2)]
        PT = [k.sb(f"PT{i}", [128, 512], BF16) for i in range(3)]
        rd = k.sb("rd", [128, 1024], F32)
        bcs = k.sb("bcs", [64, 1024], F32)
        yTt = k.sb("yTt", [64, 1024], BF16)
        stt_ = k.sb("stat", [128, 8], F32)
        dtab = k.sb("dtab", [128, NBIS], F32)
        ndtab = k.sb("ndtab", [128, NBIS], F32)
        trial = k.sb("trial", [128, NBIS + 1], F32)
        lo_t = k.sb("lo_t", [128, NBIS + 1], F32)
        cnt = k.sb("cnt", [128, NBIS], F32)
        dd = k.sb("dd", [128, NBIS], F32)
        ktm = k.sb("ktm", [128, 16, 128], F32)

        def stage_units(tok0, NS, Tq, nsub, key_col0, key_tile0, is_sample, par):
            wsc = wscs[par]
            units = []
            done = 0
            i = 0
            while done < NS:
                n = min(128, NS - done)
                units.append(lambda d=done, n=n, i=i: self.load_xT(l, tok0 + d, n, xin[i % 2], xT, d, k.act))
                done += n
                i += 1

            def fm(mt):
                ps = k.ps[6 + (mt % 2)]
                for kc in range(8):
                    k.mm(ps, ps[:, 0:NS], Wfm, Wfm[:, kc, mt * 128:(mt + 1) * 128], xT, xT[:, kc, 0:NS],
                         start=(kc == 0), stop=(kc == 7))
                if mt < 4:
                    k.cp(k.act, qaLo[par], qaLo[par][0:64, mt, 0:NS], ps, ps[0:64, 0:NS])
                    k.cp(k.act, qaHi[par], qaHi[par][64:128, mt, 0:NS], ps, ps[64:128, 0:NS])
                elif mt == 4:
                    if is_sample:
                        k.cp(k.act, kfm, kfm[:, 0, 0:NS], ps, ps[:, 0:NS])
                    else:
                        k.cp(k.act, kTb, kTb[:, key_col0:key_col0 + NS], ps, ps[:, 0:NS])
                elif mt < 9:
                    k.cp(k.act, qiLo[par], qiLo[par][0:64, mt - 5, 0:NS], ps, ps[0:64, 0:NS])
                    k.cp(k.act, qiHi[par], qiHi[par][64:128, mt - 5, 0:NS], ps, ps[64:128, 0:NS])
                else:
                    if is_sample:
                        k.cp(k.act, kfm, kfm[:, 2, 0:NS], ps, ps[:, 0:NS])
                    else:
                        k.cp(k.act, kiT2, kiT2[:, key_col0:key_col0 + NS], ps, ps[:, 0:NS])

            for mt in (5, 6, 7, 8, 9, 4):
                units.append(lambda mt=mt: fm(mt))

            def tmj(j):
                ps = k.ps[6 + (j % 2)]
                t1 = tm1[j % 2]
                for kc in range(8):
                    k.mm(ps, ps[0:Tq, 0:328], xT, xT[:, kc, j * Tq:(j + 1) * Tq], Wtm, Wtm[:, kc, 0:328],
                         start=(kc == 0), stop=(kc == 7))
                k.cp(k.act, t1, t1[0:Tq, :], ps, ps[0:Tq, 0:328])
                r0 = tok0 + j * Tq
                if is_sample:
                    r0 -= T
                    dk, dv, dki = self.nk_s, self.nv_s, self.nki_s
                else:
                    dk, dv, dki = self.nk_p, self.nv_p, self.nki_p
                k.dma(k.pool, dk[l, r0:r0 + Tq, :], t1[0:Tq, 0:128], in_t=t1, is_output=True)
                k.dma(k.pool, dv[l, r0:r0 + Tq, :], t1[0:Tq, 128:256], in_t=t1, is_output=True)
                k.dma(k.pool, dki[l, r0:r0 + Tq, :], t1[0:Tq, 256:320], in_t=t1, is_output=True)
                if not is_sample:
                    kt = key_tile0 + j
                    k.cp(k.pool, vext, vext[0:Tq, kt, :, 0:64], t1, t1[0:Tq, 128:256].rearrange("p (g d) -> p g d", g=2))
                k.ts(k.pool, wsc, wsc[0:Tq, j, :], t1, t1[0:Tq, 320:328], INDEX_SCALE, None, ALU.mult)

            for j in range(nsub):
                units.append(lambda j=j: tmj(j))
            for mt in range(4):
                units.append(lambda mt=mt: fm(mt))
            return units

        def stage_a(*args):
            for u in stage_units(*args):
                u()

        class TD:
            pass

        def idx_units(td, pending=None, quota=0):
            P = slice(0, td.Tq)
            qc = slice(td.j * td.Tq, (td.j + 1) * td.Tq)
            S, wsc, n = td.S, td.wsc, td.n
            nblk = (n + 511) // 512
            U = []

            def blk(kb):
                c0 = kb * 512
                w = min(512, n - c0)
                for h in range(8):
                    m, half = divmod(h, 2)
                    ps = k.ps[half]
                    qi = (qiLo if half == 0 else qiHi)[td.par]
                    k.mm(ps, ps[P, 0:w], qi, qi[:, m, qc], kiT2, kiT2[:, c0:c0 + w])
                    Rt = R[h % 2]
                    k.actv(Rt, Rt[P, 0:w], ps, ps[P, 0:w], AF.Relu)
                    if h == 0:
                        k.ts(k.dve, S, S[P, c0:c0 + w], Rt, Rt[P, 0:w], wsc[P, td.j, 0:1], None, ALU.mult, extra_ins=[wsc])
                    else:
                        k.stt(k.dve, S, S[P, c0:c0 + w], Rt, Rt[P, 0:w], wsc[P, td.j, h:h + 1], S, S[P, c0:c0 + w],
                              ALU.mult, ALU.add, extra_ins=[wsc])
                if kb == nblk - 1 and td.corner:
                    k.memset(k.dve, S, S[0:64, n - 64:n], -1.0e30)

            take = []
            if pending:
                for _ in range(min(quota, len(pending))):
                    take.append(pending.pop(0))
            per = (len(take) + nblk - 1) // nblk if take else 0
            for kb in range(nblk):
                U.append(lambda kb=kb: blk(kb))
                for _ in range(per):
                    if take:
                        U.append(take.pop(0))
            U.extend(take)
            return U

        def bis_units(td):
            P = slice(0, td.Tq)
            S, MB, n = td.S, td.MB, td.n
            st = stt_
            U = []

            def pro():
                k.op(k.dve, lambda: nc.vector.tensor_reduce(st[P, 0:1], S[P, 0:n], AX.X, ALU.max), outs=[st], ins=[S])
                if td.corner:
                    k.op(k.dve, lambda: nc.vector.tensor_reduce(st[P, 1:2], S[P, 0:n - 64], AX.X, ALU.min), outs=[st], ins=[S])
                    k.op(k.dve, lambda: nc.vector.tensor_reduce(st[64:128, 2:3], S[64:128, n - 64:n], AX.X, ALU.min), outs=[st], ins=[S])
                    k.tt(k.dve, st, st[64:128, 1:2], st, st[64:128, 1:2], st, st[64:128, 2:3], ALU.min)
                else:
                    k.op(k.dve, lambda: nc.vector.tensor_reduce(st[P, 1:2], S[P, 0:n], AX.X, ALU.min), outs=[st], ins=[S])
                k.tt(k.dve, st, st[P, 3:4], st, st[P, 0:1], st, st[P, 1:2], ALU.subtract)
                k.ts(k.dve, dtab, dtab[P, :], self.pow2, self.pow2[P, 0:NBIS], st[P, 3:4], None, ALU.mult, extra_ins=[st])
                k.ts(k.dve, ndtab, ndtab[P, :], dtab, dtab[P, :], -1.0, None, ALU.mult)
                k.cp(k.dve, lo_t, lo_t[P, 0:1], st, st[P, 1:2])
                if td.cnt_eng == "act":
                    k.stt(k.dve, trial, trial[P, 0:1], st, st[P, 1:2], -1.0, ndtab, ndtab[P, 0:1], ALU.mult, ALU.add)
                else:
                    k.tt(k.dve, trial, trial[P, 0:1], st, st[P, 1:2], dtab, dtab[P, 0:1], ALU.add)
            U.append(pro)

            def it_(it):
                if td.cnt_eng == "act":
                    k.actv(junk, junk[P, 0:n], S, S[P, 0:n], AF.Sign, bias=trial[P, it:it + 1],
                           accum=(cnt, cnt[P, it:it + 1]), extra_ins=[trial])
                    thr = float(511 - n)
                else:
                    k.ts(k.dve, junk, junk[P, 0:n], S, S[P, 0:n], trial[P, it:it + 1], None, ALU.is_ge, ALU.add,
                         accum=(cnt, cnt[P, it:it + 1]), extra_ins=[trial])
                    thr = 255.5
                k.ts(k.dve, dd, dd[P, it:it + 1], cnt, cnt[P, it:it + 1], thr, dtab[P, it:it + 1], ALU.is_ge, ALU.mult,
                     extra_ins=[dtab])
                k.tt(k.dve, lo_t, lo_t[P, it + 1:it + 2], lo_t, lo_t[P, it:it + 1], dd, dd[P, it:it + 1], ALU.add)
                if it + 1 < NBIS:
                    if td.cnt_eng == "act":
                        k.stt(k.dve, trial, trial[P, it + 1:it + 2], lo_t, lo_t[P, it + 1:it + 2], -1.0, ndtab, ndtab[P, it + 1:it + 2],
                              ALU.mult, ALU.add)
                    else:
                        k.tt(k.dve, trial, trial[P, it + 1:it + 2], lo_t, lo_t[P, it + 1:it + 2], dtab, dtab[P, it + 1:it + 2], ALU.add)
            for it in range(NBIS):
                U.append(lambda it=it: it_(it))

            def epi():
                lo = lo_t[P, NBIS:NBIS + 1]
                k.ts(k.dve, MB, MB[P, 0:n], S, S[P, 0:n], lo, NEG, ALU.is_lt, ALU.mult, extra_ins=[lo_t])
            U.append(epi)
            return U

        def att_units(td):
            Tq = td.Tq
            P = slice(0, Tq)
            qc = slice(td.j * Tq, (td.j + 1) * Tq)
            MB, i4 = td.MB, td.i4
            qlo, qhi = qaLo[td.par], qaHi[td.par]
            W4 = 4 * Tq
            nkt = len(td.keytiles)
            units = [(g, ti) for g in range(2) for ti in range(nkt)]

            def s_part(u):
                g, ti = u
                c0, nk, vt = td.keytiles[ti]
                psS = k.ps[2 + (u[0] * nkt + ti) % 2]
                qg = qlo if g == 0 else qhi
                k.mm(psS, psS[0:nk, 0:W4], MB, MB[P, c0:c0 + nk], i4, i4[P, 0:W4], start=True, stop=False)
                k.mm(psS, psS[0:nk, 0:W4].rearrange("p (a t) -> p a t", a=4), kTb, kTb[:, c0:c0 + nk],
                     qg, qg[:, 0:4, qc], start=False, stop=True)
                PTt = PT[(u[0] * nkt + ti) % 3]
                k.actv(PTt, PTt[0:nk, 0:W4], psS, psS[0:nk, 0:W4], AF.Exp, scale=0.125)

            def v_part(u):
                g, ti = u
                c0, nk, vt = td.keytiles[ti]
                Og = k.ps[4 + g]
                PTt = PT[(u[0] * nkt + ti) % 3]
                k.mm(Og, Og[0:65, 0:W4], vext, vext[0:nk, vt, g, :], PTt, PTt[0:nk, 0:W4],
                     start=(ti == 0), stop=(ti == nkt - 1))

            U = []
            for ui, u in enumerate(units):
                def both(ui=ui, u=u):
                    s_part(u)
                    if ui >= 1:
                        v_part(units[ui - 1])
                U.append(both)
            U.append(lambda: v_part(units[-1]))
            return U

        def run_all(units):
            for u in units:
                u()

        def idx(td):
            run_all(idx_units(td))

        def bis(td):
            run_all(bis_units(td))

        def att_main(td):
            run_all(att_units(td))

        def merge(lists):
            items = []
            for li, L_ in enumerate(lists):
                for j, u in enumerate(L_):
                    items.append(((j + 0.5) / len(L_), li, j, u))
            items.sort(key=lambda x: (x[0], x[1], x[2]))
            return [x[3] for x in items]

        def att_fin(td):
            Tq = td.Tq
            W4 = 4 * Tq
            for g in range(2):
                Og = k.ps[4 + g]
                k.actv(rd, rd[64:65, g * 512:g * 512 + W4], Og, Og[64:65, 0:W4], AF.Ln)
                k.actv(rd, rd[64:65, g * 512:g * 512 + W4], rd, rd[64:65, g * 512:g * 512 + W4], AF.Exp, scale=-1.0)
                psB = k.ps[6 + g]
                k.mm(psB, psB[0:64, 0:W4], self.ones, self.ones[64:65, 0:64], rd, rd[64:65, g * 512:g * 512 + W4])
                k.cp(k.act, bcs, bcs[:, g * 512:g * 512 + W4], psB, psB[0:64, 0:W4])
                k.tt(k.dve, yTt, yTt[:, g * 512:g * 512 + W4], Og, Og[0:64, 0:W4], bcs, bcs[:, g * 512:g * 512 + W4], ALU.mult)
                dst = self.YT[256 * g:256 * g + 256, td.tok_out0:td.tok_out0 + Tq].rearrange("(c d) t -> d c t", d=64)
                k.dma(k.pool, dst, yTt[:, g * 512:g * 512 + W4].rearrange("d (c t) -> d c t", c=4), out_t=self.YT, in_t=yTt)

        tds = []
        for st in range(T // 512):
            for j in range(4):
                td = TD()
                i = st * 4 + j
                td.st, td.j, td.Tq = st, j, 128
                td.n = st * 512 + (j + 1) * 128
                td.corner = True
                td.keytiles = [(t * 128, 128, t) for t in range(td.n // 128)]
                td.tok_out0 = st * 512 + j * 128
                td.i4 = self.i4
                td.S, td.MB = Ss[i % 2], MBs[i % 2]
                td.par, td.wsc = st % 2, wscs[st % 2]
                td.cnt_eng = "act" if (i % 2 == 1) else "dve"
                tds.append(td)
        nt = len(tds)
        stage_a(0, 512, 128, 4, 0, 0, False, 0)
        pending = []
        for i in range(nt + 2):
            lists = []
            if 1 <= i <= nt:
                lists.append(bis_units(tds[i - 1]))
            if 2 <= i:
                lists.append(att_units(tds[i - 2]))
            if i < nt:
                td = tds[i]
                nst = td.st + 1
                if td.j == 0:
                    pending = stage_units(nst * 512, 512, 128, 4, nst * 512, nst * 4, False, nst % 2) if nst < T // 512 else []
                quota = (len(pending) + (3 - td.j)) // (4 - td.j)
                if td.j < 2:
                    quota = min(quota, max(0, len(pending) - 4))
                lists.append(idx_units(td, pending, quota))
            run_all(merge(lists))
            if 2 <= i:
                att_fin(tds[i - 2])

        stage_a(T, NSQ * TS, TS, NSQ, 0, 0, True, 0)
        sds = []
        for s in range(NSQ):
            td = TD()
            td.st, td.j, td.Tq = 0, s, TS
            td.n = T + TS
            td.corner = False
            td.keytiles = [(t * 128, 128, t) for t in range(32)] + [(T, TS, 32)]
            td.tok_out0 = T + s * TS
            td.i4 = self.i4s
            td.S, td.MB = Ss[s % 2], MBs[s % 2]
            td.par, td.wsc = 0, wscs[0]
            td.cnt_eng = "act"
            sds.append(td)

        def prep_k_units(s, which):
            if which == 0:
                src = self.cache_k[l, s].rearrange("(t p) c -> p t c", p=128)
                dst = kTb
            else:
                src = self.cache_ki[l, s].rearrange("(t p) c -> p t c", p=128)
                dst = kiT2
            U = []

            def half(hf):
                if which == 0:
                    k.dma(k.sp, ktm[:, :, :], src[:, hf * 16:(hf + 1) * 16, :], out_t=ktm)
                else:
                    k.dma(k.sp, ktm[:, :, 0:64], src[:, hf * 16:(hf + 1) * 16, :], out_t=ktm)
                    k.dma(k.sp, ktm[:, :, 64:128], src[:, hf * 16:(hf + 1) * 16, :], out_t=ktm)
                for b4 in range(4):
                    blk = hf * 4 + b4
                    ps = k.ps[6 + (blk % 2)]
                    for a in range(4):
                        k.tr(ps, ps[:, a * 128:(a + 1) * 128], ktm, ktm[:, b4 * 4 + a, :], self.ident, self.ident[:, :], inc=(a == 3))
                    k.cp(k.act, dst, dst[:, blk * 512:(blk + 1) * 512], ps, ps[:, :])
            U.append(lambda: half(0))
            U.append(lambda: half(1))
            U.append(lambda: k.cp(k.pool, dst, dst[:, T:T + TS], kfm, kfm[:, which, s * TS:(s + 1) * TS]))
            return U

        def prep_kv_units(s):
            U = prep_k_units(s, 0)

            def vv():
                for g in range(2):
                    k.dma(k.pool, vext[:, 0:32, g, 0:64],
                          self.cache_v[l, s].rearrange("(t p) c -> p t c", p=128)[:, :, g * 64:(g + 1) * 64], out_t=vext)
                ps = k.ps[6 + (s % 2)]
                for kc in range(8):
                    k.mm(ps, ps[0:TS, 0:128], xT, xT[:, kc, s * TS:(s + 1) * TS], Wtm, Wtm[:, kc, 128:256],
                         start=(kc == 0), stop=(kc == 7))
                k.cp(k.act, vext, vext[0:TS, 32, :, 0:64], ps, ps[0:TS, 0:128].rearrange("p (g d) -> p g d", g=2))
            U.append(vv)
            return U

        run_all(prep_k_units(0, 2))
        idx(sds[0])
        for s in range(NSQ + 1):
            lists = []
            A = []
            if s >= 1:
                A += att_units(sds[s - 1]) + [lambda s=s: att_fin(sds[s - 1])]
            if s < NSQ:
                A += prep_kv_units(s)
            if A:
                lists.append(A)
            if s < NSQ:
                lists.append(bis_units(sds[s]))
            if s + 1 < NSQ:
                lists.append(prep_k_units(s + 1, 2) + idx_units(sds[s + 1]))
            run_all(merge(lists))
        k.end_pass()


    def rows_to_fm(self, src_ap, nrows, rows_t, dst_t, dst_fn):
        k = self.k
        k.dma(k.sp, rows_t[0:nrows, :], src_ap, out_t=rows_t)
        for grp in range(3):
            ps = k.ps[6 + (grp % 2)]
            for a in range(4):
                m = grp * 4 + a
                k.tr(ps, ps[:, a * nrows:(a + 1) * nrows], rows_t, rows_t[0:nrows, m * 128:(m + 1) * 128],
                     self.ident, self.ident[0:nrows, 0:nrows], inc=(a == 3))
            for a in range(4):
                m = grp * 4 + a
                k.cp(k.act, dst_t, dst_fn(m), ps, ps[:, a * nrows:(a + 1) * nrows])

    def p1b(self, l):
        k = self.k
        nc = self.nc
        k.begin_pass()
        wsrc = self.w_in[l].rearrange("(kc p) e -> p kc e", p=128)
        Wfm = k.sb("Wfm", [128, 8, 1536], BF16)
        Wtm = k.sb("Wtm", [128, 8, 520], BF16)
        k.dma(k.pool, Wfm[:, :, :], wsrc[:, :, O_QKV:O_QKV + 1536], out_t=Wfm)
        k.dma(k.pool, Wtm[:, :, 0:8], wsrc[:, :, O_AB:O_AB + 8], out_t=Wtm)
        k.dma(k.pool, Wtm[:, :, 8:520], wsrc[:, :, O_GB:O_GB + 512], out_t=Wtm)
        rows_t = k.sb("rows", [16, 1536], F32)
        cw = k.sb("cw", [128, 12, 4], F32)
        self.rows_to_fm(self.conv_w[l], 4, rows_t, cw, lambda m: cw[:, m, :])
        nw = k.sb("nw", [128, 128], F32)
        k.dma(k.sp, nw[:, :], self.gdn_norm_w[l:l + 1, :].to_broadcast([128, 128]), out_t=nw)
        dtb = k.sb("dtb", [128, 4], F32)
        k.dma(k.sp, dtb[:, :], self.dt_bias[l:l + 1, :].to_broadcast([128, 4]), out_t=dtb)
        negA = k.sb("negA", [128, 4], F32)
        k.dma(k.sp, negA[:, :], self.a_log[l:l + 1, :].to_broadcast([128, 4]), out_t=negA)
        k.actv(negA, negA[:, :], negA, negA[:, :], AF.Exp)
        k.ts(k.dve, negA, negA[:, :], negA, negA[:, :], -1.0, None, ALU.mult)

        xin = [k.sb("xin0", [128, D], F32)] * 2
        xT = k.sb("xT", [128, 8, 512], BF16)
        cin = k.sb("cin", [128, 12, 515], F32)
        qkvs = k.sb("qkvs", [128, 12, 512], F32)
        qkb = k.sb("qkb", [128, 8, 512], BF16)
        Sbf = k.sb("Sbf", [128, NSQ, 4, 128], BF16)
        acc = [k.sb(f"acc{i}", [128, 512], F32) for i in range(2)]
        sq = [k.sb(f"sq{i}", [128, 512], F32) for i in range(2)]
        rn = [k.sb(f"rn{i}", [128, 512], F32) for i in range(2)]
        s2s = [k.sb(f"s2_{i}", [128, 512], F32) for i in range(2)]
        Sst = k.sb("Sst", [128, NSQ, 4, 128], F32)
        last3 = rows_t
        NM = 16

        def smal(name, shape):
            return k.sb(name, shape, F32)

        NCK = 8
        gx = smal("gx", [64, NCK, 4])
        gab = smal("gab", [64, NCK, 4])
        ge1 = smal("ge1", [64, NCK, 4])
        gl1 = smal("gl1", [64, NCK, 4])
        gsp = smal("gsp", [64, NCK, 4])
        gg = smal("gg", [64, NCK, 4])
        nbeta = smal("nbeta", [64, NCK, 4])
        beta = smal("beta", [64, NCK, 4])
        E = smal("E", [64, NCK, 12])
        negeG = smal("negeG", [64, NCK, 4])
        gt128 = smal("gt128", [128, NCK, 4])
        sgA = smal("sgA", [64, NCK, 512])
        sgtmp = smal("sgtmp", [64, 512])

        class MV:
            def __init__(self, name, dt=F32):
                self.tt = k.sb(name, [64, 512], dt)
                self.C = 64
                self.nm = 8

            def set(self, C, nm):
                self.C = C
                self.nm = nm

            def v(self):
                return self.tt[0:self.C, 0:self.nm * self.C].rearrange("p (m c) -> p m c", m=self.nm)

        gU_, DT_, DsB_, W0f_, DTc_, dtmp_ = MV("gU"), MV("DT"), MV("DsB"), MV("W0f"), MV("DTc"), MV("dtmp")
        W_ = [MV(f"W{i}", BF16) for i in range(2)]
        X_ = [MV(f"X{i}", BF16) for i in range(2)]
        NT_ = [MV(f"NT{i}", BF16) for i in range(2)]
        NTf_ = [MV(f"NTf{i}", BF16) for i in range(2)]
        qkT2_ = [MV(f"qkT{i}", BF16) for i in range(2)]
        allmv = [gU_, DT_, DsB_, W0f_, DTc_, dtmp_] + W_ + X_ + NT_
        negG = smal("negG", [64, NCK, 4])
        kdec2 = [k.sb(f"kdec{i}", [64, 2, 4, 128], BF16) for i in range(2)]
        vtm2 = [smal(f"vtm{i}", [64, 2, 4, 128]) for i in range(2)]
        Z = k.sb("Z", [64, 4, 128], BF16)
        Zf = smal("Zf", [64, 4, 128])
        t1 = smal("t1", [64, 4, 128])
        vnew = k.sb("vnew", [64, 4, 128], BF16)
        o_t = smal("o", [64, 4, 128])
        osq = smal("osq", [64, 128])
        ss = smal("ss", [64, 4])
        rstd = smal("rstd", [64, 4])
        yb = smal("yb", [64, 4, 128])
        ybT = k.sb("ybT", [128, 4, 64], BF16)

        def bc(ap, shape, axis):
            return ap.unsqueeze(axis).to_broadcast(shape)

        def process(tok0, NS, nseq, L, C, is_sample, first, last_st):
            cinv = cin[:, :, 0:nseq * (L + 3)].rearrange("p m (s t) -> p m s t", s=nseq)
            qv = qkvs[:, :, 0:NS].rearrange("p m (s t) -> p m s t", s=nseq)
            qb = qkb[:, :, 0:NS].rearrange("p m (s t) -> p m s t", s=nseq)
            done = 0
            i = 0
            while done < NS:
                n = min(128, NS - done)
                self.load_xT(l, tok0 + done, n, xin[i % 2], xT, done, k.act)
                done += n
                i += 1
            if is_sample:
                self.rows_to_fm(self.state_conv[l].rearrange("s j c -> (s j) c"), 12, rows_t, cin,
                                lambda m: cinv[:, m, :, 0:3])
            elif first:
                k.memset(k.pool, cin, cinv[:, :, :, 0:3], 0.0)
            for mt in range(12):
                ps = k.ps[6 + (mt % 2)]
                for kc in range(8):
                    k.mm(ps, ps[:, 0:NS], Wfm, Wfm[:, kc, mt * 128:(mt + 1) * 128], xT, xT[:, kc, 0:NS],
                         start=(kc == 0), stop=(kc == 7))
                k.cp(k.act, cin, cinv[:, mt, :, 3:3 + L], ps, ps[:, 0:NS].rearrange("p (s t) -> p s t", s=nseq))
            if is_sample or last_st:
                for sq_ in range(nseq):
                    c1 = (sq_ + 1) * L
                    for blk in range(3):
                        ps = k.ps[6 + (blk % 2)]
                        for kc in range(8):
                            k.mm(ps, ps[0:3, 0:512], xT, xT[:, kc, c1 - 3:c1], Wfm, Wfm[:, kc, blk * 512:(blk + 1) * 512],
                                 start=(kc == 0), stop=(kc == 7))
                        k.cp(k.act, last3, last3[0:3, blk * 512:(blk + 1) * 512], ps, ps[0:3, 0:512])
                    dst = self.nconv_s[l, sq_] if is_sample else self.nconv_p[l]
                    k.dma(k.sp, dst, last3[0:3, :], in_t=last3, is_output=True)
            for m in range(12):
                a_ = acc[m % 2]
                av = a_[:, 0:NS].rearrange("p (s t) -> p s t", s=nseq)
                k.ts(k.dve, a_, av, cin, cinv[:, m, :, 0:L], cw[:, m, 0:1], None, ALU.mult, extra_ins=[cw])
                for jj in range(1, 4):
                    k.stt(k.dve, a_, av, cin, cinv[:, m, :, jj:jj + L], cw[:, m, jj:jj + 1], a_, av, ALU.mult, ALU.add,
                          extra_ins=[cw])
                s2 = s2s[m % 2]
                k.actv(s2, s2[:, 0:NS], a_, a_[:, 0:NS], AF.Exp, scale=-1.0)
                k.actv(s2, s2[:, 0:NS], s2, s2[:, 0:NS], AF.Ln, bias=1.0)
                k.actv(s2, s2[:, 0:NS], s2, s2[:, 0:NS], AF.Exp, scale=-1.0)
                k.tt(k.pool, qkvs, qkvs[:, m, 0:NS], a_, a_[:, 0:NS], s2, s2[:, 0:NS], ALU.mult)

            def n_sq(m):
                s_ = sq[m % 2]
                k.actv(s_, s_[:, 0:NS], qkvs, qkvs[:, m, 0:NS], AF.Square)
                ps = k.ps[6 + (m % 2)]
                k.mm(ps, ps[:, 0:NS], self.ones, self.ones[:, :], s_, s_[:, 0:NS])

            def n_fin(m):
                ps = k.ps[6 + (m % 2)]
                r_ = rn[m % 2]
                k.actv(r_, r_[:, 0:NS], ps, ps[:, 0:NS], AF.Ln, bias=NORM_EPS)
                k.actv(r_, r_[:, 0:NS], r_, r_[:, 0:NS], AF.Exp, scale=-0.5)
                k.stt(k.dve, qkvs, qkvs[:, m, 0:NS], qkvs, qkvs[:, m, 0:NS], (128.0 ** -0.5) if m < 4 else 1.0,
                      r_, r_[:, 0:NS], ALU.mult, ALU.mult)
                k.cp(k.act if m % 2 else k.pool, qkb, qkb[:, m, 0:NS], qkvs, qkvs[:, m, 0:NS])

            n_sq(0)
            for m in range(8):
                if m + 1 < 8:
                    n_sq(m + 1)
                n_fin(m)
            if not is_sample:
                k.cp(k.pool, cin, cinv[:, :, :, 0:3], cin, cinv[:, :, :, L:L + 3])

            nch = L // C
            if is_sample:
                batches = [[(0, 0), (1, 0)], [(2, 0), (3, 0)]]
            else:
                batches = [[(0, ci), (0, ci + 1)] for ci in range(0, nch, 2)]
            PC = slice(0, C)
            nb = 2
            nm = 8
            for mv in allmv:
                mv.set(C, nm)
            gU, DT, DsB, W0f, DTc, dtmp = gU_.tt, DT_.tt, DsB_.tt, W0f_.tt, DTc_.tt, dtmp_.tt
            gUv, DTv, DsBv, W0fv, DTcv, dtmpv = gU_.v(), DT_.v(), DsB_.v(), W0f_.v(), DTc_.v(), dtmp_.v()
            W = [w.tt for w in W_]
            Wv = [w.v() for w in W_]
            X = [x.tt for x in X_]
            Xv = [x.v() for x in X_]
            NT = [x.tt for x in NT_]
            NTv = [x.v() for x in NT_]
            mvv = lambda ps: ps[PC, 0:nm * C].rearrange("p (m c) -> p m c", m=nm)

            chunks = [c for bt in batches for c in bt]
            nck = len(chunks)

            def prep_super():
                for ci_, (sq_, ci) in enumerate(chunks):
                    col0 = sq_ * L + ci * C
                    psA = k.ps[0]
                    for kc in range(8):
                        k.mm(psA, psA[PC, 0:8], xT, xT[:, kc, col0:col0 + C], Wtm, Wtm[:, kc, 0:8], start=(kc == 0), stop=(kc == 7))
                    k.tt(k.dve, gx, gx[PC, ci_, :], psA, psA[PC, 0:4], dtb, dtb[PC, :], ALU.add)
                    k.actv(ge1, ge1[PC, ci_, :], psA, psA[PC, 4:8], AF.Exp, scale=-1.0)
                    psB = k.ps[1 + (ci_ % 2)]
                    for kc in range(8):
                        k.mm(psB, psB[PC, 0:512], xT, xT[:, kc, col0:col0 + C], Wtm, Wtm[:, kc, 8:520], start=(kc == 0), stop=(kc == 7))
                    k.actv(sgtmp, sgtmp[PC, :], psB, psB[PC, 0:512], AF.Exp, scale=-1.0)
                    k.actv(sgtmp, sgtmp[PC, :], sgtmp, sgtmp[PC, :], AF.Ln, bias=1.0)
                    k.actv(sgtmp, sgtmp[PC, :], sgtmp, sgtmp[PC, :], AF.Exp, scale=-1.0)
                    k.tt(k.dve, sgA, sgA[PC, ci_, :], psB, psB[PC, 0:512], sgtmp, sgtmp[PC, :], ALU.mult)
                    k.tt(k.pool, sgA, sgA[PC, ci_, :].rearrange("p (h e) -> p h e", h=4), sgA,
                         sgA[PC, ci_, :].rearrange("p (h e) -> p h e", h=4), nw, bc(nw[PC, :], [C, 4, 128], 1), ALU.mult)
                A_ = slice(0, nck)
                k.actv(ge1, ge1[PC, A_, :], ge1, ge1[PC, A_, :], AF.Ln, bias=1.0)
                k.actv(beta, beta[PC, A_, :], ge1, ge1[PC, A_, :], AF.Exp, scale=-1.0)
                gxa = gx[PC, A_, :]
                k.ts(k.dve, gab, gab[PC, A_, :], gx, gxa, -1.0, None, ALU.mult)
                k.tt(k.dve, gab, gab[PC, A_, :], gab, gab[PC, A_, :], gx, gxa, ALU.min)
                k.actv(gl1, gl1[PC, A_, :], gab, gab[PC, A_, :], AF.Exp)
                k.actv(gl1, gl1[PC, A_, :], gl1, gl1[PC, A_, :], AF.Ln, bias=1.0)
                k.stt(k.dve, gsp, gsp[PC, A_, :], gx, gxa, 0.0, gl1, gl1[PC, A_, :], ALU.max, ALU.add)
                k.tt(k.dve, gg, gg[PC, A_, :], gsp, gsp[PC, A_, :], negA, bc(negA[PC, :], [C, nck, 4], 1), ALU.mult)
                k.ts(k.dve, nbeta, nbeta[PC, A_, :], beta, beta[PC, A_, :], -1.0, None, ALU.mult)
                psG = k.ps[0]
                for ci_ in range(nck):
                    gcol = gg[PC, ci_, :]
                    k.mm(psG, psG[PC, ci_ * 16:ci_ * 16 + 4], self.c_U, self.c_U[PC, PC], gg, gcol)
                    k.mm(psG, psG[PC, ci_ * 16 + 4:ci_ * 16 + 8], self.c_Urev, self.c_Urev[PC, PC], gg, gcol)
                    k.mm(psG, psG[PC, ci_ * 16 + 8:ci_ * 16 + 12], self.ones, self.ones[PC, PC], gg, gcol)
                    k.mm(psG, psG[:, 256 + ci_ * 4:256 + ci_ * 4 + 4], self.ones, self.ones[PC, :], gg, gcol)
                pG = psG[PC, 0:nck * 16].rearrange("p (b e) -> p b e", b=nck)
                k.actv(E, E[PC, A_, :], psG, pG[:, :, 0:12], AF.Exp)
                k.actv(gt128, gt128[:, A_, :], psG, psG[:, 256:256 + nck * 4].rearrange("p (b e) -> p b e", b=nck), AF.Exp)
                k.ts(k.dve, negeG, negeG[PC, A_, :], E, E[PC, A_, 0:4], -1.0, None, ALU.mult)
                k.ts(k.dve, negG, negG[PC, A_, :], psG, pG[:, :, 0:4], -1.0, None, ALU.mult)

            prep_super()

            def prep_units(batch, par, b0):
                B_ = slice(b0, b0 + nb)
                kdecp, vtmp = kdec2[par], vtm2[par]
                NTfp, qkTp = NTf_[par], qkT2_[par]
                NTfp.set(C, nm)
                qkTp.set(C, nm)
                U = []

                def u_gu():
                    k.tt(k.dve, gU, gUv, self.c_U, bc(self.c_U[PC, PC], [C, nm, C], 1),
                         gg, bc(gg[PC, B_, :].rearrange("p b h -> p (b h)"), [C, nm, C], 2), ALU.mult)
                U.append(u_gu)

                def u_diff():
                    psD = k.ps[1]
                    for mi in range(nm):
                        k.mm(psD, psD[PC, mi * C:(mi + 1) * C], self.ones, self.ones[PC, PC], gU, gUv[:, mi, :])
                    k.tt(k.dve, dtmp, dtmpv, psD, mvv(psD), negG,
                         bc(negG[PC, B_, :].rearrange("p b h -> p (b h)"), [C, nm, C], 2), ALU.add)
                    k.ts(k.dve, dtmp, dtmpv, dtmp, dtmpv, 0.0, None, ALU.min)
                    k.actv(DT, DTv, dtmp, dtmpv, AF.Exp)
                    k.tt(k.pool, DTc, DTcv, DT, DTv, self.c_caus01, bc(self.c_caus01[PC, PC], [C, nm, C], 1), ALU.mult)
                    k.tt(k.pool, DsB, DsBv, DT, DTv, self.c_su01, bc(self.c_su01[PC, PC], [C, nm, C], 1), ALU.mult)
                    k.tt(k.pool, DsB, DsBv, DsB, DsBv, nbeta,
                         bc(nbeta[PC, B_, :].rearrange("p b h -> p (b h)"), [C, nm, C], 2), ALU.mult)
                U.append(u_diff)

                def u_kk():
                    psK = k.ps[2]
                    psQ = k.ps[3]
                    for bi, (sq_, ci) in enumerate(batch):
                        cs = slice(ci * C, (ci + 1) * C)
                        for h in range(4):
                            mi = bi * 4 + h
                            k.mm(psK, psK[PC, mi * C:(mi + 1) * C], qkb, qb[:, 4 + h, sq_, cs], qkb, qb[:, 4 + h, sq_, cs])
                            k.mm(psQ, psQ[PC, mi * C:(mi + 1) * C], qkb, qb[:, 4 + h, sq_, cs], qkb, qb[:, h, sq_, cs])
                    k.tt(k.dve, W0f, W0fv, psK, mvv(psK), DsB, DsBv, ALU.mult)
                    k.cp(k.pool, W[0], Wv[0], W0f, W0fv)
                    k.tt(k.dve, qkTp.tt, qkTp.v(), psQ, mvv(psQ), DTc, DTcv, ALU.mult)
                U.append(u_kk)

                def u_x0():
                    psX = k.ps[2]
                    for mi in range(nm):
                        k.tr(psX, psX[PC, mi * C:(mi + 1) * C], W0f, W0fv[:, mi, :], self.ident, self.ident[PC, PC], inc=(mi == nm - 1))
                    k.cp(k.act, X[0], Xv[0], psX, mvv(psX))
                    k.tt(k.dve, NT[0], NTv[0], W0f, W0fv, self.ident, bc(self.ident[PC, PC], [C, nm, C], 1), ALU.add)
                U.append(u_x0)

                nlev = int(round(math.log2(C)))
                for lev in range(1, nlev):
                    cur = (lev - 1) % 2
                    nxt = 1 - cur
                    lastlev = (lev == nlev - 1)

                    def u_x(cur=cur, nxt=nxt):
                        psX2 = k.ps[2]
                        for mi in range(nm):
                            k.mm(psX2, psX2[PC, mi * C:(mi + 1) * C], W[cur], Wv[cur][:, mi, :], X[cur], Xv[cur][:, mi, :])
                        k.cp(k.act, X[nxt], Xv[nxt], psX2, mvv(psX2))
                    U.append(u_x)
                    if not lastlev:
                        def u_w(cur=cur, nxt=nxt):
                            psW = k.ps[1]
                            for mi in range(nm):
                                k.mm(psW, psW[PC, mi * C:(mi + 1) * C], X[cur], Xv[cur][:, mi, :], W[cur], Wv[cur][:, mi, :])
                            k.cp(k.act, W[nxt], Wv[nxt], psW, mvv(psW))
                        U.append(u_w)

                    def u_p(cur=cur, nxt=nxt, lastlev=lastlev):
                        psP = k.ps[3]
                        for mi in range(nm):
                            k.mm(psP, psP[PC, mi * C:(mi + 1) * C], X[nxt], Xv[nxt][:, mi, :], NT[cur], NTv[cur][:, mi, :])
                        if lastlev:
                            k.tt(k.dve, NTfp.tt, NTfp.v(), psP, mvv(psP), NT[cur], NTv[cur], ALU.add)
                        else:
                            k.tt(k.dve, NT[nxt], NTv[nxt], psP, mvv(psP), NT[cur], NTv[cur], ALU.add)
                    U.append(u_p)

                def u_kv(bi):
                    sq_, ci = batch[bi]
                    cs = slice(ci * C, (ci + 1) * C)
                    psk = k.ps[0]
                    for h in range(4):
                        k.tr(psk, psk[PC, h * 128:(h + 1) * 128], qkvs, qv[:, 4 + h, sq_, cs], self.ident, self.ident[:, :], inc=(h == 3))
                    k.tt(k.dve, kdecp, kdecp[PC, bi, :, :], psk, psk[PC, :].rearrange("p (h e) -> p h e", h=4),
                         E, bc(E[PC, b0 + bi, 4:8], [C, 4, 128], 2), ALU.mult)
                    psv = k.ps[0]
                    for h in range(4):
                        k.tr(psv, psv[PC, h * 128:(h + 1) * 128], qkvs, qv[:, 8 + h, sq_, cs], self.ident, self.ident[:, :], inc=(h == 3))
                    k.cp(k.act, vtmp, vtmp[PC, bi, :, :], psv, psv[PC, :].rearrange("p (h e) -> p h e", h=4))
                for bi in range(nb):
                    U.append(lambda bi=bi: u_kv(bi))
                return U

            def rec_units(batch, par, b0):
                kdecp, vtmp = kdec2[par], vtm2[par]
                NTfv, qkTv = NTf_[par].v(), qkT2_[par].v()
                NTf, qkT = NTf_[par].tt, qkT2_[par].tt
                U = []
                for bi, (sq_, ci) in enumerate(batch):
                    cs = slice(ci * C, (ci + 1) * C)
                    Sv = Sst[:, sq_, :, :]
                    pskS, psqS, psNZ, psqkv, psdS = k.ps[4], k.ps[5], k.ps[6], k.ps[7], k.ps[6]

                    def r1(bi=bi, sq_=sq_, cs=cs, Sv=Sv):
                        for h in range(4):
                            k.mm(pskS, pskS[PC, h * 128:(h + 1) * 128], qkb, qb[:, 4 + h, sq_, cs], Sbf, Sbf[:, sq_, h, :], inc=(h == 3))
                        for h in range(4):
                            k.mm(psqS, psqS[PC, h * 128:(h + 1) * 128], qkb, qb[:, h, sq_, cs], Sbf, Sbf[:, sq_, h, :], inc=(h == 3))
                        k.tt(k.dve, Zf, Zf[PC, :, :], pskS, pskS[PC, :].rearrange("p (h e) -> p h e", h=4),
                             negeG, bc(negeG[PC, b0 + bi, :], [C, 4, 128], 2), ALU.mult)
                        k.tt(k.dve, Z, Z[PC, :, :], Zf, Zf[PC, :, :], vtmp, vtmp[PC, bi, :, :], ALU.add)
                        k.tt(k.dve, t1, t1[PC, :, :], psqS, psqS[PC, :].rearrange("p (h e) -> p h e", h=4),
                             E, bc(E[PC, b0 + bi, 0:4], [C, 4, 128], 2), ALU.mult)
                    U.append(r1)

                    def r2(bi=bi):
                        for h in range(4):
                            k.mm(psNZ, psNZ[PC, h * 128:(h + 1) * 128], NTf, NTfv[:, bi * 4 + h, :], Z, Z[PC, h, :], inc=(h == 3))
                        k.tt(k.dve, vnew, vnew[PC, :, :], psNZ, psNZ[PC, :].rearrange("p (h e) -> p h e", h=4),
                             beta, bc(beta[PC, b0 + bi, :], [C, 4, 128], 2), ALU.mult)
                    U.append(r2)

                    def r3(bi=bi, Sv=Sv, sq_=sq_):
                        for h in range(4):
                            k.mm(psdS, psdS[:, h * 128:(h + 1) * 128], kdecp, kdecp[PC, bi, h, :], vnew, vnew[PC, h, :], inc=(h == 3))
                        for h in range(4):
                            k.mm(psqkv, psqkv[PC, h * 128:(h + 1) * 128], qkT, qkTv[:, bi * 4 + h, :], vnew, vnew[PC, h, :], inc=(h == 3))
                        k.tt(k.dve, Sst, Sv, Sst, Sv, gt128, bc(gt128[:, b0 + bi, :], [128, 4, 128], 2), ALU.mult)
                        k.tt(k.dve, Sst, Sv, Sst, Sv, psdS, psdS[:, :].rearrange("p (h e) -> p h e", h=4), ALU.add)
                        k.cp(k.pool, Sbf, Sbf[:, sq_, :, :], Sst, Sv)
                        k.tt(k.dve, o_t, o_t[PC, :, :], psqkv, psqkv[PC, :].rearrange("p (h e) -> p h e", h=4), t1, t1[PC, :, :], ALU.add)
                    U.append(r3)

                    def r4(bi=bi, sq_=sq_, ci=ci):
                        for h in range(4):
                            k.actv(osq, osq[PC, :], o_t, o_t[PC, h, :], AF.Square, accum=(ss, ss[PC, h:h + 1]))
                        k.actv(rstd, rstd[PC, :], ss, ss[PC, :], AF.Ln, bias=NORM_EPS, scale=1.0 / 128.0)
                        k.actv(rstd, rstd[PC, :], rstd, rstd[PC, :], AF.Exp, scale=-0.5)
                        k.tt(k.pool, yb, yb[PC, :, :], o_t, o_t[PC, :, :], rstd, bc(rstd[PC, :], [C, 4, 128], 2), ALU.mult)
                        k.tt(k.pool, yb, yb[PC, :, :], yb, yb[PC, :, :], sgA, sgA[PC, b0 + bi, :].rearrange("p (h e) -> p h e", h=4), ALU.mult)
                        psT = k.ps[4]
                        for h in range(4):
                            k.tr(psT, psT[:, h * C:(h + 1) * C], yb, yb[PC, h, :], self.ident, self.ident[PC, PC], inc=(h == 3))
                        k.cp(k.act, ybT, ybT[:, :, 0:C], psT, psT[:, 0:4 * C].rearrange("p (h c) -> p h c", h=4))
                        tk = tok0 + sq_ * L + ci * C
                        k.dma(k.pool, self.YT[512:1024, tk:tk + C].rearrange("(m p) t -> p m t", p=128), ybT[:, :, 0:C],
                              out_t=self.YT, in_t=ybT)
                    U.append(r4)
                return U

            for u in prep_units(batches[0], 0, 0):
                u()
            for b_ in range(len(batches)):
                R = rec_units(batches[b_], b_ % 2, b_ * nb)
                N = prep_units(batches[b_ + 1], (b_ + 1) % 2, (b_ + 1) * nb) if b_ + 1 < len(batches) else []
                per = (len(N) + len(R) - 1) // len(R) if N else 0
                for r in R:
                    r()
                    for _ in range(per):
                        if N:
                            N.pop(0)()
                while N:
                    N.pop(0)()

        k.memset(k.pool, Sst, Sst[:, 0, :, :], 0.0)
        k.memset(k.pool, Sbf, Sbf[:, 0, :, :], 0.0)
        nst = T // 512
        for st in range(nst):
            process(st * 512, 512, 1, 512, 64, False, st == 0, st == nst - 1)
        k.dma(k.sp, self.nssm_p[l].rearrange("h a b -> a h b"), Sst[:, 0, :, :], in_t=Sst, is_output=True)
        for s in range(NSQ):
            k.dma(k.sp, Sst[:, s, :, :], self.state_ssm[l, s].rearrange("h a b -> a h b"), out_t=Sst)
            k.cp(k.pool, Sbf, Sbf[:, s, :, :], Sst, Sst[:, s, :, :])
        process(T, NSQ * TS, NSQ, TS, TS, True, True, True)
        for s in range(NSQ):
            k.dma(k.sp, self.nssm_s[l, s].rearrange("h a b -> a h b"), Sst[:, s, :, :], in_t=Sst, is_output=True)
        k.end_pass()


    def layer_norm(self, r, n, gbc, bbc, junk, stat, out_t):
        k = self.k
        nc = self.nc
        P = slice(0, n)
        k.actv(junk, junk[P, :], r, r[P, :], AF.Identity, accum=(stat, stat[P, 0:1]))
        k.actv(junk, junk[P, :], r, r[P, :], AF.Square, accum=(stat, stat[P, 1:2]))
        k.ts(k.dve, stat, stat[P, 2:3], stat, stat[P, 0:1], -1.0 / D, None, ALU.mult)
        k.tt(k.dve, stat, stat[P, 3:4], stat, stat[P, 2:3], stat, stat[P, 2:3], ALU.mult)
        k.stt(k.dve, stat, stat[P, 4:5], stat, stat[P, 1:2], 1.0 / D, stat, stat[P, 3:4], ALU.mult, ALU.subtract)
        k.actv(stat, stat[P, 5:6], stat, stat[P, 4:5], AF.Sqrt, bias=LN_EPS)
        k.op(k.dve, lambda: nc.vector.reciprocal(stat[P, 6:7], stat[P, 5:6]), outs=[stat], ins=[stat])
        k.ts(k.dve, r, r[P, :], r, r[P, :], stat[P, 2:3], stat[P, 6:7], ALU.add, ALU.mult, extra_ins=[stat])
        k.tt(k.pool, r, r[P, :], r, r[P, :], gbc, gbc[P, :], ALU.mult)
        k.tt(k.pool, out_t, out_t[P, :], r, r[P, :], bbc, bbc[P, :], ALU.add)

    def p2(self, l):
        k = self.k
        nc = self.nc
        k.begin_pass()
        wsrc = self.w_in[l].rearrange("(kc p) e -> p kc e", p=128)
        Wg = k.sb("Wg", [128, 8, 2048], BF16)
        k.dma(k.pool, Wg[:, :, :], wsrc[:, :, O_GATES:O_GATES + 2048], out_t=Wg)
        Wpa = k.sb("Wpa", [128, 4, D], BF16)
        k.dma(k.pool, Wpa[:, :, :], self.w_proj_a[l].rearrange("(kc p) e -> p kc e", p=128), out_t=Wpa)
        Wpb = k.sb("Wpb", [128, 4, D], BF16)
        k.dma(k.pool, Wpb[:, :, :], self.w_proj_b[l].rearrange("(kc p) e -> p kc e", p=128), out_t=Wpb)
        Wo = k.sb("Wo", [128, 8, D], BF16)
        k.dma(k.pool, Wo[:, :, :], self.w_out[l].rearrange("(kc p) e -> p kc e", p=128), out_t=Wo)
        bg = k.sb("bg", [128, 2048], F32)
        k.dma(k.sp, bg[:, :], self.b_gate[l:l + 1, :].to_broadcast([128, 2048]), out_t=bg)
        gbc = k.sb("gbc", [128, D], F32)
        k.dma(k.sp, gbc[:, :], self.ln1_g[l:l + 1, :].to_broadcast([128, D]), out_t=gbc)
        bbc = k.sb("bbc", [128, D], F32)
        k.dma(k.sp, bbc[:, :], self.ln1_b[l:l + 1, :].to_broadcast([128, D]), out_t=bbc)
        xin = [k.sb(f"xin{i}", [128, D], F32) for i in range(2)]
        xT = [k.sb(f"xT{i}", [128, 8, 128], BF16) for i in range(2)]
        yT = [k.sb(f"yT{i}", [128, 8, 128], BF16) for i in range(2)]
        sgates = [k.sb(f"sgate{i}", [128, 2048], F32) for i in range(2)]
        mixeds = [k.sb(f"mixed{i}", [128, D], F32) for i in range(2)]
        tmps = [k.sb(f"tmp{i}", [128, 512], F32) for i in range(2)]
        mixT = k.sb("mixT", [128, 8, 128], BF16)
        r = [k.sb(f"r{i}", [128, D], F32) for i in range(2)]
        junk = k.sb("junk", [128, D], BF16)
        stat = k.sb("stat", [128, 8], F32)
        tiles = [(t * 128, 128) for t in range(T // 128)] + [(T, NSQ * TS)]

        def stage_x(ti):
            tok0, n = tiles[ti]
            P = slice(0, n)
            xi, xt, yt = xin[ti % 2], xT[ti % 2], yT[ti % 2]
            sgate, mixed = sgates[ti % 2], mixeds[ti % 2]
            self.load_xT(l, tok0, n, xi, xt, 0, k.act)
            k.dma(k.sp, yt[:, :, 0:n], self.YT[:, tok0:tok0 + n].rearrange("(kc p) t -> p kc t", p=128), out_t=yt, in_t=self.YT)
            for blk in range(4):
                ps = k.ps[blk % 2]
                for kc in range(8):
                    k.mm(ps, ps[P, :], xt, xt[:, kc, 0:n], Wg, Wg[:, kc, blk * 512:(blk + 1) * 512], start=(kc == 0), stop=(kc == 7))
                k.tt(k.dve, sgate, sgate[P, blk * 512:(blk + 1) * 512], ps, ps[P, :], bg, bg[P, blk * 512:(blk + 1) * 512], ALU.add)
                k.actv(sgate, sgate[P, blk * 512:(blk + 1) * 512], sgate, sgate[P, blk * 512:(blk + 1) * 512], AF.Sigmoid)
            for blk in range(2):
                psa = k.ps[4 + blk]
                psb = k.ps[6 + blk]
                tmp = tmps[blk]
                for kc in range(4):
                    k.mm(psa, psa[P, :], yt, yt[:, kc, 0:n], Wpa, Wpa[:, kc, blk * 512:(blk + 1) * 512], start=(kc == 0), stop=(kc == 3))
                for kc in range(4):
                    k.mm(psb, psb[P, :], yt, yt[:, 4 + kc, 0:n], Wpb, Wpb[:, kc, blk * 512:(blk + 1) * 512], start=(kc == 0), stop=(kc == 3))
                cs = slice(blk * 512, (blk + 1) * 512)
                k.tt(k.dve, mixed, mixed[P, cs], psa, psa[P, :], sgate, sgate[P, cs], ALU.mult)
                k.tt(k.dve, tmp, tmp[P, :], psb, psb[P, :], sgate, sgate[P, 1024 + blk * 512:1024 + (blk + 1) * 512], ALU.mult)
                k.tt(k.pool, mixed, mixed[P, cs], mixed, mixed[P, cs], tmp, tmp[P, :], ALU.add)

        def stage_y(ti):
            tok0, n = tiles[ti]
            P = slice(0, n)
            xi, mixed, rr = xin[ti % 2], mixeds[ti % 2], r[ti % 2]
            for grp in range(2):
                ps = k.ps[2 + grp]
                for kk in range(4):
                    kc = grp * 4 + kk
                    k.tr(ps, ps[:, kk * n:(kk + 1) * n], mixed, mixed[P, kc * 128:(kc + 1) * 128], self.ident, self.ident[P, P], inc=(kk == 3))
                k.cp(k.act, mixT, mixT[:, grp * 4:(grp + 1) * 4, 0:n], ps, ps[:, 0:4 * n].rearrange("p (a t) -> p a t", a=4))
            for blk in range(2):
                ps = k.ps[2 + blk]
                for kc in range(8):
                    k.mm(ps, ps[P, :], mixT, mixT[:, kc, 0:n], Wo, Wo[:, kc, blk * 512:(blk + 1) * 512], start=(kc == 0), stop=(kc == 7))
                cs = slice(blk * 512, (blk + 1) * 512)
                k.stt(k.dve, rr, rr[P, cs], xi, xi[P, cs], ALPHA, ps, ps[P, :], ALU.mult, ALU.add)
            self.layer_norm(rr, n, gbc, bbc, junk, stat, rr)
            k.dma(k.pool, self.X1[tok0:tok0 + n, :], rr[P, :], out_t=self.X1, in_t=rr)

        stage_x(0)
        for ti in range(len(tiles)):
            if ti + 1 < len(tiles):
                stage_x(ti + 1)
            stage_y(ti)
        k.end_pass()

    def p3(self, l):
        k = self.k
        nc = self.nc
        k.begin_pass()
        Wup = k.sb("Wup", [128, 8, 2 * DFF], BF16)
        usrc = self.w_up[l].rearrange("(kc p) e -> p kc e", p=128)
        for q4 in range(4):
            k.dma(k.pool, Wup[:, :, q4 * 1408:(q4 + 1) * 1408], usrc[:, :, q4 * 1408:(q4 + 1) * 1408], out_t=Wup)
        Wdn = k.sb("Wdn", [128, 22, D], BF16)
        k.dma(k.pool, Wdn[:, :, :], self.w_down[l].rearrange("(kc p) e -> p kc e", p=128), out_t=Wdn)
        gbc = k.sb("gbc", [128, D], F32)
        k.dma(k.sp, gbc[:, :], self.ln2_g[l:l + 1, :].to_broadcast([128, D]), out_t=gbc)
        bbc = k.sb("bbc", [128, D], F32)
        k.dma(k.sp, bbc[:, :], self.ln2_b[l:l + 1, :].to_broadcast([128, D]), out_t=bbc)
        xin = [k.sb(f"xin{i}", [128, D], F32) for i in range(2)]
        xT = k.sb("xT", [128, 8, 512], BF16)
        fT = k.sb("fT", [128, 22, 512], BF16)
        tmp = [k.sb(f"tmp{i}", [128, 512], F32) for i in range(2)]
        rrs = [k.sb(f"r{i}", [128, D], F32) for i in range(2)]
        junk = k.sb("junk", [128, D], BF16)
        stat = k.sb("stat", [128, 8], F32)
        last = (l == DEPTH - 1)
        sts = [(st * 512, 512) for st in range(T // 512)] + [(T, NSQ * TS)]

        def subs_of(NS):
            subs = []
            done = 0
            while done < NS:
                n = min(128, NS - done)
                subs.append((done, n))
                done += n
            return subs

        xcnt = [0]

        def transposes(si):
            tok0, NS = sts[si]
            for (c0, n) in subs_of(NS):
                xi = xin[xcnt[0] % 2]
                xcnt[0] += 1
                k.dma(k.sp, xi[0:n, :], self.X1[tok0 + c0:tok0 + c0 + n, :], out_t=xi, in_t=self.X1)
                for grp in range(2):
                    ps = k.ps[6 + grp]
                    for kk in range(4):
                        kc = grp * 4 + kk
                        k.tr(ps, ps[:, kk * n:(kk + 1) * n], xi, xi[0:n, kc * 128:(kc + 1) * 128], self.ident, self.ident[0:n, 0:n], inc=(kk == 3))
                    k.cp(k.act, xT, xT[:, grp * 4:(grp + 1) * 4, c0:c0 + n], ps, ps[:, 0:4 * n].rearrange("p (a t) -> p a t", a=4))

        def up(si):
            tok0, NS = sts[si]
            for fc in range(22):
                psA = k.ps[(2 * fc) % 4]
                psB = k.ps[(2 * fc + 1) % 4]
                for kc in range(8):
                    k.mm(psA, psA[:, 0:NS], Wup, Wup[:, kc, fc * 128:(fc + 1) * 128], xT, xT[:, kc, 0:NS], start=(kc == 0), stop=(kc == 7))
                for kc in range(8):
                    k.mm(psB, psB[:, 0:NS], Wup, Wup[:, kc, DFF + fc * 128:DFF + (fc + 1) * 128], xT, xT[:, kc, 0:NS], start=(kc == 0), stop=(kc == 7))
                tm = tmp[fc % 2]
                k.actv(tm, tm[:, 0:NS], psA, psA[:, 0:NS], AF.Silu)
                k.tt(k.dve, fT, fT[:, fc, 0:NS], tm, tm[:, 0:NS], psB, psB[:, 0:NS], ALU.mult)

        rcnt = [0]

        def down(si):
            tok0, NS = sts[si]
            for (c0, n) in subs_of(NS):
                P = slice(0, n)
                rr = rrs[rcnt[0] % 2]
                rcnt[0] += 1
                k.dma(k.sp, rr[0:n, :], self.X1[tok0 + c0:tok0 + c0 + n, :], out_t=rr, in_t=self.X1)
                for blk in range(2):
                    ps = k.ps[4 + blk]
                    for fc in range(22):
                        k.mm(ps, ps[P, :], fT, fT[:, fc, c0:c0 + n], Wdn, Wdn[:, fc, blk * 512:(blk + 1) * 512], start=(fc == 0), stop=(fc == 21))
                    cs = slice(blk * 512, (blk + 1) * 512)
                    k.stt(k.dve, rr, rr[P, cs], rr, rr[P, cs], ALPHA, ps, ps[P, :], ALU.mult, ALU.add)
                self.layer_norm(rr, n, gbc, bbc, junk, stat, rr)
                t0 = tok0 + c0
                if not last:
                    k.dma(k.pool, self.X2[t0:t0 + n, :], rr[P, :], out_t=self.X2, in_t=rr)
                elif t0 < T:
                    k.dma(k.pool, self.y_p[t0:t0 + n, :], rr[P, :], in_t=rr, is_output=True)
                else:
                    k.dma(k.pool, self.y_s[t0 - T:t0 - T + n, :], rr[P, :], in_t=rr, is_output=True)

        transposes(0)
        for si in range(len(sts)):
            up(si)
            if si + 1 < len(sts):
                transposes(si + 1)
            down(si)
        k.end_pass()

    def build(self):
        k = self.k
        self.load_consts()
        sa = self.stop_after
        only = sa[0][:-5] if (sa is not None and sa[0].endswith("_only")) else None
        for l in range(DEPTH):
            for name, fn in (("p1a", self.p1a), ("p1b", self.p1b), ("p2", self.p2), ("p3", self.p3)):
                if only is not None and name != only:
                    continue
                fn(l)
                if sa is not None and sa[1] == l and (sa[0] == name or only == name):
                    break
            else:
                continue
            break
        k.finish()


def shard_inputs(inputs, c):
    f = lambda a: np.ascontiguousarray(a, dtype=np.float32)
    sl = slice(NSQ * c, NSQ * (c + 1))
    m = {
        "x_p": f(inputs["x_prompt"][c]),
        "x_s": f(inputs["x_sample"][sl].reshape(NSQ * TS, D)),
        "cache_k": f(inputs["cache_k"][:, sl].reshape(DEPTH, NSQ, T, 128)),
        "cache_v": f(inputs["cache_v"][:, sl].reshape(DEPTH, NSQ, T, 128)),
        "cache_ki": f(inputs["cache_kidx"][:, sl]),
        "state_conv": f(inputs["state_conv"][:, sl]),
        "state_ssm": f(inputs["state_ssm"][:, sl]),
    }
    for n in ["w_in", "b_gate", "conv_w", "a_log", "dt_bias", "gdn_norm_w", "w_proj_a", "w_proj_b", "w_out",
              "ln1_g", "ln1_b", "w_up", "w_down", "ln2_g", "ln2_b"]:
        m[n] = f(inputs[n])
    for n, v in make_consts().items():
        m["c_" + n] = v
    return m


def run(inputs, debug=False, stop_after=None, trace=False):
    prog = Prog(debug=debug, stop_after=stop_after)
    in_maps = [shard_inputs(inputs, c) for c in range(8)]
    res = run_bass_kernel_spmd(prog.nc, in_maps, core_ids=list(range(8)), trace=trace)
    return res


def kernel(**inputs):
    res = run(inputs)
    r = res.results
    cat = lambda n: np.stack([r[c][n] for c in range(8)], axis=0)
    y_p = cat("y_p")
    y_s = cat("y_s").reshape(32, TS, D)
    nk_p = np.transpose(cat("nk_p"), (1, 0, 2, 3)).reshape(DEPTH, 8, T, 2, 64)
    nv_p = np.transpose(cat("nv_p"), (1, 0, 2, 3)).reshape(DEPTH, 8, T, 2, 64)
    nki_p = np.transpose(cat("nki_p"), (1, 0, 2, 3))
    nconv_p = np.transpose(cat("nconv_p"), (1, 0, 2, 3))
    nssm_p = np.transpose(cat("nssm_p"), (1, 0, 2, 3, 4))
    nk_s = np.transpose(cat("nk_s"), (1, 0, 2, 3)).reshape(DEPTH, 32, TS, 2, 64)
    nv_s = np.transpose(cat("nv_s"), (1, 0, 2, 3)).reshape(DEPTH, 32, TS, 2, 64)
    nki_s = np.transpose(cat("nki_s"), (1, 0, 2, 3)).reshape(DEPTH, 32, TS, 64)
    nconv_s = np.transpose(cat("nconv_s"), (1, 0, 2, 3, 4)).reshape(DEPTH, 32, 3, 1536)
    nssm_s = np.transpose(cat("nssm_s"), (1, 0, 2, 3, 4, 5)).reshape(DEPTH, 32, 4, 128, 128)
    return tuple(np.ascontiguousarray(a, dtype=np.float32) for a in
                 (y_p, y_s, nk_p, nv_p, nki_p, nconv_p, nssm_p, nk_s, nv_s, nki_s, nconv_s, nssm_s))
```

```python
import math
from contextlib import ExitStack
import numpy as np
import concourse.bass as bass
import concourse.mybir as mybir
from concourse.bass_utils import run_bass_kernel_spmd

F32 = mybir.dt.float32
BF16 = mybir.dt.bfloat16
AF = mybir.ActivationFunctionType
ALU = mybir.AluOpType
AX = mybir.AxisListType

D = 1024
T = 4096
TS = 16
NSQ = 4
NTOK = T + NSQ * TS
DEPTH = 2
DIN = 5456
DFF = 2816
O_QA, O_KA, O_VA, O_QI, O_KI, O_WI, O_QKV, O_AB, O_BB, O_GB, O_GATES = 0, 512, 640, 768, 1280, 1344, 1352, 2888, 2892, 2896, 3408
ALPHA = (2 * DEPTH) ** 0.25
INDEX_SCALE = (8 * 64) ** -0.5
LN_EPS = 1e-5
NORM_EPS = 1e-6
NBIS = 16
NEG = -30000.0
KCOLS = 33 * 128


class Tok:
    __slots__ = ("sem", "val", "key")

    def __init__(self, sem, val, key):
        self.sem = sem
        self.val = val
        self.key = key


class Eng:
    def __init__(self, nc, name, h):
        self.name = name
        self.key = name
        self.h = h
        self.sem = nc.alloc_semaphore("e_" + name)
        self.cnt = 0
        self.waited = {}

    def wait(self, tok):
        if tok is None:
            return
        if self.waited.get(tok.key, 0) >= tok.val:
            return
        self.h.wait_ge(tok.sem, tok.val)
        self.waited[tok.key] = tok.val


class TT:
    def __init__(self, t, name, sbuf=True):
        self.t = t
        self.name = name
        self.sbuf = sbuf
        self.w = None
        self.r = {}
        self.dsem = None

    def __getitem__(self, idx):
        return self.t[idx]


class K:
    def __init__(self, nc):
        self.nc = nc
        self.pe = Eng(nc, "pe", nc.tensor)
        self.act = Eng(nc, "act", nc.scalar)
        self.dve = Eng(nc, "dve", nc.vector)
        self.pool = Eng(nc, "pool", nc.gpsimd)
        self.sp = Eng(nc, "sp", nc.sync)
        self.engs = [self.pe, self.act, self.dve, self.pool, self.sp]
        self.dsem_pool = []
        self.ndsem = 0
        self.out_toks = {}
        self.pass_tiles = []
        self.es = None
        self.uid = 0
        self.ps = [TT(nc.alloc_psum_tensor(f"psb{i}", [128, 512], F32), f"psb{i}") for i in range(8)]

    def begin_pass(self):
        self.es = ExitStack()
        self.pass_tiles = []

    def sb(self, name, shape, dt):
        self.uid += 1
        t = self.es.enter_context(self.nc.sbuf_tensor(f"{name}_{self.uid}", list(shape), dt))
        tt = TT(t, name)
        self.pass_tiles.append(tt)
        return tt

    def get_dsem(self):
        if self.dsem_pool:
            return self.dsem_pool.pop()
        self.ndsem += 1
        return [self.nc.alloc_semaphore(f"d{self.ndsem}"), 0]

    def barrier(self, tiles):
        toks = [Tok(e.sem, e.cnt, e.key) for e in self.engs if e.cnt > 0]
        seen = set()
        for tt in tiles:
            for ds in (tt.dsem or {}).values():
                if ds[1] > 0 and id(ds) not in seen:
                    seen.add(id(ds))
                    toks.append(Tok(ds[0], 16 * ds[1], ("d", id(ds))))
        for e in self.engs:
            for tok in toks:
                if tok.key != e.key:
                    e.wait(tok)

    def end_pass(self):
        self.barrier(self.pass_tiles)
        for tt in self.pass_tiles:
            if tt.dsem is not None:
                self.dsem_pool.extend(tt.dsem.values())
                tt.dsem = None
        self.es.close()
        self.es = None
        self.pass_tiles = []

    def op(self, e, fn, outs=(), ins=(), inc=True):
        for t in ins:
            e.wait(t.w)
        strict = e.key != "pe"
        for t in outs:
            if t.w is not None and (strict or t.w.key != e.key):
                e.wait(t.w)
            for kk, tok in t.r.items():
                if strict or kk != e.key:
                    e.wait(tok)
        inst = fn()
        if inc:
            inst.then_inc(e.sem, 1)
            e.cnt += 1
            tok = Tok(e.sem, e.cnt, e.key)
        else:
            tok = Tok(e.sem, e.cnt + 1, e.key)
        for t in outs:
            t.w = tok
            t.r = {}
        for t in ins:
            if t not in outs:
                t.r[e.key] = tok
        return inst

    def dma(self, q, out_ap, in_ap, out_t=None, in_t=None, is_output=False):
        if in_t is not None:
            q.wait(in_t.w)
        if out_t is not None:
            q.wait(out_t.w)
            for kk, tok in out_t.r.items():
                q.wait(tok)
        own = out_t if (out_t is not None and out_t.sbuf) else in_t
        if own.dsem is None:
            own.dsem = {}
        if q.key not in own.dsem:
            own.dsem[q.key] = self.get_dsem()
        ds = own.dsem[q.key]
        inst = q.h.dma_start(out=out_ap, in_=in_ap)
        ds[1] += 1
        inst.then_inc(ds[0], 16)
        tok = Tok(ds[0], 16 * ds[1], ("d", id(ds)))
        if out_t is not None:
            out_t.w = tok
            out_t.r = {}
        if in_t is not None:
            in_t.r[tok.key] = tok
        if is_output:
            self.out_toks[tok.key] = tok

    def finish(self):
        for tok in self.out_toks.values():
            self.sp.wait(tok)
        self.barrier([])

    def mm(self, out_t, out_ap, a_t, a_ap, b_t, b_ap, start=True, stop=True, inc=None):
        nc = self.nc
        ins = [a_t, b_t] if b_t is not a_t else [a_t]
        return self.op(self.pe, lambda: nc.tensor.matmul(out_ap, a_ap, b_ap, start=start, stop=stop),
                       outs=[out_t], ins=ins, inc=(stop if inc is None else inc))

    def tr(self, out_t, out_ap, a_t, a_ap, id_t, id_ap, inc=True):
        nc = self.nc
        return self.op(self.pe, lambda: nc.tensor.transpose(out_ap, a_ap, id_ap), outs=[out_t], ins=[a_t, id_t], inc=inc)

    def actv(self, out_t, out_ap, in_t, in_ap, func, bias=None, scale=None, accum=None, extra_ins=(), eng=None):
        nc = self.nc
        kw = {}
        if bias is not None:
            kw["bias"] = bias
        if scale is not None:
            kw["scale"] = scale
        outs = [out_t]
        if accum is not None:
            kw["accum_out"] = accum[1]
            outs.append(accum[0])
        return self.op(self.act, lambda: nc.scalar.activation(out_ap, in_ap, func, **kw), outs=outs,
                       ins=[in_t] + list(extra_ins))

    def ts(self, e, out_t, out_ap, in_t, in_ap, s1, s2, op0, op1=None, accum=None, extra_ins=()):
        outs = [out_t]
        kw = {}
        if op1 is not None:
            kw["op1"] = op1
        if accum is not None:
            kw["accum_out"] = accum[1]
            outs.append(accum[0])
        return self.op(e, lambda: e.h.tensor_scalar(out_ap, in_ap, s1, s2, op0, **kw), outs=outs,
                       ins=[in_t] + list(extra_ins))

    def tt(self, e, out_t, out_ap, a_t, a_ap, b_t, b_ap, op):
        ins = [a_t, b_t] if b_t is not a_t else [a_t]
        return self.op(e, lambda: e.h.tensor_tensor(out_ap, a_ap, b_ap, op), outs=[out_t], ins=ins)

    def stt(self, e, out_t, out_ap, a_t, a_ap, scalar, b_t, b_ap, op0, op1, extra_ins=()):
        ins = [a_t] + ([b_t] if b_t is not a_t else []) + list(extra_ins)
        return self.op(e, lambda: e.h.scalar_tensor_tensor(out_ap, a_ap, scalar, b_ap, op0, op1), outs=[out_t], ins=ins)

    def cp(self, e, out_t, out_ap, in_t, in_ap):
        if e is self.act:
            nc = self.nc
            return self.op(e, lambda: nc.scalar.copy(out_ap, in_ap), outs=[out_t], ins=[in_t])
        return self.op(e, lambda: e.h.tensor_copy(out_ap, in_ap), outs=[out_t], ins=[in_t])

    def memset(self, e, out_t, out_ap, val):
        return self.op(e, lambda: e.h.memset(out_ap, val), outs=[out_t], ins=[])


def make_consts():
    c = {}
    c["ident"] = np.eye(128, dtype=np.float32)
    c["ones"] = np.ones((128, 128), np.float32)
    i2 = np.zeros((128, 256), np.float32)
    i2[:, 0:128] = np.eye(128)
    i2[:, 128:256] = np.eye(128)
    c["i2"] = i2
    c["i4"] = np.tile(np.eye(128, dtype=np.float32), (1, 4))
    c["i4s"] = np.tile(np.eye(16, dtype=np.float32), (1, 4))
    i2s = np.zeros((16, 32), np.float32)
    i2s[:, 0:16] = np.eye(16)
    i2s[:, 16:32] = np.eye(16)
    c["i2s"] = i2s
    tt_, cc_ = np.meshgrid(np.arange(64), np.arange(64), indexing="ij")
    c["U"] = (tt_ <= cc_).astype(np.float32)
    c["Urev"] = (tt_ > cc_).astype(np.float32)
    c["maskT"] = np.where(cc_ >= tt_, 0.0, NEG).astype(np.float32)
    c["su01"] = (cc_ > tt_).astype(np.float32)
    c["caus01"] = (cc_ >= tt_).astype(np.float32)
    c["negones"] = -np.ones((128, 128), np.float32)
    c["pow2"] = np.tile((2.0 ** -(np.arange(32) + 1.0)).astype(np.float32)[None, :], (128, 1))
    return c


class Prog:
    def __init__(self, debug=False, stop_after=None):
        self.debug = debug
        self.stop_after = stop_after
        nc = bass.Bass("TRN2", target_bir_lowering=False)
        self.nc = nc
        self.k = K(nc)

        def din(name, shape):
            return nc.dram_tensor(name, list(shape), F32, kind="ExternalInput").ap()

        def dout(name, shape):
            return nc.dram_tensor(name, list(shape), F32, kind="ExternalOutput").ap()

        self.x_p = din("x_p", [T, D])
        self.x_s = din("x_s", [NSQ * TS, D])
        self.cache_k = din("cache_k", [DEPTH, NSQ, T, 128])
        self.cache_v = din("cache_v", [DEPTH, NSQ, T, 128])
        self.cache_ki = din("cache_ki", [DEPTH, NSQ, T, 64])
        self.state_conv = din("state_conv", [DEPTH, NSQ, 3, 1536])
        self.state_ssm = din("state_ssm", [DEPTH, NSQ, 4, 128, 128])
        self.w_in = din("w_in", [DEPTH, D, DIN])
        self.b_gate = din("b_gate", [DEPTH, 2048])
        self.conv_w = din("conv_w", [DEPTH, 4, 1536])
        self.a_log = din("a_log", [DEPTH, 4])
        self.dt_bias = din("dt_bias", [DEPTH, 4])
        self.gdn_norm_w = din("gdn_norm_w", [DEPTH, 128])
        self.w_proj_a = din("w_proj_a", [DEPTH, 512, D])
        self.w_proj_b = din("w_proj_b", [DEPTH, 512, D])
        self.w_out = din("w_out", [DEPTH, D, D])
        self.ln1_g = din("ln1_g", [DEPTH, D])
        self.ln1_b = din("ln1_b", [DEPTH, D])
        self.w_up = din("w_up", [DEPTH, D, 2 * DFF])
        self.w_down = din("w_down", [DEPTH, DFF, D])
        self.ln2_g = din("ln2_g", [DEPTH, D])
        self.ln2_b = din("ln2_b", [DEPTH, D])
        self.cst = {n: din("c_" + n, list(v.shape)) for n, v in make_consts().items()}

        self.y_p = dout("y_p", [T, D])
        self.y_s = dout("y_s", [NSQ * TS, D])
        self.nk_p = dout("nk_p", [DEPTH, T, 128])
        self.nv_p = dout("nv_p", [DEPTH, T, 128])
        self.nki_p = dout("nki_p", [DEPTH, T, 64])
        self.nconv_p = dout("nconv_p", [DEPTH, 3, 1536])
        self.nssm_p = dout("nssm_p", [DEPTH, 4, 128, 128])
        self.nk_s = dout("nk_s", [DEPTH, NSQ * TS, 128])
        self.nv_s = dout("nv_s", [DEPTH, NSQ * TS, 128])
        self.nki_s = dout("nki_s", [DEPTH, NSQ * TS, 64])
        self.nconv_s = dout("nconv_s", [DEPTH, NSQ, 3, 1536])
        self.nssm_s = dout("nssm_s", [DEPTH, NSQ, 4, 128, 128])

        skind = "ExternalOutput" if debug else "Internal"
        self.YT = TT(nc.dram_tensor("scr_yt", [D, NTOK], BF16, kind=skind).ap(), "YT", sbuf=False)
        self.X1 = TT(nc.dram_tensor("scr_x1", [NTOK, D], F32, kind=skind).ap(), "X1", sbuf=False)
        self.X2 = TT(nc.dram_tensor("scr_x2", [NTOK, D], F32, kind=skind).ap(), "X2", sbuf=False)
        self.build()

    def x_rows(self, l, tok0, n):
        if l == 0:
            if tok0 < T:
                return None, self.x_p[tok0:tok0 + n, :]
            return None, self.x_s[tok0 - T:tok0 - T + n, :]
        return self.X2, self.X2[tok0:tok0 + n, :]

    def load_consts(self):
        k = self.k
        nc = self.nc
        es = ExitStack()
        self.ces = es
        self.ctiles = []

        def csb(name, shape, dt):
            t = es.enter_context(nc.sbuf_tensor("k_" + name, list(shape), dt))
            tt = TT(t, name)
            self.ctiles.append(tt)
            return tt

        self.ident = csb("ident", [128, 128], F32)
        k.dma(k.sp, self.ident[:, :], self.cst["ident"][:, :], out_t=self.ident)
        self.ones = csb("ones", [128, 128], F32)
        k.dma(k.sp, self.ones[:, :], self.cst["ones"][:, :], out_t=self.ones)
        self.i2 = csb("i2", [128, 256], BF16)
        k.dma(k.pool, self.i2[:, :], self.cst["i2"][:, :], out_t=self.i2)
        self.i2s = csb("i2s", [16, 32], BF16)
        k.dma(k.pool, self.i2s[:, :], self.cst["i2s"][:, :], out_t=self.i2s)
        self.i4 = csb("i4", [128, 512], BF16)
        k.dma(k.pool, self.i4[:, :], self.cst["i4"][:, :], out_t=self.i4)
        self.i4s = csb("i4s", [16, 64], BF16)
        k.dma(k.pool, self.i4s[:, :], self.cst["i4s"][:, :], out_t=self.i4s)
        self.pow2 = csb("pow2", [128, 32], F32)
        k.dma(k.sp, self.pow2[:, :], self.cst["pow2"][:, :], out_t=self.pow2)
        for nm in ["U", "Urev", "maskT", "su01", "caus01"]:
            t = csb(nm, [64, 64], F32)
            k.dma(k.sp, t[:, :], self.cst[nm][:, :], out_t=t)
            setattr(self, "c_" + nm, t)
        self.negones = csb("negones", [128, 128], F32)
        k.dma(k.sp, self.negones[:, :], self.cst["negones"][:, :], out_t=self.negones)

    def load_xT(self, l, tok0, ntok, xin, xT, col0, evac):
        k = self.k
        src_t, src = self.x_rows(l, tok0, ntok)
        k.dma(k.sp, xin[0:ntok, :], src, out_t=xin, in_t=src_t)
        for grp in range(2):
            ps = k.ps[6 + grp]
            for kk in range(4):
                kc = grp * 4 + kk
                k.tr(ps, ps[:, kk * ntok:(kk + 1) * ntok], xin, xin[0:ntok, kc * 128:(kc + 1) * 128],
                     self.ident, self.ident[0:ntok, 0:ntok], inc=(kk == 3))
            k.cp(evac, xT, xT[:, grp * 4:(grp + 1) * 4, col0:col0 + ntok],
                 ps, ps[:, 0:4 * ntok].rearrange("p (a t) -> p a t", a=4))

    def p1a(self, l):
        k = self.k
        nc = self.nc
        k.begin_pass()
        wsrc = self.w_in[l].rearrange("(kc p) e -> p kc e", p=128)
        Wfm = k.sb("Wfm", [128, 8, 1280], BF16)
        Wtm = k.sb("Wtm", [128, 8, 328], BF16)

        def wl(dst_t, d0, s0, n):
            k.dma(k.pool, dst_t[:, :, d0:d0 + n], wsrc[:, :, s0:s0 + n], out_t=dst_t)

        for m in range(4):
            wl(Wfm, m * 128, O_QA + m * 64, 64)
            wl(Wfm, m * 128 + 64, O_QA + (4 + m) * 64, 64)
        wl(Wfm, 512, O_KA, 128)
        wl(Wfm, 640, O_QI, 512)
        wl(Wfm, 1152, O_KI, 64)
        wl(Wfm, 1216, O_KI, 64)
        wl(Wtm, 0, O_KA, 256)
        wl(Wtm, 256, O_KI, 64)
        wl(Wtm, 320, O_WI, 8)

        kTb = k.sb("kTb", [128, KCOLS], BF16)
        kiT2 = k.sb("kiT2", [128, KCOLS], BF16)
        vext = k.sb("vext", [128, 33, 2, 65], BF16)
        k.memset(k.pool, vext, vext[:, :, :, 64:65], 1.0)
        xin = [k.sb(f"xin{i}", [128, D], F32) for i in range(2)]
        xT = k.sb("xT", [128, 8, 512], BF16)
        qaLo = [k.sb(f"qaLo{i}", [128, 4, 512], BF16) for i in range(2)]
        qaHi = [k.sb(f"qaHi{i}", [128, 4, 512], BF16) for i in range(2)]
        qiLo = [k.sb(f"qiLo{i}", [128, 4, 512], BF16) for i in range(2)]
        qiHi = [k.sb(f"qiHi{i}", [128, 4, 512], BF16) for i in range(2)]
        for i in range(2):
            k.memset(k.pool, qaLo[i], qaLo[i][64:128, :, :], 0.0)
            k.memset(k.pool, qiLo[i], qiLo[i][64:128, :, :], 0.0)
            k.memset(k.pool, qaHi[i], qaHi[i][0:64, :, :], 0.0)
            k.memset(k.pool, qiHi[i], qiHi[i][0:64, :, :], 0.0)
        kfm = k.sb("kfm", [128, 3, 64], BF16)
        tm1 = [k.sb(f"tm1_{i}", [128, 328], F32) for i in range(2)]
        wscs = [k.sb(f"wsc{i}", [128, 4, 8], F32) for i in range(2)]
        Ss = [k.sb(f"S{i}", [128, KCOLS], F32) for i in range(2)]
        junk = k.sb("junk", [128, KCOLS], BF16)
        MBs = [k.sb(f"MB{i}", [128, KCOLS], BF16) for i in range(2)]
        R = [k.sb(f"R{i}", [128, 512], F32) for i in range(2)]
        PT = [k.sb(f"PT{i}", [128, 512], BF16) for i in range(3)]
        rd = k.sb("rd", [128, 1024], F32)
        bcs = k.sb("bcs", [64, 1024], F32)
        yTt = k.sb("yTt", [64, 1024], BF16)
        stt_ = k.sb("stat", [128, 8], F32)
        dtab = k.sb("dtab", [128, NBIS], F32)
        ndtab = k.sb("ndtab", [128, NBIS], F32)
        lt = k.sb("lt", [128, NBIS + 1, 2], F32)
        dpair = k.sb("dpair", [128, NBIS + 1, 2], F32)
        k.memset(k.pool, dpair, dpair[:, :, :], 0.0)
        cnt = k.sb("cnt", [128, NBIS], F32)
        dd = k.sb("dd", [128, NBIS], F32)
        ktm = k.sb("ktm", [128, 16, 128], F32)

        def stage_units(tok0, NS, Tq, nsub, key_col0, key_tile0, is_sample, par):
            wsc = wscs[par]
            units = []
            done = 0
            i = 0
            while done < NS:
                n = min(128, NS - done)
                units.append(lambda d=done, n=n, i=i: self.load_xT(l, tok0 + d, n, xin[i % 2], xT, d, k.act))
                done += n
                i += 1

            def fm(mt):
                ps = k.ps[6 + (mt % 2)]
                for kc in range(8):
                    k.mm(ps, ps[:, 0:NS], Wfm, Wfm[:, kc, mt * 128:(mt + 1) * 128], xT, xT[:, kc, 0:NS],
                         start=(kc == 0), stop=(kc == 7))
                if mt < 4:
                    k.cp(k.act, qaLo[par], qaLo[par][0:64, mt, 0:NS], ps, ps[0:64, 0:NS])
                    k.cp(k.act, qaHi[par], qaHi[par][64:128, mt, 0:NS], ps, ps[64:128, 0:NS])
                elif mt == 4:
                    if is_sample:
                        k.cp(k.act, kfm, kfm[:, 0, 0:NS], ps, ps[:, 0:NS])
                    else:
                        k.cp(k.act, kTb, kTb[:, key_col0:key_col0 + NS], ps, ps[:, 0:NS])
                elif mt < 9:
                    k.cp(k.act, qiLo[par], qiLo[par][0:64, mt - 5, 0:NS], ps, ps[0:64, 0:NS])
                    k.cp(k.act, qiHi[par], qiHi[par][64:128, mt - 5, 0:NS], ps, ps[64:128, 0:NS])
                else:
                    if is_sample:
                        k.cp(k.act, kfm, kfm[:, 2, 0:NS], ps, ps[:, 0:NS])
                    else:
                        k.cp(k.act, kiT2, kiT2[:, key_col0:key_col0 + NS], ps, ps[:, 0:NS])

            for mt in (5, 6, 7, 8, 9, 4):
                units.append(lambda mt=mt: fm(mt))

            def tmj(j):
                ps = k.ps[6 + (j % 2)]
                t1 = tm1[j % 2]
                for kc in range(8):
                    k.mm(ps, ps[0:Tq, 0:328], xT, xT[:, kc, j * Tq:(j + 1) * Tq], Wtm, Wtm[:, kc, 0:328],
                         start=(kc == 0), stop=(kc == 7))
                k.cp(k.act, t1, t1[0:Tq, :], ps, ps[0:Tq, 0:328])
                r0 = tok0 + j * Tq
                if is_sample:
                    r0 -= T
                    dk, dv, dki = self.nk_s, self.nv_s, self.nki_s
                else:
                    dk, dv, dki = self.nk_p, self.nv_p, self.nki_p
                k.dma(k.pool, dk[l, r0:r0 + Tq, :], t1[0:Tq, 0:128], in_t=t1, is_output=True)
                k.dma(k.pool, dv[l, r0:r0 + Tq, :], t1[0:Tq, 128:256], in_t=t1, is_output=True)
                k.dma(k.pool, dki[l, r0:r0 + Tq, :], t1[0:Tq, 256:320], in_t=t1, is_output=True)
                if not is_sample:
                    kt = key_tile0 + j
                    k.cp(k.pool, vext, vext[0:Tq, kt, :, 0:64], t1, t1[0:Tq, 128:256].rearrange("p (g d) -> p g d", g=2))
                k.ts(k.pool, wsc, wsc[0:Tq, j, :], t1, t1[0:Tq, 320:328], INDEX_SCALE, None, ALU.mult)

            for j in range(nsub):
                units.append(lambda j=j: tmj(j))
            for mt in range(4):
                units.append(lambda mt=mt: fm(mt))
            return units

        def stage_a(*args):
            for u in stage_units(*args):
                u()

        class TD:
            pass

        def idx_units(td, pending=None, quota=0):
            P = slice(0, td.Tq)
            qc = slice(td.j * td.Tq, (td.j + 1) * td.Tq)
            S, wsc, n = td.S, td.wsc, td.n
            nblk = (n + 511) // 512
            U = []

            def blk(kb):
                c0 = kb * 512
                w = min(512, n - c0)
                for h in range(8):
                    m, half = divmod(h, 2)
                    ps = k.ps[half]
                    qi = (qiLo if half == 0 else qiHi)[td.par]
                    k.mm(ps, ps[P, 0:w], qi, qi[:, m, qc], kiT2, kiT2[:, c0:c0 + w])
                    Rt = R[h % 2]
                    k.actv(Rt, Rt[P, 0:w], ps, ps[P, 0:w], AF.Relu)
                    if h == 0:
                        k.ts(k.dve, S, S[P, c0:c0 + w], Rt, Rt[P, 0:w], wsc[P, td.j, 0:1], None, ALU.mult, extra_ins=[wsc])
                    else:
                        k.stt(k.dve, S, S[P, c0:c0 + w], Rt, Rt[P, 0:w], wsc[P, td.j, h:h + 1], S, S[P, c0:c0 + w],
                              ALU.mult, ALU.add, extra_ins=[wsc])
                if kb == nblk - 1 and td.corner:
                    k.memset(k.dve, S, S[0:64, n - 64:n], -1.0e30)

            take = []
            if pending:
                for _ in range(min(quota, len(pending))):
                    take.append(pending.pop(0))
            per = (len(take) + nblk - 1) // nblk if take else 0
            for kb in range(nblk):
                U.append(lambda kb=kb: blk(kb))
                for _ in range(per):
                    if take:
                        U.append(take.pop(0))
            U.extend(take)
            return U

        def bis_units(td):
            Tq = td.Tq
            P = slice(0, Tq)
            S, MB, n = td.S, td.MB, td.n
            st = stt_
            U = []

            def pro():
                k.op(k.dve, lambda: nc.vector.tensor_reduce(st[P, 0:1], S[P, 0:n], AX.X, ALU.max), outs=[st], ins=[S])
                if td.corner:
                    k.op(k.dve, lambda: nc.vector.tensor_reduce(st[P, 1:2], S[P, 0:n - 64], AX.X, ALU.min), outs=[st], ins=[S])
                    k.op(k.dve, lambda: nc.vector.tensor_reduce(st[64:128, 2:3], S[64:128, n - 64:n], AX.X, ALU.min), outs=[st], ins=[S])
                    k.tt(k.dve, st, st[64:128, 1:2], st, st[64:128, 1:2], st, st[64:128, 2:3], ALU.min)
                else:
                    k.op(k.dve, lambda: nc.vector.tensor_reduce(st[P, 1:2], S[P, 0:n], AX.X, ALU.min), outs=[st], ins=[S])
                k.tt(k.dve, st, st[P, 3:4], st, st[P, 0:1], st, st[P, 1:2], ALU.subtract)
                k.ts(k.dve, dtab, dtab[P, :], self.pow2, self.pow2[P, 0:NBIS], st[P, 3:4], None, ALU.mult, extra_ins=[st])
                k.cp(k.dve, dpair, dpair[P, 0:NBIS, 1], dtab, dtab[P, :])
                k.cp(k.dve, lt, lt[P, 0, 0:1], st, st[P, 1:2])
                k.tt(k.dve, lt, lt[P, 0, 1:2], st, st[P, 1:2], dtab, dtab[P, 0:1], ALU.add)
            U.append(pro)

            def it_(it):
                trial = lt[P, it, 1:2]
                if td.cnt_eng == "act":
                    k.actv(junk, junk[P, 0:n], S, S[P, 0:n], AF.Sign, bias=trial, scale=-1.0,
                           accum=(cnt, cnt[P, it:it + 1]), extra_ins=[lt])
                    thr, cmp = float(n - 511), ALU.is_le
                else:
                    k.ts(k.dve, junk, junk[P, 0:n], S, S[P, 0:n], trial, None, ALU.is_ge, ALU.add,
                         accum=(cnt, cnt[P, it:it + 1]), extra_ins=[lt])
                    thr, cmp = 255.5, ALU.is_ge
                k.ts(k.dve, dd, dd[P, 0:2], cnt, cnt[P, it:it + 1].to_broadcast([Tq, 2]), thr, dtab[P, it:it + 1], cmp, ALU.mult,
                     extra_ins=[dtab])
                k.stt(k.dve, lt, lt[P, it + 1, :], dd, dd[P, 0:2], lt[P, it, 0:1], dpair, dpair[P, it + 1, :], ALU.add, ALU.add)
            for it in range(NBIS):
                U.append(lambda it=it: it_(it))

            def epi():
                k.ts(k.dve, MB, MB[P, 0:n], S, S[P, 0:n], lt[P, NBIS, 0:1], NEG, ALU.is_lt, ALU.mult, extra_ins=[lt])
            U.append(epi)
            return U

        def att_units(td):
            Tq = td.Tq
            P = slice(0, Tq)
            qc = slice(td.j * Tq, (td.j + 1) * Tq)
            MB, i4 = td.MB, td.i4
            qlo, qhi = qaLo[td.par], qaHi[td.par]
            W4 = 4 * Tq
            nkt = len(td.keytiles)
            units = [(g, ti) for g in range(2) for ti in range(nkt)]

            def s_part(u):
                g, ti = u
                c0, nk, vt = td.keytiles[ti]
                psS = k.ps[2 + (u[0] * nkt + ti) % 2]
                qg = qlo if g == 0 else qhi
                k.mm(psS, psS[0:nk, 0:W4], MB, MB[P, c0:c0 + nk], i4, i4[P, 0:W4], start=True, stop=False)
                k.mm(psS, psS[0:nk, 0:W4].rearrange("p (a t) -> p a t", a=4), kTb, kTb[:, c0:c0 + nk],
                     qg, qg[:, 0:4, qc], start=False, stop=True)
                PTt = PT[(u[0] * nkt + ti) % 3]
                k.actv(PTt, PTt[0:nk, 0:W4], psS, psS[0:nk, 0:W4], AF.Exp, scale=0.125)

            def v_part(u):
                g, ti = u
                c0, nk, vt = td.keytiles[ti]
                Og = k.ps[4 + g]
                PTt = PT[(u[0] * nkt + ti) % 3]
                k.mm(Og, Og[0:65, 0:W4], vext, vext[0:nk, vt, g, :], PTt, PTt[0:nk, 0:W4],
                     start=(ti == 0), stop=(ti == nkt - 1))

            U = []
            for ui, u in enumerate(units):
                def both(ui=ui, u=u):
                    s_part(u)
                    if ui >= 1:
                        v_part(units[ui - 1])
                U.append(both)
            U.append(lambda: v_part(units[-1]))
            return U

        def run_all(units):
            for u in units:
                u()

        def idx(td):
            run_all(idx_units(td))

        def bis(td):
            run_all(bis_units(td))

        def att_main(td):
            run_all(att_units(td))

        def merge(lists):
            items = []
            for li, L_ in enumerate(lists):
                for j, u in enumerate(L_):
                    items.append(((j + 0.5) / len(L_), li, j, u))
            items.sort(key=lambda x: (x[0], x[1], x[2]))
            return [x[3] for x in items]

        def att_fin(td):
            Tq = td.Tq
            W4 = 4 * Tq
            for g in range(2):
                Og = k.ps[4 + g]
                k.actv(rd, rd[64:65, g * 512:g * 512 + W4], Og, Og[64:65, 0:W4], AF.Ln)
                k.actv(rd, rd[64:65, g * 512:g * 512 + W4], rd, rd[64:65, g * 512:g * 512 + W4], AF.Exp, scale=-1.0)
                psB = k.ps[6 + g]
                k.mm(psB, psB[0:64, 0:W4], self.ones, self.ones[64:65, 0:64], rd, rd[64:65, g * 512:g * 512 + W4])
                k.cp(k.act, bcs, bcs[:, g * 512:g * 512 + W4], psB, psB[0:64, 0:W4])
                k.tt(k.dve, yTt, yTt[:, g * 512:g * 512 + W4], Og, Og[0:64, 0:W4], bcs, bcs[:, g * 512:g * 512 + W4], ALU.mult)
                dst = self.YT[256 * g:256 * g + 256, td.tok_out0:td.tok_out0 + Tq].rearrange("(c d) t -> d c t", d=64)
                k.dma(k.pool, dst, yTt[:, g * 512:g * 512 + W4].rearrange("d (c t) -> d c t", c=4), out_t=self.YT, in_t=yTt)

        tds = []
        for st in range(T // 512):
            for j in range(4):
                td = TD()
                i = st * 4 + j
                td.st, td.j, td.Tq = st, j, 128
                td.n = st * 512 + (j + 1) * 128
                td.corner = True
                td.keytiles = [(t * 128, 128, t) for t in range(td.n // 128)]
                td.tok_out0 = st * 512 + j * 128
                td.i4 = self.i4
                td.S, td.MB = Ss[i % 2], MBs[i % 2]
                td.par, td.wsc = st % 2, wscs[st % 2]
                td.cnt_eng = "act" if (i % 2 == 1) else "dve"
                tds.append(td)
        nt = len(tds)
        stage_a(0, 512, 128, 4, 0, 0, False, 0)
        pending = []
        for i in range(nt + 2):
            lists = []
            if 1 <= i <= nt:
                lists.append(bis_units(tds[i - 1]))
            if 2 <= i:
                lists.append(att_units(tds[i - 2]))
            if i < nt:
                td = tds[i]
                nst = td.st + 1
                if td.j == 0:
                    pending = stage_units(nst * 512, 512, 128, 4, nst * 512, nst * 4, False, nst % 2) if nst < T // 512 else []
                quota = (len(pending) + (3 - td.j)) // (4 - td.j)
                if td.j < 2:
                    quota = min(quota, max(0, len(pending) - 4))
                lists.append(idx_units(td, pending, quota))
            run_all(merge(lists))
            if 2 <= i:
                att_fin(tds[i - 2])

        stage_a(T, NSQ * TS, TS, NSQ, 0, 0, True, 0)
        sds = []
        for s in range(NSQ):
            td = TD()
            td.st, td.j, td.Tq = 0, s, TS
            td.n = T + TS
            td.corner = False
            td.keytiles = [(t * 128, 128, t) for t in range(32)] + [(T, TS, 32)]
            td.tok_out0 = T + s * TS
            td.i4 = self.i4s
            td.S, td.MB = Ss[s % 2], MBs[s % 2]
            td.par, td.wsc = 0, wscs[0]
            td.cnt_eng = "act"
            sds.append(td)

        def prep_k_units(s, which):
            if which == 0:
                src = self.cache_k[l, s].rearrange("(t p) c -> p t c", p=128)
                dst = kTb
            else:
                src = self.cache_ki[l, s].rearrange("(t p) c -> p t c", p=128)
                dst = kiT2
            U = []

            def half(hf):
                if which == 0:
                    k.dma(k.sp, ktm[:, :, :], src[:, hf * 16:(hf + 1) * 16, :], out_t=ktm)
                else:
                    k.dma(k.sp, ktm[:, :, 0:64], src[:, hf * 16:(hf + 1) * 16, :], out_t=ktm)
                    k.dma(k.sp, ktm[:, :, 64:128], src[:, hf * 16:(hf + 1) * 16, :], out_t=ktm)
                for b4 in range(4):
                    blk = hf * 4 + b4
                    ps = k.ps[6 + (blk % 2)]
                    for a in range(4):
                        k.tr(ps, ps[:, a * 128:(a + 1) * 128], ktm, ktm[:, b4 * 4 + a, :], self.ident, self.ident[:, :], inc=(a == 3))
                    k.cp(k.act, dst, dst[:, blk * 512:(blk + 1) * 512], ps, ps[:, :])
            U.append(lambda: half(0))
            U.append(lambda: half(1))
            U.append(lambda: k.cp(k.pool, dst, dst[:, T:T + TS], kfm, kfm[:, which, s * TS:(s + 1) * TS]))
            return U

        def prep_kv_units(s):
            U = prep_k_units(s, 0)

            def vv():
                for g in range(2):
                    k.dma(k.pool, vext[:, 0:32, g, 0:64],
                          self.cache_v[l, s].rearrange("(t p) c -> p t c", p=128)[:, :, g * 64:(g + 1) * 64], out_t=vext)
                ps = k.ps[6 + (s % 2)]
                for kc in range(8):
                    k.mm(ps, ps[0:TS, 0:128], xT, xT[:, kc, s * TS:(s + 1) * TS], Wtm, Wtm[:, kc, 128:256],
                         start=(kc == 0), stop=(kc == 7))
                k.cp(k.act, vext, vext[0:TS, 32, :, 0:64], ps, ps[0:TS, 0:128].rearrange("p (g d) -> p g d", g=2))
            U.append(vv)
            return U

        run_all(prep_k_units(0, 2))
        idx(sds[0])
        for s in range(NSQ + 1):
            lists = []
            A = []
            if s >= 1:
                A += att_units(sds[s - 1]) + [lambda s=s: att_fin(sds[s - 1])]
            if s < NSQ:
                A += prep_kv_units(s)
            if A:
                lists.append(A)
            if s < NSQ:
                lists.append(bis_units(sds[s]))
            if s + 1 < NSQ:
                lists.append(prep_k_units(s + 1, 2) + idx_units(sds[s + 1]))
            run_all(merge(lists))
        k.end_pass()


    def rows_to_fm(self, src_ap, nrows, rows_t, dst_t, dst_fn):
        k = self.k
        k.dma(k.sp, rows_t[0:nrows, :], src_ap, out_t=rows_t)
        for grp in range(3):
            ps = k.ps[6 + (grp % 2)]
            for a in range(4):
                m = grp * 4 + a
                k.tr(ps, ps[:, a * nrows:(a + 1) * nrows], rows_t, rows_t[0:nrows, m * 128:(m + 1) * 128],
                     self.ident, self.ident[0:nrows, 0:nrows], inc=(a == 3))
            for a in range(4):
                m = grp * 4 + a
                k.cp(k.act, dst_t, dst_fn(m), ps, ps[:, a * nrows:(a + 1) * nrows])

    def p1b(self, l):
        k = self.k
        nc = self.nc
        k.begin_pass()
        wsrc = self.w_in[l].rearrange("(kc p) e -> p kc e", p=128)
        Wfm = k.sb("Wfm", [128, 8, 1536], BF16)
        Wtm = k.sb("Wtm", [128, 8, 520], BF16)
        k.dma(k.pool, Wfm[:, :, :], wsrc[:, :, O_QKV:O_QKV + 1536], out_t=Wfm)
        k.dma(k.pool, Wtm[:, :, 0:8], wsrc[:, :, O_AB:O_AB + 8], out_t=Wtm)
        k.dma(k.pool, Wtm[:, :, 8:520], wsrc[:, :, O_GB:O_GB + 512], out_t=Wtm)
        rows_t = k.sb("rows", [16, 1536], F32)
        cw = k.sb("cw", [128, 12, 4], F32)
        self.rows_to_fm(self.conv_w[l], 4, rows_t, cw, lambda m: cw[:, m, :])
        nw = k.sb("nw", [128, 128], F32)
        k.dma(k.sp, nw[:, :], self.gdn_norm_w[l:l + 1, :].to_broadcast([128, 128]), out_t=nw)
        dtb = k.sb("dtb", [128, 4], F32)
        k.dma(k.sp, dtb[:, :], self.dt_bias[l:l + 1, :].to_broadcast([128, 4]), out_t=dtb)
        negA = k.sb("negA", [128, 4], F32)
        k.dma(k.sp, negA[:, :], self.a_log[l:l + 1, :].to_broadcast([128, 4]), out_t=negA)
        k.actv(negA, negA[:, :], negA, negA[:, :], AF.Exp)
        k.ts(k.dve, negA, negA[:, :], negA, negA[:, :], -1.0, None, ALU.mult)

        xin = [k.sb("xin0", [128, D], F32)] * 2
        xT = k.sb("xT", [128, 8, 512], BF16)
        cin = k.sb("cin", [128, 12, 515], F32)
        qkvs = k.sb("qkvs", [128, 12, 512], F32)
        qkb = k.sb("qkb", [128, 8, 512], BF16)
        Sbf = k.sb("Sbf", [128, NSQ, 4, 128], BF16)
        acc = [k.sb(f"acc{i}", [128, 512], F32) for i in range(2)]
        sq = [k.sb(f"sq{i}", [128, 512], F32) for i in range(2)]
        rn = [k.sb(f"rn{i}", [128, 512], F32) for i in range(2)]
        s2s = [k.sb(f"s2_{i}", [128, 512], F32) for i in range(2)]
        Sst = k.sb("Sst", [128, NSQ, 4, 128], F32)
        last3 = rows_t
        NM = 16

        def smal(name, shape):
            return k.sb(name, shape, F32)

        NCK = 8
        gx = smal("gx", [64, NCK, 4])
        gab = smal("gab", [64, NCK, 4])
        ge1 = smal("ge1", [64, NCK, 4])
        gl1 = smal("gl1", [64, NCK, 4])
        gsp = smal("gsp", [64, NCK, 4])
        gg = smal("gg", [64, NCK, 4])
        nbeta = smal("nbeta", [64, NCK, 4])
        beta = smal("beta", [64, NCK, 4])
        E = smal("E", [64, NCK, 12])
        negeG = smal("negeG", [64, NCK, 4])
        gt128 = smal("gt128", [128, NCK, 4])
        sgA = smal("sgA", [64, NCK, 512])
        sgtmp = smal("sgtmp", [64, 512])

        class MV:
            def __init__(self, name, dt=F32):
                self.tt = k.sb(name, [64, 512], dt)
                self.C = 64
                self.nm = 8

            def set(self, C, nm):
                self.C = C
                self.nm = nm

            def v(self):
                return self.tt[0:self.C, 0:self.nm * self.C].rearrange("p (m c) -> p m c", m=self.nm)

        gU_, DT_, DsB_, W0f_, DTc_, dtmp_ = MV("gU"), MV("DT"), MV("DsB"), MV("W0f"), MV("DTc"), MV("dtmp")
        W_ = [MV(f"W{i}", BF16) for i in range(2)]
        X_ = [MV(f"X{i}", BF16) for i in range(2)]
        NT_ = [MV(f"NT{i}", BF16) for i in range(2)]
        NTf_ = [MV(f"NTf{i}", BF16) for i in range(2)]
        qkT2_ = [MV(f"qkT{i}", BF16) for i in range(2)]
        allmv = [gU_, DT_, DsB_, W0f_, DTc_, dtmp_] + W_ + X_ + NT_
        negG = smal("negG", [64, NCK, 4])
        kdec2 = [k.sb(f"kdec{i}", [64, 2, 4, 128], BF16) for i in range(2)]
        vtm2 = [smal(f"vtm{i}", [64, 2, 4, 128]) for i in range(2)]
        Z = k.sb("Z", [64, 4, 128], BF16)
        Zf = smal("Zf", [64, 4, 128])
        t1 = smal("t1", [64, 4, 128])
        vnew = k.sb("vnew", [64, 4, 128], BF16)
        o_t = smal("o", [64, 4, 128])
        osq = smal("osq", [64, 128])
        ss = smal("ss", [64, 4])
        rstd = smal("rstd", [64, 4])
        yb = smal("yb", [64, 4, 128])
        ybT = k.sb("ybT", [128, 4, 64], BF16)

        def bc(ap, shape, axis):
            return ap.unsqueeze(axis).to_broadcast(shape)

        def front_units(tok0, NS, nseq, L, is_sample, first):
            cinv = cin[:, :, 0:nseq * (L + 3)].rearrange("p m (s t) -> p m s t", s=nseq)
            U = []
            done = 0
            i = 0
            while done < NS:
                n = min(128, NS - done)
                U.append(lambda d=done, n=n, i=i: self.load_xT(l, tok0 + d, n, xin[i % 2], xT, d, k.act))
                done += n
                i += 1
            if is_sample:
                U.append(lambda: self.rows_to_fm(self.state_conv[l].rearrange("s j c -> (s j) c"), 12, rows_t, cin,
                                                 lambda m: cinv[:, m, :, 0:3]))
            elif first:
                U.append(lambda: k.memset(k.pool, cin, cinv[:, :, :, 0:3], 0.0))

            def fm(mt):
                ps = k.ps[6 + (mt % 2)]
                for kc in range(8):
                    k.mm(ps, ps[:, 0:NS], Wfm, Wfm[:, kc, mt * 128:(mt + 1) * 128], xT, xT[:, kc, 0:NS],
                         start=(kc == 0), stop=(kc == 7))
                k.cp(k.act, cin, cinv[:, mt, :, 3:3 + L], ps, ps[:, 0:NS].rearrange("p (s t) -> p s t", s=nseq))
            for mt in range(12):
                U.append(lambda mt=mt: fm(mt))
            return U

        def process(tok0, NS, nseq, L, C, is_sample, first, last_st, extra=()):
            extra = list(extra)
            cinv = cin[:, :, 0:nseq * (L + 3)].rearrange("p m (s t) -> p m s t", s=nseq)
            qv = qkvs[:, :, 0:NS].rearrange("p m (s t) -> p m s t", s=nseq)
            qb = qkb[:, :, 0:NS].rearrange("p m (s t) -> p m s t", s=nseq)
            if is_sample or last_st:
                for sq_ in range(nseq):
                    c1 = (sq_ + 1) * L
                    for blk in range(3):
                        ps = k.ps[6 + (blk % 2)]
                        for kc in range(8):
                            k.mm(ps, ps[0:3, 0:512], xT, xT[:, kc, c1 - 3:c1], Wfm, Wfm[:, kc, blk * 512:(blk + 1) * 512],
                                 start=(kc == 0), stop=(kc == 7))
                        k.cp(k.act, last3, last3[0:3, blk * 512:(blk + 1) * 512], ps, ps[0:3, 0:512])
                    dst = self.nconv_s[l, sq_] if is_sample else self.nconv_p[l]
                    k.dma(k.sp, dst, last3[0:3, :], in_t=last3, is_output=True)
            for m in range(12):
                a_ = acc[m % 2]
                av = a_[:, 0:NS].rearrange("p (s t) -> p s t", s=nseq)
                k.ts(k.dve, a_, av, cin, cinv[:, m, :, 0:L], cw[:, m, 0:1], None, ALU.mult, extra_ins=[cw])
                for jj in range(1, 4):
                    k.stt(k.dve, a_, av, cin, cinv[:, m, :, jj:jj + L], cw[:, m, jj:jj + 1], a_, av, ALU.mult, ALU.add,
                          extra_ins=[cw])
                s2 = s2s[m % 2]
                k.actv(s2, s2[:, 0:NS], a_, a_[:, 0:NS], AF.Exp, scale=-1.0)
                k.actv(s2, s2[:, 0:NS], s2, s2[:, 0:NS], AF.Ln, bias=1.0)
                k.actv(s2, s2[:, 0:NS], s2, s2[:, 0:NS], AF.Exp, scale=-1.0)
                k.tt(k.pool, qkvs, qkvs[:, m, 0:NS], a_, a_[:, 0:NS], s2, s2[:, 0:NS], ALU.mult)

            def n_sq(m):
                s_ = sq[m % 2]
                k.actv(s_, s_[:, 0:NS], qkvs, qkvs[:, m, 0:NS], AF.Square)
                ps = k.ps[6 + (m % 2)]
                k.mm(ps, ps[:, 0:NS], self.ones, self.ones[:, :], s_, s_[:, 0:NS])

            def n_fin(m):
                ps = k.ps[6 + (m % 2)]
                r_ = rn[m % 2]
                k.actv(r_, r_[:, 0:NS], ps, ps[:, 0:NS], AF.Ln, bias=NORM_EPS)
                k.actv(r_, r_[:, 0:NS], r_, r_[:, 0:NS], AF.Exp, scale=-0.5)
                k.stt(k.dve, qkvs, qkvs[:, m, 0:NS], qkvs, qkvs[:, m, 0:NS], (128.0 ** -0.5) if m < 4 else 1.0,
                      r_, r_[:, 0:NS], ALU.mult, ALU.mult)
                k.cp(k.act if m % 2 else k.pool, qkb, qkb[:, m, 0:NS], qkvs, qkvs[:, m, 0:NS])

            n_sq(0)
            for m in range(8):
                if m + 1 < 8:
                    n_sq(m + 1)
                n_fin(m)
            if not is_sample:
                k.cp(k.pool, cin, cinv[:, :, :, 0:3], cin, cinv[:, :, :, L:L + 3])

            nch = L // C
            if is_sample:
                batches = [[(0, 0), (1, 0)], [(2, 0), (3, 0)]]
            else:
                batches = [[(0, ci), (0, ci + 1)] for ci in range(0, nch, 2)]
            PC = slice(0, C)
            nb = 2
            nm = 8
            for mv in allmv:
                mv.set(C, nm)
            gU, DT, DsB, W0f, DTc, dtmp = gU_.tt, DT_.tt, DsB_.tt, W0f_.tt, DTc_.tt, dtmp_.tt
            gUv, DTv, DsBv, W0fv, DTcv, dtmpv = gU_.v(), DT_.v(), DsB_.v(), W0f_.v(), DTc_.v(), dtmp_.v()
            W = [w.tt for w in W_]
            Wv = [w.v() for w in W_]
            X = [x.tt for x in X_]
            Xv = [x.v() for x in X_]
            NT = [x.tt for x in NT_]
            NTv = [x.v() for x in NT_]
            mvv = lambda ps: ps[PC, 0:nm * C].rearrange("p (m c) -> p m c", m=nm)

            chunks = [c for bt in batches for c in bt]
            nck = len(chunks)

            def prep_super():
                for ci_, (sq_, ci) in enumerate(chunks):
                    col0 = sq_ * L + ci * C
                    psA = k.ps[0]
                    for kc in range(8):
                        k.mm(psA, psA[PC, 0:8], xT, xT[:, kc, col0:col0 + C], Wtm, Wtm[:, kc, 0:8], start=(kc == 0), stop=(kc == 7))
                    k.tt(k.dve, gx, gx[PC, ci_, :], psA, psA[PC, 0:4], dtb, dtb[PC, :], ALU.add)
                    k.actv(ge1, ge1[PC, ci_, :], psA, psA[PC, 4:8], AF.Exp, scale=-1.0)
                    psB = k.ps[1 + (ci_ % 2)]
                    for kc in range(8):
                        k.mm(psB, psB[PC, 0:512], xT, xT[:, kc, col0:col0 + C], Wtm, Wtm[:, kc, 8:520], start=(kc == 0), stop=(kc == 7))
                    k.actv(sgtmp, sgtmp[PC, :], psB, psB[PC, 0:512], AF.Exp, scale=-1.0)
                    k.actv(sgtmp, sgtmp[PC, :], sgtmp, sgtmp[PC, :], AF.Ln, bias=1.0)
                    k.actv(sgtmp, sgtmp[PC, :], sgtmp, sgtmp[PC, :], AF.Exp, scale=-1.0)
                    k.tt(k.dve, sgA, sgA[PC, ci_, :], psB, psB[PC, 0:512], sgtmp, sgtmp[PC, :], ALU.mult)
                    k.tt(k.pool, sgA, sgA[PC, ci_, :].rearrange("p (h e) -> p h e", h=4), sgA,
                         sgA[PC, ci_, :].rearrange("p (h e) -> p h e", h=4), nw, bc(nw[PC, :], [C, 4, 128], 1), ALU.mult)
                A_ = slice(0, nck)
                k.actv(ge1, ge1[PC, A_, :], ge1, ge1[PC, A_, :], AF.Ln, bias=1.0)
                k.actv(beta, beta[PC, A_, :], ge1, ge1[PC, A_, :], AF.Exp, scale=-1.0)
                gxa = gx[PC, A_, :]
                k.ts(k.dve, gab, gab[PC, A_, :], gx, gxa, -1.0, None, ALU.mult)
                k.tt(k.dve, gab, gab[PC, A_, :], gab, gab[PC, A_, :], gx, gxa, ALU.min)
                k.actv(gl1, gl1[PC, A_, :], gab, gab[PC, A_, :], AF.Exp)
                k.actv(gl1, gl1[PC, A_, :], gl1, gl1[PC, A_, :], AF.Ln, bias=1.0)
                k.stt(k.dve, gsp, gsp[PC, A_, :], gx, gxa, 0.0, gl1, gl1[PC, A_, :], ALU.max, ALU.add)
                k.tt(k.dve, gg, gg[PC, A_, :], gsp, gsp[PC, A_, :], negA, bc(negA[PC, :], [C, nck, 4], 1), ALU.mult)
                k.ts(k.dve, nbeta, nbeta[PC, A_, :], beta, beta[PC, A_, :], -1.0, None, ALU.mult)
                psG = k.ps[0]
                for ci_ in range(nck):
                    gcol = gg[PC, ci_, :]
                    k.mm(psG, psG[PC, ci_ * 16:ci_ * 16 + 4], self.c_U, self.c_U[PC, PC], gg, gcol)
                    k.mm(psG, psG[PC, ci_ * 16 + 4:ci_ * 16 + 8], self.c_Urev, self.c_Urev[PC, PC], gg, gcol)
                    k.mm(psG, psG[PC, ci_ * 16 + 8:ci_ * 16 + 12], self.ones, self.ones[PC, PC], gg, gcol)
                    k.mm(psG, psG[:, 256 + ci_ * 4:256 + ci_ * 4 + 4], self.ones, self.ones[PC, :], gg, gcol)
                pG = psG[PC, 0:nck * 16].rearrange("p (b e) -> p b e", b=nck)
                k.actv(E, E[PC, A_, :], psG, pG[:, :, 0:12], AF.Exp)
                k.actv(gt128, gt128[:, A_, :], psG, psG[:, 256:256 + nck * 4].rearrange("p (b e) -> p b e", b=nck), AF.Exp)
                k.ts(k.dve, negeG, negeG[PC, A_, :], E, E[PC, A_, 0:4], -1.0, None, ALU.mult)
                k.ts(k.dve, negG, negG[PC, A_, :], psG, pG[:, :, 0:4], -1.0, None, ALU.mult)

            prep_super()

            def prep_units(batch, par, b0):
                B_ = slice(b0, b0 + nb)
                kdecp, vtmp = kdec2[par], vtm2[par]
                NTfp, qkTp = NTf_[par], qkT2_[par]
                NTfp.set(C, nm)
                qkTp.set(C, nm)
                U = []

                def u_gu():
                    k.tt(k.dve, gU, gUv, self.c_U, bc(self.c_U[PC, PC], [C, nm, C], 1),
                         gg, bc(gg[PC, B_, :].rearrange("p b h -> p (b h)"), [C, nm, C], 2), ALU.mult)
                U.append(u_gu)

                def u_diff():
                    psD = k.ps[1]
                    for mi in range(nm):
                        k.mm(psD, psD[PC, mi * C:(mi + 1) * C], self.ones, self.ones[PC, PC], gU, gUv[:, mi, :])
                    k.tt(k.dve, dtmp, dtmpv, psD, mvv(psD), negG,
                         bc(negG[PC, B_, :].rearrange("p b h -> p (b h)"), [C, nm, C], 2), ALU.add)
                    k.ts(k.dve, dtmp, dtmpv, dtmp, dtmpv, 0.0, None, ALU.min)
                    k.actv(DT, DTv, dtmp, dtmpv, AF.Exp)
                    k.tt(k.pool, DTc, DTcv, DT, DTv, self.c_caus01, bc(self.c_caus01[PC, PC], [C, nm, C], 1), ALU.mult)
                    k.tt(k.pool, DsB, DsBv, DT, DTv, self.c_su01, bc(self.c_su01[PC, PC], [C, nm, C], 1), ALU.mult)
                    k.tt(k.pool, DsB, DsBv, DsB, DsBv, nbeta,
                         bc(nbeta[PC, B_, :].rearrange("p b h -> p (b h)"), [C, nm, C], 2), ALU.mult)
                U.append(u_diff)

                def u_kk():
                    psK = k.ps[2]
                    psQ = k.ps[3]
                    for bi, (sq_, ci) in enumerate(batch):
                        cs = slice(ci * C, (ci + 1) * C)
                        for h in range(4):
                            mi = bi * 4 + h
                            k.mm(psK, psK[PC, mi * C:(mi + 1) * C], qkb, qb[:, 4 + h, sq_, cs], qkb, qb[:, 4 + h, sq_, cs])
                            k.mm(psQ, psQ[PC, mi * C:(mi + 1) * C], qkb, qb[:, 4 + h, sq_, cs], qkb, qb[:, h, sq_, cs])
                    k.tt(k.dve, W0f, W0fv, psK, mvv(psK), DsB, DsBv, ALU.mult)
                    k.cp(k.pool, W[0], Wv[0], W0f, W0fv)
                    k.tt(k.dve, qkTp.tt, qkTp.v(), psQ, mvv(psQ), DTc, DTcv, ALU.mult)
                U.append(u_kk)

                def u_x0():
                    psX = k.ps[2]
                    for mi in range(nm):
                        k.tr(psX, psX[PC, mi * C:(mi + 1) * C], W0f, W0fv[:, mi, :], self.ident, self.ident[PC, PC], inc=(mi == nm - 1))
                    k.cp(k.act, X[0], Xv[0], psX, mvv(psX))
                    k.tt(k.dve, NT[0], NTv[0], W0f, W0fv, self.ident, bc(self.ident[PC, PC], [C, nm, C], 1), ALU.add)
                U.append(u_x0)

                nlev = int(round(math.log2(C)))
                for lev in range(1, nlev):
                    cur = (lev - 1) % 2
                    nxt = 1 - cur
                    lastlev = (lev == nlev - 1)

                    def u_x(cur=cur, nxt=nxt):
                        psX2 = k.ps[2]
                        for mi in range(nm):
                            k.mm(psX2, psX2[PC, mi * C:(mi + 1) * C], W[cur], Wv[cur][:, mi, :], X[cur], Xv[cur][:, mi, :])
                        k.cp(k.act, X[nxt], Xv[nxt], psX2, mvv(psX2))
                    U.append(u_x)
                    if not lastlev:
                        def u_w(cur=cur, nxt=nxt):
                            psW = k.ps[1]
                            for mi in range(nm):
                                k.mm(psW, psW[PC, mi * C:(mi + 1) * C], X[cur], Xv[cur][:, mi, :], W[cur], Wv[cur][:, mi, :])
                            k.cp(k.act, W[nxt], Wv[nxt], psW, mvv(psW))
                        U.append(u_w)

                    def u_p(cur=cur, nxt=nxt, lastlev=lastlev):
                        psP = k.ps[3]
                        for mi in range(nm):
                            k.mm(psP, psP[PC, mi * C:(mi + 1) * C], X[nxt], Xv[nxt][:, mi, :], NT[cur], NTv[cur][:, mi, :])
                        if lastlev:
                            k.tt(k.dve, NTfp.tt, NTfp.v(), psP, mvv(psP), NT[cur], NTv[cur], ALU.add)
                        else:
                            k.tt(k.dve, NT[nxt], NTv[nxt], psP, mvv(psP), NT[cur], NTv[cur], ALU.add)
                    U.append(u_p)

                def u_kv(bi):
                    sq_, ci = batch[bi]
                    cs = slice(ci * C, (ci + 1) * C)
                    psk = k.ps[0]
                    for h in range(4):
                        k.tr(psk, psk[PC, h * 128:(h + 1) * 128], qkvs, qv[:, 4 + h, sq_, cs], self.ident, self.ident[:, :], inc=(h == 3))
                    k.tt(k.dve, kdecp, kdecp[PC, bi, :, :], psk, psk[PC, :].rearrange("p (h e) -> p h e", h=4),
                         E, bc(E[PC, b0 + bi, 4:8], [C, 4, 128], 2), ALU.mult)
                    psv = k.ps[0]
                    for h in range(4):
                        k.tr(psv, psv[PC, h * 128:(h + 1) * 128], qkvs, qv[:, 8 + h, sq_, cs], self.ident, self.ident[:, :], inc=(h == 3))
                    k.cp(k.act, vtmp, vtmp[PC, bi, :, :], psv, psv[PC, :].rearrange("p (h e) -> p h e", h=4))
                for bi in range(nb):
                    U.append(lambda bi=bi: u_kv(bi))
                return U

            def rec_units(batch, par, b0):
                kdecp, vtmp = kdec2[par], vtm2[par]
                NTfv, qkTv = NTf_[par].v(), qkT2_[par].v()
                NTf, qkT = NTf_[par].tt, qkT2_[par].tt
                U = []
                for bi, (sq_, ci) in enumerate(batch):
                    cs = slice(ci * C, (ci + 1) * C)
                    Sv = Sst[:, sq_, :, :]
                    pskS, psqS, psNZ, psqkv, psdS = k.ps[4], k.ps[5], k.ps[6], k.ps[7], k.ps[6]

                    def r1(bi=bi, sq_=sq_, cs=cs, Sv=Sv):
                        for h in range(4):
                            k.mm(pskS, pskS[PC, h * 128:(h + 1) * 128], qkb, qb[:, 4 + h, sq_, cs], Sbf, Sbf[:, sq_, h, :], inc=(h == 3))
                        for h in range(4):
                            k.mm(psqS, psqS[PC, h * 128:(h + 1) * 128], qkb, qb[:, h, sq_, cs], Sbf, Sbf[:, sq_, h, :], inc=(h == 3))
                        k.tt(k.dve, Zf, Zf[PC, :, :], pskS, pskS[PC, :].rearrange("p (h e) -> p h e", h=4),
                             negeG, bc(negeG[PC, b0 + bi, :], [C, 4, 128], 2), ALU.mult)
                        k.tt(k.dve, Z, Z[PC, :, :], Zf, Zf[PC, :, :], vtmp, vtmp[PC, bi, :, :], ALU.add)
                        k.tt(k.dve, t1, t1[PC, :, :], psqS, psqS[PC, :].rearrange("p (h e) -> p h e", h=4),
                             E, bc(E[PC, b0 + bi, 0:4], [C, 4, 128], 2), ALU.mult)
                    U.append(r1)

                    def r2(bi=bi):
                        for h in range(4):
                            k.mm(psNZ, psNZ[PC, h * 128:(h + 1) * 128], NTf, NTfv[:, bi * 4 + h, :], Z, Z[PC, h, :], inc=(h == 3))
                        k.tt(k.dve, vnew, vnew[PC, :, :], psNZ, psNZ[PC, :].rearrange("p (h e) -> p h e", h=4),
                             beta, bc(beta[PC, b0 + bi, :], [C, 4, 128], 2), ALU.mult)
                    U.append(r2)

                    def r3(bi=bi, Sv=Sv, sq_=sq_):
                        for h in range(4):
                            k.mm(psdS, psdS[:, h * 128:(h + 1) * 128], kdecp, kdecp[PC, bi, h, :], vnew, vnew[PC, h, :], inc=(h == 3))
                        for h in range(4):
                            k.mm(psqkv, psqkv[PC, h * 128:(h + 1) * 128], qkT, qkTv[:, bi * 4 + h, :], vnew, vnew[PC, h, :], inc=(h == 3))
                        k.tt(k.dve, Sst, Sv, Sst, Sv, gt128, bc(gt128[:, b0 + bi, :], [128, 4, 128], 2), ALU.mult)
                        k.tt(k.dve, Sst, Sv, Sst, Sv, psdS, psdS[:, :].rearrange("p (h e) -> p h e", h=4), ALU.add)
                        k.cp(k.pool, Sbf, Sbf[:, sq_, :, :], Sst, Sv)
                        k.tt(k.dve, o_t, o_t[PC, :, :], psqkv, psqkv[PC, :].rearrange("p (h e) -> p h e", h=4), t1, t1[PC, :, :], ALU.add)
                    U.append(r3)

                    def r4(bi=bi, sq_=sq_, ci=ci):
                        for h in range(4):
                            k.actv(osq, osq[PC, :], o_t, o_t[PC, h, :], AF.Square, accum=(ss, ss[PC, h:h + 1]))
                        k.actv(rstd, rstd[PC, :], ss, ss[PC, :], AF.Ln, bias=NORM_EPS, scale=1.0 / 128.0)
                        k.actv(rstd, rstd[PC, :], rstd, rstd[PC, :], AF.Exp, scale=-0.5)
                        k.tt(k.pool, yb, yb[PC, :, :], o_t, o_t[PC, :, :], rstd, bc(rstd[PC, :], [C, 4, 128], 2), ALU.mult)
                        k.tt(k.pool, yb, yb[PC, :, :], yb, yb[PC, :, :], sgA, sgA[PC, b0 + bi, :].rearrange("p (h e) -> p h e", h=4), ALU.mult)
                        psT = k.ps[4]
                        for h in range(4):
                            k.tr(psT, psT[:, h * C:(h + 1) * C], yb, yb[PC, h, :], self.ident, self.ident[PC, PC], inc=(h == 3))
                        k.cp(k.act, ybT, ybT[:, :, 0:C], psT, psT[:, 0:4 * C].rearrange("p (h c) -> p h c", h=4))
                        tk = tok0 + sq_ * L + ci * C
                        k.dma(k.pool, self.YT[512:1024, tk:tk + C].rearrange("(m p) t -> p m t", p=128), ybT[:, :, 0:C],
                              out_t=self.YT, in_t=ybT)
                    U.append(r4)
                return U

            for u in prep_units(batches[0], 0, 0):
                u()
            for b_ in range(len(batches)):
                R = rec_units(batches[b_], b_ % 2, b_ * nb)
                N = prep_units(batches[b_ + 1], (b_ + 1) % 2, (b_ + 1) * nb) if b_ + 1 < len(batches) else []
                nleft = len(batches) - b_
                XU = [extra.pop(0) for _ in range((len(extra) + nleft - 1) // nleft)] if extra else []
                per = (len(N) + len(R) - 1) // len(R) if N else 0
                perx = (len(XU) + len(R) - 1) // len(R) if XU else 0
                for r in R:
                    r()
                    for _ in range(per):
                        if N:
                            N.pop(0)()
                    for _ in range(perx):
                        if XU:
                            XU.pop(0)()
                while N:
                    N.pop(0)()
                while XU:
                    XU.pop(0)()

        k.memset(k.pool, Sst, Sst[:, 0, :, :], 0.0)
        k.memset(k.pool, Sbf, Sbf[:, 0, :, :], 0.0)
        nst = T // 512
        for u in front_units(0, 512, 1, 512, False, True):
            u()
        for st in range(nst):
            nxt = front_units((st + 1) * 512, 512, 1, 512, False, False) if st + 1 < nst else []
            process(st * 512, 512, 1, 512, 64, False, st == 0, st == nst - 1, extra=nxt)
        k.dma(k.sp, self.nssm_p[l].rearrange("h a b -> a h b"), Sst[:, 0, :, :], in_t=Sst, is_output=True)
        for s in range(NSQ):
            k.dma(k.sp, Sst[:, s, :, :], self.state_ssm[l, s].rearrange("h a b -> a h b"), out_t=Sst)
            k.cp(k.pool, Sbf, Sbf[:, s, :, :], Sst, Sst[:, s, :, :])
        for u in front_units(T, NSQ * TS, NSQ, TS, True, True):
            u()
        process(T, NSQ * TS, NSQ, TS, TS, True, True, True)
        for s in range(NSQ):
            k.dma(k.sp, self.nssm_s[l, s].rearrange("h a b -> a h b"), Sst[:, s, :, :], in_t=Sst, is_output=True)
        k.end_pass()


    def layer_norm(self, r, n, gbc, bbc, junk, stat, out_t):
        k = self.k
        nc = self.nc
        P = slice(0, n)
        k.actv(junk, junk[P, :], r, r[P, :], AF.Identity, accum=(stat, stat[P, 0:1]))
        k.actv(junk, junk[P, :], r, r[P, :], AF.Square, accum=(stat, stat[P, 1:2]))
        k.ts(k.dve, stat, stat[P, 2:3], stat, stat[P, 0:1], -1.0 / D, None, ALU.mult)
        k.tt(k.dve, stat, stat[P, 3:4], stat, stat[P, 2:3], stat, stat[P, 2:3], ALU.mult)
        k.stt(k.dve, stat, stat[P, 4:5], stat, stat[P, 1:2], 1.0 / D, stat, stat[P, 3:4], ALU.mult, ALU.subtract)
        k.actv(stat, stat[P, 5:6], stat, stat[P, 4:5], AF.Sqrt, bias=LN_EPS)
        k.op(k.dve, lambda: nc.vector.reciprocal(stat[P, 6:7], stat[P, 5:6]), outs=[stat], ins=[stat])
        k.ts(k.dve, r, r[P, :], r, r[P, :], stat[P, 2:3], stat[P, 6:7], ALU.add, ALU.mult, extra_ins=[stat])
        k.tt(k.pool, r, r[P, :], r, r[P, :], gbc, gbc[P, :], ALU.mult)
        k.tt(k.pool, out_t, out_t[P, :], r, r[P, :], bbc, bbc[P, :], ALU.add)

    def p2(self, l):
        k = self.k
        nc = self.nc
        k.begin_pass()
        wsrc = self.w_in[l].rearrange("(kc p) e -> p kc e", p=128)
        Wg = k.sb("Wg", [128, 8, 2048], BF16)
        k.dma(k.pool, Wg[:, :, :], wsrc[:, :, O_GATES:O_GATES + 2048], out_t=Wg)
        Wpa = k.sb("Wpa", [128, 4, D], BF16)
        k.dma(k.pool, Wpa[:, :, :], self.w_proj_a[l].rearrange("(kc p) e -> p kc e", p=128), out_t=Wpa)
        Wpb = k.sb("Wpb", [128, 4, D], BF16)
        k.dma(k.pool, Wpb[:, :, :], self.w_proj_b[l].rearrange("(kc p) e -> p kc e", p=128), out_t=Wpb)
        Wo = k.sb("Wo", [128, 8, D], BF16)
        k.dma(k.pool, Wo[:, :, :], self.w_out[l].rearrange("(kc p) e -> p kc e", p=128), out_t=Wo)
        bg = k.sb("bg", [128, 2048], F32)
        k.dma(k.sp, bg[:, :], self.b_gate[l:l + 1, :].to_broadcast([128, 2048]), out_t=bg)
        gbc = k.sb("gbc", [128, D], F32)
        k.dma(k.sp, gbc[:, :], self.ln1_g[l:l + 1, :].to_broadcast([128, D]), out_t=gbc)
        bbc = k.sb("bbc", [128, D], F32)
        k.dma(k.sp, bbc[:, :], self.ln1_b[l:l + 1, :].to_broadcast([128, D]), out_t=bbc)
        xin = [k.sb(f"xin{i}", [128, D], F32) for i in range(2)]
        xT = [k.sb(f"xT{i}", [128, 8, 128], BF16) for i in range(2)]
        yT = [k.sb(f"yT{i}", [128, 8, 128], BF16) for i in range(2)]
        sgates = [k.sb(f"sgate{i}", [128, 2048], F32) for i in range(2)]
        mixeds = [k.sb(f"mixed{i}", [128, D], F32) for i in range(2)]
        tmps = [k.sb(f"tmp{i}", [128, 512], F32) for i in range(2)]
        mixT = k.sb("mixT", [128, 8, 128], BF16)
        r = [k.sb(f"r{i}", [128, D], F32) for i in range(2)]
        junk = k.sb("junk", [128, D], BF16)
        stat = k.sb("stat", [128, 8], F32)
        tiles = [(t * 128, 128) for t in range(T // 128)] + [(T, NSQ * TS)]

        def stage_x(ti):
            tok0, n = tiles[ti]
            P = slice(0, n)
            xi, xt, yt = xin[ti % 2], xT[ti % 2], yT[ti % 2]
            sgate, mixed = sgates[ti % 2], mixeds[ti % 2]
            self.load_xT(l, tok0, n, xi, xt, 0, k.act)
            k.dma(k.sp, yt[:, :, 0:n], self.YT[:, tok0:tok0 + n].rearrange("(kc p) t -> p kc t", p=128), out_t=yt, in_t=self.YT)
            for blk in range(4):
                ps = k.ps[blk % 2]
                for kc in range(8):
                    k.mm(ps, ps[P, :], xt, xt[:, kc, 0:n], Wg, Wg[:, kc, blk * 512:(blk + 1) * 512], start=(kc == 0), stop=(kc == 7))
                k.tt(k.dve, sgate, sgate[P, blk * 512:(blk + 1) * 512], ps, ps[P, :], bg, bg[P, blk * 512:(blk + 1) * 512], ALU.add)
                k.actv(sgate, sgate[P, blk * 512:(blk + 1) * 512], sgate, sgate[P, blk * 512:(blk + 1) * 512], AF.Sigmoid)
            for blk in range(2):
                psa = k.ps[4 + blk]
                psb = k.ps[6 + blk]
                tmp = tmps[blk]
                for kc in range(4):
                    k.mm(psa, psa[P, :], yt, yt[:, kc, 0:n], Wpa, Wpa[:, kc, blk * 512:(blk + 1) * 512], start=(kc == 0), stop=(kc == 3))
                for kc in range(4):
                    k.mm(psb, psb[P, :], yt, yt[:, 4 + kc, 0:n], Wpb, Wpb[:, kc, blk * 512:(blk + 1) * 512], start=(kc == 0), stop=(kc == 3))
                cs = slice(blk * 512, (blk + 1) * 512)
                k.tt(k.dve, mixed, mixed[P, cs], psa, psa[P, :], sgate, sgate[P, cs], ALU.mult)
                k.tt(k.dve, tmp, tmp[P, :], psb, psb[P, :], sgate, sgate[P, 1024 + blk * 512:1024 + (blk + 1) * 512], ALU.mult)
                k.tt(k.pool, mixed, mixed[P, cs], mixed, mixed[P, cs], tmp, tmp[P, :], ALU.add)

        def stage_y(ti):
            tok0, n = tiles[ti]
            P = slice(0, n)
            xi, mixed, rr = xin[ti % 2], mixeds[ti % 2], r[ti % 2]
            for grp in range(2):
                ps = k.ps[2 + grp]
                for kk in range(4):
                    kc = grp * 4 + kk
                    k.tr(ps, ps[:, kk * n:(kk + 1) * n], mixed, mixed[P, kc * 128:(kc + 1) * 128], self.ident, self.ident[P, P], inc=(kk == 3))
                k.cp(k.act, mixT, mixT[:, grp * 4:(grp + 1) * 4, 0:n], ps, ps[:, 0:4 * n].rearrange("p (a t) -> p a t", a=4))
            for blk in range(2):
                ps = k.ps[2 + blk]
                for kc in range(8):
                    k.mm(ps, ps[P, :], mixT, mixT[:, kc, 0:n], Wo, Wo[:, kc, blk * 512:(blk + 1) * 512], start=(kc == 0), stop=(kc == 7))
                cs = slice(blk * 512, (blk + 1) * 512)
                k.stt(k.dve, rr, rr[P, cs], xi, xi[P, cs], ALPHA, ps, ps[P, :], ALU.mult, ALU.add)
            self.layer_norm(rr, n, gbc, bbc, junk, stat, rr)
            k.dma(k.pool, self.X1[tok0:tok0 + n, :], rr[P, :], out_t=self.X1, in_t=rr)

        stage_x(0)
        for ti in range(len(tiles)):
            if ti + 1 < len(tiles):
                stage_x(ti + 1)
            stage_y(ti)
        k.end_pass()

    def p3(self, l):
        k = self.k
        nc = self.nc
        k.begin_pass()
        Wup = k.sb("Wup", [128, 8, 2 * DFF], BF16)
        usrc = self.w_up[l].rearrange("(kc p) e -> p kc e", p=128)
        for q4 in range(4):
            k.dma(k.pool, Wup[:, :, q4 * 1408:(q4 + 1) * 1408], usrc[:, :, q4 * 1408:(q4 + 1) * 1408], out_t=Wup)
        Wdn = k.sb("Wdn", [128, 22, D], BF16)
        k.dma(k.pool, Wdn[:, :, :], self.w_down[l].rearrange("(kc p) e -> p kc e", p=128), out_t=Wdn)
        gbc = k.sb("gbc", [128, D], F32)
        k.dma(k.sp, gbc[:, :], self.ln2_g[l:l + 1, :].to_broadcast([128, D]), out_t=gbc)
        bbc = k.sb("bbc", [128, D], F32)
        k.dma(k.sp, bbc[:, :], self.ln2_b[l:l + 1, :].to_broadcast([128, D]), out_t=bbc)
        xin = [k.sb(f"xin{i}", [128, D], F32) for i in range(2)]
        xT = k.sb("xT", [128, 8, 512], BF16)
        fT = k.sb("fT", [128, 22, 512], BF16)
        tmp = [k.sb(f"tmp{i}", [128, 512], F32) for i in range(2)]
        rrs = [k.sb(f"r{i}", [128, D], F32) for i in range(2)]
        junk = k.sb("junk", [128, D], BF16)
        stat = k.sb("stat", [128, 8], F32)
        last = (l == DEPTH - 1)
        sts = [(st * 512, 512) for st in range(T // 512)] + [(T, NSQ * TS)]

        def subs_of(NS):
            subs = []
            done = 0
            while done < NS:
                n = min(128, NS - done)
                subs.append((done, n))
                done += n
            return subs

        xcnt = [0]

        def transposes(si):
            tok0, NS = sts[si]
            for (c0, n) in subs_of(NS):
                xi = xin[xcnt[0] % 2]
                xcnt[0] += 1
                k.dma(k.sp, xi[0:n, :], self.X1[tok0 + c0:tok0 + c0 + n, :], out_t=xi, in_t=self.X1)
                for grp in range(2):
                    ps = k.ps[6 + grp]
                    for kk in range(4):
                        kc = grp * 4 + kk
                        k.tr(ps, ps[:, kk * n:(kk + 1) * n], xi, xi[0:n, kc * 128:(kc + 1) * 128], self.ident, self.ident[0:n, 0:n], inc=(kk == 3))
                    k.cp(k.act, xT, xT[:, grp * 4:(grp + 1) * 4, c0:c0 + n], ps, ps[:, 0:4 * n].rearrange("p (a t) -> p a t", a=4))

        def up(si):
            tok0, NS = sts[si]
            for fc in range(22):
                psA = k.ps[(2 * fc) % 4]
                psB = k.ps[(2 * fc + 1) % 4]
                for kc in range(8):
                    k.mm(psA, psA[:, 0:NS], Wup, Wup[:, kc, fc * 128:(fc + 1) * 128], xT, xT[:, kc, 0:NS], start=(kc == 0), stop=(kc == 7))
                for kc in range(8):
                    k.mm(psB, psB[:, 0:NS], Wup, Wup[:, kc, DFF + fc * 128:DFF + (fc + 1) * 128], xT, xT[:, kc, 0:NS], start=(kc == 0), stop=(kc == 7))
                tm = tmp[fc % 2]
                k.actv(tm, tm[:, 0:NS], psA, psA[:, 0:NS], AF.Silu)
                k.tt(k.dve, fT, fT[:, fc, 0:NS], tm, tm[:, 0:NS], psB, psB[:, 0:NS], ALU.mult)

        rcnt = [0]

        def down(si):
            tok0, NS = sts[si]
            for (c0, n) in subs_of(NS):
                P = slice(0, n)
                rr = rrs[rcnt[0] % 2]
                rcnt[0] += 1
                k.dma(k.sp, rr[0:n, :], self.X1[tok0 + c0:tok0 + c0 + n, :], out_t=rr, in_t=self.X1)
                for blk in range(2):
                    ps = k.ps[4 + blk]
                    for fc in range(22):
                        k.mm(ps, ps[P, :], fT, fT[:, fc, c0:c0 + n], Wdn, Wdn[:, fc, blk * 512:(blk + 1) * 512], start=(fc == 0), stop=(fc == 21))
                    cs = slice(blk * 512, (blk + 1) * 512)
                    k.stt(k.dve, rr, rr[P, cs], rr, rr[P, cs], ALPHA, ps, ps[P, :], ALU.mult, ALU.add)
                self.layer_norm(rr, n, gbc, bbc, junk, stat, rr)
                t0 = tok0 + c0
                if not last:
                    k.dma(k.pool, self.X2[t0:t0 + n, :], rr[P, :], out_t=self.X2, in_t=rr)
                elif t0 < T:
                    k.dma(k.pool, self.y_p[t0:t0 + n, :], rr[P, :], in_t=rr, is_output=True)
                else:
                    k.dma(k.pool, self.y_s[t0 - T:t0 - T + n, :], rr[P, :], in_t=rr, is_output=True)

        transposes(0)
        for si in range(len(sts)):
            up(si)
            if si + 1 < len(sts):
                transposes(si + 1)
            down(si)
        k.end_pass()

    def build(self):
        k = self.k
        self.load_consts()
        sa = self.stop_after
        only = sa[0][:-5] if (sa is not None and sa[0].endswith("_only")) else None
        for l in range(DEPTH):
            for name, fn in (("p1a", self.p1a), ("p1b", self.p1b), ("p2", self.p2), ("p3", self.p3)):
                if only is not None and name != only:
                    continue
                fn(l)
                if sa is not None and sa[1] == l and (sa[0] == name or only == name):
                    break
            else:
                continue
            break
        k.finish()


def shard_inputs(inputs, c):
    f = lambda a: np.ascontiguousarray(a, dtype=np.float32)
    sl = slice(NSQ * c, NSQ * (c + 1))
    m = {
        "x_p": f(inputs["x_prompt"][c]),
        "x_s": f(inputs["x_sample"][sl].reshape(NSQ * TS, D)),
        "cache_k": f(inputs["cache_k"][:, sl].reshape(DEPTH, NSQ, T, 128)),
        "cache_v": f(inputs["cache_v"][:, sl].reshape(DEPTH, NSQ, T, 128)),
        "cache_ki": f(inputs["cache_kidx"][:, sl]),
        "state_conv": f(inputs["state_conv"][:, sl]),
        "state_ssm": f(inputs["state_ssm"][:, sl]),
    }
    for n in ["w_in", "b_gate", "conv_w", "a_log", "dt_bias", "gdn_norm_w", "w_proj_a", "w_proj_b", "w_out",
              "ln1_g", "ln1_b", "w_up", "w_down", "ln2_g", "ln2_b"]:
        m[n] = f(inputs[n])
    for n, v in make_consts().items():
        m["c_" + n] = v
    return m


def run(inputs, debug=False, stop_after=None, trace=False):
    prog = Prog(debug=debug, stop_after=stop_after)
    in_maps = [shard_inputs(inputs, c) for c in range(8)]
    res = run_bass_kernel_spmd(prog.nc, in_maps, core_ids=list(range(8)), trace=trace)
    return res


def kernel(**inputs):
    res = run(inputs)
    r = res.results
    cat = lambda n: np.stack([r[c][n] for c in range(8)], axis=0)
    y_p = cat("y_p")
    y_s = cat("y_s").reshape(32, TS, D)
    nk_p = np.transpose(cat("nk_p"), (1, 0, 2, 3)).reshape(DEPTH, 8, T, 2, 64)
    nv_p = np.transpose(cat("nv_p"), (1, 0, 2, 3)).reshape(DEPTH, 8, T, 2, 64)
    nki_p = np.transpose(cat("nki_p"), (1, 0, 2, 3))
    nconv_p = np.transpose(cat("nconv_p"), (1, 0, 2, 3))
    nssm_p = np.transpose(cat("nssm_p"), (1, 0, 2, 3, 4))
    nk_s = np.transpose(cat("nk_s"), (1, 0, 2, 3)).reshape(DEPTH, 32, TS, 2, 64)
    nv_s = np.transpose(cat("nv_s"), (1, 0, 2, 3)).reshape(DEPTH, 32, TS, 2, 64)
    nki_s = np.transpose(cat("nki_s"), (1, 0, 2, 3)).reshape(DEPTH, 32, TS, 64)
    nconv_s = np.transpose(cat("nconv_s"), (1, 0, 2, 3, 4)).reshape(DEPTH, 32, 3, 1536)
    nssm_s = np.transpose(cat("nssm_s"), (1, 0, 2, 3, 4, 5)).reshape(DEPTH, 32, 4, 128, 128)
    return tuple(np.ascontiguousarray(a, dtype=np.float32) for a in
                 (y_p, y_s, nk_p, nv_p, nki_p, nconv_p, nssm_p, nk_s, nv_s, nki_s, nconv_s, nssm_s))
```
